# Optimizing a Trainium2 kernel written in Bass

```python
import jax, jax.numpy as jnp
from jax import lax
import numpy as np

D_MODEL = 1024
BATCH = 8
SEQ = 4096
DEPTH = 2

CTX_LEN = 256
GRID_W = 64
D_FF = 2816
N_MOD = 9
EPS = 1e-6
ROPE_BASE = 10000.0
Q_BLOCK = 128
MLA_HEADS = 8
MLA_NOPE = 64
MLA_ROPE = 32
MLA_V = 64
MLA_Q_LORA = 384
MLA_KV_LORA = 256
CONV_CH = 512
CONV_WIDTH = 31
HEAD_DIM = 64
GQA_HEADS = 8
GQA_KV_HEADS = 2
NA_HEADS = 8
WIN_H = 8
WIN_W = 16

EVEN_KV_COLS = MLA_KV_LORA + MLA_ROPE
EVEN_CONV_OFF = EVEN_KV_COLS + MLA_Q_LORA
EVEN_COLS = EVEN_CONV_OFF + 2 * CONV_CH
GQA_KV_W = GQA_KV_HEADS * HEAD_DIM
GQA_Q_W = GQA_HEADS * HEAD_DIM
NA_W = NA_HEADS * HEAD_DIM
ODD_KV_COLS = 2 * GQA_KV_W + 2 * NA_W
ODD_COLS = ODD_KV_COLS + GQA_Q_W + NA_W
MIX_OUT = MLA_HEADS * MLA_V + CONV_CH

kernel_name = "hybrid_conv_mla_gqa_natten_dit_block"


def rms_norm(x, g):
    xf = x.astype(jnp.float32)
    y = xf * lax.rsqrt(jnp.mean(xf * xf, axis=-1, keepdims=True) + EPS)
    return (y * g.astype(jnp.float32)).astype(x.dtype)


def layer_norm(x, g, b):
    xf = x.astype(jnp.float32)
    mu = jnp.mean(xf, axis=-1, keepdims=True)
    var = jnp.mean(jnp.square(xf - mu), axis=-1, keepdims=True)
    y = (xf - mu) * lax.rsqrt(var + EPS)
    return (y * g.astype(jnp.float32) + b.astype(jnp.float32)).astype(x.dtype)


def axial_rope_table(n_tokens, dim):
    t = jnp.arange(n_tokens)
    row = (t // GRID_W).astype(jnp.float32)
    col = (t % GRID_W).astype(jnp.float32)
    n_pairs = dim // 4
    inv = ROPE_BASE ** (-jnp.arange(n_pairs, dtype=jnp.float32) / n_pairs)
    ang = jnp.concatenate([row[:, None] * inv, col[:, None] * inv], axis=-1)
    return jnp.cos(ang), jnp.sin(ang)


def apply_rope(x, cos, sin):
    xp = x.reshape(*x.shape[:-1], x.shape[-1] // 2, 2)
    x1, x2 = xp[..., 0], xp[..., 1]
    cs = cos[None, :, None, :].astype(x.dtype)
    sn = sin[None, :, None, :].astype(x.dtype)
    return jnp.stack([x1 * cs - x2 * sn, x1 * sn + x2 * cs], axis=-1).reshape(x.shape)


def _attend(q, k, v):
    scale = q.shape[-1] ** -0.5
    s = jnp.einsum('bqhgd,bkhd->bhgqk', q, k).astype(jnp.float32) * scale
    p = jax.nn.softmax(s, axis=-1).astype(v.dtype)
    return jnp.einsum('bhgqk,bkhd->bqhgd', p, v)


def context_attention(q, k, v):
    B, L, Hk, G, _ = q.shape
    return _attend(q, k, v).reshape(B, L, Hk * G, v.shape[-1])


def latent_attention(q, k, v, k_ctx, v_ctx):
    B, S, Hk, G, d = q.shape
    k_all = jnp.concatenate([k_ctx, k], axis=1)
    v_all = jnp.concatenate([v_ctx, v], axis=1)
    qb = q.reshape(B, S // Q_BLOCK, Q_BLOCK, Hk, G, d).transpose(1, 0, 2, 3, 4, 5)
    o = lax.map(lambda qi: _attend(qi, k_all, v_all), qb)
    return o.transpose(1, 0, 2, 3, 4, 5).reshape(B, S, Hk * G, v.shape[-1])


def neighbourhood_attention(q, k, v, k_ctx, v_ctx, rpb):
    B, S, H, d = q.shape
    rows = S // GRID_W
    kh = min(WIN_H, rows)
    scale = d ** -0.5
    qg = q.reshape(B, rows, GRID_W, H, d).transpose(1, 0, 2, 3, 4)
    kg = k.reshape(B, rows, GRID_W, H, d)
    vg = v.reshape(B, rows, GRID_W, H, d)
    col = jnp.arange(GRID_W)
    cs = jnp.clip(col - WIN_W // 2, 0, GRID_W - WIN_W)
    col_idx = cs[:, None] + jnp.arange(WIN_W)[None, :]
    dc = col_idx - col[:, None] + (WIN_W - 1)
    n_ctx = k_ctx.shape[1]

    def row_block(args):
        r, q_r = args
        rs = jnp.clip(r - kh // 2, 0, rows - kh)
        k_rows = lax.dynamic_slice_in_dim(kg, rs, kh, axis=1)
        v_rows = lax.dynamic_slice_in_dim(vg, rs, kh, axis=1)
        k_nb = k_rows[:, :, col_idx]
        v_nb = v_rows[:, :, col_idx]
        s_nb = jnp.einsum('bqhd,bjqwhd->bhqjw', q_r, k_nb).astype(jnp.float32) * scale
        dr = rs + jnp.arange(kh) - r + (WIN_H - 1)
        bias = rpb[:, dr[None, :, None], dc[:, None, :]]
        s_nb = (s_nb + bias[None].astype(jnp.float32)).reshape(B, H, GRID_W, kh * WIN_W)
        s_ctx = jnp.einsum('bqhd,bkhd->bhqk', q_r, k_ctx).astype(jnp.float32) * scale
        p = jax.nn.softmax(jnp.concatenate([s_ctx, s_nb], axis=-1), axis=-1).astype(v.dtype)
        p_ctx = p[..., :n_ctx]
        p_nb = p[..., n_ctx:].reshape(B, H, GRID_W, kh, WIN_W)
        return (jnp.einsum('bhqk,bkhd->bqhd', p_ctx, v_ctx)
                + jnp.einsum('bhqjw,bjqwhd->bqhd', p_nb, v_nb))

    o = lax.map(row_block, (jnp.arange(rows), qg))
    return o.transpose(1, 0, 2, 3, 4).reshape(B, S, H, d)


def swiglu(h, w_in, w_out):
    g, u = jnp.split(h @ w_in, 2, axis=-1)
    return (jax.nn.silu(g) * u) @ w_out


def conv_module(u, prm):
    u = u + prm["conv_glu_b"]
    a, g = jnp.split(u, 2, axis=-1)
    y = a * jax.nn.sigmoid(g)
    y = lax.conv_general_dilated(
        y, prm["conv_dw_w"][:, None, :].astype(y.dtype), window_strides=(1,),
        padding=[(CONV_WIDTH // 2, CONV_WIDTH // 2)],
        dimension_numbers=("NWC", "WIO", "NWC"), feature_group_count=CONV_CH) + prm["conv_dw_b"]
    y = layer_norm(y, prm["conv_ln_g"], prm["conv_ln_b"])
    return jax.nn.silu(y)


def mla_kv(pp, prm, rope):
    B, L, _ = pp.shape
    ckv = rms_norm(pp[..., :MLA_KV_LORA], prm["mla_kv_norm"])
    kvu = (ckv @ prm["mla_w_ukv"]).reshape(B, L, MLA_HEADS, MLA_NOPE + MLA_V)
    k_nope = rms_norm(kvu[..., :MLA_NOPE], prm["mla_k_gain"][:MLA_NOPE])
    v = kvu[..., MLA_NOPE:]
    k_rope = rms_norm(pp[..., MLA_KV_LORA:EVEN_KV_COLS], prm["mla_k_gain"][MLA_NOPE:])[:, :, None, :]
    if rope is not None:
        k_rope = apply_rope(k_rope, *rope)
    k = jnp.concatenate([k_nope, jnp.broadcast_to(k_rope, (B, L, MLA_HEADS, MLA_ROPE))], axis=-1)
    return k, v


def mla_q(pp, prm, rope):
    B, L, _ = pp.shape
    cq = rms_norm(pp[..., EVEN_KV_COLS:EVEN_CONV_OFF], prm["mla_q_norm"])
    q = (cq @ prm["mla_w_uq"]).reshape(B, L, MLA_HEADS, MLA_NOPE + MLA_ROPE)
    q_nope = rms_norm(q[..., :MLA_NOPE], prm["mla_q_gain"][:MLA_NOPE])
    q_rope = rms_norm(q[..., MLA_NOPE:], prm["mla_q_gain"][MLA_NOPE:])
    if rope is not None:
        q_rope = apply_rope(q_rope, *rope)
    return jnp.concatenate([q_nope, q_rope], axis=-1)[:, :, :, None, :]


def even_mixer(hl, hc, prm, ctx_out, ropes):
    rope_mla, _ = ropes
    B, S, _ = hl.shape
    pl = hl @ prm["w_in"]
    pc = hc @ (prm["w_in"] if ctx_out else prm["w_in"][:, :EVEN_KV_COLS])
    k_l, v_l = mla_kv(pl, prm, rope_mla)
    k_c, v_c = mla_kv(pc, prm, None)
    att_l = latent_attention(mla_q(pl, prm, rope_mla), k_l, v_l, k_c, v_c).reshape(B, S, -1)
    conv_l = conv_module(pl[..., EVEN_CONV_OFF:], prm)
    ol = jnp.concatenate([att_l, conv_l], axis=-1) @ prm["w_out"]
    oc = None
    if ctx_out:
        Bc, Lc, _ = hc.shape
        att_c = context_attention(mla_q(pc, prm, None), k_c, v_c).reshape(Bc, Lc, -1)
        conv_c = conv_module(pc[..., EVEN_CONV_OFF:], prm)
        oc = jnp.concatenate([att_c, conv_c], axis=-1) @ prm["w_out"]
    return ol, oc


def odd_kv(pp, prm, rope):
    B, L, _ = pp.shape
    o1, o2, o3 = GQA_KV_W, 2 * GQA_KV_W, 2 * GQA_KV_W + NA_W
    ck = rms_norm(pp[..., :o1].reshape(B, L, GQA_KV_HEADS, HEAD_DIM), prm["gqa_k_gain"])
    if rope is not None:
        ck = apply_rope(ck, *rope)
    cv = pp[..., o1:o2].reshape(B, L, GQA_KV_HEADS, HEAD_DIM)
    nk = rms_norm(pp[..., o2:o3].reshape(B, L, NA_HEADS, HEAD_DIM), prm["na_k_gain"])
    nv = pp[..., o3:ODD_KV_COLS].reshape(B, L, NA_HEADS, HEAD_DIM)
    return ck, cv, nk, nv


def odd_q(pp, prm, rope):
    B, L, _ = pp.shape
    cq = rms_norm(pp[..., ODD_KV_COLS:ODD_KV_COLS + GQA_Q_W].reshape(B, L, GQA_HEADS, HEAD_DIM), prm["gqa_q_gain"])
    if rope is not None:
        cq = apply_rope(cq, *rope)
    cq = cq.reshape(B, L, GQA_KV_HEADS, GQA_HEADS // GQA_KV_HEADS, HEAD_DIM)
    nq = rms_norm(pp[..., ODD_KV_COLS + GQA_Q_W:].reshape(B, L, NA_HEADS, HEAD_DIM), prm["na_q_gain"])
    return cq, nq


def odd_mixer(hl, hc, prm, ctx_out, ropes):
    _, rope_hd = ropes
    B, S, _ = hl.shape
    pl = hl @ prm["w_in"]
    pc = hc @ (prm["w_in"] if ctx_out else prm["w_in"][:, :ODD_KV_COLS])
    ck_l, cv_l, nk_l, nv_l = odd_kv(pl, prm, rope_hd)
    ck_c, cv_c, nk_c, nv_c = odd_kv(pc, prm, None)
    cq_l, nq_l = odd_q(pl, prm, rope_hd)
    gqa_l = latent_attention(cq_l, ck_l, cv_l, ck_c, cv_c).reshape(B, S, -1)
    na_l = neighbourhood_attention(nq_l, nk_l, nv_l, nk_c, nv_c, prm["na_rpb"]).reshape(B, S, -1)
    ol = jnp.concatenate([gqa_l, na_l], axis=-1) @ prm["w_out"]
    oc = None
    if ctx_out:
        Bc, Lc, _ = hc.shape
        cq_c, nq_c = odd_q(pc, prm, None)
        gqa_c = context_attention(cq_c, ck_c, cv_c).reshape(Bc, Lc, -1)
        na_c = context_attention(nq_c[:, :, :, None, :], nk_c, nv_c).reshape(Bc, Lc, -1)
        oc = jnp.concatenate([gqa_c, na_c], axis=-1) @ prm["w_out"]
    return ol, oc


def adaln(cond, prm):
    mod = jax.nn.silu(cond) @ prm["mod_w"] + prm["mod_b"]
    return jnp.split(mod, N_MOD, axis=-1)


def modulate(h, shift, scale):
    return h * (1.0 + scale) + shift


def trunk_layer(xl, xc, c, c_ctx, prm, even, ctx_out, ropes):
    mod_l = adaln(c[:, None, :], prm)
    mod_c = adaln(c_ctx[None, None, :], prm)

    def half_ffn(x, m, name):
        sh, sc, g = m
        h = modulate(rms_norm(x, prm[name + "_norm"]), sh, sc)
        return x + 0.5 * g * swiglu(h, prm[name + "_w_in"], prm[name + "_w_out"])

    xl = half_ffn(xl, mod_l[0:3], "ffn1")
    xc = half_ffn(xc, mod_c[0:3], "ffn1")
    hl = modulate(rms_norm(xl, prm["mix_norm"]), mod_l[3], mod_l[4])
    hc = modulate(rms_norm(xc, prm["mix_norm"]), mod_c[3], mod_c[4])
    mixer = even_mixer if even else odd_mixer
    ol, oc = mixer(hl, hc, prm, ctx_out, ropes)
    xl = xl + mod_l[5] * ol
    xl = half_ffn(xl, mod_l[6:9], "ffn2")
    if ctx_out:
        xc = xc + mod_c[5] * oc
        xc = half_ffn(xc, mod_c[6:9], "ffn2")
    return xl, xc


def _layer_params(key, i):
    ks = iter(jax.random.split(key, 32))

    def nrm(shape, scale):
        return jax.random.normal(next(ks), shape, jnp.float32) * scale

    def gain(n):
        return 1.0 + 0.02 * jax.random.normal(next(ks), (n,), jnp.float32)

    D = D_MODEL
    pre = f"l{i}_"
    p = {}
    p[pre + "mod_w"] = nrm((D, N_MOD * D), 0.5 * D ** -0.5)
    p[pre + "mod_b"] = nrm((N_MOD * D,), 0.02)
    p[pre + "ffn1_norm"] = gain(D)
    p[pre + "ffn1_w_in"] = nrm((D, 2 * D_FF), D ** -0.5)
    p[pre + "ffn1_w_out"] = nrm((D_FF, D), D_FF ** -0.5)
    p[pre + "mix_norm"] = gain(D)
    if i % 2 == 0:
        p[pre + "w_in"] = nrm((D, EVEN_COLS), D ** -0.5)
        p[pre + "mla_q_norm"] = gain(MLA_Q_LORA)
        p[pre + "mla_w_uq"] = nrm((MLA_Q_LORA, MLA_HEADS * (MLA_NOPE + MLA_ROPE)), MLA_Q_LORA ** -0.5)
        p[pre + "mla_kv_norm"] = gain(MLA_KV_LORA)
        p[pre + "mla_w_ukv"] = nrm((MLA_KV_LORA, MLA_HEADS * (MLA_NOPE + MLA_V)), MLA_KV_LORA ** -0.5)
        p[pre + "mla_q_gain"] = gain(MLA_NOPE + MLA_ROPE)
        p[pre + "mla_k_gain"] = gain(MLA_NOPE + MLA_ROPE)
        p[pre + "conv_glu_b"] = nrm((2 * CONV_CH,), 0.02)
        p[pre + "conv_dw_w"] = nrm((CONV_WIDTH, CONV_CH), CONV_WIDTH ** -0.5)
        p[pre + "conv_dw_b"] = nrm((CONV_CH,), 0.02)
        p[pre + "conv_ln_g"] = gain(CONV_CH)
        p[pre + "conv_ln_b"] = nrm((CONV_CH,), 0.02)
    else:
        p[pre + "w_in"] = nrm((D, ODD_COLS), D ** -0.5)
        p[pre + "gqa_q_gain"] = gain(HEAD_DIM)
        p[pre + "gqa_k_gain"] = gain(HEAD_DIM)
        p[pre + "na_q_gain"] = gain(HEAD_DIM)
        p[pre + "na_k_gain"] = gain(HEAD_DIM)
        p[pre + "na_rpb"] = nrm((NA_HEADS, 2 * WIN_H - 1, 2 * WIN_W - 1), 0.1)
    p[pre + "w_out"] = nrm((MIX_OUT, D), MIX_OUT ** -0.5)
    p[pre + "ffn2_norm"] = gain(D)
    p[pre + "ffn2_w_in"] = nrm((D, 2 * D_FF), D ** -0.5)
    p[pre + "ffn2_w_out"] = nrm((D_FF, D), D_FF ** -0.5)
    return p


def setup_inputs(seed: int = 0) -> dict:
    key = jax.random.key(seed)
    k_x, k_c, k_ctx, k_cc, k_layers = jax.random.split(key, 5)
    inputs = {
        "x": jax.random.normal(k_x, (BATCH, SEQ, D_MODEL), jnp.float32),
        "c": jax.random.normal(k_c, (BATCH, D_MODEL), jnp.float32),
        "ctx": jax.random.normal(k_ctx, (BATCH, CTX_LEN, D_MODEL), jnp.float32),
        "c_ctx": jax.random.normal(k_cc, (D_MODEL,), jnp.float32),
    }
    layer_keys = jax.random.split(k_layers, DEPTH)
    for i in range(DEPTH):
        inputs.update(_layer_params(layer_keys[i], i))
    return inputs


def reference(x, c, ctx, c_ctx,
              l0_mod_w, l0_mod_b, l0_ffn1_norm, l0_ffn1_w_in, l0_ffn1_w_out, l0_mix_norm, l0_w_in,
              l0_mla_q_norm, l0_mla_w_uq, l0_mla_kv_norm, l0_mla_w_ukv, l0_mla_q_gain, l0_mla_k_gain,
              l0_conv_glu_b, l0_conv_dw_w, l0_conv_dw_b, l0_conv_ln_g, l0_conv_ln_b,
              l0_w_out, l0_ffn2_norm, l0_ffn2_w_in, l0_ffn2_w_out,
              l1_mod_w, l1_mod_b, l1_ffn1_norm, l1_ffn1_w_in, l1_ffn1_w_out, l1_mix_norm, l1_w_in,
              l1_gqa_q_gain, l1_gqa_k_gain, l1_na_q_gain, l1_na_k_gain, l1_na_rpb,
              l1_w_out, l1_ffn2_norm, l1_ffn2_w_in, l1_ffn2_w_out):
    layers = (
        dict(mod_w=l0_mod_w, mod_b=l0_mod_b, ffn1_norm=l0_ffn1_norm, ffn1_w_in=l0_ffn1_w_in,
             ffn1_w_out=l0_ffn1_w_out, mix_norm=l0_mix_norm, w_in=l0_w_in,
             mla_q_norm=l0_mla_q_norm, mla_w_uq=l0_mla_w_uq, mla_kv_norm=l0_mla_kv_norm,
             mla_w_ukv=l0_mla_w_ukv, mla_q_gain=l0_mla_q_gain, mla_k_gain=l0_mla_k_gain,
             conv_glu_b=l0_conv_glu_b, conv_dw_w=l0_conv_dw_w, conv_dw_b=l0_conv_dw_b,
             conv_ln_g=l0_conv_ln_g, conv_ln_b=l0_conv_ln_b, w_out=l0_w_out,
             ffn2_norm=l0_ffn2_norm, ffn2_w_in=l0_ffn2_w_in, ffn2_w_out=l0_ffn2_w_out),
        dict(mod_w=l1_mod_w, mod_b=l1_mod_b, ffn1_norm=l1_ffn1_norm, ffn1_w_in=l1_ffn1_w_in,
             ffn1_w_out=l1_ffn1_w_out, mix_norm=l1_mix_norm, w_in=l1_w_in,
             gqa_q_gain=l1_gqa_q_gain, gqa_k_gain=l1_gqa_k_gain, na_q_gain=l1_na_q_gain,
             na_k_gain=l1_na_k_gain, na_rpb=l1_na_rpb, w_out=l1_w_out,
             ffn2_norm=l1_ffn2_norm, ffn2_w_in=l1_ffn2_w_in, ffn2_w_out=l1_ffn2_w_out),
    )
    S = x.shape[1]
    ropes = (axial_rope_table(S, MLA_ROPE), axial_rope_table(S, HEAD_DIM))
    xl, xc = x, ctx
    for i in range(DEPTH):
        xl, xc = trunk_layer(xl, xc, c, c_ctx, layers[i], even=(i % 2 == 0),
                             ctx_out=(i < DEPTH - 1), ropes=ropes)
    return xl
```

```python
import numpy as np
import concourse.bass as bass
import concourse.mybir as mybir
from concourse.bass_utils import run_bass_kernel_spmd

F32 = mybir.dt.float32
BF16 = mybir.dt.bfloat16
AF = mybir.ActivationFunctionType
ALU = mybir.AluOpType

D = 1024
S = 4096
NCTX = 256
TALL = S + NCTX
DFF = 2816
NFC = DFF // 128
EPS = 1e-6
GRID_W = 64
NDMASEM = 40
TILES = [(1, 0, 256)] + [(0, 256 + 512 * i, 512) for i in range(8)]


class Op:
    __slots__ = ("stream", "kind", "fn", "idx", "event", "signal", "cclock", "waits", "slot")


class _Rec:
    def __init__(self):
        self.calls = []

    def __getattr__(self, name):
        def f(*a, **kw):
            self.calls.append((name, a, kw))
            return None
        return f


def _replay(calls):
    def fn(e):
        ins = None
        for name, a, kw in calls:
            ins = getattr(e, name)(*a, **kw)
        return ins
    return fn


class Prog:
    STREAMS = ("pe", "act", "dve", "pool", "sp")

    def __init__(self, nc):
        self.nc = nc
        self.ops = {s: [] for s in self.STREAMS}
        self.clock = {s: {} for s in self.STREAMS}
        self.lastw = {}
        self.readers = {}
        self.ncomp = {s: 0 for s in self.STREAMS}
        self.pending = {}
        self.dma_slot_last = [None] * NDMASEM
        self.dma_slot_cnt = [0] * NDMASEM
        self.dma_rr = 0
        self.dmas_since_barrier = []
        self.last_comp = {}
        self.nops = 0
        self.maxops = None

    def _add(self, stream, kind, fn, reads, writes):
        if self.maxops is not None and self.nops >= self.maxops and fn is not None:
            return None
        op = Op()
        op.stream = stream
        op.kind = kind
        if fn is not None:
            rec = _Rec()
            fn(rec)
            fn = _replay(rec.calls)
        op.fn = fn
        op.signal = False
        op.slot = None
        self.nops += 1
        deps = []
        for r in reads:
            w = self.lastw.get(r)
            if w is not None:
                deps.append((w, True))
        for w_ in writes:
            lw = self.lastw.get(w_)
            if lw is not None:
                deps.append((lw, False))
            for rd in self.readers.get(w_, ()):
                deps.append((rd, False))
        for r in reads:
            self.readers.setdefault(r, []).append(op)
        for w_ in writes:
            self.readers[w_] = []
            self.lastw[w_] = op
        pend = self.pending.pop(stream, None)
        if pend:
            deps.extend((p, True) for p in pend)
        if kind == "d":
            slot = self.dma_rr
            self.dma_rr = (self.dma_rr + 1) % NDMASEM
            prev = self.dma_slot_last[slot]
            if prev is not None:
                deps.append((prev, True))
            self.dma_slot_cnt[slot] += 1
            self.dma_slot_last[slot] = op
            op.slot = slot
            op.event = (("d", slot), 16 * self.dma_slot_cnt[slot])
            self.dmas_since_barrier.append(op)
        else:
            self.ncomp[stream] += 1
            op.event = (("e", stream), self.ncomp[stream])
            self.last_comp[stream] = op
        clk = self.clock[stream]
        waits = {}
        for d, raw in deps:
            if d is op:
                continue
            if d.stream == stream and d.kind == "c":
                if stream == "pe" or not raw:
                    continue
            key, val = d.event
            if clk.get(key, 0) >= val:
                continue
            if waits.get(key, (0,))[0] < val:
                waits[key] = (val, d)
            for k, v in d.cclock.items():
                if clk.get(k, 0) < v:
                    clk[k] = v
        for key, (val, d) in waits.items():
            d.signal = True
        op.waits = {k: v[0] for k, v in waits.items()}
        cc = dict(clk)
        cc[op.event[0]] = op.event[1]
        op.cclock = cc
        if kind == "c" and stream != "pe":
            pass
        self.ops[stream].append(op)
        return op

    def pe(self, fn, reads=(), writes=()):
        return self._add("pe", "c", fn, reads, writes)

    def act(self, fn, reads=(), writes=()):
        return self._add("act", "c", fn, reads, writes)

    def dve(self, fn, reads=(), writes=()):
        return self._add("dve", "c", fn, reads, writes)

    def pool(self, fn, reads=(), writes=()):
        return self._add("pool", "c", fn, reads, writes)

    def dma(self, q, out, in_, reads=(), writes=()):
        return self._add(q, "d", lambda e: e.dma_start(out=out, in_=in_), reads, writes)

    def barrier(self):
        lst = list(self.last_comp.values()) + self.dmas_since_barrier
        self.dmas_since_barrier = []
        for s in self.STREAMS:
            self.pending[s] = list(lst)

    def emit(self):
        nc = self.nc
        self.barrier()
        fin = self._add("sp", "c", None, (), ())
        esem = {s: nc.alloc_semaphore("es_" + s) for s in self.STREAMS}
        dsem = [nc.alloc_semaphore(f"ds{i}") for i in range(NDMASEM)]
        sigcount = {}
        for s in self.STREAMS:
            cnt = 0
            m = {}
            for o in self.ops[s]:
                if o.kind == "c":
                    if o.signal:
                        cnt += 1
                    m[o.event[1]] = (cnt, o.signal)
            sigcount[s] = m

        def resolve(key, val):
            if key[0] == "d":
                return dsem[key[1]], val
            cnt, sig = sigcount[key[1]][val]
            assert sig
            return esem[key[1]], cnt

        with nc.Block() as block:
            decos = {"pe": block.tensor, "act": block.scalar, "dve": block.vector,
                     "pool": block.gpsimd, "sp": block.sync}
            for s in self.STREAMS:
                ops = self.ops[s]

                def body(eng, ops=ops, s=s):
                    for o in ops:
                        for key, val in o.waits.items():
                            sem, v = resolve(key, val)
                            eng.wait_ge(sem, v)
                        if o.fn is None:
                            continue
                        ins = o.fn(eng)
                        if o.kind == "d":
                            ins.then_inc(dsem[o.slot], 16)
                        elif o.signal:
                            ins.then_inc(esem[s], 1)
                decos[s](body)


class Arena:
    def __init__(self, nc, limit=229376):
        self.nc = nc
        self.off = 16640
        self.limit = limit
        self.n = 0

    def alloc(self, name, shape, dtype):
        esz = 4 if dtype == F32 else 2
        sz = esz
        for d in shape[1:]:
            sz *= d
        sz = (sz + 63) // 64 * 64
        assert self.off + sz <= self.limit, (name, self.off, sz)
        self.n += 1
        t = self.nc.alloc_sbuf_tensor_at(f"{name}_{self.n}", list(shape), dtype, offset=self.off)
        self.off += sz
        return t

    def mark(self):
        return self.off

    def release(self, m):
        self.off = m


def kmajor(w, ncols_pad=None):
    K, N = w.shape
    return np.ascontiguousarray(w.reshape(K // 128, 128, N).transpose(1, 0, 2))


def pvec(v):
    return np.ascontiguousarray(v.reshape(-1, 128).T)


class K:
    pass


def declare_inputs(k, nc):
    k.inp = {}
    k.inshape = {}

    def din(name, shape):
        k.inp[name] = nc.dram_tensor(name, list(shape), F32, kind="ExternalInput").ap()
        k.inshape[name] = tuple(shape)

    din("x", [S, D])
    din("ctx", [NCTX, D])
    din("cT", [128, 8, 2])
    din("cmat", [128, 4, 128])
    for l in range(2):
        din(f"l{l}_mod_w", [18, 128, 8 * 512])
        din(f"l{l}_mod_bT", [128, 72])
        din(f"l{l}_ng", [128, 3, 8])
        for w in (1, 2):
            din(f"l{l}_f{w}_win", [NFC, 128, 8 * 256])
            din(f"l{l}_f{w}_wout", [8, 128, NFC * 128])


def host_inputs(inputs, b):
    m = {}
    m["x"] = np.ascontiguousarray(inputs["x"][b])
    m["ctx"] = np.ascontiguousarray(inputs["ctx"][b])
    cT = np.stack([pvec(inputs["c"][b]), pvec(inputs["c_ctx"])], axis=-1)
    m["cT"] = np.ascontiguousarray(cT.astype(np.float32))
    return m


_SHARED = {}


def host_shared(inputs):
    m = {}
    cm = np.zeros((128, 4, 128), np.float32)
    cm[:, 0, :] = np.eye(128, dtype=np.float32)
    cm[:, 1, :] = 1.0
    for g in range(2):
        cm[g * 64:(g + 1) * 64, 2, g * 64:(g + 1) * 64] = 1.0
    cm[0:64, 3, 0:64] = 1.0
    cm[64:96, 3, 64:96] = 1.0
    m["cmat"] = cm
    for l in range(2):
        p = f"l{l}_"
        mw = inputs[p + "mod_w"]
        m[p + "mod_w"] = np.ascontiguousarray(
            mw.reshape(8, 128, 18, 512).transpose(2, 1, 0, 3).reshape(18, 128, 8 * 512))
        m[p + "mod_bT"] = pvec(inputs[p + "mod_b"])
        m[p + "ng"] = np.ascontiguousarray(np.stack(
            [pvec(inputs[p + "ffn1_norm"]), pvec(inputs[p + "mix_norm"]), pvec(inputs[p + "ffn2_norm"])], axis=1))
        for w in (1, 2):
            wi = inputs[p + f"ffn{w}_w_in"]
            g = wi[:, :DFF].reshape(8, 128, NFC, 128)
            u = wi[:, DFF:].reshape(8, 128, NFC, 128)
            gu = np.stack([g, u], axis=3)
            m[p + f"f{w}_win"] = np.ascontiguousarray(gu.transpose(2, 1, 0, 3, 4).reshape(NFC, 128, 8 * 256))
            wo = inputs[p + f"ffn{w}_w_out"]
            m[p + f"f{w}_wout"] = np.ascontiguousarray(
                wo.reshape(NFC, 128, 8, 128).transpose(2, 1, 0, 3).reshape(8, 128, NFC * 128))
    m.update(host_mixer(inputs))
    return m


def build(stop_after=None, debug=False, skip=(), maxops=None):
    nc = bass.Bass("TRN2", target_bir_lowering=False)
    k = K()
    k.nc = nc
    k.debug = debug
    P = Prog(nc)
    k.maxops = maxops
    k.P = P
    A = Arena(nc)
    k.A = A
    declare_inputs(k, nc)
    k.out = nc.dram_tensor("out", [S, D], F32, kind="ExternalOutput").ap()
    k.dbg_outs = []
    skind = "ExternalOutput" if debug else "Internal"
    k.xTd = nc.dram_tensor("xTd", [8, 128, TALL], F32, kind=skind).ap()
    k.wbf = {}

    def precast(name):
        shape = k.inshape[name]
        t = nc.dram_tensor(name + "_bf", list(shape), BF16, kind="Internal").ap()
        k.wbf[name] = t
        for i in range(shape[0]):
            P.dma("pool", t[i], k.inp[name][i], reads=(), writes=[("wbf", name, i)])

    k.PS = [nc.alloc_psum_tensor(f"psb{i}", [128, 512], F32) for i in range(8)]

    k.ident = A.alloc("ident", [128, 128], F32)
    k.cbf = A.alloc("cbf", [128, 4, 128], BF16)
    k.modv = A.alloc("modv", [128, 2, 2, 72], F32)
    k.der = A.alloc("der", [128, 2, 2, 5, 8], F32)
    k.ng = A.alloc("ng", [128, 2, 3, 8], F32)
    P.dma("sp", k.ident[:], k.inp["cmat"][:, 0, :], writes=[("ident",)])
    P.dma("pool", k.cbf[:], k.inp["cmat"], writes=[("cbf",)])
    for l in range(2):
        P.dma("sp", k.ng[:, l], k.inp[f"l{l}_ng"], writes=[("ng", l)])
    k.ones_bf = k.cbf[:, 1, :]
    k.bd64_bf = k.cbf[:, 2, :]

    for l in range(2):
        for w in (1, 2):
            precast(f"l{l}_f{w}_win")
            precast(f"l{l}_f{w}_wout")

    declare_mixer_inputs(k, nc)
    for name in ("l0_wA", "l0_wkn", "l0_wv", "l0_wq", "l0_wo", "l1_wo", "l1_wA", "l1_wv", "l1_nab"):
        precast(name)

    def dscr(name, shape, dt=BF16):
        return nc.dram_tensor(name, list(shape), dt, kind=skind if name in ("attTd",) else "Internal").ap()
    k.yTd = dscr("yTd", [4, 128, TALL])
    k.KTd = dscr("KTd", [8, 96, TALL])
    k.QTd = dscr("QTd", [8, 96, TALL])
    k.Vd = dscr("Vd", [8, 128, 34, 128])
    k.attTd = dscr("attTd", [8, 128, TALL])
    k.KdupTd = dscr("KdupTd", [2, 128, TALL])
    k.QgTd = dscr("QgTd", [4, 128, TALL])
    k.nkTd = dscr("nkTd", [4, 128, TALL])
    k.nqTd = dscr("nqTd", [4, 128, TALL])
    k.VgD = dscr("VgD", [4, 128, 34, 128])
    k.NVd = dscr("NVd", [8, 128, 34, 128])

    phases = [
        ("mod", lambda: setup_mod(k)),
        ("l0f1", lambda: ffn_phase(k, 0, 1, TILES, src="tok", dst="xT")),
        ("l0proj", lambda: l0_proj_phase(k)),
        ("l0att", lambda: attn_phase(k, "mla")),
        ("l0conv", lambda: l0_conv_phase(k)),
        ("l0mix", lambda: mixout_phase(k, 0, TILES)),
        ("l0", lambda: ffn_phase(k, 0, 2, TILES, src="xT", dst="xT")),
        ("l1f1", lambda: ffn_phase(k, 1, 1, TILES, src="xT", dst="xT")),
        ("l1proj", lambda: l1_proj_phase(k)),
        ("l1gqa", lambda: attn_phase(k, "gqa")),
        ("l1na", lambda: l1_na_phase(k)),
        ("l1mix", lambda: mixout_phase(k, 1, TILES[1:])),
        ("full", lambda: ffn_phase(k, 1, 2, TILES[1:], src="xT", dst="tok")),
    ]
    for name, fn in phases:
        if name in skip:
            continue
        n0 = P.nops
        if maxops is not None and name == stop_after:
            P.maxops = P.nops + maxops
        fn()
        P.barrier()
        if debug:
            print("phase", name, "ops", n0, P.nops, flush=True)
        if stop_after == name:
            break
    return finish(k)


def finish(k):
    k.P.emit()
    return k.nc


def setup_mod(k):
    P, A, nc = k.P, k.A, k.nc
    m = A.mark()
    cT = A.alloc("cT", [128, 8, 2], F32)
    sc = A.alloc("sc", [128, 8, 2], F32)
    mb = A.alloc("mb", [128, 72], F32)
    wt = [A.alloc(f"mw{i}", [128, 8, 512], F32) for i in range(2)]
    P.dma("sp", cT[:], k.inp["cT"], writes=[("cT",)])
    P.act(lambda e: e.activation(out=sc[:], in_=cT[:], func=AF.Silu), reads=[("cT",)], writes=[("sc",)])
    n = 0
    for l in range(2):
        P.dma("sp", mb[:], k.inp[f"l{l}_mod_bT"], writes=[("mb",)])
        ps = k.PS[l]
        for blk in range(18):
            w = wt[n % 2]
            wtok = ("mw", n % 2)
            n += 1
            P.dma("sp", w[:], k.inp[f"l{l}_mod_w"][blk].rearrange("p (k n) -> p k n", k=8), writes=[wtok])

            def mm(e, w=w, blk=blk, ps=ps):
                ins = None
                for nn in range(4):
                    j = blk * 4 + nn
                    for kc in range(8):
                        ins = e.matmul(ps[:, 2 * j:2 * j + 2], lhsT=w[:, kc, nn * 128:(nn + 1) * 128],
                                       rhs=sc[:, kc, :], start=(kc == 0), stop=(kc == 7))
                return ins
            P.pe(mm, reads=[wtok, ("sc",)], writes=[("ps", l)])
        for who in range(2):
            src = ps[:, 0:144].rearrange("p (j w) -> p j w", w=2)[:, :, who]
            P.dve(lambda e, src=src, l=l, who=who: e.tensor_tensor(out=k.modv[:, l, who, :], in0=src, in1=mb[:], op=ALU.add),
                  reads=[("ps", l), ("mb",)], writes=[("modv", l, who)])
        for who in range(2):
            for gi, mi in enumerate((1, 4, 7)):
                P.dve(lambda e, l=l, who=who, gi=gi, mi=mi: e.scalar_tensor_tensor(
                    out=k.der[:, l, who, gi, :], in0=k.modv[:, l, who, mi * 8:mi * 8 + 8], scalar=1.0,
                    in1=k.ng[:, l, gi, :], op0=ALU.add, op1=ALU.mult),
                    reads=[("modv", l, who), ("ng", l)], writes=[("der", l, who, gi)])
            for gi, mi in ((3, 2), (4, 8)):
                P.dve(lambda e, l=l, who=who, gi=gi, mi=mi: e.tensor_scalar(
                    out=k.der[:, l, who, gi, :], in0=k.modv[:, l, who, mi * 8:mi * 8 + 8], scalar1=0.5,
                    scalar2=None, op0=ALU.mult),
                    reads=[("modv", l, who)], writes=[("der", l, who, gi)])
    if k.debug:
        dm = nc.dram_tensor("dbg_modv", [128, 2 * 2 * 72], F32, kind="ExternalOutput").ap()
        P.dma("sp", dm, k.modv[:].rearrange("p a b c -> p (a b c)"),
              reads=[("modv", l, w) for l in range(2) for w in range(2)], writes=[("dbg_modv",)])
    A.release(m)


def rms_norm_mod(k, xb, xtoks, T, gp, sh, hT, htok, tmp, r1, r2, rstd, sq):
    P = k.P
    P.pool(lambda e: e.tensor_tensor(out=sq[:, :, 0:T], in0=xb[:, :, 0:T], in1=xb[:, :, 0:T], op=ALU.mult),
           reads=xtoks, writes=[("sq",)])

    def mm(e):
        ins = None
        for c in range(8):
            ins = e.matmul(k.PS[6][:, 0:T], lhsT=k.ones_bf, rhs=sq[:, c, 0:T], start=(c == 0), stop=(c == 7))
        return ins
    P.pe(mm, reads=[("sq",), ("cbf",)], writes=[("ps", 6)])
    P.dve(lambda e: e.tensor_scalar(out=r1[:, 0:T], in0=k.PS[6][:, 0:T], scalar1=1.0 / D, scalar2=EPS,
                                    op0=ALU.mult, op1=ALU.add), reads=[("ps", 6)], writes=[("r1",)])
    P.act(lambda e: e.activation(out=r2[:, 0:T], in_=r1[:, 0:T], func=AF.Sqrt), reads=[("r1",)], writes=[("r2",)])
    P.dve(lambda e: e.reciprocal(out=rstd[:, 0:T], in_=r2[:, 0:T]), reads=[("r2",)], writes=[("rstd",)])
    for c in range(8):
        tb = tmp[c % 2]
        P.dve(lambda e, c=c, tb=tb: e.scalar_tensor_tensor(out=tb[:, 0:T], in0=xb[:, c, 0:T], scalar=gp[:, c:c + 1],
                                                           in1=rstd[:, 0:T], op0=ALU.mult, op1=ALU.mult),
              reads=[xtoks[c], ("rstd",)], writes=[("tmp", c % 2)])
        P.act(lambda e, c=c, tb=tb: e.activation(out=hT[:, c, 0:T], in_=tb[:, 0:T], func=AF.Identity,
                                                 bias=sh[:, c:c + 1], scale=1.0),
              reads=[("tmp", c % 2)], writes=[(htok, c)])


def ffn_phase(k, l, which, tiles, src, dst):
    P, A, nc = k.P, k.A, k.nc
    PS = k.PS
    m = A.mark()
    xt = [A.alloc(f"xt{i}", [128, 8, 512], F32) for i in range(3)]
    hT = [A.alloc(f"hT{i}", [128, 8, 512], BF16) for i in range(2)]
    actT = [A.alloc(f"actT{i}", [128, NFC, 512], BF16) for i in range(2)]
    sq = A.alloc("sq", [128, 8, 512], BF16)
    win = [A.alloc(f"win{i}", [128, 8, 256], BF16) for i in range(3)]
    wout = [A.alloc(f"wout{i}", [128, NFC, 128], BF16) for i in range(2)]
    r1 = A.alloc("r1", [128, 512], F32)
    r2 = A.alloc("r2", [128, 512], F32)
    rstd = A.alloc("rstd", [128, 512], F32)
    tmp = [A.alloc(f"tmp{i}", [128, 512], F32) for i in range(2)]
    sg = [A.alloc(f"sg{i}", [128, 512], F32) for i in range(2)]
    tokb = [A.alloc(f"tokb{i}", [128, 1024], F32) for i in range(2)]
    gi = 0 if which == 1 else 2
    shi = 0 if which == 1 else 6
    hgi = 3 if which == 1 else 4
    wname_in = f"l{l}_f{which}_win"
    wname_out = f"l{l}_f{which}_wout"
    groups = [tiles[i:i + 2] for i in range(0, len(tiles), 2)]
    tk = 0
    wi = 0
    wo = 0
    guc = 0
    yc = 0
    tbc = 0
    for grp in groups:
        bufs = []
        for s, (who, t0, T) in enumerate(grp):
            b = tk % 3
            tk += 1
            bufs.append(b)
            xb = xt[b]
            xtoks = [("xt", b, c) for c in range(8)]
            if src == "xT":
                P.dma("sp", xb[:, :, 0:T], k.xTd[:, :, t0:t0 + T].rearrange("c p t -> p c t"), writes=xtoks)
            else:
                srcap = k.inp["ctx"] if who == 1 else k.inp["x"]
                r0 = t0 if who == 1 else t0 - NCTX
                for st in range(T // 128):
                    tb = tokb[tbc % 2]
                    ttok = ("tokb", tbc % 2)
                    tbc += 1
                    P.dma("sp", tb[:], srcap[r0 + st * 128:r0 + (st + 1) * 128, :], writes=[ttok])
                    for half in range(2):
                        bank = PS[(guc % 4)]
                        btok = ("ps", guc % 4)
                        guc += 1

                        def tr(e, tb=tb, half=half, bank=bank):
                            ins = None
                            for cc in range(4):
                                c = half * 4 + cc
                                ins = e.transpose(bank[:, cc * 128:(cc + 1) * 128], tb[:, c * 128:(c + 1) * 128], k.ident[:])
                            return ins
                        P.pe(tr, reads=[ttok, ("ident",)], writes=[btok])
                        P.act(lambda e, xb=xb, half=half, bank=bank, st=st: e.activation(
                            out=xb[:, half * 4:half * 4 + 4, st * 128:(st + 1) * 128],
                            in_=bank[:].rearrange("p (c t) -> p c t", c=4), func=AF.Copy),
                            reads=[btok], writes=xtoks[half * 4:half * 4 + 4])
            rms_norm_mod(k, xb, xtoks, T, k.der[:, l, who, gi, :], k.modv[:, l, who, shi * 8:shi * 8 + 8],
                         hT[s], ("hT", s), tmp, r1, r2, rstd, sq)
        for j in range(NFC):
            wb = win[wi % 3]
            wtok = ("win", wi % 3)
            wi += 1
            P.dma("sp", wb[:], k.wbf[wname_in][j].rearrange("p (k n) -> p k n", k=8),
                  reads=[("wbf", wname_in, j)], writes=[wtok])
            for s, (who, t0, T) in enumerate(grp):
                bg = (guc % 2) * 2
                guc += 1
                htoks = [(("hT", s), c) for c in range(8)]

                def mmg(e, wb=wb, s=s, T=T, bank=PS[bg], off=0):
                    ins = None
                    for kc in range(8):
                        ins = e.matmul(bank[:, 0:T], lhsT=wb[:, kc, off:off + 128], rhs=hT[s][:, kc, 0:T],
                                       start=(kc == 0), stop=(kc == 7))
                    return ins
                P.pe(mmg, reads=[wtok] + htoks, writes=[("ps", bg)])
                P.pe(lambda e, wb=wb, s=s, T=T, bank=PS[bg + 1], f=mmg: f(e, wb, s, T, bank, 128),
                     reads=[wtok] + htoks, writes=[("ps", bg + 1)])
                q = guc % 2
                P.act(lambda e, q=q, bg=bg, T=T: e.activation(out=sg[q][:, 0:T], in_=PS[bg][:, 0:T], func=AF.Silu),
                      reads=[("ps", bg)], writes=[("sg", q)])
                P.dve(lambda e, q=q, bg=bg, T=T, s=s, j=j: e.tensor_tensor(
                    out=actT[s][:, j, 0:T], in0=PS[bg + 1][:, 0:T], in1=sg[q][:, 0:T], op=ALU.mult),
                    reads=[("ps", bg + 1), ("sg", q)], writes=[("actT", s, j)])
        for d in range(8):
            wb = wout[wo % 2]
            wtok = ("wout", wo % 2)
            wo += 1
            P.dma("sp", wb[:], k.wbf[wname_out][d].rearrange("p (k n) -> p k n", k=NFC),
                  reads=[("wbf", wname_out, d)], writes=[wtok])
            for s, (who, t0, T) in enumerate(grp):
                by = 4 + (yc % 2)
                yc += 1
                xb = xt[bufs[s]]

                def mmy(e, wb=wb, s=s, T=T, by=by):
                    ins = None
                    for fc in range(NFC):
                        ins = e.matmul(PS[by][:, 0:T], lhsT=wb[:, fc, :], rhs=actT[s][:, fc, 0:T],
                                       start=(fc == 0), stop=(fc == NFC - 1))
                    return ins
                P.pe(mmy, reads=[wtok] + [("actT", s, j) for j in range(NFC)], writes=[("ps", by)])
                hg = k.der[:, l, who, hgi, :]
                P.dve(lambda e, xb=xb, d=d, T=T, by=by, hg=hg: e.scalar_tensor_tensor(
                    out=xb[:, d, 0:T], in0=PS[by][:, 0:T], scalar=hg[:, d:d + 1], in1=xb[:, d, 0:T],
                    op0=ALU.mult, op1=ALU.add),
                    reads=[("ps", by), ("xt", bufs[s], d)], writes=[("xt", bufs[s], d)])
        for s, (who, t0, T) in enumerate(grp):
            b = bufs[s]
            xb = xt[b]
            xtoks = [("xt", b, c) for c in range(8)]
            if dst == "xT":
                P.dma("sp", k.xTd[:, :, t0:t0 + T].rearrange("c p t -> p c t"), xb[:, :, 0:T], reads=xtoks,
                      writes=[("xTd", t0)])
            else:
                r0 = t0 - NCTX
                for st in range(T // 128):
                    ob = tokb[tbc % 2]
                    otok = ("tokb", tbc % 2)
                    tbc += 1
                    for half in range(2):
                        bi = guc % 4
                        guc += 1

                        def tr(e, xb=xb, half=half, bi=bi, st=st):
                            ins = None
                            for cc in range(4):
                                c = half * 4 + cc
                                ins = e.transpose(PS[bi][:, cc * 128:(cc + 1) * 128], xb[:, c, st * 128:(st + 1) * 128],
                                                  k.ident[:])
                            return ins
                        P.pe(tr, reads=xtoks[half * 4:half * 4 + 4] + [("ident",)], writes=[("ps", bi)])
                        P.act(lambda e, ob=ob, half=half, bi=bi: e.activation(
                            out=ob[:, half * 512:(half + 1) * 512], in_=PS[bi][:], func=AF.Copy),
                            reads=[("ps", bi)], writes=[(otok, half)])
                    P.dma("sp", k.out[r0 + st * 128:r0 + (st + 1) * 128, :], ob[:], reads=[(otok, 0), (otok, 1)],
                          writes=[("out", r0, st)])
    A.release(m)


_CACHE = {}


def kernel(**inputs):
    inputs = {kk: np.asarray(v) for kk, v in inputs.items()}
    if "nc" not in _CACHE:
        _CACHE["nc"] = build()
    nc = _CACHE["nc"]
    shared = host_shared(inputs)
    in_maps = []
    for b in range(8):
        mm = dict(shared)
        mm.update(host_inputs(inputs, b))
        in_maps.append(mm)
    res = run_bass_kernel_spmd(nc, in_maps, core_ids=list(range(8)))
    return np.stack([r["out"] for r in res.results], axis=0)


NEG = -30000.0
PERM64 = list(range(0, 64, 2)) + list(range(1, 64, 2))
SWAP64 = list(range(1, 64, 2)) + list(range(0, 64, 2))
PERM32 = list(range(0, 32, 2)) + list(range(1, 32, 2))
SWAP32 = list(range(1, 32, 2)) + list(range(0, 32, 2))


def rope_tables():
    t = np.arange(S)
    row = (t // GRID_W).astype(np.float32)
    col = (t % GRID_W).astype(np.float32)

    def tab(dim):
        npairs = dim // 4
        inv = (10000.0 ** (-np.arange(npairs, dtype=np.float32) / npairs)).astype(np.float32)
        ang = np.concatenate([row[:, None] * inv, col[:, None] * inv], axis=-1).astype(np.float32)
        return np.cos(ang).astype(np.float32).T, np.sin(ang).astype(np.float32).T
    cA, sA = tab(64)
    cB, sB = tab(32)
    ta = np.zeros((128, 2, S), np.float32)
    for p in range(128):
        pp = p % 64
        pr = pp % 32
        ta[p, 0] = cA[pr]
        ta[p, 1] = -sA[pr] if pp < 32 else sA[pr]
    tb = np.zeros((128, 2, S), np.float32)
    for p in range(64, 96):
        pp = p - 64
        pr = pp % 16
        tb[p, 0] = cB[pr]
        tb[p, 1] = -sB[pr] if pp < 16 else sB[pr]
    return ta, tb


def na_geometry():
    sigs = []
    mp = {}
    chunks = {}
    for i in range(32):
        rs0 = min(max(2 * i - 4, 0), 56)
        rs1 = min(max(2 * i + 1 - 4, 0), 56)
        lo = rs0 // 2
        hi = (rs1 + 7) // 2
        chunks[i] = list(range(lo, hi + 1))
        for kc in chunks[i]:
            sig = (rs0 - 2 * i, rs1 - (2 * i + 1), kc - i)
            if sig not in sigs:
                sigs.append(sig)
            mp[(i, kc)] = sigs.index(sig)
    return sigs, mp, chunks


NA_SIGS, NA_MAP, NA_CHUNKS = na_geometry()
NSIG = len(NA_SIGS)


def na_bias_tiles(rpb):
    out = np.full((NSIG, 8, 128, 128), NEG, np.float32)
    kp = np.arange(128)
    qf = np.arange(128)
    kr_, kc_ = kp // 64, kp % 64
    qr_, qc_ = qf // 64, qf % 64
    cs = np.clip(qc_ - 8, 0, 48)
    for si, (a0, a1, dk) in enumerate(NA_SIGS):
        rsrel = np.where(qr_ == 0, a0, a1)
        krel = (2 * dk + kr_[:, None]) - qr_[None, :]
        vr = (krel >= rsrel[None, :]) & (krel <= rsrel[None, :] + 7)
        vc = (kc_[:, None] >= cs[None, :]) & (kc_[:, None] <= cs[None, :] + 15)
        valid = vr & vc
        dr = np.clip(krel + 7, 0, 14)
        dc = np.clip(kc_[:, None] - qc_[None, :] + 15, 0, 30)
        for h in range(8):
            g = rpb[h][dr, dc]
            out[si, h] = np.where(valid, g, np.float32(NEG))
    return out.reshape(NSIG * 8, 128, 128)


def blk(w, cols):
    K_ = w.shape[0]
    o = np.zeros((K_, len(cols)), np.float32)
    idx = [i for i, c in enumerate(cols) if c is not None]
    o[:, idx] = w[:, [cols[i] for i in idx]]
    return o.reshape(K_ // 128, 128, len(cols)).transpose(1, 0, 2)


def host_mixer(inputs):
    m = {}
    ta, tb = rope_tables()
    m["ropeA"] = ta
    m["ropeB"] = tb
    w = inputs["l0_w_in"]
    blocks = []
    blocks.append(blk(w, list(range(0, 128))))
    blocks.append(blk(w, list(range(128, 256))))
    blocks.append(blk(w, [None] * 64 + [256 + i for i in PERM32] + [None] * 32))
    blocks.append(blk(w, [None] * 64 + [256 + i for i in SWAP32] + [None] * 32))
    for c in range(3):
        blocks.append(blk(w, list(range(288 + c * 128, 288 + (c + 1) * 128))))
    for c in range(8):
        blocks.append(blk(w, list(range(672 + c * 128, 672 + (c + 1) * 128))))
    m["l0_wA"] = np.ascontiguousarray(np.stack(blocks, 0).reshape(15, 128, 8 * 128))
    wk = inputs["l0_mla_w_ukv"]
    kb = [blk(wk, [(2 * pr) * 128 + i for i in range(64)] + [(2 * pr + 1) * 128 + i for i in range(64)]) for pr in range(4)]
    m["l0_wkn"] = np.ascontiguousarray(np.stack(kb, 0).reshape(4, 128, 2 * 128))
    m["l0_wv"] = np.ascontiguousarray(blk(wk, [h * 128 + 64 + i for h in range(8) for i in range(64)]).reshape(1, 128, 2 * 512))
    wq = inputs["l0_mla_w_uq"]
    qb = []
    for h in range(8):
        qb.append(blk(wq, [h * 96 + i for i in range(64)] + [h * 96 + 64 + i for i in PERM32]))
    for h in range(8):
        qb.append(blk(wq, [None] * 64 + [h * 96 + 64 + i for i in SWAP32]))
    m["l0_wq"] = np.ascontiguousarray(np.stack(qb, 0).reshape(16, 128, 3 * 96))
    for l in range(2):
        wo = inputs[f"l{l}_w_out"]
        m[f"l{l}_wo"] = np.ascontiguousarray(wo.reshape(8, 128, 8, 128).transpose(2, 1, 0, 3).reshape(8, 128, 8 * 128))
    m["l0_dw"] = np.ascontiguousarray(inputs["l0_conv_dw_w"].reshape(31, 4, 128).transpose(2, 1, 0))
    v = np.zeros((128, 32), np.float32)
    v[:, 0:2] = pvec(inputs["l0_mla_kv_norm"])
    v[:, 2:5] = pvec(inputs["l0_mla_q_norm"])
    kg = inputs["l0_mla_k_gain"]
    qg = inputs["l0_mla_q_gain"]
    v[:, 5] = np.tile(kg[:64], 2)
    v[64:96, 6] = kg[64:][PERM32]
    v[64:96, 7] = kg[64:][SWAP32]
    v[0:64, 8] = qg[:64]
    v[64:96, 8] = qg[64:][PERM32]
    v[64:96, 9] = qg[64:][SWAP32]
    v[0:64, 10] = 1.0 / 64
    v[64:96, 10] = 1.0 / 32
    gb = inputs["l0_conv_glu_b"]
    v[:, 11:15] = pvec(gb[:512])
    v[:, 15:19] = pvec(gb[512:])
    v[:, 19:23] = pvec(inputs["l0_conv_dw_b"])
    v[:, 23:27] = pvec(inputs["l0_conv_ln_g"])
    v[:, 27:31] = pvec(inputs["l0_conv_ln_b"])
    m["l0_vec"] = v
    w = inputs["l1_w_in"]
    blocks = []
    for hk in range(2):
        blocks.append(blk(w, [hk * 64 + i for i in PERM64] * 2))
    for hk in range(2):
        blocks.append(blk(w, [hk * 64 + i for i in SWAP64] * 2))
    for c in range(4):
        blocks.append(blk(w, list(range(256 + c * 128, 256 + (c + 1) * 128))))
    for c in range(4):
        blocks.append(blk(w, [1280 + (2 * c + e) * 64 + i for e in range(2) for i in PERM64]))
    for c in range(4):
        blocks.append(blk(w, [1280 + (2 * c + e) * 64 + i for e in range(2) for i in SWAP64]))
    for c in range(4):
        blocks.append(blk(w, list(range(1792 + c * 128, 1792 + (c + 1) * 128))))
    m["l1_wA"] = np.ascontiguousarray(np.stack(blocks, 0).reshape(20, 128, 8 * 128))
    m["l1_wv"] = np.ascontiguousarray(blk(w, list(range(128, 256)) + list(range(768, 1280))).reshape(1, 128, 8 * 640))
    v = np.zeros((128, 8), np.float32)
    v[:, 0] = np.tile(inputs["l1_gqa_k_gain"][PERM64], 2)
    v[:, 1] = np.tile(inputs["l1_gqa_k_gain"][SWAP64], 2)
    v[:, 2] = np.tile(inputs["l1_gqa_q_gain"][PERM64], 2)
    v[:, 3] = np.tile(inputs["l1_gqa_q_gain"][SWAP64], 2)
    v[:, 4] = np.tile(inputs["l1_na_k_gain"], 2)
    v[:, 5] = np.tile(inputs["l1_na_q_gain"], 2)
    m["l1_vec"] = v
    m["l1_nab"] = na_bias_tiles(inputs["l1_na_rpb"])
    return m


def declare_mixer_inputs(k, nc):
    def din(name, shape):
        k.inp[name] = nc.dram_tensor(name, list(shape), F32, kind="ExternalInput").ap()
        k.inshape[name] = tuple(shape)
    din("ropeA", [128, 2, S])
    din("ropeB", [128, 2, S])
    din("l0_wA", [15, 128, 1024])
    din("l0_wkn", [4, 128, 256])
    din("l0_wv", [1, 128, 1024])
    din("l0_wq", [16, 128, 288])
    din("l0_wo", [8, 128, 1024])
    din("l1_wo", [8, 128, 1024])
    din("l0_dw", [128, 4, 31])
    din("l0_vec", [128, 32])
    din("l1_wA", [20, 128, 1024])
    din("l1_wv", [1, 128, 8 * 640])
    din("l1_vec", [128, 8])
    din("l1_nab", [NSIG * 8, 128, 128])


def stats_rstd(k, ps_ap, npart, T, scale, r1, r2, rstd, rtoks=(("ps", 6),)):
    P = k.P
    P.dve(lambda e: e.tensor_scalar(out=r1[0:npart, 0:T], in0=ps_ap, scalar1=scale, scalar2=EPS,
                                    op0=ALU.mult, op1=ALU.add), reads=list(rtoks), writes=[("r1",)])
    P.act(lambda e: e.activation(out=r2[0:npart, 0:T], in_=r1[0:npart, 0:T], func=AF.Sqrt),
          reads=[("r1",)], writes=[("r2",)])
    P.dve(lambda e: e.reciprocal(out=rstd[0:npart, 0:T], in_=r2[0:npart, 0:T]), reads=[("r2",)], writes=[("rstd",)])


class PsRot:
    def __init__(self, banks=(0, 1, 2, 3, 4, 5)):
        self.banks = banks
        self.i = 0

    def next(self):
        b = self.banks[self.i % len(self.banks)]
        self.i += 1
        return b


def load_x_norm(k, l, who, t0, T, xb, xtoks, hT, w):
    P = k.P
    P.dma("sp", xb[:, :, 0:T], k.xTd[:, :, t0:t0 + T].rearrange("c p t -> p c t"), writes=xtoks)
    rms_norm_mod(k, xb, xtoks, T, k.der[:, l, who, 1, :], k.modv[:, l, who, 24:32], hT, "hTm",
                 w["tmp"], w["r1"], w["r2"], w["rstd"], w["sq"])


def mm_group(k, bank_ap, lhs_fn, rhs_fn, n, reads, btok):
    def f(e):
        ins = None
        for kc in range(n):
            ins = e.matmul(bank_ap, lhsT=lhs_fn(kc), rhs=rhs_fn(kc), start=(kc == 0), stop=(kc == n - 1))
        return ins
    k.P.pe(f, reads=reads, writes=[btok])


def common_work(k, A):
    w = {}
    w["tmp"] = [A.alloc(f"tmp{i}", [128, 512], F32) for i in range(2)]
    w["r1"] = A.alloc("r1", [128, 512], F32)
    w["r2"] = A.alloc("r2", [128, 512], F32)
    w["rstd"] = A.alloc("rstd", [128, 512], F32)
    w["sq"] = A.alloc("sq", [128, 8, 512], BF16)
    return w


def l0_proj_phase(k):
    P, A, nc, PS = k.P, k.A, k.nc, k.PS
    l = 0
    m = A.mark()
    w = common_work(k, A)
    xt = [A.alloc(f"xt{i}", [128, 8, 512], F32) for i in range(2)]
    hT = A.alloc("hTm", [128, 8, 512], BF16)
    wA = A.alloc("wA", [128, 15, 8, 128], BF16)
    wkn = A.alloc("wkn", [128, 4, 2, 128], BF16)
    wv = A.alloc("wv", [128, 2, 512], BF16)
    wq = A.alloc("wq", [128, 16, 3, 96], BF16)
    vec = A.alloc("vec", [128, 32], F32)
    rope = [A.alloc(f"rope{i}", [128, 2, 512], F32) for i in range(2)]
    ckvT = A.alloc("ckvT", [128, 2, 512], BF16)
    cqT = A.alloc("cqT", [128, 3, 512], BF16)
    krT = A.alloc("krT", [128, 512], BF16)
    yT = [A.alloc(f"yT{i}", [128, 4, 512], BF16) for i in range(2)]
    KT = [A.alloc(f"KT{i}", [128, 8, 512], BF16) for i in range(2)]
    QT = [A.alloc(f"QT{i}", [128, 8, 512], BF16) for i in range(2)]
    Vt = [A.alloc(f"Vt{i}", [128, 8, 128], BF16) for i in range(2)]
    sqs = [A.alloc(f"sqs{i}", [128, 512], BF16) for i in range(2)]
    sig = [A.alloc(f"sig{i}", [128, 512], F32) for i in range(2)]
    ra = A.alloc("ra", [128, 512], F32)
    rb = A.alloc("rb", [128, 512], F32)
    P.dma("sp", vec[:], k.inp["l0_vec"], writes=[("vec",)])
    for i in range(15):
        P.dma("sp", wA[:, i], k.wbf["l0_wA"][i].rearrange("p (k n) -> p k n", k=8), reads=[("wbf", "l0_wA", i)], writes=[("wA",)])
    for i in range(4):
        P.dma("sp", wkn[:, i], k.wbf["l0_wkn"][i].rearrange("p (k n) -> p k n", k=2), reads=[("wbf", "l0_wkn", i)], writes=[("wkn",)])
    P.dma("sp", wv[:], k.wbf["l0_wv"][0].rearrange("p (k n) -> p k n", k=2), reads=[("wbf", "l0_wv", 0)], writes=[("wv",)])
    for i in range(16):
        P.dma("sp", wq[:, i], k.wbf["l0_wq"][i].rearrange("p (k n) -> p k n", k=3), reads=[("wbf", "l0_wq", i)], writes=[("wq",)])
    sc = 96.0 ** -0.5
    P.dve(lambda e: e.tensor_scalar(out=vec[:, 8:10], in0=vec[:, 8:10], scalar1=sc, scalar2=None, op0=ALU.mult),
          reads=[("vec",)], writes=[("vec",)])
    for i in range(2):
        P.pool(lambda e, i=i: e.memset(Vt[i][:], 1.0), writes=[("Vt", i), (("Vt", i), 1)])
    rot = PsRot()
    vcnt = 0
    sqc = 0
    for ti, (who, t0, T) in enumerate(TILES):
        lat = (who == 0)
        xb = xt[ti % 2]
        xtoks = [("xtm", ti % 2, c) for c in range(8)]
        load_x_norm(k, l, who, t0, T, xb, xtoks, hT, w)
        htoks = [("hTm", c) for c in range(8)]
        rp = rope[ti % 2]
        if lat:
            P.dma("sp", rp[:, :, 0:T], k.inp["ropeB"][:, :, t0 - NCTX:t0 - NCTX + T], writes=[("rope", ti % 2)])

        def projA(bidx, M):
            b = rot.next()
            mm_group(k, PS[b][0:M, 0:T], lambda kc: wA[:, bidx, kc, 0:M], lambda kc: hT[:, kc, 0:T], 8,
                     [("wA",)] + htoks, ("ps", b))
            return b

        def square(b, M, dst_ap, dtok):
            P.act(lambda e: e.activation(out=dst_ap, in_=PS[b][0:M, 0:T], func=AF.Square), reads=[("ps", b)], writes=[dtok])

        bs = [projA(0, 128), projA(1, 128)]
        for c in range(2):
            square(bs[c], 128, w["sq"][:, c, 0:T], ("sq", c))
        mm_group(k, PS[6][:, 0:T], lambda kc: k.ones_bf, lambda kc: w["sq"][:, kc, 0:T], 2,
                 [("sq", 0), ("sq", 1), ("cbf",)], ("ps", 6))
        stats_rstd(k, PS[6][:, 0:T], 128, T, 1.0 / 256, w["r1"], w["r2"], w["rstd"])
        for c in range(2):
            P.dve(lambda e, c=c: e.scalar_tensor_tensor(out=ckvT[:, c, 0:T], in0=PS[bs[c]][:, 0:T], scalar=vec[:, c:c + 1],
                                                        in1=w["rstd"][:, 0:T], op0=ALU.mult, op1=ALU.mult),
                  reads=[("ps", bs[c]), ("rstd",), ("vec",)], writes=[("ckvT", c)])
        bm = projA(2, 96)
        bsw = projA(3, 96) if lat else None
        sb = sqs[sqc % 2]
        stok = ("sqs", sqc % 2)
        sqc += 1
        square(bm, 96, sb[0:96, 0:T], stok)
        mm_group(k, PS[6][0:96, 0:T], lambda kc: k.cbf[0:96, 3, 0:96], lambda kc: sb[0:96, 0:T], 1, [stok, ("cbf",)], ("ps", 6))
        stats_rstd(k, PS[6][64:96, 0:T], 32, T, 1.0 / 32, w["r1"][64:96], w["r2"][64:96], w["rstd"][64:96])
        R = slice(64, 96)
        if not lat:
            P.dve(lambda e: e.scalar_tensor_tensor(out=krT[R, 0:T], in0=PS[bm][R, 0:T], scalar=vec[R, 6:7],
                                                   in1=w["rstd"][R, 0:T], op0=ALU.mult, op1=ALU.mult),
                  reads=[("ps", bm), ("rstd",), ("vec",)], writes=[("krT",)])
        else:
            rope_apply(k, PS[bm], PS[bsw], bm, bsw, R, T, vec[R, 6:7], vec[R, 7:8], w["rstd"], rp, ("rope", ti % 2),
                       ra, rb, krT[R, 0:T], [("krT",)])
        bs = [projA(4 + c, 128) for c in range(3)]
        for c in range(3):
            square(bs[c], 128, w["sq"][:, c, 0:T], ("sq", c))
        mm_group(k, PS[6][:, 0:T], lambda kc: k.ones_bf, lambda kc: w["sq"][:, kc, 0:T], 3,
                 [("sq", 0), ("sq", 1), ("sq", 2), ("cbf",)], ("ps", 6))
        stats_rstd(k, PS[6][:, 0:T], 128, T, 1.0 / 384, w["r1"], w["r2"], w["rstd"])
        for c in range(3):
            P.dve(lambda e, c=c: e.scalar_tensor_tensor(out=cqT[:, c, 0:T], in0=PS[bs[c]][:, 0:T], scalar=vec[:, 2 + c:3 + c],
                                                        in1=w["rstd"][:, 0:T], op0=ALU.mult, op1=ALU.mult),
                  reads=[("ps", bs[c]), ("rstd",), ("vec",)], writes=[("cqT", c)])
        yb = yT[ti % 2]
        for ch in range(4):
            ba_ = projA(7 + ch, 128)
            bg_ = projA(11 + ch, 128)
            sg_ = sig[ch % 2]
            P.act(lambda e, bg_=bg_, sg_=sg_, ch=ch: e.activation(out=sg_[:, 0:T], in_=PS[bg_][:, 0:T], func=AF.Sigmoid,
                                                              bias=vec[:, 15 + ch:16 + ch], scale=1.0),
                  reads=[("ps", bg_), ("vec",)], writes=[("sig", ch % 2)])
            P.dve(lambda e, ba_=ba_, sg_=sg_, ch=ch: e.scalar_tensor_tensor(out=yb[:, ch, 0:T], in0=PS[ba_][:, 0:T],
                                                                        scalar=vec[:, 11 + ch:12 + ch], in1=sg_[:, 0:T],
                                                                        op0=ALU.add, op1=ALU.mult),
                  reads=[("ps", ba_), ("sig", ch % 2), ("vec",)], writes=[("yT", ti % 2, ch)])
        P.dma("sp", k.yTd[:, :, t0:t0 + T].rearrange("c p t -> p c t"), yb[:, :, 0:T],
              reads=[("yT", ti % 2, ch) for ch in range(4)], writes=[("yTd", ti)])
        Kb = KT[ti % 2]
        ktoks = [("KT", ti % 2, h) for h in range(8)]
        for pr in range(4):
            b = rot.next()
            mm_group(k, PS[b][:, 0:T], lambda kc, pr=pr: wkn[:, pr, kc, :], lambda kc: ckvT[:, kc, 0:T], 2,
                     [("wkn",), ("ckvT", 0), ("ckvT", 1)], ("ps", b))
            sb = sqs[sqc % 2]
            stok = ("sqs", sqc % 2)
            sqc += 1
            square(b, 128, sb[:, 0:T], stok)
            mm_group(k, PS[6][:, 0:T], lambda kc: k.bd64_bf, lambda kc, sb=sb: sb[:, 0:T], 1, [stok, ("cbf",)], ("ps", 6))
            stats_rstd(k, PS[6][:, 0:T], 128, T, 1.0 / 64, w["r1"], w["r2"], w["rstd"])
            for e_ in range(2):
                h = 2 * pr + e_
                P.dve(lambda e, b=b, e_=e_, h=h: e.scalar_tensor_tensor(
                    out=Kb[0:64, h, 0:T], in0=PS[b][e_ * 64:(e_ + 1) * 64, 0:T], scalar=vec[e_ * 64:(e_ + 1) * 64, 5:6],
                    in1=w["rstd"][e_ * 64:(e_ + 1) * 64, 0:T], op0=ALU.mult, op1=ALU.mult),
                    reads=[("ps", b), ("rstd",), ("vec",)], writes=[("KT", ti % 2, h)])
        for h in range(8):
            P.pool(lambda e, h=h: e.tensor_copy(out=Kb[R, h, 0:T], in_=krT[R, 0:T]), reads=[("krT",)], writes=[("KT", ti % 2, h)])
        P.dma("sp", k.KTd[:, :, t0:t0 + T].rearrange("h p t -> p h t"), Kb[0:96, :, 0:T], reads=ktoks, writes=[("KTd", ti)])
        for st in range(T // 128):
            b = rot.next()
            mm_group(k, PS[b][:, 0:512], lambda kc, st=st: ckvT[:, kc, st * 128:(st + 1) * 128], lambda kc: wv[:, kc, :], 2,
                     [("wv",), ("ckvT", 0), ("ckvT", 1)], ("ps", b))
            vb = Vt[vcnt % 2]
            vtok = ("Vt", vcnt % 2)
            vcnt += 1
            src = PS[b][:, 0:512].rearrange("p (h two d) -> p h two d", two=2, d=64)
            dst = vb[:].rearrange("p (h two) d -> p h two d", two=2)
            P.dve(lambda e, src=src, dst=dst: e.tensor_copy(out=dst[:, :, 0, 0:64], in_=src[:, :, 0, :]),
                  reads=[("ps", b)], writes=[vtok])
            P.dve(lambda e, src=src, dst=dst: e.tensor_copy(out=dst[:, :, 1, 64:128], in_=src[:, :, 1, :]),
                  reads=[("ps", b)], writes=[(vtok, 1)])
            chunk = (t0 + st * 128) // 128
            P.dma("sp", k.Vd[:, :, chunk, :].rearrange("h p d -> p h d"), vb[:], reads=[vtok, (vtok, 1)], writes=[("Vd", chunk)])
        Qb = QT[ti % 2]
        qtoks = [("QT", ti % 2, h) for h in range(8)]
        for h in range(8):
            b = rot.next()
            mm_group(k, PS[b][0:96, 0:T], lambda kc, h=h: wq[:, h, kc, :], lambda kc: cqT[:, kc, 0:T], 3,
                     [("wq",)] + [("cqT", c) for c in range(3)], ("ps", b))
            bsw = None
            if lat:
                bsw = rot.next()
                mm_group(k, PS[bsw][0:96, 0:T], lambda kc, h=h: wq[:, 8 + h, kc, :], lambda kc: cqT[:, kc, 0:T], 3,
                         [("wq",)] + [("cqT", c) for c in range(3)], ("ps", bsw))
            sb = sqs[sqc % 2]
            stok = ("sqs", sqc % 2)
            sqc += 1
            square(b, 96, sb[0:96, 0:T], stok)
            mm_group(k, PS[6][0:96, 0:T], lambda kc: k.cbf[0:96, 3, 0:96], lambda kc, sb=sb: sb[0:96, 0:T], 1, [stok, ("cbf",)], ("ps", 6))
            stats_rstd(k, PS[6][0:96, 0:T], 96, T, vec[0:96, 10:11], w["r1"], w["r2"], w["rstd"], rtoks=(("ps", 6), ("vec",)))
            nr = 96 if not lat else 64
            P.dve(lambda e, b=b, h=h, nr=nr: e.scalar_tensor_tensor(out=Qb[0:nr, h, 0:T], in0=PS[b][0:nr, 0:T], scalar=vec[0:nr, 8:9],
                                                                   in1=w["rstd"][0:nr, 0:T], op0=ALU.mult, op1=ALU.mult),
                  reads=[("ps", b), ("rstd",), ("vec",)], writes=[("QT", ti % 2, h)])
            if lat:
                rope_apply(k, PS[b], PS[bsw], b, bsw, R, T, vec[R, 8:9], vec[R, 9:10], w["rstd"], rp, ("rope", ti % 2),
                           ra, rb, Qb[R, h, 0:T], [(("QT", ti % 2, h), "r")])
        P.dma("sp", k.QTd[:, :, t0:t0 + T].rearrange("h p t -> p h t"), Qb[0:96, :, 0:T],
              reads=qtoks + [(q_, "r") for q_ in qtoks], writes=[("QTd", ti)])
    A.release(m)


def rope_apply(k, psm, pssw, bm, bsw, R, T, g, gsw, rstd, rp, rptok, ra, rb, out_ap, otoks):
    P = k.P
    P.dve(lambda e: e.scalar_tensor_tensor(out=ra[R, 0:T], in0=psm[R, 0:T], scalar=g, in1=rstd[R, 0:T],
                                           op0=ALU.mult, op1=ALU.mult),
          reads=[("ps", bm), ("rstd",), ("vec",)], writes=[("ra",)])
    P.dve(lambda e: e.scalar_tensor_tensor(out=rb[R, 0:T], in0=pssw[R, 0:T], scalar=gsw, in1=rstd[R, 0:T],
                                           op0=ALU.mult, op1=ALU.mult),
          reads=[("ps", bsw), ("rstd",), ("vec",)], writes=[("rb",)])
    P.pool(lambda e: e.tensor_tensor(out=ra[R, 0:T], in0=ra[R, 0:T], in1=rp[R, 0, 0:T], op=ALU.mult),
           reads=[("ra",), rptok], writes=[("ra",)])
    P.pool(lambda e: e.tensor_tensor(out=rb[R, 0:T], in0=rb[R, 0:T], in1=rp[R, 1, 0:T], op=ALU.mult),
           reads=[("rb",), rptok], writes=[("rb",)])
    P.dve(lambda e: e.tensor_tensor(out=out_ap, in0=ra[R, 0:T], in1=rb[R, 0:T], op=ALU.add),
          reads=[("ra",), ("rb",)], writes=otoks)


def attn_phase(k, mode):
    P, A, PS = k.P, k.A, k.PS
    m = A.mark()
    mla = (mode == "mla")
    Kt = [A.alloc(f"Kt{i}", [128, TALL], BF16) for i in range(4)]
    Vh = [A.alloc(f"Vh{i}", [128, 34, 128], BF16) for i in range(4)]
    Qt = [A.alloc(f"Qt{i}", [128, 2, 512], BF16) for i in range(2)]
    PT = [A.alloc(f"PT{i}", [128, 512], BF16) for i in range(3)]
    rec = A.alloc("rec", [128, 512], F32)
    attT = [A.alloc(f"attT{i}", [128, 512], BF16) for i in range(2)]
    qtiles = TILES if mla else TILES[1:]
    qc = 0
    sc = 0
    pc = 0
    ac = 0
    for hp in range(4):
        kb = (hp % 2) * 2
        if mla:
            for e_ in range(2):
                P.dma("sp", Kt[kb + e_][0:96, :], k.KTd[2 * hp + e_], writes=[("Kt", kb + e_)])
                P.dma("sp", Vh[kb + e_][:], k.Vd[2 * hp + e_], writes=[("Vh", kb + e_)])
        else:
            P.dma("sp", Kt[kb][:], k.KdupTd[hp // 2], writes=[("Kt", kb)])
            for e_ in range(2):
                P.dma("sp", Vh[kb + e_][:], k.VgD[(hp // 2) * 2 + e_], writes=[("Vh", kb + e_)])
        for (who, t0, T) in qtiles:
            chunks = [0, 1] if who == 1 else list(range(34))
            Qb = Qt[qc % 2]
            qtok = ("Qt", qc % 2)
            ab = attT[qc % 2]
            atok = ("attT", qc % 2)
            qc += 1
            if mla:
                P.dma("sp", Qb[0:96, :, 0:T], k.QTd[2 * hp:2 * hp + 2, :, t0:t0 + T].rearrange("h p t -> p h t"), writes=[qtok])
            else:
                P.dma("sp", Qb[:, 0, 0:T], k.QgTd[hp, :, t0:t0 + T], writes=[qtok])
            for e_ in range(2):
                if mla:
                    Ksb, ktok, r0, r1 = Kt[kb + e_], ("Kt", kb + e_), 0, 96
                    qap = Qb[0:96, e_, 0:T]
                else:
                    Ksb, ktok, r0, r1 = Kt[kb], ("Kt", kb), e_ * 64, e_ * 64 + 64
                    qap = Qb[r0:r1, 0, 0:T]
                Vsb, vtok = Vh[kb + e_], ("Vh", kb + e_)
                acc = 3 + (ac % 2)
                ac += 1
                prev = None
                n = len(chunks)
                for idx, c in enumerate(chunks):
                    sbk = sc % 3
                    sc += 1
                    P.pe(lambda e, sbk=sbk, c=c: e.matmul(PS[sbk][:, 0:T], lhsT=Ksb[r0:r1, c * 128:(c + 1) * 128], rhs=qap,
                                                          start=True, stop=True),
                         reads=[ktok, qtok], writes=[("ps", sbk)])
                    pb = pc % 3
                    pc += 1
                    P.act(lambda e, sbk=sbk, pb=pb: e.activation(out=PT[pb][:, 0:T], in_=PS[sbk][:, 0:T], func=AF.Exp),
                          reads=[("ps", sbk)], writes=[("PT", pb)])
                    if prev is not None:
                        pidx, pc_, ppb = prev
                        P.pe(lambda e, pidx=pidx, pc_=pc_, ppb=ppb: e.matmul(PS[acc][:, 0:T], lhsT=Vsb[:, pc_, :], rhs=PT[ppb][:, 0:T],
                                                                              start=(pidx == 0), stop=False),
                             reads=[vtok, ("PT", ppb)], writes=[("ps", acc)])
                    prev = (idx, c, pb)
                pidx, pc_, ppb = prev
                P.pe(lambda e: e.matmul(PS[acc][:, 0:T], lhsT=Vsb[:, pc_, :], rhs=PT[ppb][:, 0:T], start=(pidx == 0), stop=True),
                     reads=[vtok, ("PT", ppb)], writes=[("ps", acc)])
                nlo, dlo = (0, 64) if e_ == 0 else (64, 0)
                P.dve(lambda e: e.reciprocal(out=rec[nlo:nlo + 64, 0:T], in_=PS[acc][dlo:dlo + 64, 0:T]),
                      reads=[("ps", acc)], writes=[("rec", e_)])
                P.dve(lambda e: e.tensor_tensor(out=ab[nlo:nlo + 64, 0:T], in0=PS[acc][nlo:nlo + 64, 0:T], in1=rec[nlo:nlo + 64, 0:T],
                                                op=ALU.mult),
                      reads=[("ps", acc), ("rec", e_)], writes=[(atok, e_)])
            P.dma("sp", k.attTd[hp, :, t0:t0 + T], ab[:, 0:T], reads=[(atok, 0), (atok, 1)], writes=[("attTd", hp, t0)])
    A.release(m)


def l0_conv_phase(k):
    P, A, PS = k.P, k.A, k.PS
    m = A.mark()
    vec = A.alloc("vec", [128, 32], F32)
    dw = A.alloc("dw", [128, 4, 31], F32)
    diag = A.alloc("diag", [128, 4, 31, 128], BF16)
    ybuf = [A.alloc(f"ybuf{i}", [128, 4, 544], BF16) for i in range(2)]
    cv = A.alloc("cv", [128, 4, 512], F32)
    cb = A.alloc("cb", [128, 4, 512], BF16)
    sqv = A.alloc("sqv", [128, 4, 512], BF16)
    co = [A.alloc(f"co{i}", [128, 4, 512], BF16) for i in range(2)]
    mean = A.alloc("mean", [128, 512], F32)
    msq = A.alloc("msq", [128, 512], F32)
    r1 = A.alloc("r1", [128, 512], F32)
    r2 = A.alloc("r2", [128, 512], F32)
    rstd = A.alloc("rstd", [128, 512], F32)
    t1 = [A.alloc(f"t1{i}", [128, 512], F32) for i in range(2)]
    identb = k.cbf[:, 0, :]
    P.dma("sp", vec[:], k.inp["l0_vec"], writes=[("vec",)])
    P.dma("sp", dw[:], k.inp["l0_dw"], writes=[("dw",)])
    for ch in range(4):
        for j in range(31):
            P.dve(lambda e: e.tensor_scalar(out=diag[:, ch, j, :], in0=identb, scalar1=dw[:, ch, j:j + 1], scalar2=None, op0=ALU.mult),
                  reads=[("dw",), ("cbf",)], writes=[("diag", ch, j)])
    rot = PsRot(banks=(0, 1, 2, 3))
    for ti, (who, t0, T) in enumerate(TILES):
        s0, s1 = (0, NCTX) if who == 1 else (NCTX, TALL)
        lo = max(t0 - 15, s0)
        hi = min(t0 + T + 15, s1)
        yb = ybuf[ti % 2]
        toks = [("yb", ti % 2, x) for x in "LMR"]
        wr = [toks[1]]
        if lo > t0 - 15:
            P.pool(lambda e: e.memset(yb[:, :, 0:15], 0.0), writes=[toks[0]])
        else:
            wr.append(toks[0])
        if hi < t0 + T + 15:
            P.pool(lambda e: e.memset(yb[:, :, T + 15:T + 30], 0.0), writes=[toks[2]])
        else:
            wr.append(toks[2])
        P.dma("sp", yb[:, :, lo - (t0 - 15):hi - (t0 - 15)], k.yTd[:, :, lo:hi].rearrange("c p t -> p c t"), writes=wr)
        for ch in range(4):
            b = rot.next()

            def mm(e, ch=ch, b=b):
                for j in range(31):
                    e.matmul(PS[b][:, 0:T], lhsT=diag[:, ch, j, :], rhs=yb[:, ch, j:j + T], start=(j == 0), stop=(j == 30))
            P.pe(mm, reads=toks + [("diag", ch, j) for j in range(31)], writes=[("ps", b)])
            P.act(lambda e: e.activation(out=cv[:, ch, 0:T], in_=PS[b][:, 0:T], func=AF.Identity, bias=vec[:, 19 + ch:20 + ch], scale=1.0),
                  reads=[("ps", b), ("vec",)], writes=[("cv", ch)])
            P.pool(lambda e: e.tensor_copy(out=cb[:, ch, 0:T], in_=cv[:, ch, 0:T]), reads=[("cv", ch)], writes=[("cb", ch)])
            P.pool(lambda e: e.tensor_tensor(out=sqv[:, ch, 0:T], in0=cv[:, ch, 0:T], in1=cv[:, ch, 0:T], op=ALU.mult),
                   reads=[("cv", ch)], writes=[("sqv", ch)])
        mm_group(k, PS[6][:, 0:T], lambda kc: k.ones_bf, lambda kc: cb[:, kc, 0:T], 4, [("cb", c) for c in range(4)] + [("cbf",)], ("ps", 6))
        mm_group(k, PS[7][:, 0:T], lambda kc: k.ones_bf, lambda kc: sqv[:, kc, 0:T], 4, [("sqv", c) for c in range(4)] + [("cbf",)], ("ps", 7))
        P.dve(lambda e: e.tensor_scalar(out=mean[:, 0:T], in0=PS[6][:, 0:T], scalar1=1.0 / 512, scalar2=None, op0=ALU.mult),
              reads=[("ps", 6)], writes=[("mean",)])
        P.pool(lambda e: e.tensor_tensor(out=msq[:, 0:T], in0=mean[:, 0:T], in1=mean[:, 0:T], op=ALU.mult),
               reads=[("mean",)], writes=[("msq",)])
        P.dve(lambda e: e.scalar_tensor_tensor(out=r2[:, 0:T], in0=PS[7][:, 0:T], scalar=1.0 / 512, in1=msq[:, 0:T],
                                               op0=ALU.mult, op1=ALU.subtract),
              reads=[("ps", 7), ("msq",)], writes=[("r2",)])
        P.dve(lambda e: e.tensor_scalar(out=r1[:, 0:T], in0=r2[:, 0:T], scalar1=EPS, scalar2=None, op0=ALU.add),
              reads=[("r2",)], writes=[("r1",)])
        P.act(lambda e: e.activation(out=r2[:, 0:T], in_=r1[:, 0:T], func=AF.Sqrt), reads=[("r1",)], writes=[("r2",)])
        P.dve(lambda e: e.reciprocal(out=rstd[:, 0:T], in_=r2[:, 0:T]), reads=[("r2",)], writes=[("rstd",)])
        cob = co[ti % 2]
        for ch in range(4):
            tb = t1[ch % 2]
            P.dve(lambda e: e.tensor_tensor(out=tb[:, 0:T], in0=cv[:, ch, 0:T], in1=mean[:, 0:T], op=ALU.subtract),
                  reads=[("cv", ch), ("mean",)], writes=[("t1", ch % 2)])
            P.pool(lambda e: e.tensor_tensor(out=tb[:, 0:T], in0=tb[:, 0:T], in1=rstd[:, 0:T], op=ALU.mult),
                   reads=[("t1", ch % 2), ("rstd",)], writes=[("t1", ch % 2)])
            P.act(lambda e: e.activation(out=cob[:, ch, 0:T], in_=tb[:, 0:T], func=AF.Silu, bias=vec[:, 27 + ch:28 + ch],
                                         scale=vec[:, 23 + ch:24 + ch]),
                  reads=[("t1", ch % 2), ("vec",)], writes=[("co", ti % 2, ch)])
        P.dma("sp", k.attTd[4:8, :, t0:t0 + T].rearrange("c p t -> p c t"), cob[:, :, 0:T],
              reads=[("co", ti % 2, ch) for ch in range(4)], writes=[("attTd", "conv", ti)])
    A.release(m)


def mixout_phase(k, l, tiles):
    P, A, PS = k.P, k.A, k.PS
    m = A.mark()
    wo = A.alloc("wo", [128, 8, 8, 128], BF16)
    xt = [A.alloc(f"xt{i}", [128, 8, 512], F32) for i in range(2)]
    at = [A.alloc(f"at{i}", [128, 8, 512], BF16) for i in range(2)]
    name = f"l{l}_wo"
    for i in range(8):
        P.dma("sp", wo[:, i], k.wbf[name][i].rearrange("p (k n) -> p k n", k=8), reads=[("wbf", name, i)], writes=[("wo",)])
    rot = PsRot()
    for ti, (who, t0, T) in enumerate(tiles):
        xb = xt[ti % 2]
        ab = at[ti % 2]
        xtoks = [("xt", ti % 2, c) for c in range(8)]
        P.dma("sp", xb[:, :, 0:T], k.xTd[:, :, t0:t0 + T].rearrange("c p t -> p c t"), writes=xtoks)
        P.dma("sp", ab[:, :, 0:T], k.attTd[:, :, t0:t0 + T].rearrange("c p t -> p c t"), writes=[("at", ti % 2)])
        for d in range(8):
            b = rot.next()
            mm_group(k, PS[b][:, 0:T], lambda kc: wo[:, d, kc, :], lambda kc: ab[:, kc, 0:T], 8, [("wo",), ("at", ti % 2)], ("ps", b))
            P.dve(lambda e: e.scalar_tensor_tensor(out=xb[:, d, 0:T], in0=PS[b][:, 0:T], scalar=k.modv[:, l, who, 40 + d:41 + d],
                                                   in1=xb[:, d, 0:T], op0=ALU.mult, op1=ALU.add),
                  reads=[("ps", b), xtoks[d]], writes=[xtoks[d]])
        P.dma("sp", k.xTd[:, :, t0:t0 + T].rearrange("c p t -> p c t"), xb[:, :, 0:T], reads=xtoks, writes=[("xTd", t0)])
    A.release(m)


def l1_proj_phase(k):
    P, A, PS = k.P, k.A, k.PS
    l = 1
    m = A.mark()
    w = common_work(k, A)
    xt = [A.alloc(f"xt{i}", [128, 8, 512], F32) for i in range(2)]
    hT = A.alloc("hTm", [128, 8, 512], BF16)
    wA = A.alloc("wA", [128, 20, 8, 128], BF16)
    wv = A.alloc("wv", [128, 8, 640], BF16)
    vec = A.alloc("vec", [128, 8], F32)
    rope = [A.alloc(f"rope{i}", [128, 2, 512], F32) for i in range(2)]
    sqs = [A.alloc(f"sqs{i}", [128, 512], BF16) for i in range(2)]
    ra = A.alloc("ra", [128, 512], F32)
    rb = A.alloc("rb", [128, 512], F32)
    KdT = [A.alloc(f"KdT{i}", [128, 2, 512], BF16) for i in range(2)]
    nkT = [A.alloc(f"nkT{i}", [128, 4, 512], BF16) for i in range(2)]
    QgT = [A.alloc(f"QgT{i}", [128, 4, 512], BF16) for i in range(2)]
    nqT = [A.alloc(f"nqT{i}", [128, 4, 512], BF16) for i in range(2)]
    Vgt = [A.alloc(f"Vgt{i}", [128, 4, 128], BF16) for i in range(2)]
    NVt = [A.alloc(f"NVt{i}", [128, 8, 128], BF16) for i in range(2)]
    P.dma("sp", vec[:], k.inp["l1_vec"], writes=[("vec",)])
    for i in range(20):
        P.dma("sp", wA[:, i], k.wbf["l1_wA"][i].rearrange("p (k n) -> p k n", k=8), reads=[("wbf", "l1_wA", i)], writes=[("wA",)])
    P.dma("sp", wv[:], k.wbf["l1_wv"][0].rearrange("p (k n) -> p k n", k=8), reads=[("wbf", "l1_wv", 0)], writes=[("wv",)])
    for c0, c1 in ((2, 4), (5, 6)):
        P.dve(lambda e: e.tensor_scalar(out=vec[:, c0:c1], in0=vec[:, c0:c1], scalar1=0.125, scalar2=None, op0=ALU.mult),
              reads=[("vec",)], writes=[("vec",)])
    for i in range(2):
        P.pool(lambda e: e.memset(Vgt[i][:], 1.0), writes=[("Vgt", i), (("Vgt", i), 1)])
        P.pool(lambda e: e.memset(NVt[i][:], 1.0), writes=[("NVt", i), (("NVt", i), 1)])
    rot = PsRot()
    cnt = {"sq": 0, "v": 0}
    Rall = slice(0, 128)
    for ti, (who, t0, T) in enumerate(TILES):
        lat = (who == 0)
        xb = xt[ti % 2]
        xtoks = [("xtm", ti % 2, c) for c in range(8)]
        load_x_norm(k, l, who, t0, T, xb, xtoks, hT, w)
        htoks = [("hTm", c) for c in range(8)]
        rp = rope[ti % 2]
        if lat:
            P.dma("sp", rp[:, :, 0:T], k.inp["ropeA"][:, :, t0 - NCTX:t0 - NCTX + T], writes=[("rope", ti % 2)])

        def projA(bidx):
            b = rot.next()
            mm_group(k, PS[b][:, 0:T], lambda kc: wA[:, bidx, kc, :], lambda kc: hT[:, kc, 0:T], 8, [("wA",)] + htoks, ("ps", b))
            return b

        def normed_chunk(bmain, bswap, gcol, out_ap, otok, do_rope):
            b = projA(bmain)
            bsw = projA(bswap) if do_rope else None
            sb = sqs[cnt["sq"] % 2]
            stok = ("sqs", cnt["sq"] % 2)
            cnt["sq"] += 1
            P.act(lambda e: e.activation(out=sb[:, 0:T], in_=PS[b][:, 0:T], func=AF.Square), reads=[("ps", b)], writes=[stok])
            mm_group(k, PS[6][:, 0:T], lambda kc: k.bd64_bf, lambda kc: sb[:, 0:T], 1, [stok, ("cbf",)], ("ps", 6))
            stats_rstd(k, PS[6][:, 0:T], 128, T, 1.0 / 64, w["r1"], w["r2"], w["rstd"])
            if do_rope:
                rope_apply(k, PS[b], PS[bsw], b, bsw, Rall, T, vec[:, gcol:gcol + 1], vec[:, gcol + 1:gcol + 2], w["rstd"], rp,
                           ("rope", ti % 2), ra, rb, out_ap, [otok])
            else:
                P.dve(lambda e: e.scalar_tensor_tensor(out=out_ap, in0=PS[b][:, 0:T], scalar=vec[:, gcol:gcol + 1],
                                                       in1=w["rstd"][:, 0:T], op0=ALU.mult, op1=ALU.mult),
                      reads=[("ps", b), ("rstd",), ("vec",)], writes=[otok])

        Kb = KdT[ti % 2]
        for hk in range(2):
            normed_chunk(hk, 2 + hk, 0, Kb[:, hk, 0:T], ("KdT", ti % 2, hk), lat)
        P.dma("sp", k.KdupTd[:, :, t0:t0 + T].rearrange("h p t -> p h t"), Kb[:, :, 0:T],
              reads=[("KdT", ti % 2, hk) for hk in range(2)], writes=[("KdupTd", ti)])
        nb = nkT[ti % 2]
        for c in range(4):
            normed_chunk(4 + c, None, 4, nb[:, c, 0:T], ("nkT", ti % 2, c), False)
        P.dma("sp", k.nkTd[:, :, t0:t0 + T].rearrange("c p t -> p c t"), nb[:, :, 0:T],
              reads=[("nkT", ti % 2, c) for c in range(4)], writes=[("nkTd", ti)])
        for st in range(T // 128):
            b1 = rot.next()
            mm_group(k, PS[b1][:, 0:128], lambda kc: hT[:, kc, st * 128:(st + 1) * 128], lambda kc: wv[:, kc, 0:128], 8,
                     [("wv",)] + htoks, ("ps", b1))
            b2 = rot.next()
            mm_group(k, PS[b2][:, 0:512], lambda kc: hT[:, kc, st * 128:(st + 1) * 128], lambda kc: wv[:, kc, 128:640], 8,
                     [("wv",)] + htoks, ("ps", b2))
            vi = cnt["v"] % 2
            cnt["v"] += 1
            vg, nv = Vgt[vi], NVt[vi]
            vgd = vg[:].rearrange("p (hk par) d -> p hk par d", par=2)
            s1 = PS[b1][:, 0:128].rearrange("p (hk d) -> p hk d", d=64)
            P.dve(lambda e: e.tensor_copy(out=vgd[:, :, 0, 0:64], in_=s1), reads=[("ps", b1)], writes=[("Vgt", vi)])
            P.dve(lambda e: e.tensor_copy(out=vgd[:, :, 1, 64:128], in_=s1), reads=[("ps", b1)], writes=[(("Vgt", vi), 1)])
            nvd = nv[:].rearrange("p (h two) d -> p h two d", two=2)
            s2 = PS[b2][:, 0:512].rearrange("p (h two d) -> p h two d", two=2, d=64)
            P.dve(lambda e: e.tensor_copy(out=nvd[:, :, 0, 0:64], in_=s2[:, :, 0, :]), reads=[("ps", b2)], writes=[("NVt", vi)])
            P.dve(lambda e: e.tensor_copy(out=nvd[:, :, 1, 64:128], in_=s2[:, :, 1, :]), reads=[("ps", b2)],
                  writes=[(("NVt", vi), 1)])
            chunk = (t0 + st * 128) // 128
            P.dma("sp", k.VgD[:, :, chunk, :].rearrange("v p d -> p v d"), vg[:], reads=[("Vgt", vi), (("Vgt", vi), 1)],
                  writes=[("VgD", chunk)])
            P.dma("sp", k.NVd[:, :, chunk, :].rearrange("h p d -> p h d"), nv[:], reads=[("NVt", vi), (("NVt", vi), 1)],
                  writes=[("NVd", chunk)])
        if lat:
            qb = QgT[ti % 2]
            for c in range(4):
                normed_chunk(8 + c, 12 + c, 2, qb[:, c, 0:T], ("QgT", ti % 2, c), True)
            P.dma("sp", k.QgTd[:, :, t0:t0 + T].rearrange("c p t -> p c t"), qb[:, :, 0:T],
                  reads=[("QgT", ti % 2, c) for c in range(4)], writes=[("QgTd", ti)])
            nq = nqT[ti % 2]
            for c in range(4):
                normed_chunk(16 + c, None, 5, nq[:, c, 0:T], ("nqT", ti % 2, c), False)
            P.dma("sp", k.nqTd[:, :, t0:t0 + T].rearrange("c p t -> p c t"), nq[:, :, 0:T],
                  reads=[("nqT", ti % 2, c) for c in range(4)], writes=[("nqTd", ti)])
    A.release(m)


def l1_na_phase(k):
    P, A, PS = k.P, k.A, k.PS
    m = A.mark()
    nk = A.alloc("nk", [128, 4, TALL], BF16)
    NV = A.alloc("NV", [128, 8, 34, 128], BF16)
    nab = A.alloc("nab", [128, NSIG * 8, 128], BF16)
    nq = [A.alloc(f"nq{i}", [128, 4, 128], BF16) for i in range(2)]
    PT = [A.alloc(f"PT{i}", [128, 8, 128], BF16) for i in range(2)]
    rec = A.alloc("rec", [128, 128], F32)
    nat = [A.alloc(f"nat{i}", [128, 4, 128], BF16) for i in range(2)]
    identb = k.cbf[:, 0, :]
    P.dma("sp", nk[:], k.nkTd.rearrange("c p t -> p c t"), writes=[("nk",)])
    for h in range(8):
        P.dma("sp", NV[:, h], k.NVd[h], writes=[("NV", h)])
    for g in range(8):
        n0, n1 = g * NSIG, (g + 1) * NSIG
        P.dma("sp", nab[:, n0:n1, :], k.wbf["l1_nab"][n0:n1].rearrange("n p q -> p n q"),
              reads=[("wbf", "l1_nab", i) for i in range(n0, n1)], writes=[("nab", g)])
    nabtoks = [("nab", g) for g in range(8)]
    hc = 0
    for i in range(32):
        t0 = NCTX + i * 128
        nqb = nq[i % 2]
        natb = nat[i % 2]
        P.dma("sp", nqb[:], k.nqTd[:, :, t0:t0 + 128].rearrange("c p t -> p c t"), writes=[("nq", i % 2)])
        chunks = [(0, None), (1, None)] + [(2 + kc, NA_MAP[(i, kc)]) for kc in NA_CHUNKS[i]]
        n = len(chunks)
        for h in range(8):
            ch, off = h // 2, (h % 2) * 64
            sb0 = 2 * (hc % 2)
            acc = 4 + (hc % 2)
            pt = PT[hc % 2]
            pttok = ("PTn", hc % 2)
            hc += 1

            def smm(e):
                for idx, (c, sig) in enumerate(chunks):
                    o = PS[sb0 + idx // 4][:, (idx % 4) * 128:(idx % 4 + 1) * 128]
                    e.matmul(o, lhsT=nk[off:off + 64, ch, c * 128:(c + 1) * 128], rhs=nqb[off:off + 64, ch, :],
                             start=True, stop=(sig is None))
                    if sig is not None:
                        e.matmul(o, lhsT=identb, rhs=nab[:, sig * 8 + h, :], start=False, stop=True)
            P.pe(smm, reads=[("nk",), ("nq", i % 2), ("cbf",)] + nabtoks, writes=[("ps", sb0), ("ps", sb0 + 1)])
            P.act(lambda e: e.activation(out=pt[:, 0:4, :], in_=PS[sb0][:, 0:512].rearrange("p (c q) -> p c q", q=128), func=AF.Exp),
                  reads=[("ps", sb0)], writes=[(pttok, 0)])
            P.act(lambda e: e.activation(out=pt[:, 4:n, :], in_=PS[sb0 + 1][:, 0:(n - 4) * 128].rearrange("p (c q) -> p c q", q=128),
                                         func=AF.Exp),
                  reads=[("ps", sb0 + 1)], writes=[(pttok, 1)])

            def pv(e):
                for idx, (c, sig) in enumerate(chunks):
                    e.matmul(PS[acc][:, 0:128], lhsT=NV[:, h, c, :], rhs=pt[:, idx, :], start=(idx == 0), stop=(idx == n - 1))
            P.pe(pv, reads=[("NV", h), (pttok, 0), (pttok, 1)], writes=[("ps", acc)])
            nlo, dlo = (0, 64) if h % 2 == 0 else (64, 0)
            P.dve(lambda e: e.reciprocal(out=rec[nlo:nlo + 64, :], in_=PS[acc][dlo:dlo + 64, 0:128]),
                  reads=[("ps", acc)], writes=[("rec", h % 2)])
            P.dve(lambda e: e.tensor_tensor(out=natb[nlo:nlo + 64, ch, :], in0=PS[acc][nlo:nlo + 64, 0:128], in1=rec[nlo:nlo + 64, :],
                                            op=ALU.mult),
                  reads=[("ps", acc), ("rec", h % 2)], writes=[("nat", i % 2, h)])
        P.dma("sp", k.attTd[4:8, :, t0:t0 + 128].rearrange("c p t -> p c t"), natb[:],
              reads=[("nat", i % 2, h) for h in range(8)], writes=[("attTd", "na", i)])
    A.release(m)
```

```python
import numpy as np
import concourse.bass as bass
import concourse.mybir as mybir
from concourse.bass_utils import run_bass_kernel_spmd

F32 = mybir.dt.float32
BF16 = mybir.dt.bfloat16
AF = mybir.ActivationFunctionType
ALU = mybir.AluOpType

D = 1024
S = 4096
NCTX = 256
TALL = S + NCTX
DFF = 2816
NFC = DFF // 128
EPS = 1e-6
GRID_W = 64
NDMASEM = 40
TILES = [(1, 0, 256)] + [(0, 256 + 512 * i, 512) for i in range(8)]


class Op:
    __slots__ = ("stream", "kind", "fn", "idx", "event", "signal", "cclock", "waits", "slot")


class _Rec:
    def __init__(self):
        self.calls = []

    def __getattr__(self, name):
        def f(*a, **kw):
            self.calls.append((name, a, kw))
            return None
        return f


def _replay(calls):
    def fn(e):
        ins = None
        for name, a, kw in calls:
            ins = getattr(e, name)(*a, **kw)
        return ins
    return fn


class Prog:
    STREAMS = ("pe", "act", "dve", "pool", "sp")

    def __init__(self, nc):
        self.nc = nc
        self.ops = {s: [] for s in self.STREAMS}
        self.clock = {s: {} for s in self.STREAMS}
        self.lastw = {}
        self.readers = {}
        self.ncomp = {s: 0 for s in self.STREAMS}
        self.pending = {}
        self.dma_slot_last = [None] * NDMASEM
        self.dma_slot_cnt = [0] * NDMASEM
        self.dma_rr = 0
        self.dmas_since_barrier = []
        self.last_comp = {}
        self.nops = 0
        self.maxops = None

    def _add(self, stream, kind, fn, reads, writes):
        if self.maxops is not None and self.nops >= self.maxops and fn is not None:
            return None
        op = Op()
        op.stream = stream
        op.kind = kind
        if fn is not None:
            rec = _Rec()
            fn(rec)
            fn = _replay(rec.calls)
        op.fn = fn
        op.signal = False
        op.slot = None
        self.nops += 1
        deps = []
        for r in reads:
            w = self.lastw.get(r)
            if w is not None:
                deps.append((w, True))
        for w_ in writes:
            lw = self.lastw.get(w_)
            if lw is not None:
                deps.append((lw, False))
            for rd in self.readers.get(w_, ()):
                deps.append((rd, False))
        for r in reads:
            self.readers.setdefault(r, []).append(op)
        for w_ in writes:
            self.readers[w_] = []
            self.lastw[w_] = op
        pend = self.pending.pop(stream, None)
        if pend:
            deps.extend((p, True) for p in pend)
        if kind == "d":
            slot = self.dma_rr
            self.dma_rr = (self.dma_rr + 1) % NDMASEM
            prev = self.dma_slot_last[slot]
            if prev is not None:
                deps.append((prev, True))
            self.dma_slot_cnt[slot] += 1
            self.dma_slot_last[slot] = op
            op.slot = slot
            op.event = (("d", slot), 16 * self.dma_slot_cnt[slot])
            self.dmas_since_barrier.append(op)
        else:
            self.ncomp[stream] += 1
            op.event = (("e", stream), self.ncomp[stream])
            self.last_comp[stream] = op
        clk = self.clock[stream]
        waits = {}
        for d, raw in deps:
            if d is op:
                continue
            if d.stream == stream and d.kind == "c":
                if stream == "pe" or not raw:
                    continue
            key, val = d.event
            if clk.get(key, 0) >= val:
                continue
            if waits.get(key, (0,))[0] < val:
                waits[key] = (val, d)
            for k, v in d.cclock.items():
                if clk.get(k, 0) < v:
                    clk[k] = v
        for key, (val, d) in waits.items():
            d.signal = True
        op.waits = {k: v[0] for k, v in waits.items()}
        cc = dict(clk)
        cc[op.event[0]] = op.event[1]
        op.cclock = cc
        if kind == "c" and stream != "pe":
            pass
        self.ops[stream].append(op)
        return op

    def pe(self, fn, reads=(), writes=()):
        return self._add("pe", "c", fn, reads, writes)

    def act(self, fn, reads=(), writes=()):
        return self._add("act", "c", fn, reads, writes)

    def dve(self, fn, reads=(), writes=()):
        return self._add("dve", "c", fn, reads, writes)

    def pool(self, fn, reads=(), writes=()):
        return self._add("pool", "c", fn, reads, writes)

    def dma(self, q, out, in_, reads=(), writes=()):
        return self._add(q, "d", lambda e: e.dma_start(out=out, in_=in_), reads, writes)

    def barrier(self):
        lst = list(self.last_comp.values()) + self.dmas_since_barrier
        self.dmas_since_barrier = []
        for s in self.STREAMS:
            self.pending[s] = list(lst)

    def emit(self):
        nc = self.nc
        self.barrier()
        fin = self._add("sp", "c", None, (), ())
        esem = {s: nc.alloc_semaphore("es_" + s) for s in self.STREAMS}
        dsem = [nc.alloc_semaphore(f"ds{i}") for i in range(NDMASEM)]
        sigcount = {}
        for s in self.STREAMS:
            cnt = 0
            m = {}
            for o in self.ops[s]:
                if o.kind == "c":
                    if o.signal:
                        cnt += 1
                    m[o.event[1]] = (cnt, o.signal)
            sigcount[s] = m

        def resolve(key, val):
            if key[0] == "d":
                return dsem[key[1]], val
            cnt, sig = sigcount[key[1]][val]
            assert sig
            return esem[key[1]], cnt

        with nc.Block() as block:
            decos = {"pe": block.tensor, "act": block.scalar, "dve": block.vector,
                     "pool": block.gpsimd, "sp": block.sync}
            for s in self.STREAMS:
                ops = self.ops[s]

                def body(eng, ops=ops, s=s):
                    for o in ops:
                        for key, val in o.waits.items():
                            sem, v = resolve(key, val)
                            eng.wait_ge(sem, v)
                        if o.fn is None:
                            continue
                        ins = o.fn(eng)
                        if o.kind == "d":
                            ins.then_inc(dsem[o.slot], 16)
                        elif o.signal:
                            ins.then_inc(esem[s], 1)
                decos[s](body)


class Arena:
    def __init__(self, nc, limit=229376):
        self.nc = nc
        self.off = 16640
        self.limit = limit
        self.n = 0

    def alloc(self, name, shape, dtype):
        esz = 4 if dtype == F32 else 2
        sz = esz
        for d in shape[1:]:
            sz *= d
        sz = (sz + 63) // 64 * 64
        assert self.off + sz <= self.limit, (name, self.off, sz)
        self.n += 1
        t = self.nc.alloc_sbuf_tensor_at(f"{name}_{self.n}", list(shape), dtype, offset=self.off)
        self.off += sz
        return t

    def mark(self):
        return self.off

    def release(self, m):
        self.off = m


def kmajor(w, ncols_pad=None):
    K, N = w.shape
    return np.ascontiguousarray(w.reshape(K // 128, 128, N).transpose(1, 0, 2))


def pvec(v):
    return np.ascontiguousarray(v.reshape(-1, 128).T)


class K:
    pass


def declare_inputs(k, nc):
    k.inp = {}
    k.inshape = {}

    def din(name, shape):
        k.inp[name] = nc.dram_tensor(name, list(shape), F32, kind="ExternalInput").ap()
        k.inshape[name] = tuple(shape)

    din("x", [S, D])
    din("ctx", [NCTX, D])
    din("cT", [128, 8, 2])
    din("cmat", [128, 4, 128])
    for l in range(2):
        din(f"l{l}_mod_w", [18, 128, 8 * 512])
        din(f"l{l}_mod_bT", [128, 72])
        din(f"l{l}_ng", [128, 3, 8])
        for w in (1, 2):
            din(f"l{l}_f{w}_win", [NFC, 128, 8 * 256])
            din(f"l{l}_f{w}_wout", [8, 128, NFC * 128])


def host_inputs(inputs, b):
    m = {}
    m["x"] = np.ascontiguousarray(inputs["x"][b])
    m["ctx"] = np.ascontiguousarray(inputs["ctx"][b])
    cT = np.stack([pvec(inputs["c"][b]), pvec(inputs["c_ctx"])], axis=-1)
    m["cT"] = np.ascontiguousarray(cT.astype(np.float32))
    return m


_SHARED = {}


def host_shared(inputs):
    m = {}
    cm = np.zeros((128, 4, 128), np.float32)
    cm[:, 0, :] = np.eye(128, dtype=np.float32)
    cm[:, 1, :] = 1.0
    for g in range(2):
        cm[g * 64:(g + 1) * 64, 2, g * 64:(g + 1) * 64] = 1.0
    cm[0:64, 3, 0:64] = 1.0
    cm[64:96, 3, 64:96] = 1.0
    m["cmat"] = cm
    for l in range(2):
        p = f"l{l}_"
        mw = inputs[p + "mod_w"]
        m[p + "mod_w"] = np.ascontiguousarray(
            mw.reshape(8, 128, 18, 512).transpose(2, 1, 0, 3).reshape(18, 128, 8 * 512))
        m[p + "mod_bT"] = pvec(inputs[p + "mod_b"])
        m[p + "ng"] = np.ascontiguousarray(np.stack(
            [pvec(inputs[p + "ffn1_norm"]), pvec(inputs[p + "mix_norm"]), pvec(inputs[p + "ffn2_norm"])], axis=1))
        for w in (1, 2):
            wi = inputs[p + f"ffn{w}_w_in"]
            g = wi[:, :DFF].reshape(8, 128, NFC, 128)
            u = wi[:, DFF:].reshape(8, 128, NFC, 128)
            gu = np.stack([g, u], axis=3)
            m[p + f"f{w}_win"] = np.ascontiguousarray(gu.transpose(2, 1, 0, 3, 4).reshape(NFC, 128, 8 * 256))
            wo = inputs[p + f"ffn{w}_w_out"]
            m[p + f"f{w}_wout"] = np.ascontiguousarray(
                wo.reshape(NFC, 128, 8, 128).transpose(2, 1, 0, 3).reshape(8, 128, NFC * 128))
    m.update(host_mixer(inputs))
    return m


def build(stop_after=None, debug=False, skip=(), maxops=None):
    nc = bass.Bass("TRN2", target_bir_lowering=False)
    k = K()
    k.nc = nc
    k.debug = debug
    P = Prog(nc)
    k.maxops = maxops
    k.P = P
    A = Arena(nc)
    k.A = A
    declare_inputs(k, nc)
    k.out = nc.dram_tensor("out", [S, D], F32, kind="ExternalOutput").ap()
    k.dbg_outs = []
    skind = "ExternalOutput" if debug else "Internal"
    k.xTd = nc.dram_tensor("xTd", [8, 128, TALL], F32, kind=skind).ap()
    k.wbf = {}

    def precast(name):
        shape = k.inshape[name]
        t = nc.dram_tensor(name + "_bf", list(shape), BF16, kind="Internal").ap()
        k.wbf[name] = t
        for i in range(shape[0]):
            P.dma("pool", t[i], k.inp[name][i], reads=(), writes=[("wbf", name, i)])

    k.PS = [nc.alloc_psum_tensor(f"psb{i}", [128, 512], F32) for i in range(8)]

    k.ident = A.alloc("ident", [128, 128], F32)
    k.cbf = A.alloc("cbf", [128, 4, 128], BF16)
    k.modv = A.alloc("modv", [128, 2, 2, 72], F32)
    k.der = A.alloc("der", [128, 2, 2, 5, 8], F32)
    k.ng = A.alloc("ng", [128, 2, 3, 8], F32)
    P.dma("sp", k.ident[:], k.inp["cmat"][:, 0, :], writes=[("ident",)])
    P.dma("pool", k.cbf[:], k.inp["cmat"], writes=[("cbf",)])
    for l in range(2):
        P.dma("sp", k.ng[:, l], k.inp[f"l{l}_ng"], writes=[("ng", l)])
    k.ones_bf = k.cbf[:, 1, :]
    k.bd64_bf = k.cbf[:, 2, :]

    for l in range(2):
        for w in (1, 2):
            precast(f"l{l}_f{w}_win")
            precast(f"l{l}_f{w}_wout")

    declare_mixer_inputs(k, nc)
    for name in ("l0_wA", "l0_wkn", "l0_wv", "l0_wq", "l0_wo", "l1_wo", "l1_wA", "l1_wv", "l1_nab"):
        precast(name)

    def dscr(name, shape, dt=BF16):
        return nc.dram_tensor(name, list(shape), dt, kind=skind if name in ("attTd",) else "Internal").ap()
    k.yTd = dscr("yTd", [4, 128, TALL])
    k.KTd = dscr("KTd", [8, 96, TALL])
    k.QTd = dscr("QTd", [8, 96, TALL])
    k.Vd = dscr("Vd", [8, 128, 34, 128])
    k.attTd = dscr("attTd", [8, 128, TALL])
    k.KdupTd = dscr("KdupTd", [2, 128, TALL])
    k.QgTd = dscr("QgTd", [4, 128, TALL])
    k.nkTd = dscr("nkTd", [4, 128, TALL])
    k.nqTd = dscr("nqTd", [4, 128, TALL])
    k.VgD = dscr("VgD", [4, 128, 34, 128])
    k.NVd = dscr("NVd", [8, 128, 34, 128])

    phases = [
        ("mod", lambda: setup_mod(k)),
        ("l0f1", lambda: ffn_phase(k, 0, 1, TILES, src="tok", dst="xT")),
        ("l0proj", lambda: l0_proj_phase(k)),
        ("l0att", lambda: attn_phase(k, "mla")),
        ("l0conv", lambda: l0_conv_phase(k)),
        ("l0mix", lambda: mixout_phase(k, 0, TILES)),
        ("l0", lambda: ffn_phase(k, 0, 2, TILES, src="xT", dst="xT")),
        ("l1f1", lambda: ffn_phase(k, 1, 1, TILES, src="xT", dst="xT")),
        ("l1proj", lambda: l1_proj_phase(k)),
        ("l1gqa", lambda: attn_phase(k, "gqa")),
        ("l1na", lambda: l1_na_phase(k)),
        ("l1mix", lambda: mixout_phase(k, 1, TILES[1:])),
        ("full", lambda: ffn_phase(k, 1, 2, TILES[1:], src="xT", dst="tok")),
    ]
    for name, fn in phases:
        if name in skip:
            continue
        n0 = P.nops
        if maxops is not None and name == stop_after:
            P.maxops = P.nops + maxops
        fn()
        P.barrier()
        if debug:
            print("phase", name, "ops", n0, P.nops, flush=True)
        if stop_after == name:
            break
    return finish(k)


def finish(k):
    k.P.emit()
    return k.nc


def setup_mod(k):
    P, A, nc = k.P, k.A, k.nc
    m = A.mark()
    cT = A.alloc("cT", [128, 8, 2], F32)
    sc = A.alloc("sc", [128, 8, 2], F32)
    mb = A.alloc("mb", [128, 72], F32)
    wt = [A.alloc(f"mw{i}", [128, 8, 512], F32) for i in range(2)]
    P.dma("sp", cT[:], k.inp["cT"], writes=[("cT",)])
    P.act(lambda e: e.activation(out=sc[:], in_=cT[:], func=AF.Silu), reads=[("cT",)], writes=[("sc",)])
    n = 0
    for l in range(2):
        P.dma("sp", mb[:], k.inp[f"l{l}_mod_bT"], writes=[("mb",)])
        ps = k.PS[l]
        for blk in range(18):
            w = wt[n % 2]
            wtok = ("mw", n % 2)
            n += 1
            P.dma("sp", w[:], k.inp[f"l{l}_mod_w"][blk].rearrange("p (k n) -> p k n", k=8), writes=[wtok])

            def mm(e, w=w, blk=blk, ps=ps):
                ins = None
                for nn in range(4):
                    j = blk * 4 + nn
                    for kc in range(8):
                        ins = e.matmul(ps[:, 2 * j:2 * j + 2], lhsT=w[:, kc, nn * 128:(nn + 1) * 128],
                                       rhs=sc[:, kc, :], start=(kc == 0), stop=(kc == 7))
                return ins
            P.pe(mm, reads=[wtok, ("sc",)], writes=[("ps", l)])
        for who in range(2):
            src = ps[:, 0:144].rearrange("p (j w) -> p j w", w=2)[:, :, who]
            P.dve(lambda e, src=src, l=l, who=who: e.tensor_tensor(out=k.modv[:, l, who, :], in0=src, in1=mb[:], op=ALU.add),
                  reads=[("ps", l), ("mb",)], writes=[("modv", l, who)])
        for who in range(2):
            for gi, mi in enumerate((1, 4, 7)):
                P.dve(lambda e, l=l, who=who, gi=gi, mi=mi: e.scalar_tensor_tensor(
                    out=k.der[:, l, who, gi, :], in0=k.modv[:, l, who, mi * 8:mi * 8 + 8], scalar=1.0,
                    in1=k.ng[:, l, gi, :], op0=ALU.add, op1=ALU.mult),
                    reads=[("modv", l, who), ("ng", l)], writes=[("der", l, who, gi)])
            for gi, mi in ((3, 2), (4, 8)):
                P.dve(lambda e, l=l, who=who, gi=gi, mi=mi: e.tensor_scalar(
                    out=k.der[:, l, who, gi, :], in0=k.modv[:, l, who, mi * 8:mi * 8 + 8], scalar1=0.5,
                    scalar2=None, op0=ALU.mult),
                    reads=[("modv", l, who)], writes=[("der", l, who, gi)])
    if k.debug:
        dm = nc.dram_tensor("dbg_modv", [128, 2 * 2 * 72], F32, kind="ExternalOutput").ap()
        P.dma("sp", dm, k.modv[:].rearrange("p a b c -> p (a b c)"),
              reads=[("modv", l, w) for l in range(2) for w in range(2)], writes=[("dbg_modv",)])
    A.release(m)


def rms_norm_mod(k, xb, xtoks, T, gp, sh, hT, htok, tmp, r1, r2, rstd, sq):
    P = k.P
    P.pool(lambda e: e.tensor_tensor(out=sq[:, :, 0:T], in0=xb[:, :, 0:T], in1=xb[:, :, 0:T], op=ALU.mult),
           reads=xtoks, writes=[("sq", c) for c in range(8)])

    def mm(e):
        ins = None
        for c in range(8):
            ins = e.matmul(k.PS[6][:, 0:T], lhsT=k.ones_bf, rhs=sq[:, c, 0:T], start=(c == 0), stop=(c == 7))
        return ins
    P.pe(mm, reads=[("sq", c) for c in range(8)] + [("cbf",)], writes=[("ps", 6)])
    P.dve(lambda e: e.tensor_scalar(out=r1[:, 0:T], in0=k.PS[6][:, 0:T], scalar1=1.0 / D, scalar2=EPS,
                                    op0=ALU.mult, op1=ALU.add), reads=[("ps", 6)], writes=[("r1",)])
    P.act(lambda e: e.activation(out=r2[:, 0:T], in_=r1[:, 0:T], func=AF.Sqrt), reads=[("r1",)], writes=[("r2",)])
    P.dve(lambda e: e.reciprocal(out=rstd[:, 0:T], in_=r2[:, 0:T]), reads=[("r2",)], writes=[("rstd",)])
    for c in range(8):
        tb = tmp[c % 2]
        P.dve(lambda e, c=c, tb=tb: e.scalar_tensor_tensor(out=tb[:, 0:T], in0=xb[:, c, 0:T], scalar=gp[:, c:c + 1],
                                                           in1=rstd[:, 0:T], op0=ALU.mult, op1=ALU.mult),
              reads=[xtoks[c], ("rstd",)], writes=[("tmp", c % 2)])
        P.act(lambda e, c=c, tb=tb: e.activation(out=hT[:, c, 0:T], in_=tb[:, 0:T], func=AF.Identity,
                                                 bias=sh[:, c:c + 1], scale=1.0),
              reads=[("tmp", c % 2)], writes=[(htok, c)])


def ffn_phase(k, l, which, tiles, src, dst):
    P, A, nc = k.P, k.A, k.nc
    PS = k.PS
    m = A.mark()
    xt = [A.alloc(f"xt{i}", [128, 8, 512], F32) for i in range(3)]
    hT = [A.alloc(f"hT{i}", [128, 8, 512], BF16) for i in range(2)]
    actT = [A.alloc(f"actT{i}", [128, NFC, 512], BF16) for i in range(2)]
    sq = A.alloc("sq", [128, 8, 512], BF16)
    win = [A.alloc(f"win{i}", [128, 8, 256], BF16) for i in range(3)]
    wout = [A.alloc(f"wout{i}", [128, NFC, 128], BF16) for i in range(2)]
    r1 = A.alloc("r1", [128, 512], F32)
    r2 = A.alloc("r2", [128, 512], F32)
    rstd = A.alloc("rstd", [128, 512], F32)
    tmp = [A.alloc(f"tmp{i}", [128, 512], F32) for i in range(2)]
    sg = [A.alloc(f"sg{i}", [128, 512], F32) for i in range(2)]
    tokb = [A.alloc(f"tokb{i}", [128, 1024], F32) for i in range(2)]
    gi = 0 if which == 1 else 2
    shi = 0 if which == 1 else 6
    hgi = 3 if which == 1 else 4
    wname_in = f"l{l}_f{which}_win"
    wname_out = f"l{l}_f{which}_wout"
    groups = [tiles[i:i + 2] for i in range(0, len(tiles), 2)]
    tk = 0
    wi = 0
    wo = 0
    guc = 0
    yc = 0
    tbc = 0
    for grp in groups:
        bufs = []
        for s, (who, t0, T) in enumerate(grp):
            b = tk % 3
            tk += 1
            bufs.append(b)
            xb = xt[b]
            xtoks = [("xt", b, c) for c in range(8)]
            if src == "xT":
                P.dma("sp", xb[:, :, 0:T], k.xTd[:, :, t0:t0 + T].rearrange("c p t -> p c t"), writes=xtoks)
            else:
                srcap = k.inp["ctx"] if who == 1 else k.inp["x"]
                r0 = t0 if who == 1 else t0 - NCTX
                for st in range(T // 128):
                    tb = tokb[tbc % 2]
                    ttok = ("tokb", tbc % 2)
                    tbc += 1
                    P.dma("sp", tb[:], srcap[r0 + st * 128:r0 + (st + 1) * 128, :], writes=[ttok])
                    for half in range(2):
                        bank = PS[(guc % 4)]
                        btok = ("ps", guc % 4)
                        guc += 1

                        def tr(e, tb=tb, half=half, bank=bank):
                            ins = None
                            for cc in range(4):
                                c = half * 4 + cc
                                ins = e.transpose(bank[:, cc * 128:(cc + 1) * 128], tb[:, c * 128:(c + 1) * 128], k.ident[:])
                            return ins
                        P.pe(tr, reads=[ttok, ("ident",)], writes=[btok])
                        P.act(lambda e, xb=xb, half=half, bank=bank, st=st: e.activation(
                            out=xb[:, half * 4:half * 4 + 4, st * 128:(st + 1) * 128],
                            in_=bank[:].rearrange("p (c t) -> p c t", c=4), func=AF.Copy),
                            reads=[btok], writes=xtoks[half * 4:half * 4 + 4])
            rms_norm_mod(k, xb, xtoks, T, k.der[:, l, who, gi, :], k.modv[:, l, who, shi * 8:shi * 8 + 8],
                         hT[s], ("hT", s), tmp, r1, r2, rstd, sq)
        for j in range(NFC):
            wb = win[wi % 3]
            wtok = ("win", wi % 3)
            wi += 1
            P.dma("sp", wb[:], k.wbf[wname_in][j].rearrange("p (k n) -> p k n", k=8),
                  reads=[("wbf", wname_in, j)], writes=[wtok])
            for s, (who, t0, T) in enumerate(grp):
                bg = (guc % 2) * 2
                guc += 1
                htoks = [(("hT", s), c) for c in range(8)]

                def mmg(e, wb=wb, s=s, T=T, bank=PS[bg], off=0):
                    ins = None
                    for kc in range(8):
                        ins = e.matmul(bank[:, 0:T], lhsT=wb[:, kc, off:off + 128], rhs=hT[s][:, kc, 0:T],
                                       start=(kc == 0), stop=(kc == 7))
                    return ins
                P.pe(mmg, reads=[wtok] + htoks, writes=[("ps", bg)])
                P.pe(lambda e, wb=wb, s=s, T=T, bank=PS[bg + 1], f=mmg: f(e, wb, s, T, bank, 128),
                     reads=[wtok] + htoks, writes=[("ps", bg + 1)])
                q = guc % 2
                P.act(lambda e, q=q, bg=bg, T=T: e.activation(out=sg[q][:, 0:T], in_=PS[bg][:, 0:T], func=AF.Silu),
                      reads=[("ps", bg)], writes=[("sg", q)])
                P.dve(lambda e, q=q, bg=bg, T=T, s=s, j=j: e.tensor_tensor(
                    out=actT[s][:, j, 0:T], in0=PS[bg + 1][:, 0:T], in1=sg[q][:, 0:T], op=ALU.mult),
                    reads=[("ps", bg + 1), ("sg", q)], writes=[("actT", s, j)])
        for d in range(8):
            wb = wout[wo % 2]
            wtok = ("wout", wo % 2)
            wo += 1
            P.dma("sp", wb[:], k.wbf[wname_out][d].rearrange("p (k n) -> p k n", k=NFC),
                  reads=[("wbf", wname_out, d)], writes=[wtok])
            for s, (who, t0, T) in enumerate(grp):
                by = 4 + (yc % 2)
                yc += 1
                xb = xt[bufs[s]]

                def mmy(e, wb=wb, s=s, T=T, by=by):
                    ins = None
                    for fc in range(NFC):
                        ins = e.matmul(PS[by][:, 0:T], lhsT=wb[:, fc, :], rhs=actT[s][:, fc, 0:T],
                                       start=(fc == 0), stop=(fc == NFC - 1))
                    return ins
                P.pe(mmy, reads=[wtok] + [("actT", s, j) for j in range(NFC)], writes=[("ps", by)])
                hg = k.der[:, l, who, hgi, :]
                P.dve(lambda e, xb=xb, d=d, T=T, by=by, hg=hg: e.scalar_tensor_tensor(
                    out=xb[:, d, 0:T], in0=PS[by][:, 0:T], scalar=hg[:, d:d + 1], in1=xb[:, d, 0:T],
                    op0=ALU.mult, op1=ALU.add),
                    reads=[("ps", by), ("xt", bufs[s], d)], writes=[("xt", bufs[s], d)])
        for s, (who, t0, T) in enumerate(grp):
            b = bufs[s]
            xb = xt[b]
            xtoks = [("xt", b, c) for c in range(8)]
            if dst == "xT":
                P.dma("sp", k.xTd[:, :, t0:t0 + T].rearrange("c p t -> p c t"), xb[:, :, 0:T], reads=xtoks,
                      writes=[("xTd", t0)])
            else:
                r0 = t0 - NCTX
                for st in range(T // 128):
                    ob = tokb[tbc % 2]
                    otok = ("tokb", tbc % 2)
                    tbc += 1
                    for half in range(2):
                        bi = guc % 4
                        guc += 1

                        def tr(e, xb=xb, half=half, bi=bi, st=st):
                            ins = None
                            for cc in range(4):
                                c = half * 4 + cc
                                ins = e.transpose(PS[bi][:, cc * 128:(cc + 1) * 128], xb[:, c, st * 128:(st + 1) * 128],
                                                  k.ident[:])
                            return ins
                        P.pe(tr, reads=xtoks[half * 4:half * 4 + 4] + [("ident",)], writes=[("ps", bi)])
                        P.act(lambda e, ob=ob, half=half, bi=bi: e.activation(
                            out=ob[:, half * 512:(half + 1) * 512], in_=PS[bi][:], func=AF.Copy),
                            reads=[("ps", bi)], writes=[(otok, half)])
                    P.dma("sp", k.out[r0 + st * 128:r0 + (st + 1) * 128, :], ob[:], reads=[(otok, 0), (otok, 1)],
                          writes=[("out", r0, st)])
    A.release(m)


_CACHE = {}


def kernel(**inputs):
    inputs = {kk: np.asarray(v) for kk, v in inputs.items()}
    if "nc" not in _CACHE:
        _CACHE["nc"] = build()
    nc = _CACHE["nc"]
    shared = host_shared(inputs)
    in_maps = []
    for b in range(8):
        mm = dict(shared)
        mm.update(host_inputs(inputs, b))
        in_maps.append(mm)
    res = run_bass_kernel_spmd(nc, in_maps, core_ids=list(range(8)))
    return np.stack([r["out"] for r in res.results], axis=0)


NEG = -30000.0
PERM64 = list(range(0, 64, 2)) + list(range(1, 64, 2))
SWAP64 = list(range(1, 64, 2)) + list(range(0, 64, 2))
PERM32 = list(range(0, 32, 2)) + list(range(1, 32, 2))
SWAP32 = list(range(1, 32, 2)) + list(range(0, 32, 2))


def rope_tables():
    t = np.arange(S)
    row = (t // GRID_W).astype(np.float32)
    col = (t % GRID_W).astype(np.float32)

    def tab(dim):
        npairs = dim // 4
        inv = (10000.0 ** (-np.arange(npairs, dtype=np.float32) / npairs)).astype(np.float32)
        ang = np.concatenate([row[:, None] * inv, col[:, None] * inv], axis=-1).astype(np.float32)
        return np.cos(ang).astype(np.float32).T, np.sin(ang).astype(np.float32).T
    cA, sA = tab(64)
    cB, sB = tab(32)
    ta = np.zeros((128, 2, S), np.float32)
    for p in range(128):
        pp = p % 64
        pr = pp % 32
        ta[p, 0] = cA[pr]
        ta[p, 1] = -sA[pr] if pp < 32 else sA[pr]
    tb = np.zeros((128, 2, S), np.float32)
    for p in range(64, 96):
        pp = p - 64
        pr = pp % 16
        tb[p, 0] = cB[pr]
        tb[p, 1] = -sB[pr] if pp < 16 else sB[pr]
    return ta, tb


def na_geometry():
    sigs = []
    mp = {}
    chunks = {}
    for i in range(32):
        rs0 = min(max(2 * i - 4, 0), 56)
        rs1 = min(max(2 * i + 1 - 4, 0), 56)
        lo = rs0 // 2
        hi = (rs1 + 7) // 2
        chunks[i] = list(range(lo, hi + 1))
        for kc in chunks[i]:
            sig = (rs0 - 2 * i, rs1 - (2 * i + 1), kc - i)
            if sig not in sigs:
                sigs.append(sig)
            mp[(i, kc)] = sigs.index(sig)
    return sigs, mp, chunks


NA_SIGS, NA_MAP, NA_CHUNKS = na_geometry()
NSIG = len(NA_SIGS)


def na_bias_tiles(rpb):
    out = np.full((NSIG, 8, 128, 128), NEG, np.float32)
    kp = np.arange(128)
    qf = np.arange(128)
    kr_, kc_ = kp // 64, kp % 64
    qr_, qc_ = qf // 64, qf % 64
    cs = np.clip(qc_ - 8, 0, 48)
    for si, (a0, a1, dk) in enumerate(NA_SIGS):
        rsrel = np.where(qr_ == 0, a0, a1)
        krel = (2 * dk + kr_[:, None]) - qr_[None, :]
        vr = (krel >= rsrel[None, :]) & (krel <= rsrel[None, :] + 7)
        vc = (kc_[:, None] >= cs[None, :]) & (kc_[:, None] <= cs[None, :] + 15)
        valid = vr & vc
        dr = np.clip(krel + 7, 0, 14)
        dc = np.clip(kc_[:, None] - qc_[None, :] + 15, 0, 30)
        for h in range(8):
            g = rpb[h][dr, dc]
            out[si, h] = np.where(valid, g, np.float32(NEG))
    return out.reshape(NSIG * 8, 128, 128)


def blk(w, cols):
    K_ = w.shape[0]
    o = np.zeros((K_, len(cols)), np.float32)
    idx = [i for i, c in enumerate(cols) if c is not None]
    o[:, idx] = w[:, [cols[i] for i in idx]]
    return o.reshape(K_ // 128, 128, len(cols)).transpose(1, 0, 2)


def host_mixer(inputs):
    m = {}
    ta, tb = rope_tables()
    m["ropeA"] = ta
    m["ropeB"] = tb
    w = inputs["l0_w_in"]
    blocks = []
    blocks.append(blk(w, list(range(0, 128))))
    blocks.append(blk(w, list(range(128, 256))))
    blocks.append(blk(w, [None] * 64 + [256 + i for i in PERM32] + [None] * 32))
    blocks.append(blk(w, [None] * 64 + [256 + i for i in SWAP32] + [None] * 32))
    for c in range(3):
        blocks.append(blk(w, list(range(288 + c * 128, 288 + (c + 1) * 128))))
    for c in range(8):
        blocks.append(blk(w, list(range(672 + c * 128, 672 + (c + 1) * 128))))
    m["l0_wA"] = np.ascontiguousarray(np.stack(blocks, 0).reshape(15, 128, 8 * 128))
    wk = inputs["l0_mla_w_ukv"]
    kb = [blk(wk, [(2 * pr) * 128 + i for i in range(64)] + [(2 * pr + 1) * 128 + i for i in range(64)]) for pr in range(4)]
    m["l0_wkn"] = np.ascontiguousarray(np.stack(kb, 0).reshape(4, 128, 2 * 128))
    m["l0_wv"] = np.ascontiguousarray(blk(wk, [h * 128 + 64 + i for h in range(8) for i in range(64)]).reshape(1, 128, 2 * 512))
    wq = inputs["l0_mla_w_uq"]
    qb = []
    for h in range(8):
        qb.append(blk(wq, [h * 96 + i for i in range(64)] + [h * 96 + 64 + i for i in PERM32]))
    for h in range(8):
        qb.append(blk(wq, [None] * 64 + [h * 96 + 64 + i for i in SWAP32]))
    m["l0_wq"] = np.ascontiguousarray(np.stack(qb, 0).reshape(16, 128, 3 * 96))
    for l in range(2):
        wo = inputs[f"l{l}_w_out"]
        m[f"l{l}_wo"] = np.ascontiguousarray(wo.reshape(8, 128, 8, 128).transpose(2, 1, 0, 3).reshape(8, 128, 8 * 128))
    m["l0_dw"] = np.ascontiguousarray(inputs["l0_conv_dw_w"].reshape(31, 4, 128).transpose(2, 1, 0))
    v = np.zeros((128, 32), np.float32)
    v[:, 0:2] = pvec(inputs["l0_mla_kv_norm"])
    v[:, 2:5] = pvec(inputs["l0_mla_q_norm"])
    kg = inputs["l0_mla_k_gain"]
    qg = inputs["l0_mla_q_gain"]
    v[:, 5] = np.tile(kg[:64], 2)
    v[64:96, 6] = kg[64:][PERM32]
    v[64:96, 7] = kg[64:][SWAP32]
    v[0:64, 8] = qg[:64]
    v[64:96, 8] = qg[64:][PERM32]
    v[64:96, 9] = qg[64:][SWAP32]
    v[0:64, 10] = 1.0 / 64
    v[64:96, 10] = 1.0 / 32
    gb = inputs["l0_conv_glu_b"]
    v[:, 11:15] = pvec(gb[:512])
    v[:, 15:19] = pvec(gb[512:])
    v[:, 19:23] = pvec(inputs["l0_conv_dw_b"])
    v[:, 23:27] = pvec(inputs["l0_conv_ln_g"])
    v[:, 27:31] = pvec(inputs["l0_conv_ln_b"])
    m["l0_vec"] = v
    w = inputs["l1_w_in"]
    blocks = []
    for hk in range(2):
        blocks.append(blk(w, [hk * 64 + i for i in PERM64] * 2))
    for hk in range(2):
        blocks.append(blk(w, [hk * 64 + i for i in SWAP64] * 2))
    for c in range(4):
        blocks.append(blk(w, list(range(256 + c * 128, 256 + (c + 1) * 128))))
    for c in range(4):
        blocks.append(blk(w, [1280 + (2 * c + e) * 64 + i for e in range(2) for i in PERM64]))
    for c in range(4):
        blocks.append(blk(w, [1280 + (2 * c + e) * 64 + i for e in range(2) for i in SWAP64]))
    for c in range(4):
        blocks.append(blk(w, list(range(1792 + c * 128, 1792 + (c + 1) * 128))))
    m["l1_wA"] = np.ascontiguousarray(np.stack(blocks, 0).reshape(20, 128, 8 * 128))
    m["l1_wv"] = np.ascontiguousarray(blk(w, list(range(128, 256)) + list(range(768, 1280))).reshape(1, 128, 8 * 640))
    v = np.zeros((128, 8), np.float32)
    v[:, 0] = np.tile(inputs["l1_gqa_k_gain"][PERM64], 2)
    v[:, 1] = np.tile(inputs["l1_gqa_k_gain"][SWAP64], 2)
    v[:, 2] = np.tile(inputs["l1_gqa_q_gain"][PERM64], 2)
    v[:, 3] = np.tile(inputs["l1_gqa_q_gain"][SWAP64], 2)
    v[:, 4] = np.tile(inputs["l1_na_k_gain"], 2)
    v[:, 5] = np.tile(inputs["l1_na_q_gain"], 2)
    m["l1_vec"] = v
    m["l1_nab"] = na_bias_tiles(inputs["l1_na_rpb"])
    return m


def declare_mixer_inputs(k, nc):
    def din(name, shape):
        k.inp[name] = nc.dram_tensor(name, list(shape), F32, kind="ExternalInput").ap()
        k.inshape[name] = tuple(shape)
    din("ropeA", [128, 2, S])
    din("ropeB", [128, 2, S])
    din("l0_wA", [15, 128, 1024])
    din("l0_wkn", [4, 128, 256])
    din("l0_wv", [1, 128, 1024])
    din("l0_wq", [16, 128, 288])
    din("l0_wo", [8, 128, 1024])
    din("l1_wo", [8, 128, 1024])
    din("l0_dw", [128, 4, 31])
    din("l0_vec", [128, 32])
    din("l1_wA", [20, 128, 1024])
    din("l1_wv", [1, 128, 8 * 640])
    din("l1_vec", [128, 8])
    din("l1_nab", [NSIG * 8, 128, 128])


def stats_rstd(k, ps_ap, npart, T, scale, r1, r2, rstd, rtoks=(("ps", 6),)):
    P = k.P
    P.dve(lambda e: e.tensor_scalar(out=r1[0:npart, 0:T], in0=ps_ap, scalar1=scale, scalar2=EPS,
                                    op0=ALU.mult, op1=ALU.add), reads=list(rtoks), writes=[("r1",)])
    P.act(lambda e: e.activation(out=r2[0:npart, 0:T], in_=r1[0:npart, 0:T], func=AF.Sqrt),
          reads=[("r1",)], writes=[("r2",)])
    P.dve(lambda e: e.reciprocal(out=rstd[0:npart, 0:T], in_=r2[0:npart, 0:T]), reads=[("r2",)], writes=[("rstd",)])


class PsRot:
    def __init__(self, banks=(0, 1, 2, 3, 4, 5)):
        self.banks = banks
        self.i = 0

    def next(self):
        b = self.banks[self.i % len(self.banks)]
        self.i += 1
        return b


def load_x_norm(k, l, who, t0, T, xb, xtoks, hT, w):
    P = k.P
    P.dma("sp", xb[:, :, 0:T], k.xTd[:, :, t0:t0 + T].rearrange("c p t -> p c t"), writes=xtoks)
    rms_norm_mod(k, xb, xtoks, T, k.der[:, l, who, 1, :], k.modv[:, l, who, 24:32], hT, "hTm",
                 w["tmp"], w["r1"], w["r2"], w["rstd"], w["sq"])


def mm_group(k, bank_ap, lhs_fn, rhs_fn, n, reads, btok):
    def f(e):
        ins = None
        for kc in range(n):
            ins = e.matmul(bank_ap, lhsT=lhs_fn(kc), rhs=rhs_fn(kc), start=(kc == 0), stop=(kc == n - 1))
        return ins
    k.P.pe(f, reads=reads, writes=[btok])


def common_work(k, A):
    w = {}
    w["tmp"] = [A.alloc(f"tmp{i}", [128, 512], F32) for i in range(2)]
    w["r1"] = A.alloc("r1", [128, 512], F32)
    w["r2"] = A.alloc("r2", [128, 512], F32)
    w["rstd"] = A.alloc("rstd", [128, 512], F32)
    w["sq"] = A.alloc("sq", [128, 8, 512], BF16)
    return w


def l0_proj_phase(k):
    P, A, nc, PS = k.P, k.A, k.nc, k.PS
    l = 0
    m = A.mark()
    w = common_work(k, A)
    xt = [A.alloc(f"xt{i}", [128, 8, 512], F32) for i in range(2)]
    hT = A.alloc("hTm", [128, 8, 512], BF16)
    wA = A.alloc("wA", [128, 15, 8, 128], BF16)
    wkn = A.alloc("wkn", [128, 4, 2, 128], BF16)
    wv = A.alloc("wv", [128, 2, 512], BF16)
    wq = A.alloc("wq", [128, 16, 3, 96], BF16)
    vec = A.alloc("vec", [128, 32], F32)
    rope = [A.alloc(f"rope{i}", [128, 2, 512], F32) for i in range(2)]
    ckvT = A.alloc("ckvT", [128, 2, 512], BF16)
    cqT = A.alloc("cqT", [128, 3, 512], BF16)
    krT = A.alloc("krT", [128, 512], BF16)
    yT = [A.alloc(f"yT{i}", [128, 4, 512], BF16) for i in range(2)]
    KT = [A.alloc(f"KT{i}", [128, 8, 512], BF16) for i in range(2)]
    QT = [A.alloc(f"QT{i}", [128, 8, 512], BF16) for i in range(2)]
    Vt = [A.alloc(f"Vt{i}", [128, 8, 128], BF16) for i in range(2)]
    sqs = [A.alloc(f"sqs{i}", [128, 512], BF16) for i in range(2)]
    sig = [A.alloc(f"sig{i}", [128, 512], F32) for i in range(2)]
    ra = A.alloc("ra", [128, 512], F32)
    rb = A.alloc("rb", [128, 512], F32)
    P.dma("sp", vec[:], k.inp["l0_vec"], writes=[("vec",)])
    for i in range(15):
        P.dma("sp", wA[:, i], k.wbf["l0_wA"][i].rearrange("p (k n) -> p k n", k=8), reads=[("wbf", "l0_wA", i)], writes=[("wA",)])
    for i in range(4):
        P.dma("sp", wkn[:, i], k.wbf["l0_wkn"][i].rearrange("p (k n) -> p k n", k=2), reads=[("wbf", "l0_wkn", i)], writes=[("wkn",)])
    P.dma("sp", wv[:], k.wbf["l0_wv"][0].rearrange("p (k n) -> p k n", k=2), reads=[("wbf", "l0_wv", 0)], writes=[("wv",)])
    for i in range(16):
        P.dma("sp", wq[:, i], k.wbf["l0_wq"][i].rearrange("p (k n) -> p k n", k=3), reads=[("wbf", "l0_wq", i)], writes=[("wq",)])
    sc = 96.0 ** -0.5
    P.dve(lambda e: e.tensor_scalar(out=vec[:, 8:10], in0=vec[:, 8:10], scalar1=sc, scalar2=None, op0=ALU.mult),
          reads=[("vec",)], writes=[("vec",)])
    for i in range(2):
        P.pool(lambda e, i=i: e.memset(Vt[i][:], 1.0), writes=[("Vt", i), (("Vt", i), 1)])
    rot = PsRot()
    vcnt = 0
    sqc = 0
    for ti, (who, t0, T) in enumerate(TILES):
        lat = (who == 0)
        xb = xt[ti % 2]
        xtoks = [("xtm", ti % 2, c) for c in range(8)]
        load_x_norm(k, l, who, t0, T, xb, xtoks, hT, w)
        htoks = [("hTm", c) for c in range(8)]
        rp = rope[ti % 2]
        if lat:
            P.dma("sp", rp[:, :, 0:T], k.inp["ropeB"][:, :, t0 - NCTX:t0 - NCTX + T], writes=[("rope", ti % 2)])

        def projA(bidx, M):
            b = rot.next()
            mm_group(k, PS[b][0:M, 0:T], lambda kc: wA[:, bidx, kc, 0:M], lambda kc: hT[:, kc, 0:T], 8,
                     [("wA",)] + htoks, ("ps", b))
            return b

        def square(b, M, dst_ap, dtok):
            P.act(lambda e: e.activation(out=dst_ap, in_=PS[b][0:M, 0:T], func=AF.Square), reads=[("ps", b)], writes=[dtok])

        bs = [projA(0, 128), projA(1, 128)]
        for c in range(2):
            square(bs[c], 128, w["sq"][:, c, 0:T], ("sq", c))
        mm_group(k, PS[6][:, 0:T], lambda kc: k.ones_bf, lambda kc: w["sq"][:, kc, 0:T], 2,
                 [("sq", 0), ("sq", 1), ("cbf",)], ("ps", 6))
        stats_rstd(k, PS[6][:, 0:T], 128, T, 1.0 / 256, w["r1"], w["r2"], w["rstd"])
        for c in range(2):
            P.dve(lambda e, c=c: e.scalar_tensor_tensor(out=ckvT[:, c, 0:T], in0=PS[bs[c]][:, 0:T], scalar=vec[:, c:c + 1],
                                                        in1=w["rstd"][:, 0:T], op0=ALU.mult, op1=ALU.mult),
                  reads=[("ps", bs[c]), ("rstd",), ("vec",)], writes=[("ckvT", c)])
        bm = projA(2, 96)
        bsw = projA(3, 96) if lat else None
        sb = sqs[sqc % 2]
        stok = ("sqs", sqc % 2)
        sqc += 1
        square(bm, 96, sb[0:96, 0:T], stok)
        mm_group(k, PS[6][0:96, 0:T], lambda kc: k.cbf[0:96, 3, 0:96], lambda kc: sb[0:96, 0:T], 1, [stok, ("cbf",)], ("ps", 6))
        stats_rstd(k, PS[6][64:96, 0:T], 32, T, 1.0 / 32, w["r1"][64:96], w["r2"][64:96], w["rstd"][64:96])
        R = slice(64, 96)
        if not lat:
            P.dve(lambda e: e.scalar_tensor_tensor(out=krT[R, 0:T], in0=PS[bm][R, 0:T], scalar=vec[R, 6:7],
                                                   in1=w["rstd"][R, 0:T], op0=ALU.mult, op1=ALU.mult),
                  reads=[("ps", bm), ("rstd",), ("vec",)], writes=[("krT",)])
        else:
            rope_apply(k, PS[bm], PS[bsw], bm, bsw, R, T, vec[R, 6:7], vec[R, 7:8], w["rstd"], rp, ("rope", ti % 2),
                       ra, rb, krT[R, 0:T], [("krT",)])
        bs = [projA(4 + c, 128) for c in range(3)]
        for c in range(3):
            square(bs[c], 128, w["sq"][:, c, 0:T], ("sq", c))
        mm_group(k, PS[6][:, 0:T], lambda kc: k.ones_bf, lambda kc: w["sq"][:, kc, 0:T], 3,
                 [("sq", 0), ("sq", 1), ("sq", 2), ("cbf",)], ("ps", 6))
        stats_rstd(k, PS[6][:, 0:T], 128, T, 1.0 / 384, w["r1"], w["r2"], w["rstd"])
        for c in range(3):
            P.dve(lambda e, c=c: e.scalar_tensor_tensor(out=cqT[:, c, 0:T], in0=PS[bs[c]][:, 0:T], scalar=vec[:, 2 + c:3 + c],
                                                        in1=w["rstd"][:, 0:T], op0=ALU.mult, op1=ALU.mult),
                  reads=[("ps", bs[c]), ("rstd",), ("vec",)], writes=[("cqT", c)])
        yb = yT[ti % 2]
        for ch in range(4):
            ba_ = projA(7 + ch, 128)
            bg_ = projA(11 + ch, 128)
            sg_ = sig[ch % 2]
            P.act(lambda e, bg_=bg_, sg_=sg_, ch=ch: e.activation(out=sg_[:, 0:T], in_=PS[bg_][:, 0:T], func=AF.Sigmoid,
                                                              bias=vec[:, 15 + ch:16 + ch], scale=1.0),
                  reads=[("ps", bg_), ("vec",)], writes=[("sig", ch % 2)])
            P.dve(lambda e, ba_=ba_, sg_=sg_, ch=ch: e.scalar_tensor_tensor(out=yb[:, ch, 0:T], in0=PS[ba_][:, 0:T],
                                                                        scalar=vec[:, 11 + ch:12 + ch], in1=sg_[:, 0:T],
                                                                        op0=ALU.add, op1=ALU.mult),
                  reads=[("ps", ba_), ("sig", ch % 2), ("vec",)], writes=[("yT", ti % 2, ch)])
        P.dma("sp", k.yTd[:, :, t0:t0 + T].rearrange("c p t -> p c t"), yb[:, :, 0:T],
              reads=[("yT", ti % 2, ch) for ch in range(4)], writes=[("yTd", ti)])
        Kb = KT[ti % 2]
        ktoks = [("KT", ti % 2, h) for h in range(8)]
        for pr in range(4):
            b = rot.next()
            mm_group(k, PS[b][:, 0:T], lambda kc, pr=pr: wkn[:, pr, kc, :], lambda kc: ckvT[:, kc, 0:T], 2,
                     [("wkn",), ("ckvT", 0), ("ckvT", 1)], ("ps", b))
            sb = sqs[sqc % 2]
            stok = ("sqs", sqc % 2)
            sqc += 1
            square(b, 128, sb[:, 0:T], stok)
            mm_group(k, PS[6][:, 0:T], lambda kc: k.bd64_bf, lambda kc, sb=sb: sb[:, 0:T], 1, [stok, ("cbf",)], ("ps", 6))
            stats_rstd(k, PS[6][:, 0:T], 128, T, 1.0 / 64, w["r1"], w["r2"], w["rstd"])
            for e_ in range(2):
                h = 2 * pr + e_
                P.dve(lambda e, b=b, e_=e_, h=h: e.scalar_tensor_tensor(
                    out=Kb[0:64, h, 0:T], in0=PS[b][e_ * 64:(e_ + 1) * 64, 0:T], scalar=vec[e_ * 64:(e_ + 1) * 64, 5:6],
                    in1=w["rstd"][e_ * 64:(e_ + 1) * 64, 0:T], op0=ALU.mult, op1=ALU.mult),
                    reads=[("ps", b), ("rstd",), ("vec",)], writes=[("KT", ti % 2, h)])
        for h in range(8):
            P.pool(lambda e, h=h: e.tensor_copy(out=Kb[R, h, 0:T], in_=krT[R, 0:T]), reads=[("krT",)], writes=[("KT", ti % 2, h)])
        P.dma("sp", k.KTd[:, :, t0:t0 + T].rearrange("h p t -> p h t"), Kb[0:96, :, 0:T], reads=ktoks, writes=[("KTd", ti)])
        for st in range(T // 128):
            b = rot.next()
            mm_group(k, PS[b][:, 0:512], lambda kc, st=st: ckvT[:, kc, st * 128:(st + 1) * 128], lambda kc: wv[:, kc, :], 2,
                     [("wv",), ("ckvT", 0), ("ckvT", 1)], ("ps", b))
            vb = Vt[vcnt % 2]
            vtok = ("Vt", vcnt % 2)
            vcnt += 1
            src = PS[b][:, 0:512].rearrange("p (h two d) -> p h two d", two=2, d=64)
            dst = vb[:].rearrange("p (h two) d -> p h two d", two=2)
            P.dve(lambda e, src=src, dst=dst: e.tensor_copy(out=dst[:, :, 0, 0:64], in_=src[:, :, 0, :]),
                  reads=[("ps", b)], writes=[vtok])
            P.dve(lambda e, src=src, dst=dst: e.tensor_copy(out=dst[:, :, 1, 64:128], in_=src[:, :, 1, :]),
                  reads=[("ps", b)], writes=[(vtok, 1)])
            chunk = (t0 + st * 128) // 128
            P.dma("sp", k.Vd[:, :, chunk, :].rearrange("h p d -> p h d"), vb[:], reads=[vtok, (vtok, 1)], writes=[("Vd", chunk)])
        Qb = QT[ti % 2]
        qtoks = [("QT", ti % 2, h) for h in range(8)]
        for h in range(8):
            b = rot.next()
            mm_group(k, PS[b][0:96, 0:T], lambda kc, h=h: wq[:, h, kc, :], lambda kc: cqT[:, kc, 0:T], 3,
                     [("wq",)] + [("cqT", c) for c in range(3)], ("ps", b))
            bsw = None
            if lat:
                bsw = rot.next()
                mm_group(k, PS[bsw][0:96, 0:T], lambda kc, h=h: wq[:, 8 + h, kc, :], lambda kc: cqT[:, kc, 0:T], 3,
                         [("wq",)] + [("cqT", c) for c in range(3)], ("ps", bsw))
            sb = sqs[sqc % 2]
            stok = ("sqs", sqc % 2)
            sqc += 1
            square(b, 96, sb[0:96, 0:T], stok)
            mm_group(k, PS[6][0:96, 0:T], lambda kc: k.cbf[0:96, 3, 0:96], lambda kc, sb=sb: sb[0:96, 0:T], 1, [stok, ("cbf",)], ("ps", 6))
            stats_rstd(k, PS[6][0:96, 0:T], 96, T, vec[0:96, 10:11], w["r1"], w["r2"], w["rstd"], rtoks=(("ps", 6), ("vec",)))
            nr = 96 if not lat else 64
            P.dve(lambda e, b=b, h=h, nr=nr: e.scalar_tensor_tensor(out=Qb[0:nr, h, 0:T], in0=PS[b][0:nr, 0:T], scalar=vec[0:nr, 8:9],
                                                                   in1=w["rstd"][0:nr, 0:T], op0=ALU.mult, op1=ALU.mult),
                  reads=[("ps", b), ("rstd",), ("vec",)], writes=[("QT", ti % 2, h)])
            if lat:
                rope_apply(k, PS[b], PS[bsw], b, bsw, R, T, vec[R, 8:9], vec[R, 9:10], w["rstd"], rp, ("rope", ti % 2),
                           ra, rb, Qb[R, h, 0:T], [(("QT", ti % 2, h), "r")])
        P.dma("sp", k.QTd[:, :, t0:t0 + T].rearrange("h p t -> p h t"), Qb[0:96, :, 0:T],
              reads=qtoks + [(q_, "r") for q_ in qtoks], writes=[("QTd", ti)])
    A.release(m)


def rope_apply(k, psm, pssw, bm, bsw, R, T, g, gsw, rstd, rp, rptok, ra, rb, out_ap, otoks):
    P = k.P
    P.dve(lambda e: e.scalar_tensor_tensor(out=ra[R, 0:T], in0=psm[R, 0:T], scalar=g, in1=rstd[R, 0:T],
                                           op0=ALU.mult, op1=ALU.mult),
          reads=[("ps", bm), ("rstd",), ("vec",)], writes=[("ra",)])
    P.dve(lambda e: e.scalar_tensor_tensor(out=rb[R, 0:T], in0=pssw[R, 0:T], scalar=gsw, in1=rstd[R, 0:T],
                                           op0=ALU.mult, op1=ALU.mult),
          reads=[("ps", bsw), ("rstd",), ("vec",)], writes=[("rb",)])
    P.pool(lambda e: e.tensor_tensor(out=ra[R, 0:T], in0=ra[R, 0:T], in1=rp[R, 0, 0:T], op=ALU.mult),
           reads=[("ra",), rptok], writes=[("ra",)])
    P.pool(lambda e: e.tensor_tensor(out=rb[R, 0:T], in0=rb[R, 0:T], in1=rp[R, 1, 0:T], op=ALU.mult),
           reads=[("rb",), rptok], writes=[("rb",)])
    P.dve(lambda e: e.tensor_tensor(out=out_ap, in0=ra[R, 0:T], in1=rb[R, 0:T], op=ALU.add),
          reads=[("ra",), ("rb",)], writes=otoks)


def attn_phase(k, mode):
    P, A, PS = k.P, k.A, k.PS
    m = A.mark()
    mla = (mode == "mla")
    Kt = [A.alloc(f"Kt{i}", [128, TALL], BF16) for i in range(4)]
    Vh = [A.alloc(f"Vh{i}", [128, 34, 128], BF16) for i in range(4)]
    Qt = [A.alloc(f"Qt{i}", [128, 2, 512], BF16) for i in range(2)]
    PT = [A.alloc(f"PT{i}", [128, 512], BF16) for i in range(3)]
    rec = A.alloc("rec", [128, 512], F32)
    attT = [A.alloc(f"attT{i}", [128, 512], BF16) for i in range(2)]
    qtiles = TILES if mla else TILES[1:]
    qc = 0
    sc = 0
    pc = 0
    ac = 0
    for hp in range(4):
        kb = (hp % 2) * 2
        if mla:
            for e_ in range(2):
                P.dma("sp", Kt[kb + e_][0:96, :], k.KTd[2 * hp + e_], writes=[("Kt", kb + e_)])
                P.dma("sp", Vh[kb + e_][:], k.Vd[2 * hp + e_], writes=[("Vh", kb + e_)])
        else:
            for e_ in range(2):
                P.dma("sp", Kt[kb + e_][:], k.KdupTd[hp // 2], writes=[("Kt", kb + e_)])
                P.dve(lambda e: e.memset(Kt[kb + e_][(1 - e_) * 64:(2 - e_) * 64, :], 0.0), writes=[("Kt", kb + e_)])
                P.dma("sp", Vh[kb + e_][:], k.VgD[(hp // 2) * 2 + e_], writes=[("Vh", kb + e_)])
        for (who, t0, T) in qtiles:
            chunks = [0, 1] if who == 1 else list(range(34))
            Qb = Qt[qc % 2]
            qtok = ("Qt", qc % 2)
            ab = attT[qc % 2]
            atok = ("attT", qc % 2)
            qc += 1
            if mla:
                P.dma("sp", Qb[0:96, :, 0:T], k.QTd[2 * hp:2 * hp + 2, :, t0:t0 + T].rearrange("h p t -> p h t"), writes=[qtok])
            else:
                P.dma("sp", Qb[:, 0, 0:T], k.QgTd[hp, :, t0:t0 + T], writes=[qtok])
            for e_ in range(2):
                if mla:
                    Ksb, ktok, r0, r1 = Kt[kb + e_], ("Kt", kb + e_), 0, 96
                    qap = Qb[0:96, e_, 0:T]
                else:
                    Ksb, ktok, r0, r1 = Kt[kb + e_], ("Kt", kb + e_), 0, 128
                    qap = Qb[:, 0, 0:T]
                Vsb, vtok = Vh[kb + e_], ("Vh", kb + e_)
                acc = 3 + (ac % 2)
                ac += 1
                prev = None
                n = len(chunks)
                for idx, c in enumerate(chunks):
                    sbk = sc % 3
                    sc += 1
                    P.pe(lambda e, sbk=sbk, c=c: e.matmul(PS[sbk][:, 0:T], lhsT=Ksb[r0:r1, c * 128:(c + 1) * 128], rhs=qap,
                                                          start=True, stop=True),
                         reads=[ktok, qtok], writes=[("ps", sbk)])
                    pb = pc % 3
                    pc += 1
                    P.act(lambda e, sbk=sbk, pb=pb: e.activation(out=PT[pb][:, 0:T], in_=PS[sbk][:, 0:T], func=AF.Exp),
                          reads=[("ps", sbk)], writes=[("PT", pb)])
                    if prev is not None:
                        pidx, pc_, ppb = prev
                        P.pe(lambda e, pidx=pidx, pc_=pc_, ppb=ppb: e.matmul(PS[acc][:, 0:T], lhsT=Vsb[:, pc_, :], rhs=PT[ppb][:, 0:T],
                                                                              start=(pidx == 0), stop=False),
                             reads=[vtok, ("PT", ppb)], writes=[("ps", acc)])
                    prev = (idx, c, pb)
                pidx, pc_, ppb = prev
                P.pe(lambda e: e.matmul(PS[acc][:, 0:T], lhsT=Vsb[:, pc_, :], rhs=PT[ppb][:, 0:T], start=(pidx == 0), stop=True),
                     reads=[vtok, ("PT", ppb)], writes=[("ps", acc)])
                nlo, dlo = (0, 64) if e_ == 0 else (64, 0)
                P.dve(lambda e: e.reciprocal(out=rec[nlo:nlo + 64, 0:T], in_=PS[acc][dlo:dlo + 64, 0:T]),
                      reads=[("ps", acc)], writes=[("rec", e_)])
                P.dve(lambda e: e.tensor_tensor(out=ab[nlo:nlo + 64, 0:T], in0=PS[acc][nlo:nlo + 64, 0:T], in1=rec[nlo:nlo + 64, 0:T],
                                                op=ALU.mult),
                      reads=[("ps", acc), ("rec", e_)], writes=[(atok, e_)])
            P.dma("sp", k.attTd[hp, :, t0:t0 + T], ab[:, 0:T], reads=[(atok, 0), (atok, 1)], writes=[("attTd", hp, t0)])
    A.release(m)


def l0_conv_phase(k):
    P, A, PS = k.P, k.A, k.PS
    m = A.mark()
    vec = A.alloc("vec", [128, 32], F32)
    dw = A.alloc("dw", [128, 4, 31], F32)
    diag = A.alloc("diag", [128, 4, 31, 128], BF16)
    ybuf = [A.alloc(f"ybuf{i}", [128, 4, 544], BF16) for i in range(2)]
    cv = A.alloc("cv", [128, 4, 512], F32)
    cb = A.alloc("cb", [128, 4, 512], BF16)
    sqv = A.alloc("sqv", [128, 4, 512], BF16)
    co = [A.alloc(f"co{i}", [128, 4, 512], BF16) for i in range(2)]
    mean = A.alloc("mean", [128, 512], F32)
    msq = A.alloc("msq", [128, 512], F32)
    r1 = A.alloc("r1", [128, 512], F32)
    r2 = A.alloc("r2", [128, 512], F32)
    rstd = A.alloc("rstd", [128, 512], F32)
    t1 = [A.alloc(f"t1{i}", [128, 512], F32) for i in range(2)]
    identb = k.cbf[:, 0, :]
    P.dma("sp", vec[:], k.inp["l0_vec"], writes=[("vec",)])
    P.dma("sp", dw[:], k.inp["l0_dw"], writes=[("dw",)])
    for ch in range(4):
        for j in range(31):
            P.dve(lambda e: e.tensor_scalar(out=diag[:, ch, j, :], in0=identb, scalar1=dw[:, ch, j:j + 1], scalar2=None, op0=ALU.mult),
                  reads=[("dw",), ("cbf",)], writes=[("diag", ch, j)])
    rot = PsRot(banks=(0, 1, 2, 3))
    for ti, (who, t0, T) in enumerate(TILES):
        s0, s1 = (0, NCTX) if who == 1 else (NCTX, TALL)
        lo = max(t0 - 15, s0)
        hi = min(t0 + T + 15, s1)
        yb = ybuf[ti % 2]
        toks = [("yb", ti % 2, x) for x in "LMR"]
        wr = [toks[1]]
        if lo > t0 - 15:
            P.pool(lambda e: e.memset(yb[:, :, 0:15], 0.0), writes=[toks[0]])
        else:
            wr.append(toks[0])
        if hi < t0 + T + 15:
            P.pool(lambda e: e.memset(yb[:, :, T + 15:T + 30], 0.0), writes=[toks[2]])
        else:
            wr.append(toks[2])
        P.dma("sp", yb[:, :, lo - (t0 - 15):hi - (t0 - 15)], k.yTd[:, :, lo:hi].rearrange("c p t -> p c t"), writes=wr)
        for ch in range(4):
            b = rot.next()

            def mm(e, ch=ch, b=b):
                for j in range(31):
                    e.matmul(PS[b][:, 0:T], lhsT=diag[:, ch, j, :], rhs=yb[:, ch, j:j + T], start=(j == 0), stop=(j == 30))
            P.pe(mm, reads=toks + [("diag", ch, j) for j in range(31)], writes=[("ps", b)])
            P.act(lambda e: e.activation(out=cv[:, ch, 0:T], in_=PS[b][:, 0:T], func=AF.Identity, bias=vec[:, 19 + ch:20 + ch], scale=1.0),
                  reads=[("ps", b), ("vec",)], writes=[("cv", ch)])
            P.pool(lambda e: e.tensor_copy(out=cb[:, ch, 0:T], in_=cv[:, ch, 0:T]), reads=[("cv", ch)], writes=[("cb", ch)])
            P.pool(lambda e: e.tensor_tensor(out=sqv[:, ch, 0:T], in0=cv[:, ch, 0:T], in1=cv[:, ch, 0:T], op=ALU.mult),
                   reads=[("cv", ch)], writes=[("sqv", ch)])
        mm_group(k, PS[6][:, 0:T], lambda kc: k.ones_bf, lambda kc: cb[:, kc, 0:T], 4, [("cb", c) for c in range(4)] + [("cbf",)], ("ps", 6))
        mm_group(k, PS[7][:, 0:T], lambda kc: k.ones_bf, lambda kc: sqv[:, kc, 0:T], 4, [("sqv", c) for c in range(4)] + [("cbf",)], ("ps", 7))
        P.dve(lambda e: e.tensor_scalar(out=mean[:, 0:T], in0=PS[6][:, 0:T], scalar1=1.0 / 512, scalar2=None, op0=ALU.mult),
              reads=[("ps", 6)], writes=[("mean",)])
        P.pool(lambda e: e.tensor_tensor(out=msq[:, 0:T], in0=mean[:, 0:T], in1=mean[:, 0:T], op=ALU.mult),
               reads=[("mean",)], writes=[("msq",)])
        P.dve(lambda e: e.scalar_tensor_tensor(out=r2[:, 0:T], in0=PS[7][:, 0:T], scalar=1.0 / 512, in1=msq[:, 0:T],
                                               op0=ALU.mult, op1=ALU.subtract),
              reads=[("ps", 7), ("msq",)], writes=[("r2",)])
        P.dve(lambda e: e.tensor_scalar(out=r1[:, 0:T], in0=r2[:, 0:T], scalar1=EPS, scalar2=None, op0=ALU.add),
              reads=[("r2",)], writes=[("r1",)])
        P.act(lambda e: e.activation(out=r2[:, 0:T], in_=r1[:, 0:T], func=AF.Sqrt), reads=[("r1",)], writes=[("r2",)])
        P.dve(lambda e: e.reciprocal(out=rstd[:, 0:T], in_=r2[:, 0:T]), reads=[("r2",)], writes=[("rstd",)])
        cob = co[ti % 2]
        for ch in range(4):
            tb = t1[ch % 2]
            P.dve(lambda e: e.tensor_tensor(out=tb[:, 0:T], in0=cv[:, ch, 0:T], in1=mean[:, 0:T], op=ALU.subtract),
                  reads=[("cv", ch), ("mean",)], writes=[("t1", ch % 2)])
            P.pool(lambda e: e.tensor_tensor(out=tb[:, 0:T], in0=tb[:, 0:T], in1=rstd[:, 0:T], op=ALU.mult),
                   reads=[("t1", ch % 2), ("rstd",)], writes=[("t1", ch % 2)])
            P.act(lambda e: e.activation(out=cob[:, ch, 0:T], in_=tb[:, 0:T], func=AF.Silu, bias=vec[:, 27 + ch:28 + ch],
                                         scale=vec[:, 23 + ch:24 + ch]),
                  reads=[("t1", ch % 2), ("vec",)], writes=[("co", ti % 2, ch)])
        P.dma("sp", k.attTd[4:8, :, t0:t0 + T].rearrange("c p t -> p c t"), cob[:, :, 0:T],
              reads=[("co", ti % 2, ch) for ch in range(4)], writes=[("attTd", "conv", ti)])
    A.release(m)


def mixout_phase(k, l, tiles):
    P, A, PS = k.P, k.A, k.PS
    m = A.mark()
    wo = A.alloc("wo", [128, 8, 8, 128], BF16)
    xt = [A.alloc(f"xt{i}", [128, 8, 512], F32) for i in range(2)]
    at = [A.alloc(f"at{i}", [128, 8, 512], BF16) for i in range(2)]
    name = f"l{l}_wo"
    for i in range(8):
        P.dma("sp", wo[:, i], k.wbf[name][i].rearrange("p (k n) -> p k n", k=8), reads=[("wbf", name, i)], writes=[("wo",)])
    rot = PsRot()
    for ti, (who, t0, T) in enumerate(tiles):
        xb = xt[ti % 2]
        ab = at[ti % 2]
        xtoks = [("xt", ti % 2, c) for c in range(8)]
        P.dma("sp", xb[:, :, 0:T], k.xTd[:, :, t0:t0 + T].rearrange("c p t -> p c t"), writes=xtoks)
        P.dma("sp", ab[:, :, 0:T], k.attTd[:, :, t0:t0 + T].rearrange("c p t -> p c t"), writes=[("at", ti % 2)])
        for d in range(8):
            b = rot.next()
            mm_group(k, PS[b][:, 0:T], lambda kc: wo[:, d, kc, :], lambda kc: ab[:, kc, 0:T], 8, [("wo",), ("at", ti % 2)], ("ps", b))
            P.dve(lambda e: e.scalar_tensor_tensor(out=xb[:, d, 0:T], in0=PS[b][:, 0:T], scalar=k.modv[:, l, who, 40 + d:41 + d],
                                                   in1=xb[:, d, 0:T], op0=ALU.mult, op1=ALU.add),
                  reads=[("ps", b), xtoks[d]], writes=[xtoks[d]])
        P.dma("sp", k.xTd[:, :, t0:t0 + T].rearrange("c p t -> p c t"), xb[:, :, 0:T], reads=xtoks, writes=[("xTd", t0)])
    A.release(m)


def l1_proj_phase(k):
    P, A, PS = k.P, k.A, k.PS
    l = 1
    m = A.mark()
    w = common_work(k, A)
    xt = [A.alloc(f"xt{i}", [128, 8, 512], F32) for i in range(2)]
    hT = A.alloc("hTm", [128, 8, 512], BF16)
    wA = A.alloc("wA", [128, 20, 8, 128], BF16)
    wv = A.alloc("wv", [128, 8, 640], BF16)
    vec = A.alloc("vec", [128, 8], F32)
    rope = [A.alloc(f"rope{i}", [128, 2, 512], F32) for i in range(2)]
    sqs = [A.alloc(f"sqs{i}", [128, 512], BF16) for i in range(2)]
    ra = A.alloc("ra", [128, 512], F32)
    rb = A.alloc("rb", [128, 512], F32)
    KdT = [A.alloc(f"KdT{i}", [128, 2, 512], BF16) for i in range(2)]
    nkT = [A.alloc(f"nkT{i}", [128, 4, 512], BF16) for i in range(2)]
    QgT = [A.alloc(f"QgT{i}", [128, 4, 512], BF16) for i in range(2)]
    nqT = [A.alloc(f"nqT{i}", [128, 4, 512], BF16) for i in range(2)]
    Vgt = [A.alloc(f"Vgt{i}", [128, 4, 128], BF16) for i in range(2)]
    NVt = [A.alloc(f"NVt{i}", [128, 8, 128], BF16) for i in range(2)]
    P.dma("sp", vec[:], k.inp["l1_vec"], writes=[("vec",)])
    for i in range(20):
        P.dma("sp", wA[:, i], k.wbf["l1_wA"][i].rearrange("p (k n) -> p k n", k=8), reads=[("wbf", "l1_wA", i)], writes=[("wA",)])
    P.dma("sp", wv[:], k.wbf["l1_wv"][0].rearrange("p (k n) -> p k n", k=8), reads=[("wbf", "l1_wv", 0)], writes=[("wv",)])
    for c0, c1 in ((2, 4), (5, 6)):
        P.dve(lambda e: e.tensor_scalar(out=vec[:, c0:c1], in0=vec[:, c0:c1], scalar1=0.125, scalar2=None, op0=ALU.mult),
              reads=[("vec",)], writes=[("vec",)])
    for i in range(2):
        P.pool(lambda e: e.memset(Vgt[i][:], 1.0), writes=[("Vgt", i), (("Vgt", i), 1)])
        P.pool(lambda e: e.memset(NVt[i][:], 1.0), writes=[("NVt", i), (("NVt", i), 1)])
    rot = PsRot()
    cnt = {"sq": 0, "v": 0}
    Rall = slice(0, 128)
    for ti, (who, t0, T) in enumerate(TILES):
        lat = (who == 0)
        xb = xt[ti % 2]
        xtoks = [("xtm", ti % 2, c) for c in range(8)]
        load_x_norm(k, l, who, t0, T, xb, xtoks, hT, w)
        htoks = [("hTm", c) for c in range(8)]
        rp = rope[ti % 2]
        if lat:
            P.dma("sp", rp[:, :, 0:T], k.inp["ropeA"][:, :, t0 - NCTX:t0 - NCTX + T], writes=[("rope", ti % 2)])

        def projA(bidx):
            b = rot.next()
            mm_group(k, PS[b][:, 0:T], lambda kc: wA[:, bidx, kc, :], lambda kc: hT[:, kc, 0:T], 8, [("wA",)] + htoks, ("ps", b))
            return b

        def normed_chunk(bmain, bswap, gcol, out_ap, otok, do_rope):
            b = projA(bmain)
            bsw = projA(bswap) if do_rope else None
            sb = sqs[cnt["sq"] % 2]
            stok = ("sqs", cnt["sq"] % 2)
            cnt["sq"] += 1
            P.act(lambda e: e.activation(out=sb[:, 0:T], in_=PS[b][:, 0:T], func=AF.Square), reads=[("ps", b)], writes=[stok])
            mm_group(k, PS[6][:, 0:T], lambda kc: k.bd64_bf, lambda kc: sb[:, 0:T], 1, [stok, ("cbf",)], ("ps", 6))
            stats_rstd(k, PS[6][:, 0:T], 128, T, 1.0 / 64, w["r1"], w["r2"], w["rstd"])
            if do_rope:
                rope_apply(k, PS[b], PS[bsw], b, bsw, Rall, T, vec[:, gcol:gcol + 1], vec[:, gcol + 1:gcol + 2], w["rstd"], rp,
                           ("rope", ti % 2), ra, rb, out_ap, [otok])
            else:
                P.dve(lambda e: e.scalar_tensor_tensor(out=out_ap, in0=PS[b][:, 0:T], scalar=vec[:, gcol:gcol + 1],
                                                       in1=w["rstd"][:, 0:T], op0=ALU.mult, op1=ALU.mult),
                      reads=[("ps", b), ("rstd",), ("vec",)], writes=[otok])

        Kb = KdT[ti % 2]
        for hk in range(2):
            normed_chunk(hk, 2 + hk, 0, Kb[:, hk, 0:T], ("KdT", ti % 2, hk), lat)
        P.dma("sp", k.KdupTd[:, :, t0:t0 + T].rearrange("h p t -> p h t"), Kb[:, :, 0:T],
              reads=[("KdT", ti % 2, hk) for hk in range(2)], writes=[("KdupTd", ti)])
        nb = nkT[ti % 2]
        for c in range(4):
            normed_chunk(4 + c, None, 4, nb[:, c, 0:T], ("nkT", ti % 2, c), False)
        P.dma("sp", k.nkTd[:, :, t0:t0 + T].rearrange("c p t -> p c t"), nb[:, :, 0:T],
              reads=[("nkT", ti % 2, c) for c in range(4)], writes=[("nkTd", ti)])
        for st in range(T // 128):
            b1 = rot.next()
            mm_group(k, PS[b1][:, 0:128], lambda kc: hT[:, kc, st * 128:(st + 1) * 128], lambda kc: wv[:, kc, 0:128], 8,
                     [("wv",)] + htoks, ("ps", b1))
            b2 = rot.next()
            mm_group(k, PS[b2][:, 0:512], lambda kc: hT[:, kc, st * 128:(st + 1) * 128], lambda kc: wv[:, kc, 128:640], 8,
                     [("wv",)] + htoks, ("ps", b2))
            vi = cnt["v"] % 2
            cnt["v"] += 1
            vg, nv = Vgt[vi], NVt[vi]
            vgd = vg[:].rearrange("p (hk par) d -> p hk par d", par=2)
            s1 = PS[b1][:, 0:128].rearrange("p (hk d) -> p hk d", d=64)
            P.dve(lambda e: e.tensor_copy(out=vgd[:, :, 0, 0:64], in_=s1), reads=[("ps", b1)], writes=[("Vgt", vi)])
            P.dve(lambda e: e.tensor_copy(out=vgd[:, :, 1, 64:128], in_=s1), reads=[("ps", b1)], writes=[(("Vgt", vi), 1)])
            nvd = nv[:].rearrange("p (h two) d -> p h two d", two=2)
            s2 = PS[b2][:, 0:512].rearrange("p (h two d) -> p h two d", two=2, d=64)
            P.dve(lambda e: e.tensor_copy(out=nvd[:, :, 0, 0:64], in_=s2[:, :, 0, :]), reads=[("ps", b2)], writes=[("NVt", vi)])
            P.dve(lambda e: e.tensor_copy(out=nvd[:, :, 1, 64:128], in_=s2[:, :, 1, :]), reads=[("ps", b2)],
                  writes=[(("NVt", vi), 1)])
            chunk = (t0 + st * 128) // 128
            P.dma("sp", k.VgD[:, :, chunk, :].rearrange("v p d -> p v d"), vg[:], reads=[("Vgt", vi), (("Vgt", vi), 1)],
                  writes=[("VgD", chunk)])
            P.dma("sp", k.NVd[:, :, chunk, :].rearrange("h p d -> p h d"), nv[:], reads=[("NVt", vi), (("NVt", vi), 1)],
                  writes=[("NVd", chunk)])
        if lat:
            qb = QgT[ti % 2]
            for c in range(4):
                normed_chunk(8 + c, 12 + c, 2, qb[:, c, 0:T], ("QgT", ti % 2, c), True)
            P.dma("sp", k.QgTd[:, :, t0:t0 + T].rearrange("c p t -> p c t"), qb[:, :, 0:T],
                  reads=[("QgT", ti % 2, c) for c in range(4)], writes=[("QgTd", ti)])
            nq = nqT[ti % 2]
            for c in range(4):
                normed_chunk(16 + c, None, 5, nq[:, c, 0:T], ("nqT", ti % 2, c), False)
            P.dma("sp", k.nqTd[:, :, t0:t0 + T].rearrange("c p t -> p c t"), nq[:, :, 0:T],
                  reads=[("nqT", ti % 2, c) for c in range(4)], writes=[("nqTd", ti)])
    A.release(m)


def l1_na_phase(k):
    P, A, PS = k.P, k.A, k.PS
    m = A.mark()
    nk = A.alloc("nk", [128, 4, TALL], BF16)
    NV = A.alloc("NV", [128, 8, 34, 128], BF16)
    nab = A.alloc("nab", [128, NSIG * 8, 128], BF16)
    nq = [A.alloc(f"nq{i}", [128, 4, 128], BF16) for i in range(2)]
    PT = [A.alloc(f"PT{i}", [128, 8, 128], BF16) for i in range(2)]
    rec = A.alloc("rec", [128, 128], F32)
    nat = [A.alloc(f"nat{i}", [128, 4, 128], BF16) for i in range(2)]
    identb = k.cbf[:, 0, :]
    P.dma("sp", nk[:], k.nkTd.rearrange("c p t -> p c t"), writes=[("nk",)])
    for h in range(8):
        P.dma("sp", NV[:, h], k.NVd[h], writes=[("NV", h)])
    for g in range(8):
        n0, n1 = g * NSIG, (g + 1) * NSIG
        P.dma("sp", nab[:, n0:n1, :], k.wbf["l1_nab"][n0:n1].rearrange("n p q -> p n q"),
              reads=[("wbf", "l1_nab", i) for i in range(n0, n1)], writes=[("nab", g)])
    nabtoks = [("nab", g) for g in range(8)]
    hc = 0
    for i in range(32):
        t0 = NCTX + i * 128
        nqb = nq[i % 2]
        natb = nat[i % 2]
        P.dma("sp", nqb[:], k.nqTd[:, :, t0:t0 + 128].rearrange("c p t -> p c t"), writes=[("nq", i % 2)])
        chunks = [(0, None), (1, None)] + [(2 + kc, NA_MAP[(i, kc)]) for kc in NA_CHUNKS[i]]
        n = len(chunks)
        for h in range(8):
            ch, off = h // 2, (h % 2) * 64
            sb0 = 2 * (hc % 2)
            acc = 4 + (hc % 2)
            pt = PT[hc % 2]
            pttok = ("PTn", hc % 2)
            hc += 1

            def smm(e):
                for idx, (c, sig) in enumerate(chunks):
                    o = PS[sb0 + idx // 4][:, (idx % 4) * 128:(idx % 4 + 1) * 128]
                    e.matmul(o, lhsT=nk[off:off + 64, ch, c * 128:(c + 1) * 128], rhs=nqb[off:off + 64, ch, :],
                             start=True, stop=(sig is None))
                    if sig is not None:
                        e.matmul(o, lhsT=identb, rhs=nab[:, sig * 8 + h, :], start=False, stop=True)
            P.pe(smm, reads=[("nk",), ("nq", i % 2), ("cbf",)] + nabtoks, writes=[("ps", sb0), ("ps", sb0 + 1)])
            P.act(lambda e: e.activation(out=pt[:, 0:4, :], in_=PS[sb0][:, 0:512].rearrange("p (c q) -> p c q", q=128), func=AF.Exp),
                  reads=[("ps", sb0)], writes=[(pttok, 0)])
            P.act(lambda e: e.activation(out=pt[:, 4:n, :], in_=PS[sb0 + 1][:, 0:(n - 4) * 128].rearrange("p (c q) -> p c q", q=128),
                                         func=AF.Exp),
                  reads=[("ps", sb0 + 1)], writes=[(pttok, 1)])

            def pv(e):
                for idx, (c, sig) in enumerate(chunks):
                    e.matmul(PS[acc][:, 0:128], lhsT=NV[:, h, c, :], rhs=pt[:, idx, :], start=(idx == 0), stop=(idx == n - 1))
            P.pe(pv, reads=[("NV", h), (pttok, 0), (pttok, 1)], writes=[("ps", acc)])
            nlo, dlo = (0, 64) if h % 2 == 0 else (64, 0)
            P.dve(lambda e: e.reciprocal(out=rec[nlo:nlo + 64, :], in_=PS[acc][dlo:dlo + 64, 0:128]),
                  reads=[("ps", acc)], writes=[("rec", h % 2)])
            P.dve(lambda e: e.tensor_tensor(out=natb[nlo:nlo + 64, ch, :], in0=PS[acc][nlo:nlo + 64, 0:128], in1=rec[nlo:nlo + 64, :],
                                            op=ALU.mult),
                  reads=[("ps", acc), ("rec", h % 2)], writes=[("nat", i % 2, h)])
        P.dma("sp", k.attTd[4:8, :, t0:t0 + 128].rearrange("c p t -> p c t"), natb[:],
              reads=[("nat", i % 2, h) for h in range(8)], writes=[("attTd", "na", i)])
    A.release(m)
```

```python
import numpy as np
import concourse.bass as bass
import concourse.mybir as mybir
from concourse.bass_utils import run_bass_kernel_spmd

F32 = mybir.dt.float32
BF16 = mybir.dt.bfloat16
AF = mybir.ActivationFunctionType
ALU = mybir.AluOpType

D = 1024
S = 4096
NCTX = 256
TALL = S + NCTX
DFF = 2816
NFC = DFF // 128
EPS = 1e-6
GRID_W = 64
NDMASEM = 40
TILES = [(1, 0, 256)] + [(0, 256 + 512 * i, 512) for i in range(8)]


class Op:
    __slots__ = ("stream", "kind", "fn", "idx", "event", "signal", "cclock", "waits", "slot")


class _Rec:
    def __init__(self):
        self.calls = []

    def __getattr__(self, name):
        def f(*a, **kw):
            self.calls.append((name, a, kw))
            return None
        return f


def _replay(calls):
    def fn(e):
        ins = None
        for name, a, kw in calls:
            ins = getattr(e, name)(*a, **kw)
        return ins
    return fn


class Prog:
    STREAMS = ("pe", "act", "dve", "pool", "sp")

    def __init__(self, nc):
        self.nc = nc
        self.ops = {s: [] for s in self.STREAMS}
        self.clock = {s: {} for s in self.STREAMS}
        self.lastw = {}
        self.readers = {}
        self.ncomp = {s: 0 for s in self.STREAMS}
        self.pending = {}
        self.dma_slot_last = [None] * NDMASEM
        self.dma_slot_cnt = [0] * NDMASEM
        self.dma_rr = 0
        self.dmas_since_barrier = []
        self.last_comp = {}
        self.nops = 0
        self.maxops = None

    def _add(self, stream, kind, fn, reads, writes):
        if self.maxops is not None and self.nops >= self.maxops and fn is not None:
            return None
        op = Op()
        op.stream = stream
        op.kind = kind
        if fn is not None:
            rec = _Rec()
            fn(rec)
            fn = _replay(rec.calls)
        op.fn = fn
        op.signal = False
        op.slot = None
        self.nops += 1
        deps = []
        for r in reads:
            w = self.lastw.get(r)
            if w is not None:
                deps.append((w, True))
        for w_ in writes:
            lw = self.lastw.get(w_)
            if lw is not None:
                deps.append((lw, False))
            for rd in self.readers.get(w_, ()):
                deps.append((rd, False))
        for r in reads:
            self.readers.setdefault(r, []).append(op)
        for w_ in writes:
            self.readers[w_] = []
            self.lastw[w_] = op
        pend = self.pending.pop(stream, None)
        if pend:
            deps.extend((p, True) for p in pend)
        if kind == "d":
            slot = self.dma_rr
            self.dma_rr = (self.dma_rr + 1) % NDMASEM
            prev = self.dma_slot_last[slot]
            if prev is not None:
                deps.append((prev, True))
            self.dma_slot_cnt[slot] += 1
            self.dma_slot_last[slot] = op
            op.slot = slot
            op.event = (("d", slot), 16 * self.dma_slot_cnt[slot])
            self.dmas_since_barrier.append(op)
        else:
            self.ncomp[stream] += 1
            op.event = (("e", stream), self.ncomp[stream])
            self.last_comp[stream] = op
        clk = self.clock[stream]
        waits = {}
        for d, raw in deps:
            if d is op:
                continue
            if d.stream == stream and d.kind == "c":
                if stream == "pe" or not raw:
                    continue
            key, val = d.event
            if clk.get(key, 0) >= val:
                continue
            if waits.get(key, (0,))[0] < val:
                waits[key] = (val, d)
            for k, v in d.cclock.items():
                if clk.get(k, 0) < v:
                    clk[k] = v
        for key, (val, d) in waits.items():
            d.signal = True
        op.waits = {k: v[0] for k, v in waits.items()}
        cc = dict(clk)
        cc[op.event[0]] = op.event[1]
        op.cclock = cc
        if kind == "c" and stream != "pe":
            pass
        self.ops[stream].append(op)
        return op

    def pe(self, fn, reads=(), writes=()):
        return self._add("pe", "c", fn, reads, writes)

    def act(self, fn, reads=(), writes=()):
        return self._add("act", "c", fn, reads, writes)

    def dve(self, fn, reads=(), writes=()):
        return self._add("dve", "c", fn, reads, writes)

    def pool(self, fn, reads=(), writes=()):
        return self._add("pool", "c", fn, reads, writes)

    def dma(self, q, out, in_, reads=(), writes=()):
        return self._add(q, "d", lambda e: e.dma_start(out=out, in_=in_), reads, writes)

    def barrier(self):
        lst = list(self.last_comp.values()) + self.dmas_since_barrier
        self.dmas_since_barrier = []
        for s in self.STREAMS:
            self.pending[s] = list(lst)

    def emit(self):
        nc = self.nc
        self.barrier()
        fin = self._add("sp", "c", None, (), ())
        esem = {s: nc.alloc_semaphore("es_" + s) for s in self.STREAMS}
        dsem = [nc.alloc_semaphore(f"ds{i}") for i in range(NDMASEM)]
        sigcount = {}
        for s in self.STREAMS:
            cnt = 0
            m = {}
            for o in self.ops[s]:
                if o.kind == "c":
                    if o.signal:
                        cnt += 1
                    m[o.event[1]] = (cnt, o.signal)
            sigcount[s] = m

        def resolve(key, val):
            if key[0] == "d":
                return dsem[key[1]], val
            cnt, sig = sigcount[key[1]][val]
            assert sig
            return esem[key[1]], cnt

        with nc.Block() as block:
            decos = {"pe": block.tensor, "act": block.scalar, "dve": block.vector,
                     "pool": block.gpsimd, "sp": block.sync}
            for s in self.STREAMS:
                ops = self.ops[s]

                def body(eng, ops=ops, s=s):
                    for o in ops:
                        for key, val in o.waits.items():
                            sem, v = resolve(key, val)
                            eng.wait_ge(sem, v)
                        if o.fn is None:
                            continue
                        ins = o.fn(eng)
                        if o.kind == "d":
                            ins.then_inc(dsem[o.slot], 16)
                        elif o.signal:
                            ins.then_inc(esem[s], 1)
                decos[s](body)


class Arena:
    def __init__(self, nc, limit=229376):
        self.nc = nc
        self.off = 16640
        self.limit = limit
        self.n = 0

    def alloc(self, name, shape, dtype):
        esz = 4 if dtype == F32 else 2
        sz = esz
        for d in shape[1:]:
            sz *= d
        sz = (sz + 63) // 64 * 64
        assert self.off + sz <= self.limit, (name, self.off, sz)
        self.n += 1
        t = self.nc.alloc_sbuf_tensor_at(f"{name}_{self.n}", list(shape), dtype, offset=self.off)
        self.off += sz
        return t

    def mark(self):
        return self.off

    def release(self, m):
        self.off = m


def kmajor(w, ncols_pad=None):
    K, N = w.shape
    return np.ascontiguousarray(w.reshape(K // 128, 128, N).transpose(1, 0, 2))


def pvec(v):
    return np.ascontiguousarray(v.reshape(-1, 128).T)


class K:
    pass


def declare_inputs(k, nc):
    k.inp = {}
    k.inshape = {}

    def din(name, shape):
        k.inp[name] = nc.dram_tensor(name, list(shape), F32, kind="ExternalInput").ap()
        k.inshape[name] = tuple(shape)

    din("x", [S, D])
    din("ctx", [NCTX, D])
    din("cT", [128, 8, 2])
    din("cmat", [128, 4, 128])
    for l in range(2):
        din(f"l{l}_mod_w", [18, 128, 8 * 512])
        din(f"l{l}_mod_bT", [128, 72])
        din(f"l{l}_ng", [128, 3, 8])
        for w in (1, 2):
            din(f"l{l}_f{w}_win", [NFC, 128, 8 * 256])
            din(f"l{l}_f{w}_wout", [8, 128, NFC * 128])


def host_inputs(inputs, b):
    m = {}
    m["x"] = np.ascontiguousarray(inputs["x"][b])
    m["ctx"] = np.ascontiguousarray(inputs["ctx"][b])
    cT = np.stack([pvec(inputs["c"][b]), pvec(inputs["c_ctx"])], axis=-1)
    m["cT"] = np.ascontiguousarray(cT.astype(np.float32))
    return m


_SHARED = {}


def host_shared(inputs):
    m = {}
    cm = np.zeros((128, 4, 128), np.float32)
    cm[:, 0, :] = np.eye(128, dtype=np.float32)
    cm[:, 1, :] = 1.0
    for g in range(2):
        cm[g * 64:(g + 1) * 64, 2, g * 64:(g + 1) * 64] = 1.0
    cm[0:64, 3, 0:64] = 1.0
    cm[64:96, 3, 64:96] = 1.0
    m["cmat"] = cm
    for l in range(2):
        p = f"l{l}_"
        mw = inputs[p + "mod_w"]
        m[p + "mod_w"] = np.ascontiguousarray(
            mw.reshape(8, 128, 18, 512).transpose(2, 1, 0, 3).reshape(18, 128, 8 * 512))
        m[p + "mod_bT"] = pvec(inputs[p + "mod_b"])
        m[p + "ng"] = np.ascontiguousarray(np.stack(
            [pvec(inputs[p + "ffn1_norm"]), pvec(inputs[p + "mix_norm"]), pvec(inputs[p + "ffn2_norm"])], axis=1))
        for w in (1, 2):
            wi = inputs[p + f"ffn{w}_w_in"]
            g = wi[:, :DFF].reshape(8, 128, NFC, 128)
            u = wi[:, DFF:].reshape(8, 128, NFC, 128)
            gu = np.stack([g, u], axis=3)
            m[p + f"f{w}_win"] = np.ascontiguousarray(gu.transpose(2, 1, 0, 3, 4).reshape(NFC, 128, 8 * 256))
            wo = inputs[p + f"ffn{w}_w_out"]
            m[p + f"f{w}_wout"] = np.ascontiguousarray(
                wo.reshape(NFC, 128, 8, 128).transpose(2, 1, 0, 3).reshape(8, 128, NFC * 128))
    m.update(host_mixer(inputs))
    return m


def build(stop_after=None, debug=False, skip=(), maxops=None):
    nc = bass.Bass("TRN2", target_bir_lowering=False)
    k = K()
    k.nc = nc
    k.debug = debug
    P = Prog(nc)
    k.maxops = maxops
    k.P = P
    A = Arena(nc)
    k.A = A
    declare_inputs(k, nc)
    k.out = nc.dram_tensor("out", [S, D], F32, kind="ExternalOutput").ap()
    k.dbg_outs = []
    skind = "ExternalOutput" if debug else "Internal"
    k.xTd = nc.dram_tensor("xTd", [8, 128, TALL], F32, kind=skind).ap()
    k.wbf = {}

    def precast(name):
        shape = k.inshape[name]
        t = nc.dram_tensor(name + "_bf", list(shape), BF16, kind="Internal").ap()
        k.wbf[name] = t
        for i in range(shape[0]):
            P.dma("pool", t[i], k.inp[name][i], reads=(), writes=[("wbf", name, i)])

    k.PS = [nc.alloc_psum_tensor(f"psb{i}", [128, 512], F32) for i in range(8)]

    k.ident = A.alloc("ident", [128, 128], F32)
    k.cbf = A.alloc("cbf", [128, 4, 128], BF16)
    k.modv = A.alloc("modv", [128, 2, 2, 72], F32)
    k.der = A.alloc("der", [128, 2, 2, 5, 8], F32)
    k.ng = A.alloc("ng", [128, 2, 3, 8], F32)
    P.dma("sp", k.ident[:], k.inp["cmat"][:, 0, :], writes=[("ident",)])
    P.dma("pool", k.cbf[:], k.inp["cmat"], writes=[("cbf",)])
    for l in range(2):
        P.dma("sp", k.ng[:, l], k.inp[f"l{l}_ng"], writes=[("ng", l)])
    k.ones_bf = k.cbf[:, 1, :]
    k.bd64_bf = k.cbf[:, 2, :]

    for l in range(2):
        for w in (1, 2):
            precast(f"l{l}_f{w}_win")
            precast(f"l{l}_f{w}_wout")

    declare_mixer_inputs(k, nc)
    for name in ("l0_wA", "l0_wkn", "l0_wv", "l0_wq", "l0_wo", "l1_wo", "l1_wA", "l1_wv", "l1_nab"):
        precast(name)

    def dscr(name, shape, dt=BF16):
        return nc.dram_tensor(name, list(shape), dt, kind=skind if name in ("attTd",) else "Internal").ap()
    k.yTd = dscr("yTd", [4, 128, TALL])
    k.KTd = dscr("KTd", [8, 96, TALL])
    k.QTd = dscr("QTd", [8, 96, TALL])
    k.Vd = dscr("Vd", [8, 128, 34, 128])
    k.attTd = dscr("attTd", [8, 128, TALL])
    k.KdupTd = dscr("KdupTd", [2, 128, TALL])
    k.QgTd = dscr("QgTd", [4, 128, TALL])
    k.nkTd = dscr("nkTd", [4, 128, TALL])
    k.nqTd = dscr("nqTd", [4, 128, TALL])
    k.VgD = dscr("VgD", [4, 128, 34, 128])
    k.NVd = dscr("NVd", [8, 128, 34, 128])

    phases = [
        ("mod", lambda: setup_mod(k)),
        ("l0f1", lambda: ffn_phase(k, 0, 1, TILES, src="tok", dst="xT")),
        ("l0proj", lambda: l0_proj_phase(k)),
        ("l0att", lambda: attn_phase(k, "mla")),
        ("l0conv", lambda: l0_conv_phase(k)),
        ("l0mix", lambda: mixout_phase(k, 0, TILES)),
        ("l0", lambda: ffn_phase(k, 0, 2, TILES, src="xT", dst="xT")),
        ("l1f1", lambda: ffn_phase(k, 1, 1, TILES, src="xT", dst="xT")),
        ("l1proj", lambda: l1_proj_phase(k)),
        ("l1gqa", lambda: attn_phase(k, "gqa")),
        ("l1na", lambda: l1_na_phase(k)),
        ("l1mix", lambda: mixout_phase(k, 1, TILES[1:])),
        ("full", lambda: ffn_phase(k, 1, 2, TILES[1:], src="xT", dst="tok")),
    ]
    for name, fn in phases:
        if name in skip:
            continue
        n0 = P.nops
        if maxops is not None and name == stop_after:
            P.maxops = P.nops + maxops
        fn()
        P.barrier()
        if debug:
            print("phase", name, "ops", n0, P.nops, flush=True)
        if stop_after == name:
            break
    return finish(k)


def finish(k):
    k.P.emit()
    return k.nc


def setup_mod(k):
    P, A, nc = k.P, k.A, k.nc
    m = A.mark()
    cT = A.alloc("cT", [128, 8, 2], F32)
    sc = A.alloc("sc", [128, 8, 2], F32)
    mb = A.alloc("mb", [128, 72], F32)
    wt = [A.alloc(f"mw{i}", [128, 8, 512], F32) for i in range(2)]
    P.dma("sp", cT[:], k.inp["cT"], writes=[("cT",)])
    P.act(lambda e: e.activation(out=sc[:], in_=cT[:], func=AF.Silu), reads=[("cT",)], writes=[("sc",)])
    n = 0
    for l in range(2):
        P.dma("sp", mb[:], k.inp[f"l{l}_mod_bT"], writes=[("mb",)])
        ps = k.PS[l]
        for blk in range(18):
            w = wt[n % 2]
            wtok = ("mw", n % 2)
            n += 1
            P.dma("sp", w[:], k.inp[f"l{l}_mod_w"][blk].rearrange("p (k n) -> p k n", k=8), writes=[wtok])

            def mm(e, w=w, blk=blk, ps=ps):
                ins = None
                for nn in range(4):
                    j = blk * 4 + nn
                    for kc in range(8):
                        ins = e.matmul(ps[:, 2 * j:2 * j + 2], lhsT=w[:, kc, nn * 128:(nn + 1) * 128],
                                       rhs=sc[:, kc, :], start=(kc == 0), stop=(kc == 7))
                return ins
            P.pe(mm, reads=[wtok, ("sc",)], writes=[("ps", l)])
        for who in range(2):
            src = ps[:, 0:144].rearrange("p (j w) -> p j w", w=2)[:, :, who]
            P.dve(lambda e, src=src, l=l, who=who: e.tensor_tensor(out=k.modv[:, l, who, :], in0=src, in1=mb[:], op=ALU.add),
                  reads=[("ps", l), ("mb",)], writes=[("modv", l, who)])
        for who in range(2):
            for gi, mi in enumerate((1, 4, 7)):
                P.dve(lambda e, l=l, who=who, gi=gi, mi=mi: e.scalar_tensor_tensor(
                    out=k.der[:, l, who, gi, :], in0=k.modv[:, l, who, mi * 8:mi * 8 + 8], scalar=1.0,
                    in1=k.ng[:, l, gi, :], op0=ALU.add, op1=ALU.mult),
                    reads=[("modv", l, who), ("ng", l)], writes=[("der", l, who, gi)])
            for gi, mi in ((3, 2), (4, 8)):
                P.dve(lambda e, l=l, who=who, gi=gi, mi=mi: e.tensor_scalar(
                    out=k.der[:, l, who, gi, :], in0=k.modv[:, l, who, mi * 8:mi * 8 + 8], scalar1=0.5,
                    scalar2=None, op0=ALU.mult),
                    reads=[("modv", l, who)], writes=[("der", l, who, gi)])
    if k.debug:
        dm = nc.dram_tensor("dbg_modv", [128, 2 * 2 * 72], F32, kind="ExternalOutput").ap()
        P.dma("sp", dm, k.modv[:].rearrange("p a b c -> p (a b c)"),
              reads=[("modv", l, w) for l in range(2) for w in range(2)], writes=[("dbg_modv",)])
    A.release(m)


def rms_norm_mod(k, xb, xtoks, T, gp, sh, hT, htok, tmp, r1, r2, rstd, sq):
    P = k.P
    P.pool(lambda e: e.tensor_tensor(out=sq[:, :, 0:T], in0=xb[:, :, 0:T], in1=xb[:, :, 0:T], op=ALU.mult),
           reads=xtoks, writes=[("sq", c) for c in range(8)])

    def mm(e):
        ins = None
        for c in range(8):
            ins = e.matmul(k.PS[6][:, 0:T], lhsT=k.ones_bf, rhs=sq[:, c, 0:T], start=(c == 0), stop=(c == 7))
        return ins
    P.pe(mm, reads=[("sq", c) for c in range(8)] + [("cbf",)], writes=[("ps", 6)])
    P.dve(lambda e: e.tensor_scalar(out=r1[:, 0:T], in0=k.PS[6][:, 0:T], scalar1=1.0 / D, scalar2=EPS,
                                    op0=ALU.mult, op1=ALU.add), reads=[("ps", 6)], writes=[("r1",)])
    P.act(lambda e: e.activation(out=r2[:, 0:T], in_=r1[:, 0:T], func=AF.Sqrt), reads=[("r1",)], writes=[("r2",)])
    P.dve(lambda e: e.reciprocal(out=rstd[:, 0:T], in_=r2[:, 0:T]), reads=[("r2",)], writes=[("rstd",)])
    for c in range(8):
        tb = tmp[c % 2]
        P.dve(lambda e, c=c, tb=tb: e.scalar_tensor_tensor(out=tb[:, 0:T], in0=xb[:, c, 0:T], scalar=gp[:, c:c + 1],
                                                           in1=rstd[:, 0:T], op0=ALU.mult, op1=ALU.mult),
              reads=[xtoks[c], ("rstd",)], writes=[("tmp", c % 2)])
        P.act(lambda e, c=c, tb=tb: e.activation(out=hT[:, c, 0:T], in_=tb[:, 0:T], func=AF.Identity,
                                                 bias=sh[:, c:c + 1], scale=1.0),
              reads=[("tmp", c % 2)], writes=[(htok, c)])


def ffn_phase(k, l, which, tiles, src, dst):
    P, A, nc = k.P, k.A, k.nc
    PS = k.PS
    m = A.mark()
    xt = [A.alloc(f"xt{i}", [128, 8, 512], F32) for i in range(3)]
    hT = [A.alloc(f"hT{i}", [128, 8, 512], BF16) for i in range(2)]
    actT = [A.alloc(f"actT{i}", [128, NFC, 512], BF16) for i in range(2)]
    sq = A.alloc("sq", [128, 8, 512], BF16)
    win = [A.alloc(f"win{i}", [128, 8, 256], BF16) for i in range(3)]
    wout = [A.alloc(f"wout{i}", [128, NFC, 128], BF16) for i in range(2)]
    r1 = A.alloc("r1", [128, 512], F32)
    r2 = A.alloc("r2", [128, 512], F32)
    rstd = A.alloc("rstd", [128, 512], F32)
    tmp = [A.alloc(f"tmp{i}", [128, 512], F32) for i in range(2)]
    sg = [A.alloc(f"sg{i}", [128, 512], F32) for i in range(2)]
    tokb = [A.alloc(f"tokb{i}", [128, 1024], F32) for i in range(2)]
    gi = 0 if which == 1 else 2
    shi = 0 if which == 1 else 6
    hgi = 3 if which == 1 else 4
    wname_in = f"l{l}_f{which}_win"
    wname_out = f"l{l}_f{which}_wout"
    groups = [tiles[i:i + 2] for i in range(0, len(tiles), 2)]
    tk = 0
    wi = 0
    wo = 0
    guc = 0
    yc = 0
    tbc = 0
    for grp in groups:
        bufs = []
        for s, (who, t0, T) in enumerate(grp):
            b = tk % 3
            tk += 1
            bufs.append(b)
            xb = xt[b]
            xtoks = [("xt", b, c) for c in range(8)]
            if src == "xT":
                P.dma("sp", xb[:, :, 0:T], k.xTd[:, :, t0:t0 + T].rearrange("c p t -> p c t"), writes=xtoks)
            else:
                srcap = k.inp["ctx"] if who == 1 else k.inp["x"]
                r0 = t0 if who == 1 else t0 - NCTX
                for st in range(T // 128):
                    tb = tokb[tbc % 2]
                    ttok = ("tokb", tbc % 2)
                    tbc += 1
                    P.dma("sp", tb[:], srcap[r0 + st * 128:r0 + (st + 1) * 128, :], writes=[ttok])
                    for half in range(2):
                        bank = PS[(guc % 4)]
                        btok = ("ps", guc % 4)
                        guc += 1

                        def tr(e, tb=tb, half=half, bank=bank):
                            ins = None
                            for cc in range(4):
                                c = half * 4 + cc
                                ins = e.transpose(bank[:, cc * 128:(cc + 1) * 128], tb[:, c * 128:(c + 1) * 128], k.ident[:])
                            return ins
                        P.pe(tr, reads=[ttok, ("ident",)], writes=[btok])
                        P.act(lambda e, xb=xb, half=half, bank=bank, st=st: e.activation(
                            out=xb[:, half * 4:half * 4 + 4, st * 128:(st + 1) * 128],
                            in_=bank[:].rearrange("p (c t) -> p c t", c=4), func=AF.Copy),
                            reads=[btok], writes=xtoks[half * 4:half * 4 + 4])
            rms_norm_mod(k, xb, xtoks, T, k.der[:, l, who, gi, :], k.modv[:, l, who, shi * 8:shi * 8 + 8],
                         hT[s], ("hT", s), tmp, r1, r2, rstd, sq)
        for j in range(NFC):
            wb = win[wi % 3]
            wtok = ("win", wi % 3)
            wi += 1
            P.dma("sp", wb[:], k.wbf[wname_in][j].rearrange("p (k n) -> p k n", k=8),
                  reads=[("wbf", wname_in, j)], writes=[wtok])
            for s, (who, t0, T) in enumerate(grp):
                bg = (guc % 2) * 2
                guc += 1
                htoks = [(("hT", s), c) for c in range(8)]

                def mmg(e, wb=wb, s=s, T=T, bank=PS[bg], off=0):
                    ins = None
                    for kc in range(8):
                        ins = e.matmul(bank[:, 0:T], lhsT=wb[:, kc, off:off + 128], rhs=hT[s][:, kc, 0:T],
                                       start=(kc == 0), stop=(kc == 7))
                    return ins
                P.pe(mmg, reads=[wtok] + htoks, writes=[("ps", bg)])
                P.pe(lambda e, wb=wb, s=s, T=T, bank=PS[bg + 1], f=mmg: f(e, wb, s, T, bank, 128),
                     reads=[wtok] + htoks, writes=[("ps", bg + 1)])
                q = guc % 2
                P.act(lambda e, q=q, bg=bg, T=T: e.activation(out=sg[q][:, 0:T], in_=PS[bg][:, 0:T], func=AF.Silu),
                      reads=[("ps", bg)], writes=[("sg", q)])
                P.dve(lambda e, q=q, bg=bg, T=T, s=s, j=j: e.tensor_tensor(
                    out=actT[s][:, j, 0:T], in0=PS[bg + 1][:, 0:T], in1=sg[q][:, 0:T], op=ALU.mult),
                    reads=[("ps", bg + 1), ("sg", q)], writes=[("actT", s, j)])
        for d in range(8):
            wb = wout[wo % 2]
            wtok = ("wout", wo % 2)
            wo += 1
            P.dma("sp", wb[:], k.wbf[wname_out][d].rearrange("p (k n) -> p k n", k=NFC),
                  reads=[("wbf", wname_out, d)], writes=[wtok])
            for s, (who, t0, T) in enumerate(grp):
                by = 4 + (yc % 2)
                yc += 1
                xb = xt[bufs[s]]

                def mmy(e, wb=wb, s=s, T=T, by=by):
                    ins = None
                    for fc in range(NFC):
                        ins = e.matmul(PS[by][:, 0:T], lhsT=wb[:, fc, :], rhs=actT[s][:, fc, 0:T],
                                       start=(fc == 0), stop=(fc == NFC - 1))
                    return ins
                P.pe(mmy, reads=[wtok] + [("actT", s, j) for j in range(NFC)], writes=[("ps", by)])
                hg = k.der[:, l, who, hgi, :]
                P.dve(lambda e, xb=xb, d=d, T=T, by=by, hg=hg: e.scalar_tensor_tensor(
                    out=xb[:, d, 0:T], in0=PS[by][:, 0:T], scalar=hg[:, d:d + 1], in1=xb[:, d, 0:T],
                    op0=ALU.mult, op1=ALU.add),
                    reads=[("ps", by), ("xt", bufs[s], d)], writes=[("xt", bufs[s], d)])
        for s, (who, t0, T) in enumerate(grp):
            b = bufs[s]
            xb = xt[b]
            xtoks = [("xt", b, c) for c in range(8)]
            if dst == "xT":
                P.dma("sp", k.xTd[:, :, t0:t0 + T].rearrange("c p t -> p c t"), xb[:, :, 0:T], reads=xtoks,
                      writes=[("xTd", t0)])
            else:
                r0 = t0 - NCTX
                for st in range(T // 128):
                    ob = tokb[tbc % 2]
                    otok = ("tokb", tbc % 2)
                    tbc += 1
                    for half in range(2):
                        bi = guc % 4
                        guc += 1

                        def tr(e, xb=xb, half=half, bi=bi, st=st):
                            ins = None
                            for cc in range(4):
                                c = half * 4 + cc
                                ins = e.transpose(PS[bi][:, cc * 128:(cc + 1) * 128], xb[:, c, st * 128:(st + 1) * 128],
                                                  k.ident[:])
                            return ins
                        P.pe(tr, reads=xtoks[half * 4:half * 4 + 4] + [("ident",)], writes=[("ps", bi)])
                        P.act(lambda e, ob=ob, half=half, bi=bi: e.activation(
                            out=ob[:, half * 512:(half + 1) * 512], in_=PS[bi][:], func=AF.Copy),
                            reads=[("ps", bi)], writes=[(otok, half)])
                    P.dma("sp", k.out[r0 + st * 128:r0 + (st + 1) * 128, :], ob[:], reads=[(otok, 0), (otok, 1)],
                          writes=[("out", r0, st)])
    A.release(m)


_CACHE = {}


def kernel(**inputs):
    inputs = {kk: np.asarray(v) for kk, v in inputs.items()}
    if "nc" not in _CACHE:
        _CACHE["nc"] = build()
    nc = _CACHE["nc"]
    shared = host_shared(inputs)
    in_maps = []
    for b in range(8):
        mm = dict(shared)
        mm.update(host_inputs(inputs, b))
        in_maps.append(mm)
    res = run_bass_kernel_spmd(nc, in_maps, core_ids=list(range(8)))
    return np.stack([r["out"] for r in res.results], axis=0)


NEG = -30000.0
PERM64 = list(range(0, 64, 2)) + list(range(1, 64, 2))
SWAP64 = list(range(1, 64, 2)) + list(range(0, 64, 2))
PERM32 = list(range(0, 32, 2)) + list(range(1, 32, 2))
SWAP32 = list(range(1, 32, 2)) + list(range(0, 32, 2))


def rope_tables():
    t = np.arange(S)
    row = (t // GRID_W).astype(np.float32)
    col = (t % GRID_W).astype(np.float32)

    def tab(dim):
        npairs = dim // 4
        inv = (10000.0 ** (-np.arange(npairs, dtype=np.float32) / npairs)).astype(np.float32)
        ang = np.concatenate([row[:, None] * inv, col[:, None] * inv], axis=-1).astype(np.float32)
        return np.cos(ang).astype(np.float32).T, np.sin(ang).astype(np.float32).T
    cA, sA = tab(64)
    cB, sB = tab(32)
    ta = np.zeros((128, 2, S), np.float32)
    for p in range(128):
        pp = p % 64
        pr = pp % 32
        ta[p, 0] = cA[pr]
        ta[p, 1] = -sA[pr] if pp < 32 else sA[pr]
    tb = np.zeros((128, 2, S), np.float32)
    for p in range(64, 96):
        pp = p - 64
        pr = pp % 16
        tb[p, 0] = cB[pr]
        tb[p, 1] = -sB[pr] if pp < 16 else sB[pr]
    return ta, tb


def na_geometry():
    sigs = []
    mp = {}
    chunks = {}
    for i in range(32):
        rs0 = min(max(2 * i - 4, 0), 56)
        rs1 = min(max(2 * i + 1 - 4, 0), 56)
        lo = rs0 // 2
        hi = (rs1 + 7) // 2
        chunks[i] = list(range(lo, hi + 1))
        for kc in chunks[i]:
            sig = (rs0 - 2 * i, rs1 - (2 * i + 1), kc - i)
            if sig not in sigs:
                sigs.append(sig)
            mp[(i, kc)] = sigs.index(sig)
    return sigs, mp, chunks


NA_SIGS, NA_MAP, NA_CHUNKS = na_geometry()
NSIG = len(NA_SIGS)


def na_bias_tiles(rpb):
    out = np.full((NSIG, 8, 128, 128), NEG, np.float32)
    kp = np.arange(128)
    qf = np.arange(128)
    kr_, kc_ = kp // 64, kp % 64
    qr_, qc_ = qf // 64, qf % 64
    cs = np.clip(qc_ - 8, 0, 48)
    for si, (a0, a1, dk) in enumerate(NA_SIGS):
        rsrel = np.where(qr_ == 0, a0, a1)
        krel = (2 * dk + kr_[:, None]) - qr_[None, :]
        vr = (krel >= rsrel[None, :]) & (krel <= rsrel[None, :] + 7)
        vc = (kc_[:, None] >= cs[None, :]) & (kc_[:, None] <= cs[None, :] + 15)
        valid = vr & vc
        dr = np.clip(krel + 7, 0, 14)
        dc = np.clip(kc_[:, None] - qc_[None, :] + 15, 0, 30)
        for h in range(8):
            g = rpb[h][dr, dc]
            out[si, h] = np.where(valid, g, np.float32(NEG))
    return out.reshape(NSIG * 8, 128, 128)


def blk(w, cols):
    K_ = w.shape[0]
    o = np.zeros((K_, len(cols)), np.float32)
    idx = [i for i, c in enumerate(cols) if c is not None]
    o[:, idx] = w[:, [cols[i] for i in idx]]
    return o.reshape(K_ // 128, 128, len(cols)).transpose(1, 0, 2)


def host_mixer(inputs):
    m = {}
    ta, tb = rope_tables()
    m["ropeA"] = ta
    m["ropeB"] = tb
    w = inputs["l0_w_in"]
    blocks = []
    blocks.append(blk(w, list(range(0, 128))))
    blocks.append(blk(w, list(range(128, 256))))
    blocks.append(blk(w, [None] * 64 + [256 + i for i in PERM32] + [None] * 32))
    blocks.append(blk(w, [None] * 64 + [256 + i for i in SWAP32] + [None] * 32))
    for c in range(3):
        blocks.append(blk(w, list(range(288 + c * 128, 288 + (c + 1) * 128))))
    for c in range(8):
        blocks.append(blk(w, list(range(672 + c * 128, 672 + (c + 1) * 128))))
    m["l0_wA"] = np.ascontiguousarray(np.stack(blocks, 0).reshape(15, 128, 8 * 128))
    wk = inputs["l0_mla_w_ukv"]
    kb = [blk(wk, [(2 * pr) * 128 + i for i in range(64)] + [(2 * pr + 1) * 128 + i for i in range(64)]) for pr in range(4)]
    m["l0_wkn"] = np.ascontiguousarray(np.stack(kb, 0).reshape(4, 128, 2 * 128))
    m["l0_wv"] = np.ascontiguousarray(blk(wk, [h * 128 + 64 + i for h in range(8) for i in range(64)]).reshape(1, 128, 2 * 512))
    wq = inputs["l0_mla_w_uq"]
    qb = []
    for h in range(8):
        qb.append(blk(wq, [h * 96 + i for i in range(64)] + [h * 96 + 64 + i for i in PERM32]))
    for h in range(8):
        qb.append(blk(wq, [None] * 64 + [h * 96 + 64 + i for i in SWAP32]))
    m["l0_wq"] = np.ascontiguousarray(np.stack(qb, 0).reshape(16, 128, 3 * 96))
    for l in range(2):
        wo = inputs[f"l{l}_w_out"]
        m[f"l{l}_wo"] = np.ascontiguousarray(wo.reshape(8, 128, 8, 128).transpose(2, 1, 0, 3).reshape(8, 128, 8 * 128))
    m["l0_dw"] = np.ascontiguousarray(inputs["l0_conv_dw_w"].reshape(31, 4, 128).transpose(2, 1, 0))
    v = np.zeros((128, 32), np.float32)
    v[:, 0:2] = pvec(inputs["l0_mla_kv_norm"])
    v[:, 2:5] = pvec(inputs["l0_mla_q_norm"])
    kg = inputs["l0_mla_k_gain"]
    qg = inputs["l0_mla_q_gain"]
    v[:, 5] = np.tile(kg[:64], 2)
    v[64:96, 6] = kg[64:][PERM32]
    v[64:96, 7] = kg[64:][SWAP32]
    v[0:64, 8] = qg[:64]
    v[64:96, 8] = qg[64:][PERM32]
    v[64:96, 9] = qg[64:][SWAP32]
    v[0:64, 10] = 1.0 / 64
    v[64:96, 10] = 1.0 / 32
    gb = inputs["l0_conv_glu_b"]
    v[:, 11:15] = pvec(gb[:512])
    v[:, 15:19] = pvec(gb[512:])
    v[:, 19:23] = pvec(inputs["l0_conv_dw_b"])
    v[:, 23:27] = pvec(inputs["l0_conv_ln_g"])
    v[:, 27:31] = pvec(inputs["l0_conv_ln_b"])
    m["l0_vec"] = v
    w = inputs["l1_w_in"]
    blocks = []
    for hk in range(2):
        blocks.append(blk(w, [hk * 64 + i for i in PERM64] * 2))
    for hk in range(2):
        blocks.append(blk(w, [hk * 64 + i for i in SWAP64] * 2))
    for c in range(4):
        blocks.append(blk(w, list(range(256 + c * 128, 256 + (c + 1) * 128))))
    for c in range(4):
        blocks.append(blk(w, [1280 + (2 * c + e) * 64 + i for e in range(2) for i in PERM64]))
    for c in range(4):
        blocks.append(blk(w, [1280 + (2 * c + e) * 64 + i for e in range(2) for i in SWAP64]))
    for c in range(4):
        blocks.append(blk(w, list(range(1792 + c * 128, 1792 + (c + 1) * 128))))
    m["l1_wA"] = np.ascontiguousarray(np.stack(blocks, 0).reshape(20, 128, 8 * 128))
    m["l1_wv"] = np.ascontiguousarray(blk(w, list(range(128, 256)) + list(range(768, 1280))).reshape(1, 128, 8 * 640))
    v = np.zeros((128, 8), np.float32)
    v[:, 0] = np.tile(inputs["l1_gqa_k_gain"][PERM64], 2)
    v[:, 1] = np.tile(inputs["l1_gqa_k_gain"][SWAP64], 2)
    v[:, 2] = np.tile(inputs["l1_gqa_q_gain"][PERM64], 2)
    v[:, 3] = np.tile(inputs["l1_gqa_q_gain"][SWAP64], 2)
    v[:, 4] = np.tile(inputs["l1_na_k_gain"], 2)
    v[:, 5] = np.tile(inputs["l1_na_q_gain"], 2)
    m["l1_vec"] = v
    m["l1_nab"] = na_bias_tiles(inputs["l1_na_rpb"])
    return m


def declare_mixer_inputs(k, nc):
    def din(name, shape):
        k.inp[name] = nc.dram_tensor(name, list(shape), F32, kind="ExternalInput").ap()
        k.inshape[name] = tuple(shape)
    din("ropeA", [128, 2, S])
    din("ropeB", [128, 2, S])
    din("l0_wA", [15, 128, 1024])
    din("l0_wkn", [4, 128, 256])
    din("l0_wv", [1, 128, 1024])
    din("l0_wq", [16, 128, 288])
    din("l0_wo", [8, 128, 1024])
    din("l1_wo", [8, 128, 1024])
    din("l0_dw", [128, 4, 31])
    din("l0_vec", [128, 32])
    din("l1_wA", [20, 128, 1024])
    din("l1_wv", [1, 128, 8 * 640])
    din("l1_vec", [128, 8])
    din("l1_nab", [NSIG * 8, 128, 128])


def stats_rstd(k, st, rows, T, scale, extra_reads=()):
    P = k.P
    i = st["i"]
    r1, r2, rstd, bank = st["r1"], st["r2"], st["rstd"], st["bank"]
    P.dve(lambda e: e.tensor_scalar(out=r1[rows, 0:T], in0=k.PS[bank][rows, 0:T], scalar1=scale, scalar2=EPS,
                                    op0=ALU.mult, op1=ALU.add), reads=[("ps", bank)] + list(extra_reads), writes=[("r1", i)])
    P.act(lambda e: e.activation(out=r2[rows, 0:T], in_=r1[rows, 0:T], func=AF.Sqrt),
          reads=[("r1", i)], writes=[("r2", i)])
    P.dve(lambda e: e.reciprocal(out=rstd[rows, 0:T], in_=r2[rows, 0:T]), reads=[("r2", i)], writes=[("rstd", i)])


class PsRot:
    def __init__(self, banks=(0, 1, 2, 3, 4, 5)):
        self.banks = banks
        self.i = 0

    def next(self):
        b = self.banks[self.i % len(self.banks)]
        self.i += 1
        return b


def load_x_norm(k, l, who, t0, T, xb, xtoks, hT, w):
    P = k.P
    P.dma("sp", xb[:, :, 0:T], k.xTd[:, :, t0:t0 + T].rearrange("c p t -> p c t"), writes=xtoks)
    rms_norm_mod(k, xb, xtoks, T, k.der[:, l, who, 1, :], k.modv[:, l, who, 24:32], hT, "hTm",
                 w["tmp"], w["r1"], w["r2"], w["rstd"], w["sq"])


def mm_group(k, bank_ap, lhs_fn, rhs_fn, n, reads, btok):
    def f(e):
        ins = None
        for kc in range(n):
            ins = e.matmul(bank_ap, lhsT=lhs_fn(kc), rhs=rhs_fn(kc), start=(kc == 0), stop=(kc == n - 1))
        return ins
    k.P.pe(f, reads=reads, writes=[btok])


def common_work(k, A):
    w = {}
    w["tmp"] = [A.alloc(f"tmp{i}", [128, 512], F32) for i in range(2)]
    w["r1"] = A.alloc("r1", [128, 512], F32)
    w["r2"] = A.alloc("r2", [128, 512], F32)
    w["rstd"] = A.alloc("rstd", [128, 512], F32)
    w["sq"] = A.alloc("sq", [128, 8, 512], BF16)
    w["stats"] = [dict(i=i, r1=A.alloc(f"sr1_{i}", [128, 512], F32), r2=A.alloc(f"sr2_{i}", [128, 512], F32),
                       rstd=A.alloc(f"srstd_{i}", [128, 512], F32)) for i in range(3)]
    w["si"] = 0
    w["sqs"] = [A.alloc(f"sqs{i}", [128, 512], BF16) for i in range(4)]
    w["sqi"] = 0
    w["rab"] = [(A.alloc(f"ra{i}", [128, 512], F32), A.alloc(f"rb{i}", [128, 512], F32)) for i in range(2)]
    w["rai"] = 0
    return w


def l0_proj_phase(k):
    P, A, nc, PS = k.P, k.A, k.nc, k.PS
    l = 0
    m = A.mark()
    w = common_work(k, A)
    xt = [A.alloc(f"xt{i}", [128, 8, 512], F32) for i in range(2)]
    hT = A.alloc("hTm", [128, 8, 512], BF16)
    wA = A.alloc("wA", [128, 15, 8, 128], BF16)
    wkn = A.alloc("wkn", [128, 4, 2, 128], BF16)
    wv = A.alloc("wv", [128, 2, 512], BF16)
    wq = A.alloc("wq", [128, 16, 3, 96], BF16)
    vec = A.alloc("vec", [128, 32], F32)
    rope = [A.alloc(f"rope{i}", [128, 2, 512], F32) for i in range(2)]
    ckvT = A.alloc("ckvT", [128, 2, 512], BF16)
    cqT = A.alloc("cqT", [128, 3, 512], BF16)
    krT = A.alloc("krT", [128, 512], BF16)
    yT = [A.alloc(f"yT{i}", [128, 4, 512], BF16) for i in range(2)]
    KT = [A.alloc(f"KT{i}", [128, 8, 512], BF16) for i in range(2)]
    QT = [A.alloc(f"QT{i}", [128, 8, 512], BF16) for i in range(2)]
    Vt = [A.alloc(f"Vt{i}", [128, 8, 128], BF16) for i in range(2)]
    sig = [A.alloc(f"sig{i}", [128, 512], F32) for i in range(2)]
    P.dma("sp", vec[:], k.inp["l0_vec"], writes=[("vec",)])
    for i in range(15):
        P.dma("sp", wA[:, i], k.wbf["l0_wA"][i].rearrange("p (k n) -> p k n", k=8), reads=[("wbf", "l0_wA", i)], writes=[("wA",)])
    for i in range(4):
        P.dma("sp", wkn[:, i], k.wbf["l0_wkn"][i].rearrange("p (k n) -> p k n", k=2), reads=[("wbf", "l0_wkn", i)], writes=[("wkn",)])
    P.dma("sp", wv[:], k.wbf["l0_wv"][0].rearrange("p (k n) -> p k n", k=2), reads=[("wbf", "l0_wv", 0)], writes=[("wv",)])
    for i in range(16):
        P.dma("sp", wq[:, i], k.wbf["l0_wq"][i].rearrange("p (k n) -> p k n", k=3), reads=[("wbf", "l0_wq", i)], writes=[("wq",)])
    sc = 96.0 ** -0.5
    P.dve(lambda e: e.tensor_scalar(out=vec[:, 8:10], in0=vec[:, 8:10], scalar1=sc, scalar2=None, op0=ALU.mult),
          reads=[("vec",)], writes=[("vec",)])
    for i in range(2):
        P.dve(lambda e, i=i: e.memset(Vt[i][:], 1.0), writes=[("Vt", i), (("Vt", i), 1)])
    rot = PsRot()
    vcnt = 0
    ALLR = slice(0, 128)
    R = slice(64, 96)
    R96 = slice(0, 96)
    for ti, (who, t0, T) in enumerate(TILES):
        lat = (who == 0)
        xb = xt[ti % 2]
        xtoks = [("xtm", ti % 2, c) for c in range(8)]
        load_x_norm(k, l, who, t0, T, xb, xtoks, hT, w)
        htoks = [("hTm", c) for c in range(8)]
        rp = rope[ti % 2]
        if lat:
            P.dma("sp", rp[:, :, 0:T], k.inp["ropeB"][:, :, t0 - NCTX:t0 - NCTX + T], writes=[("rope", ti % 2)])

        def projA(bidx, M):
            b = rot.next()
            mm_group(k, PS[b][0:M, 0:T], lambda kc: wA[:, bidx, kc, 0:M], lambda kc: hT[:, kc, 0:T], 8,
                     [("wA",)] + htoks, ("ps", b))
            return b

        def square(b, M, dst_ap, dtok):
            P.act(lambda e: e.activation(out=dst_ap, in_=PS[b][0:M, 0:T], func=AF.Square), reads=[("ps", b)], writes=[dtok])

        bs = [projA(0, 128), projA(1, 128)]
        for c in range(2):
            square(bs[c], 128, w["sq"][:, c, 0:T], ("sq", c))
        st = next_stat(w)
        mm_group(k, PS[st["bank"]][:, 0:T], lambda kc: k.ones_bf, lambda kc: w["sq"][:, kc, 0:T], 2,
                 [("sq", 0), ("sq", 1), ("cbf",)], ("ps", st["bank"]))
        stats_rstd(k, st, ALLR, T, 1.0 / 256)
        for c in range(2):
            P.dve(lambda e: e.scalar_tensor_tensor(out=ckvT[:, c, 0:T], in0=PS[bs[c]][:, 0:T], scalar=vec[:, c:c + 1],
                                                   in1=st["rstd"][:, 0:T], op0=ALU.mult, op1=ALU.mult),
                  reads=[("ps", bs[c]), ("rstd", st["i"]), ("vec",)], writes=[("ckvT", c)])
        bm = projA(2, 96)
        bsw = projA(3, 96) if lat else None
        sb, stok = next_sqs(w)
        square(bm, 96, sb[0:96, 0:T], stok)
        st = next_stat(w)
        mm_group(k, PS[st["bank"]][0:96, 0:T], lambda kc: k.cbf[0:96, 3, 0:96], lambda kc: sb[0:96, 0:T], 1, [stok, ("cbf",)],
                 ("ps", st["bank"]))
        stats_rstd(k, st, R, T, 1.0 / 32)
        if not lat:
            P.dve(lambda e: e.scalar_tensor_tensor(out=krT[R, 0:T], in0=PS[bm][R, 0:T], scalar=vec[R, 6:7],
                                                   in1=st["rstd"][R, 0:T], op0=ALU.mult, op1=ALU.mult),
                  reads=[("ps", bm), ("rstd", st["i"]), ("vec",)], writes=[("krT",)])
        else:
            rope_apply(k, w, bm, bsw, R, T, vec[R, 6:7], vec[R, 7:8], st, rp, ("rope", ti % 2), krT[R, 0:T], [("krT",)])
        bs = [projA(4 + c, 128) for c in range(3)]
        for c in range(3):
            square(bs[c], 128, w["sq"][:, 2 + c, 0:T], ("sq", 2 + c))
        st = next_stat(w)
        mm_group(k, PS[st["bank"]][:, 0:T], lambda kc: k.ones_bf, lambda kc: w["sq"][:, 2 + kc, 0:T], 3,
                 [("sq", 2), ("sq", 3), ("sq", 4), ("cbf",)], ("ps", st["bank"]))
        stats_rstd(k, st, ALLR, T, 1.0 / 384)
        for c in range(3):
            P.dve(lambda e: e.scalar_tensor_tensor(out=cqT[:, c, 0:T], in0=PS[bs[c]][:, 0:T], scalar=vec[:, 2 + c:3 + c],
                                                   in1=st["rstd"][:, 0:T], op0=ALU.mult, op1=ALU.mult),
                  reads=[("ps", bs[c]), ("rstd", st["i"]), ("vec",)], writes=[("cqT", c)])
        yb = yT[ti % 2]
        for ch in range(4):
            ba_ = projA(7 + ch, 128)
            bg_ = projA(11 + ch, 128)
            sg_ = sig[ch % 2]
            P.act(lambda e: e.activation(out=sg_[:, 0:T], in_=PS[bg_][:, 0:T], func=AF.Sigmoid,
                                         bias=vec[:, 15 + ch:16 + ch], scale=1.0),
                  reads=[("ps", bg_), ("vec",)], writes=[("sig", ch % 2)])
            P.dve(lambda e: e.scalar_tensor_tensor(out=yb[:, ch, 0:T], in0=PS[ba_][:, 0:T],
                                                   scalar=vec[:, 11 + ch:12 + ch], in1=sg_[:, 0:T],
                                                   op0=ALU.add, op1=ALU.mult),
                  reads=[("ps", ba_), ("sig", ch % 2), ("vec",)], writes=[("yT", ti % 2, ch)])
        P.dma("sp", k.yTd[:, :, t0:t0 + T].rearrange("c p t -> p c t"), yb[:, :, 0:T],
              reads=[("yT", ti % 2, ch) for ch in range(4)], writes=[("yTd", ti)])
        Kb = KT[ti % 2]
        ktoks = [("KT", ti % 2, h) for h in range(8)]
        for pr in range(4):
            b = rot.next()
            mm_group(k, PS[b][:, 0:T], lambda kc: wkn[:, pr, kc, :], lambda kc: ckvT[:, kc, 0:T], 2,
                     [("wkn",), ("ckvT", 0), ("ckvT", 1)], ("ps", b))
            sb, stok = next_sqs(w)
            square(b, 128, sb[:, 0:T], stok)
            st = next_stat(w)
            mm_group(k, PS[st["bank"]][:, 0:T], lambda kc: k.bd64_bf, lambda kc: sb[:, 0:T], 1, [stok, ("cbf",)], ("ps", st["bank"]))
            stats_rstd(k, st, ALLR, T, 1.0 / 64)
            for e_ in range(2):
                h = 2 * pr + e_
                P.dve(lambda e: e.scalar_tensor_tensor(
                    out=Kb[0:64, h, 0:T], in0=PS[b][e_ * 64:(e_ + 1) * 64, 0:T], scalar=vec[e_ * 64:(e_ + 1) * 64, 5:6],
                    in1=st["rstd"][e_ * 64:(e_ + 1) * 64, 0:T], op0=ALU.mult, op1=ALU.mult),
                    reads=[("ps", b), ("rstd", st["i"]), ("vec",)], writes=[("KT", ti % 2, h)])
        for h in range(8):
            P.pool(lambda e: e.tensor_copy(out=Kb[R, h, 0:T], in_=krT[R, 0:T]), reads=[("krT",)], writes=[(("KT", ti % 2, h), "r")])
        P.dma("sp", k.KTd[:, :, t0:t0 + T].rearrange("h p t -> p h t"), Kb[0:96, :, 0:T],
              reads=ktoks + [(kt_, "r") for kt_ in ktoks], writes=[("KTd", ti)])
        for s_ in range(T // 128):
            b = rot.next()
            mm_group(k, PS[b][:, 0:512], lambda kc: ckvT[:, kc, s_ * 128:(s_ + 1) * 128], lambda kc: wv[:, kc, :], 2,
                     [("wv",), ("ckvT", 0), ("ckvT", 1)], ("ps", b))
            vb = Vt[vcnt % 2]
            vtok = ("Vt", vcnt % 2)
            vcnt += 1
            src = PS[b][:, 0:512].rearrange("p (h two d) -> p h two d", two=2, d=64)
            dst = vb[:].rearrange("p (h two) d -> p h two d", two=2)
            P.dve(lambda e: e.tensor_copy(out=dst[:, :, 0, 0:64], in_=src[:, :, 0, :]), reads=[("ps", b)], writes=[vtok])
            P.dve(lambda e: e.tensor_copy(out=dst[:, :, 1, 64:128], in_=src[:, :, 1, :]), reads=[("ps", b)], writes=[(vtok, 1)])
            chunk = (t0 + s_ * 128) // 128
            P.dma("sp", k.Vd[:, :, chunk, :].rearrange("h p d -> p h d"), vb[:], reads=[vtok, (vtok, 1)], writes=[("Vd", chunk)])
        Qb = QT[ti % 2]
        qtoks = [("QT", ti % 2, h) for h in range(8)]
        for h in range(8):
            b = rot.next()
            mm_group(k, PS[b][0:96, 0:T], lambda kc: wq[:, h, kc, :], lambda kc: cqT[:, kc, 0:T], 3,
                     [("wq",)] + [("cqT", c) for c in range(3)], ("ps", b))
            bsw = None
            if lat:
                bsw = rot.next()
                mm_group(k, PS[bsw][0:96, 0:T], lambda kc: wq[:, 8 + h, kc, :], lambda kc: cqT[:, kc, 0:T], 3,
                         [("wq",)] + [("cqT", c) for c in range(3)], ("ps", bsw))
            sb, stok = next_sqs(w)
            square(b, 96, sb[0:96, 0:T], stok)
            st = next_stat(w)
            mm_group(k, PS[st["bank"]][0:96, 0:T], lambda kc: k.cbf[0:96, 3, 0:96], lambda kc: sb[0:96, 0:T], 1, [stok, ("cbf",)],
                     ("ps", st["bank"]))
            stats_rstd(k, st, R96, T, vec[0:96, 10:11], extra_reads=[("vec",)])
            nr = 96 if not lat else 64
            P.dve(lambda e: e.scalar_tensor_tensor(out=Qb[0:nr, h, 0:T], in0=PS[b][0:nr, 0:T], scalar=vec[0:nr, 8:9],
                                                   in1=st["rstd"][0:nr, 0:T], op0=ALU.mult, op1=ALU.mult),
                  reads=[("ps", b), ("rstd", st["i"]), ("vec",)], writes=[("QT", ti % 2, h)])
            if lat:
                rope_apply(k, w, b, bsw, R, T, vec[R, 8:9], vec[R, 9:10], st, rp, ("rope", ti % 2),
                           Qb[R, h, 0:T], [(("QT", ti % 2, h), "r")])
        P.dma("sp", k.QTd[:, :, t0:t0 + T].rearrange("h p t -> p h t"), Qb[0:96, :, 0:T],
              reads=qtoks + [(q_, "r") for q_ in qtoks], writes=[("QTd", ti)])
    A.release(m)


def rope_apply(k, w, bm, bsw, R, T, g, gsw, st, rp, rptok, out_ap, otoks):
    P = k.P
    j = w["rai"] % 2
    w["rai"] += 1
    ra, rb = w["rab"][j]
    rstd, rtok = st["rstd"], ("rstd", st["i"])
    P.dve(lambda e: e.scalar_tensor_tensor(out=ra[R, 0:T], in0=k.PS[bm][R, 0:T], scalar=g, in1=rstd[R, 0:T],
                                           op0=ALU.mult, op1=ALU.mult),
          reads=[("ps", bm), rtok, ("vec",)], writes=[("ra", j)])
    P.dve(lambda e: e.scalar_tensor_tensor(out=rb[R, 0:T], in0=k.PS[bsw][R, 0:T], scalar=gsw, in1=rstd[R, 0:T],
                                           op0=ALU.mult, op1=ALU.mult),
          reads=[("ps", bsw), rtok, ("vec",)], writes=[("rb", j)])
    P.pool(lambda e: e.tensor_tensor(out=ra[R, 0:T], in0=ra[R, 0:T], in1=rp[R, 0, 0:T], op=ALU.mult),
           reads=[("ra", j), rptok], writes=[("ra", j)])
    P.pool(lambda e: e.tensor_tensor(out=rb[R, 0:T], in0=rb[R, 0:T], in1=rp[R, 1, 0:T], op=ALU.mult),
           reads=[("rb", j), rptok], writes=[("rb", j)])
    P.dve(lambda e: e.tensor_tensor(out=out_ap, in0=ra[R, 0:T], in1=rb[R, 0:T], op=ALU.add),
          reads=[("ra", j), ("rb", j)], writes=otoks)


def next_stat(w):
    st = w["stats"][w["si"] % len(w["stats"])]
    st["bank"] = 6 + (w["si"] % 2)
    w["si"] += 1
    return st


def next_sqs(w):
    i = w["sqi"] % len(w["sqs"])
    w["sqi"] += 1
    return w["sqs"][i], ("sqs", i)


def attn_phase(k, mode):
    P, A, PS = k.P, k.A, k.PS
    m = A.mark()
    mla = (mode == "mla")
    Kt = [A.alloc(f"Kt{i}", [128, TALL], BF16) for i in range(4)]
    Vh = [A.alloc(f"Vh{i}", [128, 34, 128], BF16) for i in range(4)]
    Qt = [A.alloc(f"Qt{i}", [128, 2, 512], BF16) for i in range(2)]
    PT = [A.alloc(f"PT{i}", [128, 512], BF16) for i in range(4)]
    rec = A.alloc("rec", [128, 512], F32)
    attT = [A.alloc(f"attT{i}", [128, 512], BF16) for i in range(2)]
    qtiles = TILES if mla else TILES[1:]
    qc = 0
    sc = 0
    pc = 0
    ac = 0
    for hp in range(4):
        kb = (hp % 2) * 2
        if mla:
            for e_ in range(2):
                P.dma("sp", Kt[kb + e_][0:96, :], k.KTd[2 * hp + e_], writes=[("Kt", kb + e_)])
                P.dma("sp", Vh[kb + e_][:], k.Vd[2 * hp + e_], writes=[("Vh", kb + e_)])
        else:
            for e_ in range(2):
                P.dma("sp", Kt[kb + e_][:], k.KdupTd[hp // 2], writes=[("Kt", kb + e_)])
                P.dve(lambda e: e.memset(Kt[kb + e_][(1 - e_) * 64:(2 - e_) * 64, :], 0.0), writes=[("Kt", kb + e_)])
                P.dma("sp", Vh[kb + e_][:], k.VgD[(hp // 2) * 2 + e_], writes=[("Vh", kb + e_)])
        for (who, t0, T) in qtiles:
            chunks = [0, 1] if who == 1 else list(range(34))
            Qb = Qt[qc % 2]
            qtok = ("Qt", qc % 2)
            ab = attT[qc % 2]
            atok = ("attT", qc % 2)
            qc += 1
            if mla:
                P.dma("sp", Qb[0:96, :, 0:T], k.QTd[2 * hp:2 * hp + 2, :, t0:t0 + T].rearrange("h p t -> p h t"), writes=[qtok])
            else:
                P.dma("sp", Qb[:, 0, 0:T], k.QgTd[hp, :, t0:t0 + T], writes=[qtok])
            for e_ in range(2):
                if mla:
                    Ksb, ktok, r0, r1 = Kt[kb + e_], ("Kt", kb + e_), 0, 96
                    qap = Qb[0:96, e_, 0:T]
                else:
                    Ksb, ktok, r0, r1 = Kt[kb + e_], ("Kt", kb + e_), 0, 128
                    qap = Qb[:, 0, 0:T]
                Vsb, vtok = Vh[kb + e_], ("Vh", kb + e_)
                acc = 4 + (ac % 2)
                ac += 1
                n = len(chunks)
                pend = []
                LOOK = 2

                def do_pv(item):
                    pidx, pc_, ppb = item
                    P.pe(lambda e: e.matmul(PS[acc][:, 0:T], lhsT=Vsb[:, pc_, :], rhs=PT[ppb][:, 0:T],
                                            start=(pidx == 0), stop=(pidx == n - 1)),
                         reads=[vtok, ("PT", ppb)], writes=[("ps", acc)])
                for idx, c in enumerate(chunks):
                    sbk = sc % 4
                    sc += 1
                    P.pe(lambda e: e.matmul(PS[sbk][:, 0:T], lhsT=Ksb[r0:r1, c * 128:(c + 1) * 128], rhs=qap,
                                            start=True, stop=True),
                         reads=[ktok, qtok], writes=[("ps", sbk)])
                    pb = pc % 4
                    pc += 1
                    P.act(lambda e: e.activation(out=PT[pb][:, 0:T], in_=PS[sbk][:, 0:T], func=AF.Exp),
                          reads=[("ps", sbk)], writes=[("PT", pb)])
                    pend.append((idx, c, pb))
                    if len(pend) > LOOK:
                        do_pv(pend.pop(0))
                while pend:
                    do_pv(pend.pop(0))
                nlo, dlo = (0, 64) if e_ == 0 else (64, 0)
                P.dve(lambda e: e.reciprocal(out=rec[nlo:nlo + 64, 0:T], in_=PS[acc][dlo:dlo + 64, 0:T]),
                      reads=[("ps", acc)], writes=[("rec", e_)])
                P.dve(lambda e: e.tensor_tensor(out=ab[nlo:nlo + 64, 0:T], in0=PS[acc][nlo:nlo + 64, 0:T], in1=rec[nlo:nlo + 64, 0:T],
                                                op=ALU.mult),
                      reads=[("ps", acc), ("rec", e_)], writes=[(atok, e_)])
            P.dma("sp", k.attTd[hp, :, t0:t0 + T], ab[:, 0:T], reads=[(atok, 0), (atok, 1)], writes=[("attTd", hp, t0)])
    A.release(m)


def l0_conv_phase(k):
    P, A, PS = k.P, k.A, k.PS
    m = A.mark()
    vec = A.alloc("vec", [128, 32], F32)
    dw = A.alloc("dw", [128, 4, 31], F32)
    diag = A.alloc("diag", [128, 4, 31, 128], BF16)
    ybuf = [A.alloc(f"ybuf{i}", [128, 4, 544], BF16) for i in range(2)]
    cv = A.alloc("cv", [128, 4, 512], F32)
    cb = A.alloc("cb", [128, 4, 512], BF16)
    sqv = A.alloc("sqv", [128, 4, 512], BF16)
    co = [A.alloc(f"co{i}", [128, 4, 512], BF16) for i in range(2)]
    mean = A.alloc("mean", [128, 512], F32)
    msq = A.alloc("msq", [128, 512], F32)
    r1 = A.alloc("r1", [128, 512], F32)
    r2 = A.alloc("r2", [128, 512], F32)
    rstd = A.alloc("rstd", [128, 512], F32)
    t1 = [A.alloc(f"t1{i}", [128, 512], F32) for i in range(2)]
    identb = k.cbf[:, 0, :]
    P.dma("sp", vec[:], k.inp["l0_vec"], writes=[("vec",)])
    P.dma("sp", dw[:], k.inp["l0_dw"], writes=[("dw",)])
    for ch in range(4):
        for j in range(31):
            P.dve(lambda e: e.tensor_scalar(out=diag[:, ch, j, :], in0=identb, scalar1=dw[:, ch, j:j + 1], scalar2=None, op0=ALU.mult),
                  reads=[("dw",), ("cbf",)], writes=[("diag", ch, j)])
    rot = PsRot(banks=(0, 1, 2, 3))
    for ti, (who, t0, T) in enumerate(TILES):
        s0, s1 = (0, NCTX) if who == 1 else (NCTX, TALL)
        lo = max(t0 - 15, s0)
        hi = min(t0 + T + 15, s1)
        yb = ybuf[ti % 2]
        toks = [("yb", ti % 2, x) for x in "LMR"]
        wr = [toks[1]]
        if lo > t0 - 15:
            P.pool(lambda e: e.memset(yb[:, :, 0:15], 0.0), writes=[toks[0]])
        else:
            wr.append(toks[0])
        if hi < t0 + T + 15:
            P.pool(lambda e: e.memset(yb[:, :, T + 15:T + 30], 0.0), writes=[toks[2]])
        else:
            wr.append(toks[2])
        P.dma("sp", yb[:, :, lo - (t0 - 15):hi - (t0 - 15)], k.yTd[:, :, lo:hi].rearrange("c p t -> p c t"), writes=wr)
        for ch in range(4):
            b = rot.next()

            def mm(e, ch=ch, b=b):
                for j in range(31):
                    e.matmul(PS[b][:, 0:T], lhsT=diag[:, ch, j, :], rhs=yb[:, ch, j:j + T], start=(j == 0), stop=(j == 30))
            P.pe(mm, reads=toks + [("diag", ch, j) for j in range(31)], writes=[("ps", b)])
            P.act(lambda e: e.activation(out=cv[:, ch, 0:T], in_=PS[b][:, 0:T], func=AF.Identity, bias=vec[:, 19 + ch:20 + ch], scale=1.0),
                  reads=[("ps", b), ("vec",)], writes=[("cv", ch)])
            P.pool(lambda e: e.tensor_copy(out=cb[:, ch, 0:T], in_=cv[:, ch, 0:T]), reads=[("cv", ch)], writes=[("cb", ch)])
            P.pool(lambda e: e.tensor_tensor(out=sqv[:, ch, 0:T], in0=cv[:, ch, 0:T], in1=cv[:, ch, 0:T], op=ALU.mult),
                   reads=[("cv", ch)], writes=[("sqv", ch)])
        mm_group(k, PS[6][:, 0:T], lambda kc: k.ones_bf, lambda kc: cb[:, kc, 0:T], 4, [("cb", c) for c in range(4)] + [("cbf",)], ("ps", 6))
        mm_group(k, PS[7][:, 0:T], lambda kc: k.ones_bf, lambda kc: sqv[:, kc, 0:T], 4, [("sqv", c) for c in range(4)] + [("cbf",)], ("ps", 7))
        P.dve(lambda e: e.tensor_scalar(out=mean[:, 0:T], in0=PS[6][:, 0:T], scalar1=1.0 / 512, scalar2=None, op0=ALU.mult),
              reads=[("ps", 6)], writes=[("mean",)])
        P.pool(lambda e: e.tensor_tensor(out=msq[:, 0:T], in0=mean[:, 0:T], in1=mean[:, 0:T], op=ALU.mult),
               reads=[("mean",)], writes=[("msq",)])
        P.dve(lambda e: e.scalar_tensor_tensor(out=r2[:, 0:T], in0=PS[7][:, 0:T], scalar=1.0 / 512, in1=msq[:, 0:T],
                                               op0=ALU.mult, op1=ALU.subtract),
              reads=[("ps", 7), ("msq",)], writes=[("r2",)])
        P.dve(lambda e: e.tensor_scalar(out=r1[:, 0:T], in0=r2[:, 0:T], scalar1=EPS, scalar2=None, op0=ALU.add),
              reads=[("r2",)], writes=[("r1",)])
        P.act(lambda e: e.activation(out=r2[:, 0:T], in_=r1[:, 0:T], func=AF.Sqrt), reads=[("r1",)], writes=[("r2",)])
        P.dve(lambda e: e.reciprocal(out=rstd[:, 0:T], in_=r2[:, 0:T]), reads=[("r2",)], writes=[("rstd",)])
        cob = co[ti % 2]
        for ch in range(4):
            tb = t1[ch % 2]
            P.dve(lambda e: e.tensor_tensor(out=tb[:, 0:T], in0=cv[:, ch, 0:T], in1=mean[:, 0:T], op=ALU.subtract),
                  reads=[("cv", ch), ("mean",)], writes=[("t1", ch % 2)])
            P.pool(lambda e: e.tensor_tensor(out=tb[:, 0:T], in0=tb[:, 0:T], in1=rstd[:, 0:T], op=ALU.mult),
                   reads=[("t1", ch % 2), ("rstd",)], writes=[("t1", ch % 2)])
            P.act(lambda e: e.activation(out=cob[:, ch, 0:T], in_=tb[:, 0:T], func=AF.Silu, bias=vec[:, 27 + ch:28 + ch],
                                         scale=vec[:, 23 + ch:24 + ch]),
                  reads=[("t1", ch % 2), ("vec",)], writes=[("co", ti % 2, ch)])
        P.dma("sp", k.attTd[4:8, :, t0:t0 + T].rearrange("c p t -> p c t"), cob[:, :, 0:T],
              reads=[("co", ti % 2, ch) for ch in range(4)], writes=[("attTd", "conv", ti)])
    A.release(m)


def mixout_phase(k, l, tiles):
    P, A, PS = k.P, k.A, k.PS
    m = A.mark()
    wo = A.alloc("wo", [128, 8, 8, 128], BF16)
    xt = [A.alloc(f"xt{i}", [128, 8, 512], F32) for i in range(2)]
    at = [A.alloc(f"at{i}", [128, 8, 512], BF16) for i in range(2)]
    name = f"l{l}_wo"
    for i in range(8):
        P.dma("sp", wo[:, i], k.wbf[name][i].rearrange("p (k n) -> p k n", k=8), reads=[("wbf", name, i)], writes=[("wo",)])
    rot = PsRot()
    for ti, (who, t0, T) in enumerate(tiles):
        xb = xt[ti % 2]
        ab = at[ti % 2]
        xtoks = [("xt", ti % 2, c) for c in range(8)]
        P.dma("sp", xb[:, :, 0:T], k.xTd[:, :, t0:t0 + T].rearrange("c p t -> p c t"), writes=xtoks)
        P.dma("sp", ab[:, :, 0:T], k.attTd[:, :, t0:t0 + T].rearrange("c p t -> p c t"), writes=[("at", ti % 2)])
        for d in range(8):
            b = rot.next()
            mm_group(k, PS[b][:, 0:T], lambda kc: wo[:, d, kc, :], lambda kc: ab[:, kc, 0:T], 8, [("wo",), ("at", ti % 2)], ("ps", b))
            P.dve(lambda e: e.scalar_tensor_tensor(out=xb[:, d, 0:T], in0=PS[b][:, 0:T], scalar=k.modv[:, l, who, 40 + d:41 + d],
                                                   in1=xb[:, d, 0:T], op0=ALU.mult, op1=ALU.add),
                  reads=[("ps", b), xtoks[d]], writes=[xtoks[d]])
        P.dma("sp", k.xTd[:, :, t0:t0 + T].rearrange("c p t -> p c t"), xb[:, :, 0:T], reads=xtoks, writes=[("xTd", t0)])
    A.release(m)


def l1_proj_phase(k):
    P, A, PS = k.P, k.A, k.PS
    l = 1
    m = A.mark()
    w = common_work(k, A)
    xt = [A.alloc(f"xt{i}", [128, 8, 512], F32) for i in range(2)]
    hT = A.alloc("hTm", [128, 8, 512], BF16)
    wA = A.alloc("wA", [128, 20, 8, 128], BF16)
    wv = A.alloc("wv", [128, 8, 640], BF16)
    vec = A.alloc("vec", [128, 8], F32)
    rope = [A.alloc(f"rope{i}", [128, 2, 512], F32) for i in range(2)]
    KdT = [A.alloc(f"KdT{i}", [128, 2, 512], BF16) for i in range(2)]
    nkT = [A.alloc(f"nkT{i}", [128, 4, 512], BF16) for i in range(2)]
    QgT = [A.alloc(f"QgT{i}", [128, 4, 512], BF16) for i in range(2)]
    nqT = [A.alloc(f"nqT{i}", [128, 4, 512], BF16) for i in range(2)]
    Vgt = [A.alloc(f"Vgt{i}", [128, 4, 128], BF16) for i in range(2)]
    NVt = [A.alloc(f"NVt{i}", [128, 8, 128], BF16) for i in range(2)]
    P.dma("sp", vec[:], k.inp["l1_vec"], writes=[("vec",)])
    for i in range(20):
        P.dma("sp", wA[:, i], k.wbf["l1_wA"][i].rearrange("p (k n) -> p k n", k=8), reads=[("wbf", "l1_wA", i)], writes=[("wA",)])
    P.dma("sp", wv[:], k.wbf["l1_wv"][0].rearrange("p (k n) -> p k n", k=8), reads=[("wbf", "l1_wv", 0)], writes=[("wv",)])
    for c0, c1 in ((2, 4), (5, 6)):
        P.dve(lambda e: e.tensor_scalar(out=vec[:, c0:c1], in0=vec[:, c0:c1], scalar1=0.125, scalar2=None, op0=ALU.mult),
              reads=[("vec",)], writes=[("vec",)])
    for i in range(2):
        P.dve(lambda e: e.memset(Vgt[i][:], 1.0), writes=[("Vgt", i), (("Vgt", i), 1)])
        P.dve(lambda e: e.memset(NVt[i][:], 1.0), writes=[("NVt", i), (("NVt", i), 1)])
    rot = PsRot()
    cnt = {"sq": 0, "v": 0}
    Rall = slice(0, 128)
    for ti, (who, t0, T) in enumerate(TILES):
        lat = (who == 0)
        xb = xt[ti % 2]
        xtoks = [("xtm", ti % 2, c) for c in range(8)]
        load_x_norm(k, l, who, t0, T, xb, xtoks, hT, w)
        htoks = [("hTm", c) for c in range(8)]
        rp = rope[ti % 2]
        if lat:
            P.dma("sp", rp[:, :, 0:T], k.inp["ropeA"][:, :, t0 - NCTX:t0 - NCTX + T], writes=[("rope", ti % 2)])

        def projA(bidx):
            b = rot.next()
            mm_group(k, PS[b][:, 0:T], lambda kc: wA[:, bidx, kc, :], lambda kc: hT[:, kc, 0:T], 8, [("wA",)] + htoks, ("ps", b))
            return b

        def normed_chunk(bmain, bswap, gcol, out_ap, otok, do_rope):
            b = projA(bmain)
            bsw = projA(bswap) if do_rope else None
            sb, stok = next_sqs(w)
            P.act(lambda e: e.activation(out=sb[:, 0:T], in_=PS[b][:, 0:T], func=AF.Square), reads=[("ps", b)], writes=[stok])
            st = next_stat(w)
            mm_group(k, PS[st["bank"]][:, 0:T], lambda kc: k.bd64_bf, lambda kc: sb[:, 0:T], 1, [stok, ("cbf",)], ("ps", st["bank"]))
            stats_rstd(k, st, Rall, T, 1.0 / 64)
            if do_rope:
                rope_apply(k, w, b, bsw, Rall, T, vec[:, gcol:gcol + 1], vec[:, gcol + 1:gcol + 2], st, rp,
                           ("rope", ti % 2), out_ap, [otok])
            else:
                P.dve(lambda e: e.scalar_tensor_tensor(out=out_ap, in0=PS[b][:, 0:T], scalar=vec[:, gcol:gcol + 1],
                                                       in1=st["rstd"][:, 0:T], op0=ALU.mult, op1=ALU.mult),
                      reads=[("ps", b), ("rstd", st["i"]), ("vec",)], writes=[otok])

        Kb = KdT[ti % 2]
        for hk in range(2):
            normed_chunk(hk, 2 + hk, 0, Kb[:, hk, 0:T], ("KdT", ti % 2, hk), lat)
        P.dma("sp", k.KdupTd[:, :, t0:t0 + T].rearrange("h p t -> p h t"), Kb[:, :, 0:T],
              reads=[("KdT", ti % 2, hk) for hk in range(2)], writes=[("KdupTd", ti)])
        nb = nkT[ti % 2]
        for c in range(4):
            normed_chunk(4 + c, None, 4, nb[:, c, 0:T], ("nkT", ti % 2, c), False)
        P.dma("sp", k.nkTd[:, :, t0:t0 + T].rearrange("c p t -> p c t"), nb[:, :, 0:T],
              reads=[("nkT", ti % 2, c) for c in range(4)], writes=[("nkTd", ti)])
        for st in range(T // 128):
            b1 = rot.next()
            mm_group(k, PS[b1][:, 0:128], lambda kc: hT[:, kc, st * 128:(st + 1) * 128], lambda kc: wv[:, kc, 0:128], 8,
                     [("wv",)] + htoks, ("ps", b1))
            b2 = rot.next()
            mm_group(k, PS[b2][:, 0:512], lambda kc: hT[:, kc, st * 128:(st + 1) * 128], lambda kc: wv[:, kc, 128:640], 8,
                     [("wv",)] + htoks, ("ps", b2))
            vi = cnt["v"] % 2
            cnt["v"] += 1
            vg, nv = Vgt[vi], NVt[vi]
            vgd = vg[:].rearrange("p (hk par) d -> p hk par d", par=2)
            s1 = PS[b1][:, 0:128].rearrange("p (hk d) -> p hk d", d=64)
            P.dve(lambda e: e.tensor_copy(out=vgd[:, :, 0, 0:64], in_=s1), reads=[("ps", b1)], writes=[("Vgt", vi)])
            P.dve(lambda e: e.tensor_copy(out=vgd[:, :, 1, 64:128], in_=s1), reads=[("ps", b1)], writes=[(("Vgt", vi), 1)])
            nvd = nv[:].rearrange("p (h two) d -> p h two d", two=2)
            s2 = PS[b2][:, 0:512].rearrange("p (h two d) -> p h two d", two=2, d=64)
            P.dve(lambda e: e.tensor_copy(out=nvd[:, :, 0, 0:64], in_=s2[:, :, 0, :]), reads=[("ps", b2)], writes=[("NVt", vi)])
            P.dve(lambda e: e.tensor_copy(out=nvd[:, :, 1, 64:128], in_=s2[:, :, 1, :]), reads=[("ps", b2)],
                  writes=[(("NVt", vi), 1)])
            chunk = (t0 + st * 128) // 128
            P.dma("sp", k.VgD[:, :, chunk, :].rearrange("v p d -> p v d"), vg[:], reads=[("Vgt", vi), (("Vgt", vi), 1)],
                  writes=[("VgD", chunk)])
            P.dma("sp", k.NVd[:, :, chunk, :].rearrange("h p d -> p h d"), nv[:], reads=[("NVt", vi), (("NVt", vi), 1)],
                  writes=[("NVd", chunk)])
        if lat:
            qb = QgT[ti % 2]
            for c in range(4):
                normed_chunk(8 + c, 12 + c, 2, qb[:, c, 0:T], ("QgT", ti % 2, c), True)
            P.dma("sp", k.QgTd[:, :, t0:t0 + T].rearrange("c p t -> p c t"), qb[:, :, 0:T],
                  reads=[("QgT", ti % 2, c) for c in range(4)], writes=[("QgTd", ti)])
            nq = nqT[ti % 2]
            for c in range(4):
                normed_chunk(16 + c, None, 5, nq[:, c, 0:T], ("nqT", ti % 2, c), False)
            P.dma("sp", k.nqTd[:, :, t0:t0 + T].rearrange("c p t -> p c t"), nq[:, :, 0:T],
                  reads=[("nqT", ti % 2, c) for c in range(4)], writes=[("nqTd", ti)])
    A.release(m)


def l1_na_phase(k):
    P, A, PS = k.P, k.A, k.PS
    m = A.mark()
    nk = A.alloc("nk", [128, 4, TALL], BF16)
    NV = A.alloc("NV", [128, 8, 34, 128], BF16)
    nab = A.alloc("nab", [128, NSIG * 8, 128], BF16)
    nq = [[A.alloc(f"nq{i}{e_}", [128, 4, 128], BF16) for e_ in range(2)] for i in range(2)]
    PT = [A.alloc(f"PT{i}", [128, 8, 128], BF16) for i in range(2)]
    for i in range(2):
        for e_ in range(2):
            P.dve(lambda e: e.memset(nq[i][e_][:], 0.0), writes=[("nq", i, e_)])
    rec = A.alloc("rec", [128, 128], F32)
    nat = [A.alloc(f"nat{i}", [128, 4, 128], BF16) for i in range(2)]
    identb = k.cbf[:, 0, :]
    P.dma("sp", nk[:], k.nkTd.rearrange("c p t -> p c t"), writes=[("nk",)])
    for h in range(8):
        P.dma("sp", NV[:, h], k.NVd[h], writes=[("NV", h)])
    for g in range(8):
        n0, n1 = g * NSIG, (g + 1) * NSIG
        P.dma("sp", nab[:, n0:n1, :], k.wbf["l1_nab"][n0:n1].rearrange("n p q -> p n q"),
              reads=[("wbf", "l1_nab", i) for i in range(n0, n1)], writes=[("nab", g)])
    nabtoks = [("nab", g) for g in range(8)]
    hc = 0
    for i in range(32):
        t0 = NCTX + i * 128
        nqb = nq[i % 2]
        natb = nat[i % 2]
        for e_ in range(2):
            P.dma("sp", nqb[e_][e_ * 64:(e_ + 1) * 64], k.nqTd[:, e_ * 64:(e_ + 1) * 64, t0:t0 + 128].rearrange("c p t -> p c t"),
                  writes=[("nq", i % 2, e_)])
        chunks = [(0, None), (1, None)] + [(2 + kc, NA_MAP[(i, kc)]) for kc in NA_CHUNKS[i]]
        n = len(chunks)
        for h in range(8):
            ch, off = h // 2, (h % 2) * 64
            sb0 = 2 * (hc % 2)
            acc = 4 + (hc % 2)
            pt = PT[hc % 2]
            pttok = ("PTn", hc % 2)
            hc += 1

            def smm(e):
                for idx, (c, sig) in enumerate(chunks):
                    o = PS[sb0 + idx // 4][:, (idx % 4) * 128:(idx % 4 + 1) * 128]
                    e.matmul(o, lhsT=nk[:, ch, c * 128:(c + 1) * 128], rhs=nqb[h % 2][:, ch, :],
                             start=True, stop=(sig is None))
                    if sig is not None:
                        e.matmul(o, lhsT=identb, rhs=nab[:, sig * 8 + h, :], start=False, stop=True)
            P.pe(smm, reads=[("nk",), ("nq", i % 2, h % 2), ("cbf",)] + nabtoks, writes=[("ps", sb0), ("ps", sb0 + 1)])
            P.act(lambda e: e.activation(out=pt[:, 0:4, :], in_=PS[sb0][:, 0:512].rearrange("p (c q) -> p c q", q=128), func=AF.Exp),
                  reads=[("ps", sb0)], writes=[(pttok, 0)])
            P.act(lambda e: e.activation(out=pt[:, 4:n, :], in_=PS[sb0 + 1][:, 0:(n - 4) * 128].rearrange("p (c q) -> p c q", q=128),
                                         func=AF.Exp),
                  reads=[("ps", sb0 + 1)], writes=[(pttok, 1)])

            def pv(e):
                for idx, (c, sig) in enumerate(chunks):
                    e.matmul(PS[acc][:, 0:128], lhsT=NV[:, h, c, :], rhs=pt[:, idx, :], start=(idx == 0), stop=(idx == n - 1))
            P.pe(pv, reads=[("NV", h), (pttok, 0), (pttok, 1)], writes=[("ps", acc)])
            nlo, dlo = (0, 64) if h % 2 == 0 else (64, 0)
            P.dve(lambda e: e.reciprocal(out=rec[nlo:nlo + 64, :], in_=PS[acc][dlo:dlo + 64, 0:128]),
                  reads=[("ps", acc)], writes=[("rec", h % 2)])
            P.dve(lambda e: e.tensor_tensor(out=natb[nlo:nlo + 64, ch, :], in0=PS[acc][nlo:nlo + 64, 0:128], in1=rec[nlo:nlo + 64, :],
                                            op=ALU.mult),
                  reads=[("ps", acc), ("rec", h % 2)], writes=[("nat", i % 2, h)])
        P.dma("sp", k.attTd[4:8, :, t0:t0 + 128].rearrange("c p t -> p c t"), natb[:],
              reads=[("nat", i % 2, h) for h in range(8)], writes=[("attTd", "na", i)])
    A.release(m)
```

```python
import numpy as np
import concourse.bass as bass
import concourse.mybir as mybir
from concourse.bass_utils import run_bass_kernel_spmd

F32 = mybir.dt.float32
BF16 = mybir.dt.bfloat16
AF = mybir.ActivationFunctionType
ALU = mybir.AluOpType

D = 1024
S = 4096
NCTX = 256
TALL = S + NCTX
DFF = 2816
NFC = DFF // 128
EPS = 1e-6
GRID_W = 64
NDMASEM = 40
RING = {"sp": (0, 28), "pool": (28, 12)}
TILES = [(1, 0, 256)] + [(0, 256 + 512 * i, 512) for i in range(8)]


class Op:
    __slots__ = ("stream", "kind", "fn", "idx", "event", "signal", "cclock", "waits", "slot")


class _Rec:
    def __init__(self):
        self.calls = []

    def __getattr__(self, name):
        def f(*a, **kw):
            self.calls.append((name, a, kw))
            return None
        return f


def _replay(calls):
    def fn(e):
        ins = None
        for name, a, kw in calls:
            ins = getattr(e, name)(*a, **kw)
        return ins
    return fn


class Prog:
    STREAMS = ("pe", "act", "dve", "pool", "sp")

    def __init__(self, nc):
        self.nc = nc
        self.ops = {s: [] for s in self.STREAMS}
        self.clock = {s: {} for s in self.STREAMS}
        self.lastw = {}
        self.readers = {}
        self.ncomp = {s: 0 for s in self.STREAMS}
        self.pending = {}
        self.dma_slot_last = [None] * NDMASEM
        self.dma_slot_cnt = [0] * NDMASEM
        self.dma_rr = {"sp": 0, "pool": 0}
        self.dmas_since_barrier = []
        self.bg_next = False
        self.bg = []
        self.last_comp = {}
        self.nops = 0
        self.maxops = None

    def _add(self, stream, kind, fn, reads, writes):
        if self.maxops is not None and self.nops >= self.maxops and fn is not None:
            return None
        op = Op()
        op.stream = stream
        op.kind = kind
        if fn is not None:
            rec = _Rec()
            fn(rec)
            fn = _replay(rec.calls)
        op.fn = fn
        op.signal = False
        op.slot = None
        self.nops += 1
        deps = []
        for r in reads:
            w = self.lastw.get(r)
            if w is not None:
                deps.append((w, True))
        for w_ in writes:
            lw = self.lastw.get(w_)
            if lw is not None:
                deps.append((lw, False))
            for rd in self.readers.get(w_, ()):
                deps.append((rd, False))
        for r in reads:
            self.readers.setdefault(r, []).append(op)
        for w_ in writes:
            self.readers[w_] = []
            self.lastw[w_] = op
        pend = self.pending.pop(stream, None)
        if pend:
            deps.extend((p, True) for p in pend)
        if kind == "d":
            base, cnt_ = RING[stream]
            slot = base + self.dma_rr[stream]
            self.dma_rr[stream] = (self.dma_rr[stream] + 1) % cnt_
            prev = self.dma_slot_last[slot]
            if prev is not None:
                deps.append((prev, True))
            self.dma_slot_cnt[slot] += 1
            self.dma_slot_last[slot] = op
            op.slot = slot
            op.event = (("d", slot), 16 * self.dma_slot_cnt[slot])
            if self.bg_next:
                self.bg.append(op)
            else:
                self.dmas_since_barrier.append(op)
        else:
            self.ncomp[stream] += 1
            op.event = (("e", stream), self.ncomp[stream])
            self.last_comp[stream] = op
        clk = self.clock[stream]
        waits = {}
        for d, raw in deps:
            if d is op:
                continue
            if d.stream == stream and d.kind == "c":
                if stream == "pe" or not raw:
                    continue
            key, val = d.event
            if clk.get(key, 0) >= val:
                continue
            if waits.get(key, (0,))[0] < val:
                waits[key] = (val, d)
            for k, v in d.cclock.items():
                if clk.get(k, 0) < v:
                    clk[k] = v
        for key, (val, d) in waits.items():
            d.signal = True
        op.waits = {k: v[0] for k, v in waits.items()}
        cc = dict(clk)
        cc[op.event[0]] = op.event[1]
        op.cclock = cc
        if kind == "c" and stream != "pe":
            pass
        self.ops[stream].append(op)
        return op

    def pe(self, fn, reads=(), writes=()):
        return self._add("pe", "c", fn, reads, writes)

    def act(self, fn, reads=(), writes=()):
        return self._add("act", "c", fn, reads, writes)

    def dve(self, fn, reads=(), writes=()):
        return self._add("dve", "c", fn, reads, writes)

    def pool(self, fn, reads=(), writes=()):
        return self._add("pool", "c", fn, reads, writes)

    def dma(self, q, out, in_, reads=(), writes=()):
        return self._add(q, "d", lambda e: e.dma_start(out=out, in_=in_), reads, writes)

    def barrier(self):
        lst = list(self.last_comp.values()) + self.dmas_since_barrier
        self.dmas_since_barrier = []
        for s in self.STREAMS:
            self.pending[s] = list(lst)

    def emit(self):
        nc = self.nc
        self.dmas_since_barrier = self.dmas_since_barrier + self.bg
        self.barrier()
        fin = self._add("sp", "c", None, (), ())
        esem = {s: nc.alloc_semaphore("es_" + s) for s in self.STREAMS}
        dsem = [nc.alloc_semaphore(f"ds{i}") for i in range(NDMASEM)]
        sigcount = {}
        for s in self.STREAMS:
            cnt = 0
            m = {}
            for o in self.ops[s]:
                if o.kind == "c":
                    if o.signal:
                        cnt += 1
                    m[o.event[1]] = (cnt, o.signal)
            sigcount[s] = m

        def resolve(key, val):
            if key[0] == "d":
                return dsem[key[1]], val
            cnt, sig = sigcount[key[1]][val]
            assert sig
            return esem[key[1]], cnt

        with nc.Block() as block:
            decos = {"pe": block.tensor, "act": block.scalar, "dve": block.vector,
                     "pool": block.gpsimd, "sp": block.sync}
            for s in self.STREAMS:
                ops = self.ops[s]

                def body(eng, ops=ops, s=s):
                    for o in ops:
                        for key, val in o.waits.items():
                            sem, v = resolve(key, val)
                            eng.wait_ge(sem, v)
                        if o.fn is None:
                            continue
                        ins = o.fn(eng)
                        if o.kind == "d":
                            ins.then_inc(dsem[o.slot], 16)
                        elif o.signal:
                            ins.then_inc(esem[s], 1)
                decos[s](body)


class Arena:
    def __init__(self, nc, limit=229376):
        self.nc = nc
        self.off = 16640
        self.limit = limit
        self.n = 0

    def alloc(self, name, shape, dtype):
        esz = 4 if dtype == F32 else 2
        sz = esz
        for d in shape[1:]:
            sz *= d
        sz = (sz + 63) // 64 * 64
        assert self.off + sz <= self.limit, (name, self.off, sz)
        self.n += 1
        t = self.nc.alloc_sbuf_tensor_at(f"{name}_{self.n}", list(shape), dtype, offset=self.off)
        self.off += sz
        return t

    def mark(self):
        return self.off

    def release(self, m):
        self.off = m


def kmajor(w, ncols_pad=None):
    K, N = w.shape
    return np.ascontiguousarray(w.reshape(K // 128, 128, N).transpose(1, 0, 2))


def pvec(v):
    return np.ascontiguousarray(v.reshape(-1, 128).T)


class K:
    pass


def declare_inputs(k, nc):
    k.inp = {}
    k.inshape = {}

    def din(name, shape):
        k.inp[name] = nc.dram_tensor(name, list(shape), F32, kind="ExternalInput").ap()
        k.inshape[name] = tuple(shape)

    din("x", [S, D])
    din("ctx", [NCTX, D])
    din("cT", [128, 8, 2])
    din("cmat", [128, 4, 128])
    for l in range(2):
        din(f"l{l}_mod_w", [18, 128, 8 * 512])
        din(f"l{l}_mod_bT", [128, 72])
        din(f"l{l}_ng", [128, 3, 8])
        for w in (1, 2):
            din(f"l{l}_f{w}_win", [NFC, 128, 8 * 256])
            din(f"l{l}_f{w}_wout", [8, 128, NFC * 128])


def host_inputs(inputs, b):
    m = {}
    m["x"] = np.ascontiguousarray(inputs["x"][b])
    m["ctx"] = np.ascontiguousarray(inputs["ctx"][b])
    cT = np.stack([pvec(inputs["c"][b]), pvec(inputs["c_ctx"])], axis=-1)
    m["cT"] = np.ascontiguousarray(cT.astype(np.float32))
    return m


_SHARED = {}


def host_shared(inputs):
    m = {}
    cm = np.zeros((128, 4, 128), np.float32)
    cm[:, 0, :] = np.eye(128, dtype=np.float32)
    cm[:, 1, :] = 1.0
    for g in range(2):
        cm[g * 64:(g + 1) * 64, 2, g * 64:(g + 1) * 64] = 1.0
    cm[0:64, 3, 0:64] = 1.0
    cm[64:96, 3, 64:96] = 1.0
    m["cmat"] = cm
    for l in range(2):
        p = f"l{l}_"
        mw = inputs[p + "mod_w"]
        m[p + "mod_w"] = np.ascontiguousarray(
            mw.reshape(8, 128, 18, 512).transpose(2, 1, 0, 3).reshape(18, 128, 8 * 512))
        m[p + "mod_bT"] = pvec(inputs[p + "mod_b"])
        m[p + "ng"] = np.ascontiguousarray(np.stack(
            [pvec(inputs[p + "ffn1_norm"]), pvec(inputs[p + "mix_norm"]), pvec(inputs[p + "ffn2_norm"])], axis=1))
        for w in (1, 2):
            wi = inputs[p + f"ffn{w}_w_in"]
            g = wi[:, :DFF].reshape(8, 128, NFC, 128)
            u = wi[:, DFF:].reshape(8, 128, NFC, 128)
            gu = np.stack([g, u], axis=3)
            m[p + f"f{w}_win"] = np.ascontiguousarray(gu.transpose(2, 1, 0, 3, 4).reshape(NFC, 128, 8 * 256))
            wo = inputs[p + f"ffn{w}_w_out"]
            m[p + f"f{w}_wout"] = np.ascontiguousarray(
                wo.reshape(NFC, 128, 8, 128).transpose(2, 1, 0, 3).reshape(8, 128, NFC * 128))
    m.update(host_mixer(inputs))
    return m


def build(stop_after=None, debug=False, skip=(), maxops=None):
    nc = bass.Bass("TRN2", target_bir_lowering=False)
    k = K()
    k.nc = nc
    k.debug = debug
    P = Prog(nc)
    k.maxops = maxops
    k.P = P
    A = Arena(nc)
    k.A = A
    declare_inputs(k, nc)
    k.out = nc.dram_tensor("out", [S, D], F32, kind="ExternalOutput").ap()
    k.dbg_outs = []
    skind = "ExternalOutput" if debug else "Internal"
    k.xTd = nc.dram_tensor("xTd", [8, 128, TALL], F32, kind=skind).ap()
    k.wbf = {}

    def precast(name):
        shape = k.inshape[name]
        t = nc.dram_tensor(name + "_bf", list(shape), BF16, kind="Internal").ap()
        k.wbf[name] = t
        P.bg_next = True
        for i in range(shape[0]):
            P.dma("pool", t[i], k.inp[name][i], reads=(), writes=[("wbf", name, i)])
        P.bg_next = False

    k.PS = [nc.alloc_psum_tensor(f"psb{i}", [128, 512], F32) for i in range(8)]

    k.ident = A.alloc("ident", [128, 128], F32)
    k.cbf = A.alloc("cbf", [128, 4, 128], BF16)
    k.modv = A.alloc("modv", [128, 2, 2, 72], F32)
    k.der = A.alloc("der", [128, 2, 2, 5, 8], F32)
    k.ng = A.alloc("ng", [128, 2, 3, 8], F32)
    k.epsv = A.alloc("epsv", [128, 1], F32)
    P.dve(lambda e: e.memset(k.epsv[:], EPS), writes=[("epsv",)])
    P.dma("sp", k.ident[:], k.inp["cmat"][:, 0, :], writes=[("ident",)])
    P.dma("pool", k.cbf[:], k.inp["cmat"], writes=[("cbf",)])
    for l in range(2):
        P.dma("sp", k.ng[:, l], k.inp[f"l{l}_ng"], writes=[("ng", l)])
    k.ones_bf = k.cbf[:, 1, :]
    k.bd64_bf = k.cbf[:, 2, :]

    for l in range(2):
        for w in (1, 2):
            precast(f"l{l}_f{w}_win")
            precast(f"l{l}_f{w}_wout")

    declare_mixer_inputs(k, nc)
    for name in ("l0_wA", "l0_wkn", "l0_wv", "l0_wq", "l0_wo", "l1_wo", "l1_wA", "l1_wv", "l1_nab"):
        precast(name)

    def dscr(name, shape, dt=BF16):
        return nc.dram_tensor(name, list(shape), dt, kind=skind if name in ("attTd",) else "Internal").ap()
    k.yTd = dscr("yTd", [4, 128, TALL])
    k.KTd = dscr("KTd", [8, 96, TALL])
    k.QTd = dscr("QTd", [8, 96, TALL])
    k.Vd = dscr("Vd", [8, 128, 34, 128])
    k.attTd = dscr("attTd", [8, 128, TALL])
    k.KdupTd = dscr("KdupTd", [2, 128, TALL])
    k.QgTd = dscr("QgTd", [4, 128, TALL])
    k.nkTd = dscr("nkTd", [4, 128, TALL])
    k.nqTd = dscr("nqTd", [4, 128, TALL])
    k.VgD = dscr("VgD", [4, 128, 34, 128])
    k.NVd = dscr("NVd", [8, 128, 34, 128])

    phases = [
        ("mod", lambda: setup_mod(k)),
        ("l0f1", lambda: ffn_phase(k, 0, 1, TILES, src="tok", dst="xT")),
        ("l0proj", lambda: l0_proj_phase(k)),
        ("l0att", lambda: attn_phase(k, "mla")),
        ("l0conv", lambda: l0_conv_phase(k)),
        ("l0mix", lambda: mixout_phase(k, 0, TILES)),
        ("l0", lambda: ffn_phase(k, 0, 2, TILES, src="xT", dst="xT")),
        ("l1f1", lambda: ffn_phase(k, 1, 1, TILES, src="xT", dst="xT")),
        ("l1proj", lambda: l1_proj_phase(k)),
        ("l1gqa", lambda: attn_phase(k, "gqa")),
        ("l1na", lambda: l1_na_phase(k)),
        ("l1mix", lambda: mixout_phase(k, 1, TILES[1:])),
        ("full", lambda: ffn_phase(k, 1, 2, TILES[1:], src="xT", dst="tok")),
    ]
    for name, fn in phases:
        if name in skip:
            continue
        n0 = P.nops
        if maxops is not None and name == stop_after:
            P.maxops = P.nops + maxops
        fn()
        P.barrier()
        if debug:
            print("phase", name, "ops", n0, P.nops, flush=True)
        if stop_after == name:
            break
    return finish(k)


def finish(k):
    k.P.emit()
    return k.nc


def setup_mod(k):
    P, A, nc = k.P, k.A, k.nc
    m = A.mark()
    cT = A.alloc("cT", [128, 8, 2], F32)
    sc = A.alloc("sc", [128, 8, 2], F32)
    mb = A.alloc("mb", [128, 72], F32)
    wt = [A.alloc(f"mw{i}", [128, 8, 512], F32) for i in range(2)]
    P.dma("sp", cT[:], k.inp["cT"], writes=[("cT",)])
    P.act(lambda e: e.activation(out=sc[:], in_=cT[:], func=AF.Silu), reads=[("cT",)], writes=[("sc",)])
    n = 0
    for l in range(2):
        P.dma("sp", mb[:], k.inp[f"l{l}_mod_bT"], writes=[("mb",)])
        ps = k.PS[l]
        for blk in range(18):
            w = wt[n % 2]
            wtok = ("mw", n % 2)
            n += 1
            P.dma("sp", w[:], k.inp[f"l{l}_mod_w"][blk].rearrange("p (k n) -> p k n", k=8), writes=[wtok])

            def mm(e, w=w, blk=blk, ps=ps):
                ins = None
                for nn in range(4):
                    j = blk * 4 + nn
                    for kc in range(8):
                        ins = e.matmul(ps[:, 2 * j:2 * j + 2], lhsT=w[:, kc, nn * 128:(nn + 1) * 128],
                                       rhs=sc[:, kc, :], start=(kc == 0), stop=(kc == 7))
                return ins
            P.pe(mm, reads=[wtok, ("sc",)], writes=[("ps", l)])
        for who in range(2):
            src = ps[:, 0:144].rearrange("p (j w) -> p j w", w=2)[:, :, who]
            P.dve(lambda e, src=src, l=l, who=who: e.tensor_tensor(out=k.modv[:, l, who, :], in0=src, in1=mb[:], op=ALU.add),
                  reads=[("ps", l), ("mb",)], writes=[("modv", l, who)])
        for who in range(2):
            for gi, mi in enumerate((1, 4, 7)):
                P.dve(lambda e, l=l, who=who, gi=gi, mi=mi: e.scalar_tensor_tensor(
                    out=k.der[:, l, who, gi, :], in0=k.modv[:, l, who, mi * 8:mi * 8 + 8], scalar=1.0,
                    in1=k.ng[:, l, gi, :], op0=ALU.add, op1=ALU.mult),
                    reads=[("modv", l, who), ("ng", l)], writes=[("der", l, who, gi)])
            for gi, mi in ((3, 2), (4, 8)):
                P.dve(lambda e, l=l, who=who, gi=gi, mi=mi: e.tensor_scalar(
                    out=k.der[:, l, who, gi, :], in0=k.modv[:, l, who, mi * 8:mi * 8 + 8], scalar1=0.5,
                    scalar2=None, op0=ALU.mult),
                    reads=[("modv", l, who)], writes=[("der", l, who, gi)])
    if k.debug:
        dm = nc.dram_tensor("dbg_modv", [128, 2 * 2 * 72], F32, kind="ExternalOutput").ap()
        P.dma("sp", dm, k.modv[:].rearrange("p a b c -> p (a b c)"),
              reads=[("modv", l, w) for l in range(2) for w in range(2)], writes=[("dbg_modv",)])
    A.release(m)


def rms_norm_mod(k, xb, xtoks, T, gp, sh, hT, htok, tmp, r1, r2, rstd, sq):
    P = k.P
    P.act(lambda e: e.activation(out=sq[:, :, 0:T], in_=xb[:, :, 0:T], func=AF.Square),
          reads=xtoks, writes=[("sq", c) for c in range(8)])

    def mm(e):
        ins = None
        for c in range(8):
            ins = e.matmul(k.PS[6][:, 0:T], lhsT=k.ones_bf, rhs=sq[:, c, 0:T], start=(c == 0), stop=(c == 7))
        return ins
    P.pe(mm, reads=[("sq", c) for c in range(8)] + [("cbf",)], writes=[("ps", 6)])
    P.act(lambda e: e.activation(out=r2[:, 0:T], in_=k.PS[6][:, 0:T], func=AF.Sqrt, bias=k.epsv[:, 0:1], scale=1.0 / D),
          reads=[("ps", 6), ("epsv",)], writes=[("r2",)])
    P.dve(lambda e: e.reciprocal(out=rstd[:, 0:T], in_=r2[:, 0:T]), reads=[("r2",)], writes=[("rstd",)])
    for c in range(8):
        tb = tmp[c % 2]
        P.dve(lambda e, c=c, tb=tb: e.scalar_tensor_tensor(out=tb[:, 0:T], in0=xb[:, c, 0:T], scalar=gp[:, c:c + 1],
                                                           in1=rstd[:, 0:T], op0=ALU.mult, op1=ALU.mult),
              reads=[xtoks[c], ("rstd",)], writes=[("tmp", c % 2)])
        P.act(lambda e, c=c, tb=tb: e.activation(out=hT[:, c, 0:T], in_=tb[:, 0:T], func=AF.Identity,
                                                 bias=sh[:, c:c + 1], scale=1.0),
              reads=[("tmp", c % 2)], writes=[(htok, c)])


def ffn_phase(k, l, which, tiles, src, dst):
    P, A, nc = k.P, k.A, k.nc
    PS = k.PS
    m = A.mark()
    xt = [A.alloc(f"xt{i}", [128, 8, 512], F32) for i in range(3)]
    hT = [A.alloc(f"hT{i}", [128, 8, 512], BF16) for i in range(2)]
    actT = [A.alloc(f"actT{i}", [128, NFC, 512], BF16) for i in range(2)]
    sq = A.alloc("sq", [128, 8, 512], BF16)
    win = [A.alloc(f"win{i}", [128, 8, 256], BF16) for i in range(3)]
    wout = [A.alloc(f"wout{i}", [128, NFC, 128], BF16) for i in range(2)]
    r1 = A.alloc("r1", [128, 512], F32)
    r2 = A.alloc("r2", [128, 512], F32)
    rstd = A.alloc("rstd", [128, 512], F32)
    tmp = [A.alloc(f"tmp{i}", [128, 512], F32) for i in range(2)]
    sg = [A.alloc(f"sg{i}", [128, 512], F32) for i in range(3)]
    tokb = [A.alloc(f"tokb{i}", [128, 1024], F32) for i in range(2)]
    gi = 0 if which == 1 else 2
    shi = 0 if which == 1 else 6
    hgi = 3 if which == 1 else 4
    wname_in = f"l{l}_f{which}_win"
    wname_out = f"l{l}_f{which}_wout"
    groups = [tiles[i:i + 2] for i in range(0, len(tiles), 2)]
    tk = 0
    wi = 0
    wo = 0
    guc = 0
    yc = 0
    tbc = 0
    for grp in groups:
        bufs = []
        for s, (who, t0, T) in enumerate(grp):
            b = tk % 3
            tk += 1
            bufs.append(b)
            xb = xt[b]
            xtoks = [("xt", b, c) for c in range(8)]
            if src == "xT":
                P.dma("sp", xb[:, :, 0:T], k.xTd[:, :, t0:t0 + T].rearrange("c p t -> p c t"), writes=xtoks)
            else:
                srcap = k.inp["ctx"] if who == 1 else k.inp["x"]
                r0 = t0 if who == 1 else t0 - NCTX
                for st in range(T // 128):
                    tb = tokb[tbc % 2]
                    ttok = ("tokb", tbc % 2)
                    tbc += 1
                    P.dma("sp", tb[:], srcap[r0 + st * 128:r0 + (st + 1) * 128, :], writes=[ttok])
                    for half in range(2):
                        bank = PS[(guc % 4)]
                        btok = ("ps", guc % 4)
                        guc += 1

                        def tr(e, tb=tb, half=half, bank=bank):
                            ins = None
                            for cc in range(4):
                                c = half * 4 + cc
                                ins = e.transpose(bank[:, cc * 128:(cc + 1) * 128], tb[:, c * 128:(c + 1) * 128], k.ident[:])
                            return ins
                        P.pe(tr, reads=[ttok, ("ident",)], writes=[btok])
                        P.act(lambda e, xb=xb, half=half, bank=bank, st=st: e.activation(
                            out=xb[:, half * 4:half * 4 + 4, st * 128:(st + 1) * 128],
                            in_=bank[:].rearrange("p (c t) -> p c t", c=4), func=AF.Copy),
                            reads=[btok], writes=xtoks[half * 4:half * 4 + 4])
            rms_norm_mod(k, xb, xtoks, T, k.der[:, l, who, gi, :], k.modv[:, l, who, shi * 8:shi * 8 + 8],
                         hT[s], ("hT", s), tmp, r1, r2, rstd, sq)
        for j in range(NFC):
            wb = win[wi % 3]
            wtok = ("win", wi % 3)
            wi += 1
            P.dma("sp", wb[:], k.wbf[wname_in][j].rearrange("p (k n) -> p k n", k=8),
                  reads=[("wbf", wname_in, j)], writes=[wtok])
            for s, (who, t0, T) in enumerate(grp):
                bg = (guc % 3) * 2
                guc += 1
                htoks = [(("hT", s), c) for c in range(8)]

                def mmg(e, wb=wb, s=s, T=T, bank=PS[bg], off=0):
                    ins = None
                    for kc in range(8):
                        ins = e.matmul(bank[:, 0:T], lhsT=wb[:, kc, off:off + 128], rhs=hT[s][:, kc, 0:T],
                                       start=(kc == 0), stop=(kc == 7))
                    return ins
                P.pe(mmg, reads=[wtok] + htoks, writes=[("ps", bg)])
                P.pe(lambda e, wb=wb, s=s, T=T, bank=PS[bg + 1], f=mmg: f(e, wb, s, T, bank, 128),
                     reads=[wtok] + htoks, writes=[("ps", bg + 1)])
                q = guc % 3
                P.act(lambda e, q=q, bg=bg, T=T: e.activation(out=sg[q][:, 0:T], in_=PS[bg][:, 0:T], func=AF.Silu),
                      reads=[("ps", bg)], writes=[("sg", q)])
                P.dve(lambda e, q=q, bg=bg, T=T, s=s, j=j: e.tensor_tensor(
                    out=actT[s][:, j, 0:T], in0=PS[bg + 1][:, 0:T], in1=sg[q][:, 0:T], op=ALU.mult),
                    reads=[("ps", bg + 1), ("sg", q)], writes=[("actT", s, j)])
        for d in range(8):
            wb = wout[wo % 2]
            wtok = ("wout", wo % 2)
            wo += 1
            P.dma("sp", wb[:], k.wbf[wname_out][d].rearrange("p (k n) -> p k n", k=NFC),
                  reads=[("wbf", wname_out, d)], writes=[wtok])
            for s, (who, t0, T) in enumerate(grp):
                by = 6 + (yc % 2)
                yc += 1
                xb = xt[bufs[s]]

                def mmy(e, wb=wb, s=s, T=T, by=by):
                    ins = None
                    for fc in range(NFC):
                        ins = e.matmul(PS[by][:, 0:T], lhsT=wb[:, fc, :], rhs=actT[s][:, fc, 0:T],
                                       start=(fc == 0), stop=(fc == NFC - 1))
                    return ins
                P.pe(mmy, reads=[wtok] + [("actT", s, j) for j in range(NFC)], writes=[("ps", by)])
                hg = k.der[:, l, who, hgi, :]
                P.dve(lambda e, xb=xb, d=d, T=T, by=by, hg=hg: e.scalar_tensor_tensor(
                    out=xb[:, d, 0:T], in0=PS[by][:, 0:T], scalar=hg[:, d:d + 1], in1=xb[:, d, 0:T],
                    op0=ALU.mult, op1=ALU.add),
                    reads=[("ps", by), ("xt", bufs[s], d)], writes=[("xt", bufs[s], d)])
        for s, (who, t0, T) in enumerate(grp):
            b = bufs[s]
            xb = xt[b]
            xtoks = [("xt", b, c) for c in range(8)]
            if dst == "xT":
                P.dma("sp", k.xTd[:, :, t0:t0 + T].rearrange("c p t -> p c t"), xb[:, :, 0:T], reads=xtoks,
                      writes=[("xTd", t0)])
            else:
                r0 = t0 - NCTX
                for st in range(T // 128):
                    ob = tokb[tbc % 2]
                    otok = ("tokb", tbc % 2)
                    tbc += 1
                    for half in range(2):
                        bi = guc % 4
                        guc += 1

                        def tr(e, xb=xb, half=half, bi=bi, st=st):
                            ins = None
                            for cc in range(4):
                                c = half * 4 + cc
                                ins = e.transpose(PS[bi][:, cc * 128:(cc + 1) * 128], xb[:, c, st * 128:(st + 1) * 128],
                                                  k.ident[:])
                            return ins
                        P.pe(tr, reads=xtoks[half * 4:half * 4 + 4] + [("ident",)], writes=[("ps", bi)])
                        P.act(lambda e, ob=ob, half=half, bi=bi: e.activation(
                            out=ob[:, half * 512:(half + 1) * 512], in_=PS[bi][:], func=AF.Copy),
                            reads=[("ps", bi)], writes=[(otok, half)])
                    P.dma("sp", k.out[r0 + st * 128:r0 + (st + 1) * 128, :], ob[:], reads=[(otok, 0), (otok, 1)],
                          writes=[("out", r0, st)])
    A.release(m)


_CACHE = {}


def kernel(**inputs):
    inputs = {kk: np.asarray(v) for kk, v in inputs.items()}
    if "nc" not in _CACHE:
        _CACHE["nc"] = build()
    nc = _CACHE["nc"]
    shared = host_shared(inputs)
    in_maps = []
    for b in range(8):
        mm = dict(shared)
        mm.update(host_inputs(inputs, b))
        in_maps.append(mm)
    res = run_bass_kernel_spmd(nc, in_maps, core_ids=list(range(8)))
    return np.stack([r["out"] for r in res.results], axis=0)


NEG = -30000.0
PERM64 = list(range(0, 64, 2)) + list(range(1, 64, 2))
SWAP64 = list(range(1, 64, 2)) + list(range(0, 64, 2))
PERM32 = list(range(0, 32, 2)) + list(range(1, 32, 2))
SWAP32 = list(range(1, 32, 2)) + list(range(0, 32, 2))


def rope_tables():
    t = np.arange(S)
    row = (t // GRID_W).astype(np.float32)
    col = (t % GRID_W).astype(np.float32)

    def tab(dim):
        npairs = dim // 4
        inv = (10000.0 ** (-np.arange(npairs, dtype=np.float32) / npairs)).astype(np.float32)
        ang = np.concatenate([row[:, None] * inv, col[:, None] * inv], axis=-1).astype(np.float32)
        return np.cos(ang).astype(np.float32).T, np.sin(ang).astype(np.float32).T
    cA, sA = tab(64)
    cB, sB = tab(32)
    ta = np.zeros((128, 2, S), np.float32)
    for p in range(128):
        pp = p % 64
        pr = pp % 32
        ta[p, 0] = cA[pr]
        ta[p, 1] = -sA[pr] if pp < 32 else sA[pr]
    tb = np.zeros((128, 2, S), np.float32)
    for p in range(64, 96):
        pp = p - 64
        pr = pp % 16
        tb[p, 0] = cB[pr]
        tb[p, 1] = -sB[pr] if pp < 16 else sB[pr]
    return ta, tb


def na_geometry():
    sigs = []
    mp = {}
    chunks = {}
    for i in range(32):
        rs0 = min(max(2 * i - 4, 0), 56)
        rs1 = min(max(2 * i + 1 - 4, 0), 56)
        lo = rs0 // 2
        hi = (rs1 + 7) // 2
        chunks[i] = list(range(lo, hi + 1))
        for kc in chunks[i]:
            sig = (rs0 - 2 * i, rs1 - (2 * i + 1), kc - i)
            if sig not in sigs:
                sigs.append(sig)
            mp[(i, kc)] = sigs.index(sig)
    return sigs, mp, chunks


NA_SIGS, NA_MAP, NA_CHUNKS = na_geometry()
NSIG = len(NA_SIGS)


def na_bias_tiles(rpb):
    out = np.full((NSIG, 8, 128, 128), NEG, np.float32)
    kp = np.arange(128)
    qf = np.arange(128)
    kr_, kc_ = kp // 64, kp % 64
    qr_, qc_ = qf // 64, qf % 64
    cs = np.clip(qc_ - 8, 0, 48)
    for si, (a0, a1, dk) in enumerate(NA_SIGS):
        rsrel = np.where(qr_ == 0, a0, a1)
        krel = (2 * dk + kr_[:, None]) - qr_[None, :]
        vr = (krel >= rsrel[None, :]) & (krel <= rsrel[None, :] + 7)
        vc = (kc_[:, None] >= cs[None, :]) & (kc_[:, None] <= cs[None, :] + 15)
        valid = vr & vc
        dr = np.clip(krel + 7, 0, 14)
        dc = np.clip(kc_[:, None] - qc_[None, :] + 15, 0, 30)
        for h in range(8):
            g = rpb[h][dr, dc]
            out[si, h] = np.where(valid, g, np.float32(NEG))
    return out.reshape(NSIG * 8, 128, 128)


def blk(w, cols):
    K_ = w.shape[0]
    o = np.zeros((K_, len(cols)), np.float32)
    idx = [i for i, c in enumerate(cols) if c is not None]
    o[:, idx] = w[:, [cols[i] for i in idx]]
    return o.reshape(K_ // 128, 128, len(cols)).transpose(1, 0, 2)


def host_mixer(inputs):
    m = {}
    ta, tb = rope_tables()
    m["ropeA"] = ta
    m["ropeB"] = tb
    w = inputs["l0_w_in"]
    blocks = []
    blocks.append(blk(w, list(range(0, 128))))
    blocks.append(blk(w, list(range(128, 256))))
    blocks.append(blk(w, [None] * 64 + [256 + i for i in PERM32] + [None] * 32))
    blocks.append(blk(w, [None] * 64 + [256 + i for i in SWAP32] + [None] * 32))
    for c in range(3):
        blocks.append(blk(w, list(range(288 + c * 128, 288 + (c + 1) * 128))))
    for c in range(8):
        blocks.append(blk(w, list(range(672 + c * 128, 672 + (c + 1) * 128))))
    m["l0_wA"] = np.ascontiguousarray(np.stack(blocks, 0).reshape(15, 128, 8 * 128))
    wk = inputs["l0_mla_w_ukv"]
    kb = [blk(wk, [(2 * pr) * 128 + i for i in range(64)] + [(2 * pr + 1) * 128 + i for i in range(64)]) for pr in range(4)]
    m["l0_wkn"] = np.ascontiguousarray(np.stack(kb, 0).reshape(4, 128, 2 * 128))
    m["l0_wv"] = np.ascontiguousarray(blk(wk, [h * 128 + 64 + i for h in range(8) for i in range(64)]).reshape(1, 128, 2 * 512))
    wq = inputs["l0_mla_w_uq"]
    qb = []
    for h in range(8):
        qb.append(blk(wq, [h * 96 + i for i in range(64)] + [h * 96 + 64 + i for i in PERM32]))
    for h in range(8):
        qb.append(blk(wq, [None] * 64 + [h * 96 + 64 + i for i in SWAP32]))
    m["l0_wq"] = np.ascontiguousarray(np.stack(qb, 0).reshape(16, 128, 3 * 96))
    for l in range(2):
        wo = inputs[f"l{l}_w_out"]
        m[f"l{l}_wo"] = np.ascontiguousarray(wo.reshape(8, 128, 8, 128).transpose(2, 1, 0, 3).reshape(8, 128, 8 * 128))
    m["l0_dw"] = np.ascontiguousarray(inputs["l0_conv_dw_w"].reshape(31, 4, 128).transpose(2, 1, 0))
    v = np.zeros((128, 32), np.float32)
    v[:, 0:2] = pvec(inputs["l0_mla_kv_norm"])
    v[:, 2:5] = pvec(inputs["l0_mla_q_norm"])
    kg = inputs["l0_mla_k_gain"]
    qg = inputs["l0_mla_q_gain"]
    v[:, 5] = np.tile(kg[:64], 2)
    v[64:96, 6] = kg[64:][PERM32]
    v[64:96, 7] = kg[64:][SWAP32]
    v[0:64, 8] = qg[:64]
    v[64:96, 8] = qg[64:][PERM32]
    v[64:96, 9] = qg[64:][SWAP32]
    v[0:64, 10] = 1.0 / 64
    v[64:96, 10] = 1.0 / 32
    gb = inputs["l0_conv_glu_b"]
    v[:, 11:15] = pvec(gb[:512])
    v[:, 15:19] = pvec(gb[512:])
    v[:, 19:23] = pvec(inputs["l0_conv_dw_b"])
    v[:, 23:27] = pvec(inputs["l0_conv_ln_g"])
    v[:, 27:31] = pvec(inputs["l0_conv_ln_b"])
    m["l0_vec"] = v
    w = inputs["l1_w_in"]
    blocks = []
    for hk in range(2):
        blocks.append(blk(w, [hk * 64 + i for i in PERM64] * 2))
    for hk in range(2):
        blocks.append(blk(w, [hk * 64 + i for i in SWAP64] * 2))
    for c in range(4):
        blocks.append(blk(w, list(range(256 + c * 128, 256 + (c + 1) * 128))))
    for c in range(4):
        blocks.append(blk(w, [1280 + (2 * c + e) * 64 + i for e in range(2) for i in PERM64]))
    for c in range(4):
        blocks.append(blk(w, [1280 + (2 * c + e) * 64 + i for e in range(2) for i in SWAP64]))
    for c in range(4):
        blocks.append(blk(w, list(range(1792 + c * 128, 1792 + (c + 1) * 128))))
    m["l1_wA"] = np.ascontiguousarray(np.stack(blocks, 0).reshape(20, 128, 8 * 128))
    m["l1_wv"] = np.ascontiguousarray(blk(w, list(range(128, 256)) + list(range(768, 1280))).reshape(1, 128, 8 * 640))
    v = np.zeros((128, 8), np.float32)
    v[:, 0] = np.tile(inputs["l1_gqa_k_gain"][PERM64], 2)
    v[:, 1] = np.tile(inputs["l1_gqa_k_gain"][SWAP64], 2)
    v[:, 2] = np.tile(inputs["l1_gqa_q_gain"][PERM64], 2)
    v[:, 3] = np.tile(inputs["l1_gqa_q_gain"][SWAP64], 2)
    v[:, 4] = np.tile(inputs["l1_na_k_gain"], 2)
    v[:, 5] = np.tile(inputs["l1_na_q_gain"], 2)
    m["l1_vec"] = v
    m["l1_nab"] = na_bias_tiles(inputs["l1_na_rpb"])
    return m


def declare_mixer_inputs(k, nc):
    def din(name, shape):
        k.inp[name] = nc.dram_tensor(name, list(shape), F32, kind="ExternalInput").ap()
        k.inshape[name] = tuple(shape)
    din("ropeA", [128, 2, S])
    din("ropeB", [128, 2, S])
    din("l0_wA", [15, 128, 1024])
    din("l0_wkn", [4, 128, 256])
    din("l0_wv", [1, 128, 1024])
    din("l0_wq", [16, 128, 288])
    din("l0_wo", [8, 128, 1024])
    din("l1_wo", [8, 128, 1024])
    din("l0_dw", [128, 4, 31])
    din("l0_vec", [128, 32])
    din("l1_wA", [20, 128, 1024])
    din("l1_wv", [1, 128, 8 * 640])
    din("l1_vec", [128, 8])
    din("l1_nab", [NSIG * 8, 128, 128])


def stats_rstd(k, st, rows, T, scale, extra_reads=()):
    P = k.P
    i = st["i"]
    r2, rstd, bank = st["r2"], st["rstd"], st["bank"]
    P.act(lambda e: e.activation(out=r2[rows, 0:T], in_=k.PS[bank][rows, 0:T], func=AF.Sqrt, bias=k.epsv[rows, 0:1], scale=scale),
          reads=[("ps", bank), ("epsv",)] + list(extra_reads), writes=[("r2", i)])
    P.dve(lambda e: e.reciprocal(out=rstd[rows, 0:T], in_=r2[rows, 0:T]), reads=[("r2", i)], writes=[("rstd", i)])


class PsRot:
    def __init__(self, banks=(0, 1, 2, 3, 4, 5)):
        self.banks = banks
        self.i = 0

    def next(self):
        b = self.banks[self.i % len(self.banks)]
        self.i += 1
        return b


def load_x_norm(k, l, who, t0, T, xb, xtoks, hT, w):
    P = k.P
    P.dma("sp", xb[:, :, 0:T], k.xTd[:, :, t0:t0 + T].rearrange("c p t -> p c t"), writes=xtoks)
    rms_norm_mod(k, xb, xtoks, T, k.der[:, l, who, 1, :], k.modv[:, l, who, 24:32], hT, "hTm",
                 w["tmp"], w["r1"], w["r2"], w["rstd"], w["sq"])


def mm_group(k, bank_ap, lhs_fn, rhs_fn, n, reads, btok):
    def f(e):
        ins = None
        for kc in range(n):
            ins = e.matmul(bank_ap, lhsT=lhs_fn(kc), rhs=rhs_fn(kc), start=(kc == 0), stop=(kc == n - 1))
        return ins
    k.P.pe(f, reads=reads, writes=[btok])


def common_work(k, A):
    w = {}
    w["tmp"] = [A.alloc(f"tmp{i}", [128, 512], F32) for i in range(2)]
    w["r1"] = A.alloc("r1", [128, 512], F32)
    w["r2"] = A.alloc("r2", [128, 512], F32)
    w["rstd"] = A.alloc("rstd", [128, 512], F32)
    w["sq"] = A.alloc("sq", [128, 8, 512], BF16)
    w["stats"] = [dict(i=i, r1=A.alloc(f"sr1_{i}", [128, 512], F32), r2=A.alloc(f"sr2_{i}", [128, 512], F32),
                       rstd=A.alloc(f"srstd_{i}", [128, 512], F32)) for i in range(3)]
    w["si"] = 0
    w["sqs"] = [A.alloc(f"sqs{i}", [128, 512], BF16) for i in range(4)]
    w["sqi"] = 0
    w["rab"] = [(A.alloc(f"ra{i}", [128, 512], F32), A.alloc(f"rb{i}", [128, 512], F32)) for i in range(2)]
    w["rai"] = 0
    return w


def l0_proj_phase(k):
    P, A, nc, PS = k.P, k.A, k.nc, k.PS
    l = 0
    m = A.mark()
    w = common_work(k, A)
    xt = [A.alloc(f"xt{i}", [128, 8, 512], F32) for i in range(2)]
    hT = A.alloc("hTm", [128, 8, 512], BF16)
    wA = A.alloc("wA", [128, 15, 8, 128], BF16)
    wkn = A.alloc("wkn", [128, 4, 2, 128], BF16)
    wv = A.alloc("wv", [128, 2, 512], BF16)
    wq = A.alloc("wq", [128, 16, 3, 96], BF16)
    vec = A.alloc("vec", [128, 32], F32)
    rope = [A.alloc(f"rope{i}", [128, 2, 512], F32) for i in range(2)]
    ckvT = A.alloc("ckvT", [128, 2, 512], BF16)
    cqT = A.alloc("cqT", [128, 3, 512], BF16)
    krT = A.alloc("krT", [128, 512], BF16)
    yT = [A.alloc(f"yT{i}", [128, 4, 512], BF16) for i in range(2)]
    KT = [A.alloc(f"KT{i}", [128, 8, 512], BF16) for i in range(2)]
    QT = [A.alloc(f"QT{i}", [128, 8, 512], BF16) for i in range(2)]
    Vt = [A.alloc(f"Vt{i}", [128, 8, 128], BF16) for i in range(2)]
    sig = [A.alloc(f"sig{i}", [128, 512], F32) for i in range(2)]
    P.dma("sp", vec[:], k.inp["l0_vec"], writes=[("vec",)])
    for i in range(15):
        P.dma("sp", wA[:, i], k.wbf["l0_wA"][i].rearrange("p (k n) -> p k n", k=8), reads=[("wbf", "l0_wA", i)], writes=[("wA",)])
    for i in range(4):
        P.dma("sp", wkn[:, i], k.wbf["l0_wkn"][i].rearrange("p (k n) -> p k n", k=2), reads=[("wbf", "l0_wkn", i)], writes=[("wkn",)])
    P.dma("sp", wv[:], k.wbf["l0_wv"][0].rearrange("p (k n) -> p k n", k=2), reads=[("wbf", "l0_wv", 0)], writes=[("wv",)])
    for i in range(16):
        P.dma("sp", wq[:, i], k.wbf["l0_wq"][i].rearrange("p (k n) -> p k n", k=3), reads=[("wbf", "l0_wq", i)], writes=[("wq",)])
    sc = 96.0 ** -0.5
    P.dve(lambda e: e.tensor_scalar(out=vec[:, 8:10], in0=vec[:, 8:10], scalar1=sc, scalar2=None, op0=ALU.mult),
          reads=[("vec",)], writes=[("vec",)])
    for i in range(2):
        P.dve(lambda e, i=i: e.memset(Vt[i][:], 1.0), writes=[("Vt", i), (("Vt", i), 1)])
    rot = PsRot()
    vcnt = 0
    ALLR = slice(0, 128)
    R = slice(64, 96)
    R96 = slice(0, 96)
    for ti, (who, t0, T) in enumerate(TILES):
        lat = (who == 0)
        xb = xt[ti % 2]
        xtoks = [("xtm", ti % 2, c) for c in range(8)]
        load_x_norm(k, l, who, t0, T, xb, xtoks, hT, w)
        htoks = [("hTm", c) for c in range(8)]
        rp = rope[ti % 2]
        if lat:
            P.dma("sp", rp[:, :, 0:T], k.inp["ropeB"][:, :, t0 - NCTX:t0 - NCTX + T], writes=[("rope", ti % 2)])

        def projA(bidx, M):
            b = rot.next()
            mm_group(k, PS[b][0:M, 0:T], lambda kc: wA[:, bidx, kc, 0:M], lambda kc: hT[:, kc, 0:T], 8,
                     [("wA",)] + htoks, ("ps", b))
            return b

        def square(b, M, dst_ap, dtok):
            P.act(lambda e: e.activation(out=dst_ap, in_=PS[b][0:M, 0:T], func=AF.Square), reads=[("ps", b)], writes=[dtok])

        bs = [projA(0, 128), projA(1, 128)]
        for c in range(2):
            square(bs[c], 128, w["sq"][:, c, 0:T], ("sq", c))
        st = next_stat(w)
        mm_group(k, PS[st["bank"]][:, 0:T], lambda kc: k.ones_bf, lambda kc: w["sq"][:, kc, 0:T], 2,
                 [("sq", 0), ("sq", 1), ("cbf",)], ("ps", st["bank"]))
        stats_rstd(k, st, ALLR, T, 1.0 / 256)
        for c in range(2):
            P.dve(lambda e: e.scalar_tensor_tensor(out=ckvT[:, c, 0:T], in0=PS[bs[c]][:, 0:T], scalar=vec[:, c:c + 1],
                                                   in1=st["rstd"][:, 0:T], op0=ALU.mult, op1=ALU.mult),
                  reads=[("ps", bs[c]), ("rstd", st["i"]), ("vec",)], writes=[("ckvT", c)])
        bm = projA(2, 96)
        bsw = projA(3, 96) if lat else None
        sb, stok = next_sqs(w)
        square(bm, 96, sb[0:96, 0:T], stok)
        st = next_stat(w)
        mm_group(k, PS[st["bank"]][0:96, 0:T], lambda kc: k.cbf[0:96, 3, 0:96], lambda kc: sb[0:96, 0:T], 1, [stok, ("cbf",)],
                 ("ps", st["bank"]))
        stats_rstd(k, st, R, T, 1.0 / 32)
        if not lat:
            P.dve(lambda e: e.scalar_tensor_tensor(out=krT[R, 0:T], in0=PS[bm][R, 0:T], scalar=vec[R, 6:7],
                                                   in1=st["rstd"][R, 0:T], op0=ALU.mult, op1=ALU.mult),
                  reads=[("ps", bm), ("rstd", st["i"]), ("vec",)], writes=[("krT",)])
        else:
            rope_apply(k, w, bm, bsw, R, T, vec[R, 6:7], vec[R, 7:8], st, rp, ("rope", ti % 2), krT[R, 0:T], [("krT",)])
        bs = [projA(4 + c, 128) for c in range(3)]
        for c in range(3):
            square(bs[c], 128, w["sq"][:, 2 + c, 0:T], ("sq", 2 + c))
        st = next_stat(w)
        mm_group(k, PS[st["bank"]][:, 0:T], lambda kc: k.ones_bf, lambda kc: w["sq"][:, 2 + kc, 0:T], 3,
                 [("sq", 2), ("sq", 3), ("sq", 4), ("cbf",)], ("ps", st["bank"]))
        stats_rstd(k, st, ALLR, T, 1.0 / 384)
        for c in range(3):
            P.dve(lambda e: e.scalar_tensor_tensor(out=cqT[:, c, 0:T], in0=PS[bs[c]][:, 0:T], scalar=vec[:, 2 + c:3 + c],
                                                   in1=st["rstd"][:, 0:T], op0=ALU.mult, op1=ALU.mult),
                  reads=[("ps", bs[c]), ("rstd", st["i"]), ("vec",)], writes=[("cqT", c)])
        yb = yT[ti % 2]
        for ch in range(4):
            ba_ = projA(7 + ch, 128)
            bg_ = projA(11 + ch, 128)
            sg_ = sig[ch % 2]
            P.act(lambda e: e.activation(out=sg_[:, 0:T], in_=PS[bg_][:, 0:T], func=AF.Sigmoid,
                                         bias=vec[:, 15 + ch:16 + ch], scale=1.0),
                  reads=[("ps", bg_), ("vec",)], writes=[("sig", ch % 2)])
            P.dve(lambda e: e.scalar_tensor_tensor(out=yb[:, ch, 0:T], in0=PS[ba_][:, 0:T],
                                                   scalar=vec[:, 11 + ch:12 + ch], in1=sg_[:, 0:T],
                                                   op0=ALU.add, op1=ALU.mult),
                  reads=[("ps", ba_), ("sig", ch % 2), ("vec",)], writes=[("yT", ti % 2, ch)])
        P.dma("sp", k.yTd[:, :, t0:t0 + T].rearrange("c p t -> p c t"), yb[:, :, 0:T],
              reads=[("yT", ti % 2, ch) for ch in range(4)], writes=[("yTd", ti)])
        Kb = KT[ti % 2]
        ktoks = [("KT", ti % 2, h) for h in range(8)]
        for pr in range(4):
            b = rot.next()
            mm_group(k, PS[b][:, 0:T], lambda kc: wkn[:, pr, kc, :], lambda kc: ckvT[:, kc, 0:T], 2,
                     [("wkn",), ("ckvT", 0), ("ckvT", 1)], ("ps", b))
            sb, stok = next_sqs(w)
            square(b, 128, sb[:, 0:T], stok)
            st = next_stat(w)
            mm_group(k, PS[st["bank"]][:, 0:T], lambda kc: k.bd64_bf, lambda kc: sb[:, 0:T], 1, [stok, ("cbf",)], ("ps", st["bank"]))
            stats_rstd(k, st, ALLR, T, 1.0 / 64)
            for e_ in range(2):
                h = 2 * pr + e_
                P.dve(lambda e: e.scalar_tensor_tensor(
                    out=Kb[0:64, h, 0:T], in0=PS[b][e_ * 64:(e_ + 1) * 64, 0:T], scalar=vec[e_ * 64:(e_ + 1) * 64, 5:6],
                    in1=st["rstd"][e_ * 64:(e_ + 1) * 64, 0:T], op0=ALU.mult, op1=ALU.mult),
                    reads=[("ps", b), ("rstd", st["i"]), ("vec",)], writes=[("KT", ti % 2, h)])
        for h in range(8):
            P.pool(lambda e: e.tensor_copy(out=Kb[R, h, 0:T], in_=krT[R, 0:T]), reads=[("krT",)], writes=[(("KT", ti % 2, h), "r")])
        P.dma("sp", k.KTd[:, :, t0:t0 + T].rearrange("h p t -> p h t"), Kb[0:96, :, 0:T],
              reads=ktoks + [(kt_, "r") for kt_ in ktoks], writes=[("KTd", ti)])
        for s_ in range(T // 128):
            b = rot.next()
            mm_group(k, PS[b][:, 0:512], lambda kc: ckvT[:, kc, s_ * 128:(s_ + 1) * 128], lambda kc: wv[:, kc, :], 2,
                     [("wv",), ("ckvT", 0), ("ckvT", 1)], ("ps", b))
            vb = Vt[vcnt % 2]
            vtok = ("Vt", vcnt % 2)
            vcnt += 1
            src = PS[b][:, 0:512].rearrange("p (h two d) -> p h two d", two=2, d=64)
            dst = vb[:].rearrange("p (h two) d -> p h two d", two=2)
            P.dve(lambda e: e.tensor_copy(out=dst[:, :, 0, 0:64], in_=src[:, :, 0, :]), reads=[("ps", b)], writes=[vtok])
            P.dve(lambda e: e.tensor_copy(out=dst[:, :, 1, 64:128], in_=src[:, :, 1, :]), reads=[("ps", b)], writes=[(vtok, 1)])
            chunk = (t0 + s_ * 128) // 128
            P.dma("sp", k.Vd[:, :, chunk, :].rearrange("h p d -> p h d"), vb[:], reads=[vtok, (vtok, 1)], writes=[("Vd", chunk)])
        Qb = QT[ti % 2]
        qtoks = [("QT", ti % 2, h) for h in range(8)]
        for h in range(8):
            b = rot.next()
            mm_group(k, PS[b][0:96, 0:T], lambda kc: wq[:, h, kc, :], lambda kc: cqT[:, kc, 0:T], 3,
                     [("wq",)] + [("cqT", c) for c in range(3)], ("ps", b))
            bsw = None
            if lat:
                bsw = rot.next()
                mm_group(k, PS[bsw][0:96, 0:T], lambda kc: wq[:, 8 + h, kc, :], lambda kc: cqT[:, kc, 0:T], 3,
                         [("wq",)] + [("cqT", c) for c in range(3)], ("ps", bsw))
            sb, stok = next_sqs(w)
            square(b, 96, sb[0:96, 0:T], stok)
            st = next_stat(w)
            mm_group(k, PS[st["bank"]][0:96, 0:T], lambda kc: k.cbf[0:96, 3, 0:96], lambda kc: sb[0:96, 0:T], 1, [stok, ("cbf",)],
                     ("ps", st["bank"]))
            stats_rstd(k, st, R96, T, vec[0:96, 10:11], extra_reads=[("vec",)])
            nr = 96 if not lat else 64
            P.dve(lambda e: e.scalar_tensor_tensor(out=Qb[0:nr, h, 0:T], in0=PS[b][0:nr, 0:T], scalar=vec[0:nr, 8:9],
                                                   in1=st["rstd"][0:nr, 0:T], op0=ALU.mult, op1=ALU.mult),
                  reads=[("ps", b), ("rstd", st["i"]), ("vec",)], writes=[("QT", ti % 2, h)])
            if lat:
                rope_apply(k, w, b, bsw, R, T, vec[R, 8:9], vec[R, 9:10], st, rp, ("rope", ti % 2),
                           Qb[R, h, 0:T], [(("QT", ti % 2, h), "r")])
        P.dma("sp", k.QTd[:, :, t0:t0 + T].rearrange("h p t -> p h t"), Qb[0:96, :, 0:T],
              reads=qtoks + [(q_, "r") for q_ in qtoks], writes=[("QTd", ti)])
    A.release(m)


def rope_apply(k, w, bm, bsw, R, T, g, gsw, st, rp, rptok, out_ap, otoks):
    P = k.P
    j = w["rai"] % 2
    w["rai"] += 1
    ra, rb = w["rab"][j]
    rstd, rtok = st["rstd"], ("rstd", st["i"])
    P.dve(lambda e: e.scalar_tensor_tensor(out=ra[R, 0:T], in0=k.PS[bm][R, 0:T], scalar=g, in1=rstd[R, 0:T],
                                           op0=ALU.mult, op1=ALU.mult),
          reads=[("ps", bm), rtok, ("vec",)], writes=[("ra", j)])
    P.dve(lambda e: e.scalar_tensor_tensor(out=rb[R, 0:T], in0=k.PS[bsw][R, 0:T], scalar=gsw, in1=rstd[R, 0:T],
                                           op0=ALU.mult, op1=ALU.mult),
          reads=[("ps", bsw), rtok, ("vec",)], writes=[("rb", j)])
    P.pool(lambda e: e.tensor_tensor(out=ra[R, 0:T], in0=ra[R, 0:T], in1=rp[R, 0, 0:T], op=ALU.mult),
           reads=[("ra", j), rptok], writes=[("ra", j)])
    P.pool(lambda e: e.tensor_tensor(out=rb[R, 0:T], in0=rb[R, 0:T], in1=rp[R, 1, 0:T], op=ALU.mult),
           reads=[("rb", j), rptok], writes=[("rb", j)])
    P.dve(lambda e: e.tensor_tensor(out=out_ap, in0=ra[R, 0:T], in1=rb[R, 0:T], op=ALU.add),
          reads=[("ra", j), ("rb", j)], writes=otoks)


def next_stat(w):
    st = w["stats"][w["si"] % len(w["stats"])]
    st["bank"] = 6 + (w["si"] % 2)
    w["si"] += 1
    return st


def next_sqs(w):
    i = w["sqi"] % len(w["sqs"])
    w["sqi"] += 1
    return w["sqs"][i], ("sqs", i)


def attn_phase(k, mode):
    P, A, PS = k.P, k.A, k.PS
    m = A.mark()
    mla = (mode == "mla")
    Kt = [A.alloc(f"Kt{i}", [128, TALL], BF16) for i in range(4)]
    Vh = [A.alloc(f"Vh{i}", [128, 34, 128], BF16) for i in range(4)]
    Qt = [A.alloc(f"Qt{i}", [128, 2, 512], BF16) for i in range(2)]
    PT = [A.alloc(f"PT{i}", [128, 512], BF16) for i in range(4)]
    rec = A.alloc("rec", [128, 512], F32)
    attT = [A.alloc(f"attT{i}", [128, 512], BF16) for i in range(2)]
    qtiles = TILES if mla else TILES[1:]
    qc = 0
    sc = 0
    pc = 0
    ac = 0
    for hp in range(4):
        kb = (hp % 2) * 2
        if mla:
            for e_ in range(2):
                P.dma("sp", Kt[kb + e_][0:96, :], k.KTd[2 * hp + e_], writes=[("Kt", kb + e_)])
                P.dma("sp", Vh[kb + e_][:], k.Vd[2 * hp + e_], writes=[("Vh", kb + e_)])
        else:
            for e_ in range(2):
                P.dma("sp", Kt[kb + e_][:], k.KdupTd[hp // 2], writes=[("Kt", kb + e_)])
                P.dve(lambda e: e.memset(Kt[kb + e_][(1 - e_) * 64:(2 - e_) * 64, :], 0.0), writes=[("Kt", kb + e_)])
                P.dma("sp", Vh[kb + e_][:], k.VgD[(hp // 2) * 2 + e_], writes=[("Vh", kb + e_)])
        for (who, t0, T) in qtiles:
            chunks = [0, 1] if who == 1 else list(range(34))
            Qb = Qt[qc % 2]
            qtok = ("Qt", qc % 2)
            ab = attT[qc % 2]
            atok = ("attT", qc % 2)
            qc += 1
            if mla:
                P.dma("sp", Qb[0:96, :, 0:T], k.QTd[2 * hp:2 * hp + 2, :, t0:t0 + T].rearrange("h p t -> p h t"), writes=[qtok])
            else:
                P.dma("sp", Qb[:, 0, 0:T], k.QgTd[hp, :, t0:t0 + T], writes=[qtok])
            for e_ in range(2):
                if mla:
                    Ksb, ktok, r0, r1 = Kt[kb + e_], ("Kt", kb + e_), 0, 96
                    qap = Qb[0:96, e_, 0:T]
                else:
                    Ksb, ktok, r0, r1 = Kt[kb + e_], ("Kt", kb + e_), 0, 128
                    qap = Qb[:, 0, 0:T]
                Vsb, vtok = Vh[kb + e_], ("Vh", kb + e_)
                acc = 4 + (ac % 2)
                ac += 1
                n = len(chunks)
                pend = []
                LOOK = 2

                def do_pv(item):
                    pidx, pc_, ppb = item
                    P.pe(lambda e: e.matmul(PS[acc][:, 0:T], lhsT=Vsb[:, pc_, :], rhs=PT[ppb][:, 0:T],
                                            start=(pidx == 0), stop=(pidx == n - 1)),
                         reads=[vtok, ("PT", ppb)], writes=[("ps", acc)])
                for idx, c in enumerate(chunks):
                    sbk = sc % 4
                    sc += 1
                    P.pe(lambda e: e.matmul(PS[sbk][:, 0:T], lhsT=Ksb[r0:r1, c * 128:(c + 1) * 128], rhs=qap,
                                            start=True, stop=True),
                         reads=[ktok, qtok], writes=[("ps", sbk)])
                    pb = pc % 4
                    pc += 1
                    P.act(lambda e: e.activation(out=PT[pb][:, 0:T], in_=PS[sbk][:, 0:T], func=AF.Exp),
                          reads=[("ps", sbk)], writes=[("PT", pb)])
                    pend.append((idx, c, pb))
                    if len(pend) > LOOK:
                        do_pv(pend.pop(0))
                while pend:
                    do_pv(pend.pop(0))
                nlo, dlo = (0, 64) if e_ == 0 else (64, 0)
                P.dve(lambda e: e.reciprocal(out=rec[nlo:nlo + 64, 0:T], in_=PS[acc][dlo:dlo + 64, 0:T]),
                      reads=[("ps", acc)], writes=[("rec", e_)])
                P.dve(lambda e: e.tensor_tensor(out=ab[nlo:nlo + 64, 0:T], in0=PS[acc][nlo:nlo + 64, 0:T], in1=rec[nlo:nlo + 64, 0:T],
                                                op=ALU.mult),
                      reads=[("ps", acc), ("rec", e_)], writes=[(atok, e_)])
            P.dma("sp", k.attTd[hp, :, t0:t0 + T], ab[:, 0:T], reads=[(atok, 0), (atok, 1)], writes=[("attTd", hp, t0)])
    A.release(m)


def l0_conv_phase(k):
    P, A, PS = k.P, k.A, k.PS
    m = A.mark()
    vec = A.alloc("vec", [128, 32], F32)
    dw = A.alloc("dw", [128, 4, 31], F32)
    diag = A.alloc("diag", [128, 4, 31, 128], BF16)
    ybuf = [A.alloc(f"ybuf{i}", [128, 4, 544], BF16) for i in range(2)]
    cv = A.alloc("cv", [128, 4, 512], F32)
    cb = A.alloc("cb", [128, 4, 512], BF16)
    sqv = A.alloc("sqv", [128, 4, 512], BF16)
    co = [A.alloc(f"co{i}", [128, 4, 512], BF16) for i in range(2)]
    mean = A.alloc("mean", [128, 512], F32)
    msq = A.alloc("msq", [128, 512], F32)
    r1 = A.alloc("r1", [128, 512], F32)
    r2 = A.alloc("r2", [128, 512], F32)
    rstd = A.alloc("rstd", [128, 512], F32)
    t1 = [A.alloc(f"t1{i}", [128, 512], F32) for i in range(2)]
    identb = k.cbf[:, 0, :]
    P.dma("sp", vec[:], k.inp["l0_vec"], writes=[("vec",)])
    P.dma("sp", dw[:], k.inp["l0_dw"], writes=[("dw",)])
    for ch in range(4):
        for j in range(31):
            P.dve(lambda e: e.tensor_scalar(out=diag[:, ch, j, :], in0=identb, scalar1=dw[:, ch, j:j + 1], scalar2=None, op0=ALU.mult),
                  reads=[("dw",), ("cbf",)], writes=[("diag", ch, j)])
    rot = PsRot(banks=(0, 1, 2, 3))
    for ti, (who, t0, T) in enumerate(TILES):
        s0, s1 = (0, NCTX) if who == 1 else (NCTX, TALL)
        lo = max(t0 - 15, s0)
        hi = min(t0 + T + 15, s1)
        yb = ybuf[ti % 2]
        toks = [("yb", ti % 2, x) for x in "LMR"]
        wr = [toks[1]]
        if lo > t0 - 15:
            P.pool(lambda e: e.memset(yb[:, :, 0:15], 0.0), writes=[toks[0]])
        else:
            wr.append(toks[0])
        if hi < t0 + T + 15:
            P.pool(lambda e: e.memset(yb[:, :, T + 15:T + 30], 0.0), writes=[toks[2]])
        else:
            wr.append(toks[2])
        P.dma("sp", yb[:, :, lo - (t0 - 15):hi - (t0 - 15)], k.yTd[:, :, lo:hi].rearrange("c p t -> p c t"), writes=wr)
        for ch in range(4):
            b = rot.next()

            def mm(e, ch=ch, b=b):
                for j in range(31):
                    e.matmul(PS[b][:, 0:T], lhsT=diag[:, ch, j, :], rhs=yb[:, ch, j:j + T], start=(j == 0), stop=(j == 30))
            P.pe(mm, reads=toks + [("diag", ch, j) for j in range(31)], writes=[("ps", b)])
            P.act(lambda e: e.activation(out=cv[:, ch, 0:T], in_=PS[b][:, 0:T], func=AF.Identity, bias=vec[:, 19 + ch:20 + ch], scale=1.0),
                  reads=[("ps", b), ("vec",)], writes=[("cv", ch)])
            P.pool(lambda e: e.tensor_copy(out=cb[:, ch, 0:T], in_=cv[:, ch, 0:T]), reads=[("cv", ch)], writes=[("cb", ch)])
            P.pool(lambda e: e.tensor_tensor(out=sqv[:, ch, 0:T], in0=cv[:, ch, 0:T], in1=cv[:, ch, 0:T], op=ALU.mult),
                   reads=[("cv", ch)], writes=[("sqv", ch)])
        mm_group(k, PS[6][:, 0:T], lambda kc: k.ones_bf, lambda kc: cb[:, kc, 0:T], 4, [("cb", c) for c in range(4)] + [("cbf",)], ("ps", 6))
        mm_group(k, PS[7][:, 0:T], lambda kc: k.ones_bf, lambda kc: sqv[:, kc, 0:T], 4, [("sqv", c) for c in range(4)] + [("cbf",)], ("ps", 7))
        P.dve(lambda e: e.tensor_scalar(out=mean[:, 0:T], in0=PS[6][:, 0:T], scalar1=1.0 / 512, scalar2=None, op0=ALU.mult),
              reads=[("ps", 6)], writes=[("mean",)])
        P.pool(lambda e: e.tensor_tensor(out=msq[:, 0:T], in0=mean[:, 0:T], in1=mean[:, 0:T], op=ALU.mult),
               reads=[("mean",)], writes=[("msq",)])
        P.dve(lambda e: e.scalar_tensor_tensor(out=r2[:, 0:T], in0=PS[7][:, 0:T], scalar=1.0 / 512, in1=msq[:, 0:T],
                                               op0=ALU.mult, op1=ALU.subtract),
              reads=[("ps", 7), ("msq",)], writes=[("r2",)])
        P.dve(lambda e: e.tensor_scalar(out=r1[:, 0:T], in0=r2[:, 0:T], scalar1=EPS, scalar2=None, op0=ALU.add),
              reads=[("r2",)], writes=[("r1",)])
        P.act(lambda e: e.activation(out=r2[:, 0:T], in_=r1[:, 0:T], func=AF.Sqrt), reads=[("r1",)], writes=[("r2",)])
        P.dve(lambda e: e.reciprocal(out=rstd[:, 0:T], in_=r2[:, 0:T]), reads=[("r2",)], writes=[("rstd",)])
        cob = co[ti % 2]
        for ch in range(4):
            tb = t1[ch % 2]
            P.dve(lambda e: e.tensor_tensor(out=tb[:, 0:T], in0=cv[:, ch, 0:T], in1=mean[:, 0:T], op=ALU.subtract),
                  reads=[("cv", ch), ("mean",)], writes=[("t1", ch % 2)])
            P.pool(lambda e: e.tensor_tensor(out=tb[:, 0:T], in0=tb[:, 0:T], in1=rstd[:, 0:T], op=ALU.mult),
                   reads=[("t1", ch % 2), ("rstd",)], writes=[("t1", ch % 2)])
            P.act(lambda e: e.activation(out=cob[:, ch, 0:T], in_=tb[:, 0:T], func=AF.Silu, bias=vec[:, 27 + ch:28 + ch],
                                         scale=vec[:, 23 + ch:24 + ch]),
                  reads=[("t1", ch % 2), ("vec",)], writes=[("co", ti % 2, ch)])
        P.dma("sp", k.attTd[4:8, :, t0:t0 + T].rearrange("c p t -> p c t"), cob[:, :, 0:T],
              reads=[("co", ti % 2, ch) for ch in range(4)], writes=[("attTd", "conv", ti)])
    A.release(m)


def mixout_phase(k, l, tiles):
    P, A, PS = k.P, k.A, k.PS
    m = A.mark()
    wo = A.alloc("wo", [128, 8, 8, 128], BF16)
    xt = [A.alloc(f"xt{i}", [128, 8, 512], F32) for i in range(2)]
    at = [A.alloc(f"at{i}", [128, 8, 512], BF16) for i in range(2)]
    name = f"l{l}_wo"
    for i in range(8):
        P.dma("sp", wo[:, i], k.wbf[name][i].rearrange("p (k n) -> p k n", k=8), reads=[("wbf", name, i)], writes=[("wo",)])
    rot = PsRot()
    for ti, (who, t0, T) in enumerate(tiles):
        xb = xt[ti % 2]
        ab = at[ti % 2]
        xtoks = [("xt", ti % 2, c) for c in range(8)]
        P.dma("sp", xb[:, :, 0:T], k.xTd[:, :, t0:t0 + T].rearrange("c p t -> p c t"), writes=xtoks)
        P.dma("sp", ab[:, :, 0:T], k.attTd[:, :, t0:t0 + T].rearrange("c p t -> p c t"), writes=[("at", ti % 2)])
        for d in range(8):
            b = rot.next()
            mm_group(k, PS[b][:, 0:T], lambda kc: wo[:, d, kc, :], lambda kc: ab[:, kc, 0:T], 8, [("wo",), ("at", ti % 2)], ("ps", b))
            P.dve(lambda e: e.scalar_tensor_tensor(out=xb[:, d, 0:T], in0=PS[b][:, 0:T], scalar=k.modv[:, l, who, 40 + d:41 + d],
                                                   in1=xb[:, d, 0:T], op0=ALU.mult, op1=ALU.add),
                  reads=[("ps", b), xtoks[d]], writes=[xtoks[d]])
        P.dma("sp", k.xTd[:, :, t0:t0 + T].rearrange("c p t -> p c t"), xb[:, :, 0:T], reads=xtoks, writes=[("xTd", t0)])
    A.release(m)


def l1_proj_phase(k):
    P, A, PS = k.P, k.A, k.PS
    l = 1
    m = A.mark()
    w = common_work(k, A)
    xt = [A.alloc(f"xt{i}", [128, 8, 512], F32) for i in range(2)]
    hT = A.alloc("hTm", [128, 8, 512], BF16)
    wA = A.alloc("wA", [128, 20, 8, 128], BF16)
    wv = A.alloc("wv", [128, 8, 640], BF16)
    vec = A.alloc("vec", [128, 8], F32)
    rope = [A.alloc(f"rope{i}", [128, 2, 512], F32) for i in range(2)]
    KdT = [A.alloc(f"KdT{i}", [128, 2, 512], BF16) for i in range(2)]
    nkT = [A.alloc(f"nkT{i}", [128, 4, 512], BF16) for i in range(2)]
    QgT = [A.alloc(f"QgT{i}", [128, 4, 512], BF16) for i in range(2)]
    nqT = [A.alloc(f"nqT{i}", [128, 4, 512], BF16) for i in range(2)]
    Vgt = [A.alloc(f"Vgt{i}", [128, 4, 128], BF16) for i in range(2)]
    NVt = [A.alloc(f"NVt{i}", [128, 8, 128], BF16) for i in range(2)]
    P.dma("sp", vec[:], k.inp["l1_vec"], writes=[("vec",)])
    for i in range(20):
        P.dma("sp", wA[:, i], k.wbf["l1_wA"][i].rearrange("p (k n) -> p k n", k=8), reads=[("wbf", "l1_wA", i)], writes=[("wA",)])
    P.dma("sp", wv[:], k.wbf["l1_wv"][0].rearrange("p (k n) -> p k n", k=8), reads=[("wbf", "l1_wv", 0)], writes=[("wv",)])
    for c0, c1 in ((2, 4), (5, 6)):
        P.dve(lambda e: e.tensor_scalar(out=vec[:, c0:c1], in0=vec[:, c0:c1], scalar1=0.125, scalar2=None, op0=ALU.mult),
              reads=[("vec",)], writes=[("vec",)])
    for i in range(2):
        P.dve(lambda e: e.memset(Vgt[i][:], 1.0), writes=[("Vgt", i), (("Vgt", i), 1)])
        P.dve(lambda e: e.memset(NVt[i][:], 1.0), writes=[("NVt", i), (("NVt", i), 1)])
    rot = PsRot()
    cnt = {"sq": 0, "v": 0}
    Rall = slice(0, 128)
    for ti, (who, t0, T) in enumerate(TILES):
        lat = (who == 0)
        xb = xt[ti % 2]
        xtoks = [("xtm", ti % 2, c) for c in range(8)]
        load_x_norm(k, l, who, t0, T, xb, xtoks, hT, w)
        htoks = [("hTm", c) for c in range(8)]
        rp = rope[ti % 2]
        if lat:
            P.dma("sp", rp[:, :, 0:T], k.inp["ropeA"][:, :, t0 - NCTX:t0 - NCTX + T], writes=[("rope", ti % 2)])

        def projA(bidx):
            b = rot.next()
            mm_group(k, PS[b][:, 0:T], lambda kc: wA[:, bidx, kc, :], lambda kc: hT[:, kc, 0:T], 8, [("wA",)] + htoks, ("ps", b))
            return b

        def normed_chunk(bmain, bswap, gcol, out_ap, otok, do_rope):
            b = projA(bmain)
            bsw = projA(bswap) if do_rope else None
            sb, stok = next_sqs(w)
            P.act(lambda e: e.activation(out=sb[:, 0:T], in_=PS[b][:, 0:T], func=AF.Square), reads=[("ps", b)], writes=[stok])
            st = next_stat(w)
            mm_group(k, PS[st["bank"]][:, 0:T], lambda kc: k.bd64_bf, lambda kc: sb[:, 0:T], 1, [stok, ("cbf",)], ("ps", st["bank"]))
            stats_rstd(k, st, Rall, T, 1.0 / 64)
            if do_rope:
                rope_apply(k, w, b, bsw, Rall, T, vec[:, gcol:gcol + 1], vec[:, gcol + 1:gcol + 2], st, rp,
                           ("rope", ti % 2), out_ap, [otok])
            else:
                P.dve(lambda e: e.scalar_tensor_tensor(out=out_ap, in0=PS[b][:, 0:T], scalar=vec[:, gcol:gcol + 1],
                                                       in1=st["rstd"][:, 0:T], op0=ALU.mult, op1=ALU.mult),
                      reads=[("ps", b), ("rstd", st["i"]), ("vec",)], writes=[otok])

        Kb = KdT[ti % 2]
        for hk in range(2):
            normed_chunk(hk, 2 + hk, 0, Kb[:, hk, 0:T], ("KdT", ti % 2, hk), lat)
        P.dma("sp", k.KdupTd[:, :, t0:t0 + T].rearrange("h p t -> p h t"), Kb[:, :, 0:T],
              reads=[("KdT", ti % 2, hk) for hk in range(2)], writes=[("KdupTd", ti)])
        nb = nkT[ti % 2]
        for c in range(4):
            normed_chunk(4 + c, None, 4, nb[:, c, 0:T], ("nkT", ti % 2, c), False)
        P.dma("sp", k.nkTd[:, :, t0:t0 + T].rearrange("c p t -> p c t"), nb[:, :, 0:T],
              reads=[("nkT", ti % 2, c) for c in range(4)], writes=[("nkTd", ti)])
        for st in range(T // 128):
            b1 = rot.next()
            mm_group(k, PS[b1][:, 0:128], lambda kc: hT[:, kc, st * 128:(st + 1) * 128], lambda kc: wv[:, kc, 0:128], 8,
                     [("wv",)] + htoks, ("ps", b1))
            b2 = rot.next()
            mm_group(k, PS[b2][:, 0:512], lambda kc: hT[:, kc, st * 128:(st + 1) * 128], lambda kc: wv[:, kc, 128:640], 8,
                     [("wv",)] + htoks, ("ps", b2))
            vi = cnt["v"] % 2
            cnt["v"] += 1
            vg, nv = Vgt[vi], NVt[vi]
            vgd = vg[:].rearrange("p (hk par) d -> p hk par d", par=2)
            s1 = PS[b1][:, 0:128].rearrange("p (hk d) -> p hk d", d=64)
            P.dve(lambda e: e.tensor_copy(out=vgd[:, :, 0, 0:64], in_=s1), reads=[("ps", b1)], writes=[("Vgt", vi)])
            P.dve(lambda e: e.tensor_copy(out=vgd[:, :, 1, 64:128], in_=s1), reads=[("ps", b1)], writes=[(("Vgt", vi), 1)])
            nvd = nv[:].rearrange("p (h two) d -> p h two d", two=2)
            s2 = PS[b2][:, 0:512].rearrange("p (h two d) -> p h two d", two=2, d=64)
            P.dve(lambda e: e.tensor_copy(out=nvd[:, :, 0, 0:64], in_=s2[:, :, 0, :]), reads=[("ps", b2)], writes=[("NVt", vi)])
            P.dve(lambda e: e.tensor_copy(out=nvd[:, :, 1, 64:128], in_=s2[:, :, 1, :]), reads=[("ps", b2)],
                  writes=[(("NVt", vi), 1)])
            chunk = (t0 + st * 128) // 128
            P.dma("sp", k.VgD[:, :, chunk, :].rearrange("v p d -> p v d"), vg[:], reads=[("Vgt", vi), (("Vgt", vi), 1)],
                  writes=[("VgD", chunk)])
            P.dma("sp", k.NVd[:, :, chunk, :].rearrange("h p d -> p h d"), nv[:], reads=[("NVt", vi), (("NVt", vi), 1)],
                  writes=[("NVd", chunk)])
        if lat:
            qb = QgT[ti % 2]
            for c in range(4):
                normed_chunk(8 + c, 12 + c, 2, qb[:, c, 0:T], ("QgT", ti % 2, c), True)
            P.dma("sp", k.QgTd[:, :, t0:t0 + T].rearrange("c p t -> p c t"), qb[:, :, 0:T],
                  reads=[("QgT", ti % 2, c) for c in range(4)], writes=[("QgTd", ti)])
            nq = nqT[ti % 2]
            for c in range(4):
                normed_chunk(16 + c, None, 5, nq[:, c, 0:T], ("nqT", ti % 2, c), False)
            P.dma("sp", k.nqTd[:, :, t0:t0 + T].rearrange("c p t -> p c t"), nq[:, :, 0:T],
                  reads=[("nqT", ti % 2, c) for c in range(4)], writes=[("nqTd", ti)])
    A.release(m)


def l1_na_phase(k):
    P, A, PS = k.P, k.A, k.PS
    m = A.mark()
    nk = A.alloc("nk", [128, 4, TALL], BF16)
    NV = A.alloc("NV", [128, 8, 34, 128], BF16)
    nab = A.alloc("nab", [128, NSIG * 8, 128], BF16)
    nq = [[A.alloc(f"nq{i}{e_}", [128, 4, 128], BF16) for e_ in range(2)] for i in range(2)]
    PT = [A.alloc(f"PT{i}", [128, 8, 128], BF16) for i in range(3)]
    for i in range(2):
        for e_ in range(2):
            P.dve(lambda e: e.memset(nq[i][e_][:], 0.0), writes=[("nq", i, e_)])
    rec = A.alloc("rec", [128, 128], F32)
    nat = [A.alloc(f"nat{i}", [128, 4, 128], BF16) for i in range(2)]
    identb = k.cbf[:, 0, :]
    P.dma("sp", nk[:], k.nkTd.rearrange("c p t -> p c t"), writes=[("nk",)])
    for h in range(8):
        P.dma("sp", NV[:, h], k.NVd[h], writes=[("NV", h)])
    for g in range(8):
        n0, n1 = g * NSIG, (g + 1) * NSIG
        P.dma("sp", nab[:, n0:n1, :], k.wbf["l1_nab"][n0:n1].rearrange("n p q -> p n q"),
              reads=[("wbf", "l1_nab", i) for i in range(n0, n1)], writes=[("nab", g)])
    nabtoks = [("nab", g) for g in range(8)]
    hc = 0
    for i in range(32):
        t0 = NCTX + i * 128
        nqb = nq[i % 2]
        natb = nat[i % 2]
        for e_ in range(2):
            P.dma("sp", nqb[e_][e_ * 64:(e_ + 1) * 64], k.nqTd[:, e_ * 64:(e_ + 1) * 64, t0:t0 + 128].rearrange("c p t -> p c t"),
                  writes=[("nq", i % 2, e_)])
        chunks = [(0, None), (1, None)] + [(2 + kc, NA_MAP[(i, kc)]) for kc in NA_CHUNKS[i]]
        n = len(chunks)
        for h in range(8):
            ch, off = h // 2, (h % 2) * 64
            sb0 = 2 * (hc % 3)
            acc = 6 + (hc % 2)
            pt = PT[hc % 3]
            pttok = ("PTn", hc % 3)
            hc += 1

            def smm(e):
                for idx, (c, sig) in enumerate(chunks):
                    o = PS[sb0 + idx // 4][:, (idx % 4) * 128:(idx % 4 + 1) * 128]
                    e.matmul(o, lhsT=nk[:, ch, c * 128:(c + 1) * 128], rhs=nqb[h % 2][:, ch, :],
                             start=True, stop=(sig is None))
                    if sig is not None:
                        e.matmul(o, lhsT=identb, rhs=nab[:, sig * 8 + h, :], start=False, stop=True)
            P.pe(smm, reads=[("nk",), ("nq", i % 2, h % 2), ("cbf",)] + nabtoks, writes=[("ps", sb0), ("ps", sb0 + 1)])
            P.act(lambda e: e.activation(out=pt[:, 0:4, :], in_=PS[sb0][:, 0:512].rearrange("p (c q) -> p c q", q=128), func=AF.Exp),
                  reads=[("ps", sb0)], writes=[(pttok, 0)])
            P.act(lambda e: e.activation(out=pt[:, 4:n, :], in_=PS[sb0 + 1][:, 0:(n - 4) * 128].rearrange("p (c q) -> p c q", q=128),
                                         func=AF.Exp),
                  reads=[("ps", sb0 + 1)], writes=[(pttok, 1)])

            def pv(e):
                for idx, (c, sig) in enumerate(chunks):
                    e.matmul(PS[acc][:, 0:128], lhsT=NV[:, h, c, :], rhs=pt[:, idx, :], start=(idx == 0), stop=(idx == n - 1))
            P.pe(pv, reads=[("NV", h), (pttok, 0), (pttok, 1)], writes=[("ps", acc)])
            nlo, dlo = (0, 64) if h % 2 == 0 else (64, 0)
            P.dve(lambda e: e.reciprocal(out=rec[nlo:nlo + 64, :], in_=PS[acc][dlo:dlo + 64, 0:128]),
                  reads=[("ps", acc)], writes=[("rec", h % 2)])
            P.dve(lambda e: e.tensor_tensor(out=natb[nlo:nlo + 64, ch, :], in0=PS[acc][nlo:nlo + 64, 0:128], in1=rec[nlo:nlo + 64, :],
                                            op=ALU.mult),
                  reads=[("ps", acc), ("rec", h % 2)], writes=[("nat", i % 2, h)])
        P.dma("sp", k.attTd[4:8, :, t0:t0 + 128].rearrange("c p t -> p c t"), natb[:],
              reads=[("nat", i % 2, h) for h in range(8)], writes=[("attTd", "na", i)])
    A.release(m)
```

```python
import numpy as np
import concourse.bass as bass
import concourse.mybir as mybir
from concourse.bass_utils import run_bass_kernel_spmd

F32 = mybir.dt.float32
BF16 = mybir.dt.bfloat16
AF = mybir.ActivationFunctionType
ALU = mybir.AluOpType

D = 1024
S = 4096
NCTX = 256
TALL = S + NCTX
DFF = 2816
NFC = DFF // 128
EPS = 1e-6
GRID_W = 64
NDMASEM = 40
RING = {"sp": (0, 28), "pool": (28, 12)}
TILES = [(1, 0, 256)] + [(0, 256 + 512 * i, 512) for i in range(8)]


class Op:
    __slots__ = ("stream", "kind", "fn", "idx", "event", "signal", "cclock", "waits", "slot")


class _Rec:
    def __init__(self):
        self.calls = []

    def __getattr__(self, name):
        def f(*a, **kw):
            self.calls.append((name, a, kw))
            return None
        return f


def _replay(calls):
    def fn(e):
        ins = None
        for name, a, kw in calls:
            ins = getattr(e, name)(*a, **kw)
        return ins
    return fn


class Prog:
    STREAMS = ("pe", "act", "dve", "pool", "sp")

    def __init__(self, nc):
        self.nc = nc
        self.ops = {s: [] for s in self.STREAMS}
        self.clock = {s: {} for s in self.STREAMS}
        self.lastw = {}
        self.readers = {}
        self.ncomp = {s: 0 for s in self.STREAMS}
        self.pending = {}
        self.dma_slot_last = [None] * NDMASEM
        self.dma_slot_cnt = [0] * NDMASEM
        self.dma_rr = {"sp": 0, "pool": 0}
        self.dmas_since_barrier = []
        self.bg_next = False
        self.bg = []
        self.last_comp = {}
        self.nops = 0
        self.maxops = None

    def _add(self, stream, kind, fn, reads, writes):
        if self.maxops is not None and self.nops >= self.maxops and fn is not None:
            return None
        op = Op()
        op.stream = stream
        op.kind = kind
        if fn is not None:
            rec = _Rec()
            fn(rec)
            fn = _replay(rec.calls)
        op.fn = fn
        op.signal = False
        op.slot = None
        self.nops += 1
        deps = []
        for r in reads:
            w = self.lastw.get(r)
            if w is not None:
                deps.append((w, True))
        for w_ in writes:
            lw = self.lastw.get(w_)
            if lw is not None:
                deps.append((lw, False))
            for rd in self.readers.get(w_, ()):
                deps.append((rd, False))
        for r in reads:
            self.readers.setdefault(r, []).append(op)
        for w_ in writes:
            self.readers[w_] = []
            self.lastw[w_] = op
        pend = self.pending.pop(stream, None)
        if pend:
            deps.extend((p, True) for p in pend)
        if kind == "d":
            base, cnt_ = RING[stream]
            slot = base + self.dma_rr[stream]
            self.dma_rr[stream] = (self.dma_rr[stream] + 1) % cnt_
            prev = self.dma_slot_last[slot]
            if prev is not None:
                deps.append((prev, True))
            self.dma_slot_cnt[slot] += 1
            self.dma_slot_last[slot] = op
            op.slot = slot
            op.event = (("d", slot), 16 * self.dma_slot_cnt[slot])
            if self.bg_next:
                self.bg.append(op)
            else:
                self.dmas_since_barrier.append(op)
        else:
            self.ncomp[stream] += 1
            op.event = (("e", stream), self.ncomp[stream])
            self.last_comp[stream] = op
        clk = self.clock[stream]
        waits = {}
        for d, raw in deps:
            if d is op:
                continue
            if d.stream == stream and d.kind == "c":
                if stream == "pe" or not raw:
                    continue
            key, val = d.event
            if clk.get(key, 0) >= val:
                continue
            if waits.get(key, (0,))[0] < val:
                waits[key] = (val, d)
            for k, v in d.cclock.items():
                if clk.get(k, 0) < v:
                    clk[k] = v
        for key, (val, d) in waits.items():
            d.signal = True
        op.waits = {k: v[0] for k, v in waits.items()}
        cc = dict(clk)
        cc[op.event[0]] = op.event[1]
        op.cclock = cc
        if kind == "c" and stream != "pe":
            pass
        self.ops[stream].append(op)
        return op

    def pe(self, fn, reads=(), writes=()):
        return self._add("pe", "c", fn, reads, writes)

    def act(self, fn, reads=(), writes=()):
        return self._add("act", "c", fn, reads, writes)

    def dve(self, fn, reads=(), writes=()):
        return self._add("dve", "c", fn, reads, writes)

    def pool(self, fn, reads=(), writes=()):
        return self._add("pool", "c", fn, reads, writes)

    def dma(self, q, out, in_, reads=(), writes=()):
        return self._add(q, "d", lambda e: e.dma_start(out=out, in_=in_), reads, writes)

    def barrier(self):
        lst = list(self.last_comp.values()) + self.dmas_since_barrier
        self.dmas_since_barrier = []
        for s in self.STREAMS:
            self.pending[s] = list(lst)

    def emit(self):
        nc = self.nc
        self.dmas_since_barrier = self.dmas_since_barrier + self.bg
        self.barrier()
        fin = self._add("sp", "c", None, (), ())
        esem = {s: nc.alloc_semaphore("es_" + s) for s in self.STREAMS}
        dsem = [nc.alloc_semaphore(f"ds{i}") for i in range(NDMASEM)]
        sigcount = {}
        for s in self.STREAMS:
            cnt = 0
            m = {}
            for o in self.ops[s]:
                if o.kind == "c":
                    if o.signal:
                        cnt += 1
                    m[o.event[1]] = (cnt, o.signal)
            sigcount[s] = m

        def resolve(key, val):
            if key[0] == "d":
                return dsem[key[1]], val
            cnt, sig = sigcount[key[1]][val]
            assert sig
            return esem[key[1]], cnt

        with nc.Block() as block:
            decos = {"pe": block.tensor, "act": block.scalar, "dve": block.vector,
                     "pool": block.gpsimd, "sp": block.sync}
            for s in self.STREAMS:
                ops = self.ops[s]

                def body(eng, ops=ops, s=s):
                    for o in ops:
                        for key, val in o.waits.items():
                            sem, v = resolve(key, val)
                            eng.wait_ge(sem, v)
                        if o.fn is None:
                            continue
                        ins = o.fn(eng)
                        if o.kind == "d":
                            ins.then_inc(dsem[o.slot], 16)
                        elif o.signal:
                            ins.then_inc(esem[s], 1)
                decos[s](body)


class Arena:
    def __init__(self, nc, limit=229376):
        self.nc = nc
        self.off = 16640
        self.limit = limit
        self.n = 0

    def alloc(self, name, shape, dtype):
        esz = 4 if dtype == F32 else 2
        sz = esz
        for d in shape[1:]:
            sz *= d
        sz = (sz + 63) // 64 * 64
        assert self.off + sz <= self.limit, (name, self.off, sz)
        self.n += 1
        t = self.nc.alloc_sbuf_tensor_at(f"{name}_{self.n}", list(shape), dtype, offset=self.off)
        self.off += sz
        return t

    def mark(self):
        return self.off

    def release(self, m):
        self.off = m


def kmajor(w, ncols_pad=None):
    K, N = w.shape
    return np.ascontiguousarray(w.reshape(K // 128, 128, N).transpose(1, 0, 2))


def pvec(v):
    return np.ascontiguousarray(v.reshape(-1, 128).T)


class K:
    pass


def declare_inputs(k, nc):
    k.inp = {}
    k.inshape = {}

    def din(name, shape):
        k.inp[name] = nc.dram_tensor(name, list(shape), F32, kind="ExternalInput").ap()
        k.inshape[name] = tuple(shape)

    din("x", [S, D])
    din("ctx", [NCTX, D])
    din("cT", [128, 8, 2])
    din("cmat", [128, 4, 128])
    for l in range(2):
        din(f"l{l}_mod_w", [18, 128, 8 * 512])
        din(f"l{l}_mod_b2", [2, 9216])
        din(f"l{l}_ng", [128, 3, 8])
        for w in (1, 2):
            din(f"l{l}_f{w}_win", [NFC, 128, 8 * 256])
            din(f"l{l}_f{w}_wout", [8, 128, NFC * 128])


def host_inputs(inputs, b):
    m = {}
    m["x"] = np.ascontiguousarray(inputs["x"][b])
    m["ctx"] = np.ascontiguousarray(inputs["ctx"][b])
    cT = np.stack([pvec(inputs["c"][b]), pvec(inputs["c_ctx"])], axis=-1)
    m["cT"] = np.ascontiguousarray(cT.astype(np.float32))
    return m


_SHARED = {}


def host_shared(inputs):
    m = {}
    cm = np.zeros((128, 4, 128), np.float32)
    cm[:, 0, :] = np.eye(128, dtype=np.float32)
    cm[:, 1, :] = 1.0
    for g in range(2):
        cm[g * 64:(g + 1) * 64, 2, g * 64:(g + 1) * 64] = 1.0
    cm[0:64, 3, 0:64] = 1.0
    cm[64:96, 3, 64:96] = 1.0
    m["cmat"] = cm
    for l in range(2):
        p = f"l{l}_"
        mw = inputs[p + "mod_w"]
        m[p + "mod_w"] = np.ascontiguousarray(
            mw.reshape(8, 128, 18, 512).transpose(2, 1, 0, 3).reshape(18, 128, 8 * 512))
        m[p + "mod_b2"] = np.ascontiguousarray(np.stack([inputs[p + "mod_b"], inputs[p + "mod_b"]], axis=0))
        m[p + "ng"] = np.ascontiguousarray(np.stack(
            [pvec(inputs[p + "ffn1_norm"]), pvec(inputs[p + "mix_norm"]), pvec(inputs[p + "ffn2_norm"])], axis=1))
        for w in (1, 2):
            wi = inputs[p + f"ffn{w}_w_in"]
            g = wi[:, :DFF].reshape(8, 128, NFC, 128)
            u = wi[:, DFF:].reshape(8, 128, NFC, 128)
            gu = np.stack([g, u], axis=3)
            m[p + f"f{w}_win"] = np.ascontiguousarray(gu.transpose(2, 1, 0, 3, 4).reshape(NFC, 128, 8 * 256))
            wo = inputs[p + f"ffn{w}_w_out"]
            m[p + f"f{w}_wout"] = np.ascontiguousarray(
                wo.reshape(NFC, 128, 8, 128).transpose(2, 1, 0, 3).reshape(8, 128, NFC * 128))
    m.update(host_mixer(inputs))
    return m


def build(stop_after=None, debug=False, skip=(), maxops=None):
    nc = bass.Bass("TRN2", target_bir_lowering=False)
    k = K()
    k.nc = nc
    k.debug = debug
    P = Prog(nc)
    k.maxops = maxops
    k.P = P
    A = Arena(nc)
    k.A = A
    declare_inputs(k, nc)
    k.out = nc.dram_tensor("out", [S, D], F32, kind="ExternalOutput").ap()
    k.dbg_outs = []
    skind = "ExternalOutput" if debug else "Internal"
    k.xTd = nc.dram_tensor("xTd", [8, 128, TALL], F32, kind=skind).ap()
    k.wbf = {}

    def precast(name):
        shape = k.inshape[name]
        t = nc.dram_tensor(name + "_bf", list(shape), BF16, kind="Internal").ap()
        k.wbf[name] = t
        P.bg_next = True
        for i in range(shape[0]):
            P.dma("pool", t[i], k.inp[name][i], reads=(), writes=[("wbf", name, i)])
        P.bg_next = False

    k.PS = [nc.alloc_psum_tensor(f"psb{i}", [128, 512], F32) for i in range(8)]

    k.ident = A.alloc("ident", [128, 128], F32)
    k.cbf = A.alloc("cbf", [128, 4, 128], BF16)
    k.modv = A.alloc("modv", [128, 2, 2, 72], F32)
    k.der = A.alloc("der", [128, 2, 2, 5, 8], F32)
    k.ng = A.alloc("ng", [128, 2, 3, 8], F32)
    k.epsv = A.alloc("epsv", [128, 1], F32)
    P.dve(lambda e: e.memset(k.epsv[:], EPS), writes=[("epsv",)])
    P.dma("sp", k.ident[:], k.inp["cmat"][:, 0, :], writes=[("ident",)])
    P.dma("pool", k.cbf[:], k.inp["cmat"], writes=[("cbf",)])
    for l in range(2):
        P.dma("sp", k.ng[:, l], k.inp[f"l{l}_ng"], writes=[("ng", l)])
    k.ones_bf = k.cbf[:, 1, :]
    k.bd64_bf = k.cbf[:, 2, :]

    for l in range(2):
        for w in (1, 2):
            precast(f"l{l}_f{w}_win")
            precast(f"l{l}_f{w}_wout")

    declare_mixer_inputs(k, nc)
    for name in ("l0_wA", "l0_wkn", "l0_wv", "l0_wq", "l0_wo", "l1_wo", "l1_wA", "l1_wv", "l1_nab"):
        precast(name)

    def dscr(name, shape, dt=BF16):
        return nc.dram_tensor(name, list(shape), dt, kind=skind if name in ("attTd",) else "Internal").ap()
    k.yTd = dscr("yTd", [4, 128, TALL])
    k.KTd = dscr("KTd", [8, 96, TALL])
    k.QTd = dscr("QTd", [8, 96, TALL])
    k.Vd = dscr("Vd", [8, 128, 34, 128])
    k.attTd = dscr("attTd", [8, 128, TALL])
    k.KdupTd = dscr("KdupTd", [2, 128, TALL])
    k.QgTd = dscr("QgTd", [4, 128, TALL])
    k.nkTd = dscr("nkTd", [4, 128, TALL])
    k.nqTd = dscr("nqTd", [4, 128, TALL])
    k.VgD = dscr("VgD", [4, 128, 34, 128])
    k.NVd = dscr("NVd", [8, 128, 34, 128])

    phases = [
        ("mod", lambda: setup_mod(k)),
        ("l0f1", lambda: ffn_phase(k, 0, 1, TILES, src="tok", dst="xT")),
        ("l0proj", lambda: l0_proj_phase(k)),
        ("l0att", lambda: attn_phase(k, "mla")),
        ("l0conv", lambda: l0_conv_phase(k)),
        ("l0mix", lambda: mixout_phase(k, 0, TILES)),
        ("l0", lambda: ffn_phase(k, 0, 2, TILES, src="xT", dst="xT")),
        ("l1f1", lambda: ffn_phase(k, 1, 1, TILES, src="xT", dst="xT")),
        ("l1proj", lambda: l1_proj_phase(k)),
        ("l1gqa", lambda: attn_phase(k, "gqa")),
        ("l1na", lambda: l1_na_phase(k)),
        ("l1mix", lambda: mixout_phase(k, 1, TILES[1:])),
        ("full", lambda: ffn_phase(k, 1, 2, TILES[1:], src="xT", dst="tok")),
    ]
    for name, fn in phases:
        if name in skip:
            continue
        n0 = P.nops
        if maxops is not None and name == stop_after:
            P.maxops = P.nops + maxops
        fn()
        P.barrier()
        if debug:
            print("phase", name, "ops", n0, P.nops, flush=True)
        if stop_after == name:
            break
    return finish(k)


def finish(k):
    k.P.emit()
    return k.nc


def setup_mod(k):
    P, A, nc = k.P, k.A, k.nc
    PS = k.PS
    m = A.mark()
    cT = A.alloc("cT", [128, 8, 2], F32)
    sc = A.alloc("sc", [128, 8, 2], F32)
    mb2 = A.alloc("mb2", [2, 9216], F32)
    modrow = A.alloc("modrow", [2, 9216], F32)
    wt = [A.alloc(f"mw{i}", [128, 8, 512], F32) for i in range(3)]
    P.dma("sp", cT[:], k.inp["cT"], writes=[("cT",)])
    P.act(lambda e: e.activation(out=sc[:], in_=cT[:], func=AF.Silu), reads=[("cT",)], writes=[("sc",)])
    n = 0
    for l in range(2):
        P.dma("sp", mb2[:], k.inp[f"l{l}_mod_b2"], writes=[("mb2",)])
        for blk in range(18):
            w = wt[n % 3]
            wtok = ("mw", n % 3)
            b = n % 4
            n += 1
            P.dma("sp", w[:], k.inp[f"l{l}_mod_w"][blk].rearrange("p (k n) -> p k n", k=8), writes=[wtok])

            def mm(e):
                for kc in range(8):
                    e.matmul(PS[b][0:2, 0:512], lhsT=sc[:, kc, :], rhs=w[:, kc, :], start=(kc == 0), stop=(kc == 7))
            P.pe(mm, reads=[wtok, ("sc",)], writes=[("ps", b)])
            P.dve(lambda e: e.tensor_tensor(out=modrow[0:2, blk * 512:(blk + 1) * 512], in0=PS[b][0:2, 0:512],
                                            in1=mb2[0:2, blk * 512:(blk + 1) * 512], op=ALU.add),
                  reads=[("ps", b), ("mb2",)], writes=[("modrow", blk)])
        pst = PS[4 + l]

        def tr(e):
            for j in range(72):
                e.transpose(pst[:, 2 * j:2 * j + 2], modrow[0:2, j * 128:(j + 1) * 128], k.ident[0:2, 0:2])
        P.pe(tr, reads=[("modrow", blk) for blk in range(18)] + [("ident",)], writes=[("ps", 4 + l)])
        for who in range(2):
            src = pst[:, 0:144].rearrange("p (j w) -> p j w", w=2)[:, :, who]
            P.dve(lambda e: e.tensor_copy(out=k.modv[:, l, who, :], in_=src),
                  reads=[("ps", 4 + l)], writes=[("modv", l, who)])
        for who in range(2):
            for gi, mi in enumerate((1, 4, 7)):
                P.dve(lambda e: e.scalar_tensor_tensor(
                    out=k.der[:, l, who, gi, :], in0=k.modv[:, l, who, mi * 8:mi * 8 + 8], scalar=1.0,
                    in1=k.ng[:, l, gi, :], op0=ALU.add, op1=ALU.mult),
                    reads=[("modv", l, who), ("ng", l)], writes=[("der", l, who, gi)])
            for gi, mi in ((3, 2), (4, 8)):
                P.dve(lambda e: e.tensor_scalar(
                    out=k.der[:, l, who, gi, :], in0=k.modv[:, l, who, mi * 8:mi * 8 + 8], scalar1=0.5,
                    scalar2=None, op0=ALU.mult),
                    reads=[("modv", l, who)], writes=[("der", l, who, gi)])
    if k.debug:
        dm = nc.dram_tensor("dbg_modv", [128, 2 * 2 * 72], F32, kind="ExternalOutput").ap()
        P.dma("sp", dm, k.modv[:].rearrange("p a b c -> p (a b c)"),
              reads=[("modv", l, w) for l in range(2) for w in range(2)], writes=[("dbg_modv",)])
    A.release(m)


def rms_norm_mod(k, xb, xtoks, T, gp, sh, hT, htok, tmp, r1, r2, rstd, sq):
    P = k.P
    P.act(lambda e: e.activation(out=sq[:, :, 0:T], in_=xb[:, :, 0:T], func=AF.Square),
          reads=xtoks, writes=[("sq", c) for c in range(8)])

    def mm(e):
        ins = None
        for c in range(8):
            ins = e.matmul(k.PS[6][:, 0:T], lhsT=k.ones_bf, rhs=sq[:, c, 0:T], start=(c == 0), stop=(c == 7))
        return ins
    P.pe(mm, reads=[("sq", c) for c in range(8)] + [("cbf",)], writes=[("ps", 6)])
    P.act(lambda e: e.activation(out=r2[:, 0:T], in_=k.PS[6][:, 0:T], func=AF.Sqrt, bias=k.epsv[:, 0:1], scale=1.0 / D),
          reads=[("ps", 6), ("epsv",)], writes=[("r2",)])
    P.dve(lambda e: e.reciprocal(out=rstd[:, 0:T], in_=r2[:, 0:T]), reads=[("r2",)], writes=[("rstd",)])
    for c in range(8):
        tb = tmp[c % 2]
        P.dve(lambda e, c=c, tb=tb: e.scalar_tensor_tensor(out=tb[:, 0:T], in0=xb[:, c, 0:T], scalar=gp[:, c:c + 1],
                                                           in1=rstd[:, 0:T], op0=ALU.mult, op1=ALU.mult),
              reads=[xtoks[c], ("rstd",)], writes=[("tmp", c % 2)])
        P.act(lambda e, c=c, tb=tb: e.activation(out=hT[:, c, 0:T], in_=tb[:, 0:T], func=AF.Identity,
                                                 bias=sh[:, c:c + 1], scale=1.0),
              reads=[("tmp", c % 2)], writes=[(htok, c)])


def ffn_phase(k, l, which, tiles, src, dst):
    P, A, nc = k.P, k.A, k.nc
    PS = k.PS
    m = A.mark()
    xt = [A.alloc(f"xt{i}", [128, 8, 512], F32) for i in range(3)]
    hT = [A.alloc(f"hT{i}", [128, 8, 512], BF16) for i in range(2)]
    actT = [A.alloc(f"actT{i}", [128, NFC, 512], BF16) for i in range(2)]
    sq = A.alloc("sq", [128, 8, 512], BF16)
    win = [A.alloc(f"win{i}", [128, 8, 256], BF16) for i in range(3)]
    wout = [A.alloc(f"wout{i}", [128, NFC, 128], BF16) for i in range(2)]
    r1 = A.alloc("r1", [128, 512], F32)
    r2 = A.alloc("r2", [128, 512], F32)
    rstd = A.alloc("rstd", [128, 512], F32)
    tmp = [A.alloc(f"tmp{i}", [128, 512], F32) for i in range(2)]
    sg = [A.alloc(f"sg{i}", [128, 512], F32) for i in range(3)]
    tokb = [A.alloc(f"tokb{i}", [128, 1024], F32) for i in range(2)]
    gi = 0 if which == 1 else 2
    shi = 0 if which == 1 else 6
    hgi = 3 if which == 1 else 4
    wname_in = f"l{l}_f{which}_win"
    wname_out = f"l{l}_f{which}_wout"
    groups = [tiles[i:i + 2] for i in range(0, len(tiles), 2)]
    tk = 0
    wi = 0
    wo = 0
    guc = 0
    yc = 0
    tbc = 0
    for grp in groups:
        bufs = []
        for s, (who, t0, T) in enumerate(grp):
            b = tk % 3
            tk += 1
            bufs.append(b)
            xb = xt[b]
            xtoks = [("xt", b, c) for c in range(8)]
            if src == "xT":
                P.dma("sp", xb[:, :, 0:T], k.xTd[:, :, t0:t0 + T].rearrange("c p t -> p c t"), writes=xtoks)
            else:
                srcap = k.inp["ctx"] if who == 1 else k.inp["x"]
                r0 = t0 if who == 1 else t0 - NCTX
                for st in range(T // 128):
                    tb = tokb[tbc % 2]
                    ttok = ("tokb", tbc % 2)
                    tbc += 1
                    P.dma("sp", tb[:], srcap[r0 + st * 128:r0 + (st + 1) * 128, :], writes=[ttok])
                    for half in range(2):
                        bank = PS[(guc % 4)]
                        btok = ("ps", guc % 4)
                        guc += 1

                        def tr(e, tb=tb, half=half, bank=bank):
                            ins = None
                            for cc in range(4):
                                c = half * 4 + cc
                                ins = e.transpose(bank[:, cc * 128:(cc + 1) * 128], tb[:, c * 128:(c + 1) * 128], k.ident[:])
                            return ins
                        P.pe(tr, reads=[ttok, ("ident",)], writes=[btok])
                        P.act(lambda e, xb=xb, half=half, bank=bank, st=st: e.activation(
                            out=xb[:, half * 4:half * 4 + 4, st * 128:(st + 1) * 128],
                            in_=bank[:].rearrange("p (c t) -> p c t", c=4), func=AF.Copy),
                            reads=[btok], writes=xtoks[half * 4:half * 4 + 4])
            rms_norm_mod(k, xb, xtoks, T, k.der[:, l, who, gi, :], k.modv[:, l, who, shi * 8:shi * 8 + 8],
                         hT[s], ("hT", s), tmp, r1, r2, rstd, sq)
        for j in range(NFC):
            wb = win[wi % 3]
            wtok = ("win", wi % 3)
            wi += 1
            P.dma("sp", wb[:], k.wbf[wname_in][j].rearrange("p (k n) -> p k n", k=8),
                  reads=[("wbf", wname_in, j)], writes=[wtok])
            for s, (who, t0, T) in enumerate(grp):
                bg = (guc % 3) * 2
                guc += 1
                htoks = [(("hT", s), c) for c in range(8)]

                def mmg(e, wb=wb, s=s, T=T, bank=PS[bg], off=0):
                    ins = None
                    for kc in range(8):
                        ins = e.matmul(bank[:, 0:T], lhsT=wb[:, kc, off:off + 128], rhs=hT[s][:, kc, 0:T],
                                       start=(kc == 0), stop=(kc == 7))
                    return ins
                P.pe(mmg, reads=[wtok] + htoks, writes=[("ps", bg)])
                P.pe(lambda e, wb=wb, s=s, T=T, bank=PS[bg + 1], f=mmg: f(e, wb, s, T, bank, 128),
                     reads=[wtok] + htoks, writes=[("ps", bg + 1)])
                q = guc % 3
                P.act(lambda e, q=q, bg=bg, T=T: e.activation(out=sg[q][:, 0:T], in_=PS[bg][:, 0:T], func=AF.Silu),
                      reads=[("ps", bg)], writes=[("sg", q)])
                P.dve(lambda e, q=q, bg=bg, T=T, s=s, j=j: e.tensor_tensor(
                    out=actT[s][:, j, 0:T], in0=PS[bg + 1][:, 0:T], in1=sg[q][:, 0:T], op=ALU.mult),
                    reads=[("ps", bg + 1), ("sg", q)], writes=[("actT", s, j)])
        for d in range(8):
            wb = wout[wo % 2]
            wtok = ("wout", wo % 2)
            wo += 1
            P.dma("sp", wb[:], k.wbf[wname_out][d].rearrange("p (k n) -> p k n", k=NFC),
                  reads=[("wbf", wname_out, d)], writes=[wtok])
            for s, (who, t0, T) in enumerate(grp):
                by = 6 + (yc % 2)
                yc += 1
                xb = xt[bufs[s]]

                def mmy(e, wb=wb, s=s, T=T, by=by):
                    ins = None
                    for fc in range(NFC):
                        ins = e.matmul(PS[by][:, 0:T], lhsT=wb[:, fc, :], rhs=actT[s][:, fc, 0:T],
                                       start=(fc == 0), stop=(fc == NFC - 1))
                    return ins
                P.pe(mmy, reads=[wtok] + [("actT", s, j) for j in range(NFC)], writes=[("ps", by)])
                hg = k.der[:, l, who, hgi, :]
                P.dve(lambda e, xb=xb, d=d, T=T, by=by, hg=hg: e.scalar_tensor_tensor(
                    out=xb[:, d, 0:T], in0=PS[by][:, 0:T], scalar=hg[:, d:d + 1], in1=xb[:, d, 0:T],
                    op0=ALU.mult, op1=ALU.add),
                    reads=[("ps", by), ("xt", bufs[s], d)], writes=[("xt", bufs[s], d)])
        for s, (who, t0, T) in enumerate(grp):
            b = bufs[s]
            xb = xt[b]
            xtoks = [("xt", b, c) for c in range(8)]
            if dst == "xT":
                P.dma("sp", k.xTd[:, :, t0:t0 + T].rearrange("c p t -> p c t"), xb[:, :, 0:T], reads=xtoks,
                      writes=[("xTd", t0)])
            else:
                r0 = t0 - NCTX
                for st in range(T // 128):
                    ob = tokb[tbc % 2]
                    otok = ("tokb", tbc % 2)
                    tbc += 1
                    for half in range(2):
                        bi = guc % 4
                        guc += 1

                        def tr(e, xb=xb, half=half, bi=bi, st=st):
                            ins = None
                            for cc in range(4):
                                c = half * 4 + cc
                                ins = e.transpose(PS[bi][:, cc * 128:(cc + 1) * 128], xb[:, c, st * 128:(st + 1) * 128],
                                                  k.ident[:])
                            return ins
                        P.pe(tr, reads=xtoks[half * 4:half * 4 + 4] + [("ident",)], writes=[("ps", bi)])
                        P.act(lambda e, ob=ob, half=half, bi=bi: e.activation(
                            out=ob[:, half * 512:(half + 1) * 512], in_=PS[bi][:], func=AF.Copy),
                            reads=[("ps", bi)], writes=[(otok, half)])
                    P.dma("sp", k.out[r0 + st * 128:r0 + (st + 1) * 128, :], ob[:], reads=[(otok, 0), (otok, 1)],
                          writes=[("out", r0, st)])
    A.release(m)


_CACHE = {}


def kernel(**inputs):
    inputs = {kk: np.asarray(v) for kk, v in inputs.items()}
    if "nc" not in _CACHE:
        _CACHE["nc"] = build()
    nc = _CACHE["nc"]
    shared = host_shared(inputs)
    in_maps = []
    for b in range(8):
        mm = dict(shared)
        mm.update(host_inputs(inputs, b))
        in_maps.append(mm)
    res = run_bass_kernel_spmd(nc, in_maps, core_ids=list(range(8)))
    return np.stack([r["out"] for r in res.results], axis=0)


NEG = -30000.0
PERM64 = list(range(0, 64, 2)) + list(range(1, 64, 2))
SWAP64 = list(range(1, 64, 2)) + list(range(0, 64, 2))
PERM32 = list(range(0, 32, 2)) + list(range(1, 32, 2))
SWAP32 = list(range(1, 32, 2)) + list(range(0, 32, 2))


def rope_tables():
    t = np.arange(S)
    row = (t // GRID_W).astype(np.float32)
    col = (t % GRID_W).astype(np.float32)

    def tab(dim):
        npairs = dim // 4
        inv = (10000.0 ** (-np.arange(npairs, dtype=np.float32) / npairs)).astype(np.float32)
        ang = np.concatenate([row[:, None] * inv, col[:, None] * inv], axis=-1).astype(np.float32)
        return np.cos(ang).astype(np.float32).T, np.sin(ang).astype(np.float32).T
    cA, sA = tab(64)
    cB, sB = tab(32)
    ta = np.zeros((128, 2, S), np.float32)
    for p in range(128):
        pp = p % 64
        pr = pp % 32
        ta[p, 0] = cA[pr]
        ta[p, 1] = -sA[pr] if pp < 32 else sA[pr]
    tb = np.zeros((128, 2, S), np.float32)
    for p in range(64, 96):
        pp = p - 64
        pr = pp % 16
        tb[p, 0] = cB[pr]
        tb[p, 1] = -sB[pr] if pp < 16 else sB[pr]
    return ta, tb


def na_geometry():
    sigs = []
    mp = {}
    chunks = {}
    for i in range(32):
        rs0 = min(max(2 * i - 4, 0), 56)
        rs1 = min(max(2 * i + 1 - 4, 0), 56)
        lo = rs0 // 2
        hi = (rs1 + 7) // 2
        chunks[i] = list(range(lo, hi + 1))
        for kc in chunks[i]:
            sig = (rs0 - 2 * i, rs1 - (2 * i + 1), kc - i)
            if sig not in sigs:
                sigs.append(sig)
            mp[(i, kc)] = sigs.index(sig)
    return sigs, mp, chunks


NA_SIGS, NA_MAP, NA_CHUNKS = na_geometry()
NSIG = len(NA_SIGS)


def na_bias_tiles(rpb):
    out = np.full((NSIG, 8, 128, 128), NEG, np.float32)
    kp = np.arange(128)
    qf = np.arange(128)
    kr_, kc_ = kp // 64, kp % 64
    qr_, qc_ = qf // 64, qf % 64
    cs = np.clip(qc_ - 8, 0, 48)
    for si, (a0, a1, dk) in enumerate(NA_SIGS):
        rsrel = np.where(qr_ == 0, a0, a1)
        krel = (2 * dk + kr_[:, None]) - qr_[None, :]
        vr = (krel >= rsrel[None, :]) & (krel <= rsrel[None, :] + 7)
        vc = (kc_[:, None] >= cs[None, :]) & (kc_[:, None] <= cs[None, :] + 15)
        valid = vr & vc
        dr = np.clip(krel + 7, 0, 14)
        dc = np.clip(kc_[:, None] - qc_[None, :] + 15, 0, 30)
        for h in range(8):
            g = rpb[h][dr, dc]
            out[si, h] = np.where(valid, g, np.float32(NEG))
    return out.reshape(NSIG * 8, 128, 128)


def blk(w, cols):
    K_ = w.shape[0]
    o = np.zeros((K_, len(cols)), np.float32)
    idx = [i for i, c in enumerate(cols) if c is not None]
    o[:, idx] = w[:, [cols[i] for i in idx]]
    return o.reshape(K_ // 128, 128, len(cols)).transpose(1, 0, 2)


def host_mixer(inputs):
    m = {}
    ta, tb = rope_tables()
    m["ropeA"] = ta
    m["ropeB"] = tb
    w = inputs["l0_w_in"]
    blocks = []
    blocks.append(blk(w, list(range(0, 128))))
    blocks.append(blk(w, list(range(128, 256))))
    blocks.append(blk(w, [None] * 64 + [256 + i for i in PERM32] + [None] * 32))
    blocks.append(blk(w, [None] * 64 + [256 + i for i in SWAP32] + [None] * 32))
    for c in range(3):
        blocks.append(blk(w, list(range(288 + c * 128, 288 + (c + 1) * 128))))
    for c in range(8):
        blocks.append(blk(w, list(range(672 + c * 128, 672 + (c + 1) * 128))))
    m["l0_wA"] = np.ascontiguousarray(np.stack(blocks, 0).reshape(15, 128, 8 * 128))
    wk = inputs["l0_mla_w_ukv"]
    kb = [blk(wk, [(2 * pr) * 128 + i for i in range(64)] + [(2 * pr + 1) * 128 + i for i in range(64)]) for pr in range(4)]
    m["l0_wkn"] = np.ascontiguousarray(np.stack(kb, 0).reshape(4, 128, 2 * 128))
    m["l0_wv"] = np.ascontiguousarray(blk(wk, [h * 128 + 64 + i for h in range(8) for i in range(64)]).reshape(1, 128, 2 * 512))
    wq = inputs["l0_mla_w_uq"]
    qb = []
    for h in range(8):
        qb.append(blk(wq, [h * 96 + i for i in range(64)] + [h * 96 + 64 + i for i in PERM32]))
    for h in range(8):
        qb.append(blk(wq, [None] * 64 + [h * 96 + 64 + i for i in SWAP32]))
    m["l0_wq"] = np.ascontiguousarray(np.stack(qb, 0).reshape(16, 128, 3 * 96))
    for l in range(2):
        wo = inputs[f"l{l}_w_out"]
        m[f"l{l}_wo"] = np.ascontiguousarray(wo.reshape(8, 128, 8, 128).transpose(2, 1, 0, 3).reshape(8, 128, 8 * 128))
    m["l0_dw"] = np.ascontiguousarray(inputs["l0_conv_dw_w"].reshape(31, 4, 128).transpose(2, 1, 0))
    v = np.zeros((128, 32), np.float32)
    v[:, 0:2] = pvec(inputs["l0_mla_kv_norm"])
    v[:, 2:5] = pvec(inputs["l0_mla_q_norm"])
    kg = inputs["l0_mla_k_gain"]
    qg = inputs["l0_mla_q_gain"]
    v[:, 5] = np.tile(kg[:64], 2)
    v[64:96, 6] = kg[64:][PERM32]
    v[64:96, 7] = kg[64:][SWAP32]
    v[0:64, 8] = qg[:64]
    v[64:96, 8] = qg[64:][PERM32]
    v[64:96, 9] = qg[64:][SWAP32]
    v[0:64, 10] = 1.0 / 64
    v[64:96, 10] = 1.0 / 32
    gb = inputs["l0_conv_glu_b"]
    v[:, 11:15] = pvec(gb[:512])
    v[:, 15:19] = pvec(gb[512:])
    v[:, 19:23] = pvec(inputs["l0_conv_dw_b"])
    v[:, 23:27] = pvec(inputs["l0_conv_ln_g"])
    v[:, 27:31] = pvec(inputs["l0_conv_ln_b"])
    m["l0_vec"] = v
    w = inputs["l1_w_in"]
    blocks = []
    for hk in range(2):
        blocks.append(blk(w, [hk * 64 + i for i in PERM64] * 2))
    for hk in range(2):
        blocks.append(blk(w, [hk * 64 + i for i in SWAP64] * 2))
    for c in range(4):
        blocks.append(blk(w, list(range(256 + c * 128, 256 + (c + 1) * 128))))
    for c in range(4):
        blocks.append(blk(w, [1280 + (2 * c + e) * 64 + i for e in range(2) for i in PERM64]))
    for c in range(4):
        blocks.append(blk(w, [1280 + (2 * c + e) * 64 + i for e in range(2) for i in SWAP64]))
    for c in range(4):
        blocks.append(blk(w, list(range(1792 + c * 128, 1792 + (c + 1) * 128))))
    m["l1_wA"] = np.ascontiguousarray(np.stack(blocks, 0).reshape(20, 128, 8 * 128))
    m["l1_wv"] = np.ascontiguousarray(blk(w, list(range(128, 256)) + list(range(768, 1280))).reshape(1, 128, 8 * 640))
    v = np.zeros((128, 8), np.float32)
    v[:, 0] = np.tile(inputs["l1_gqa_k_gain"][PERM64], 2)
    v[:, 1] = np.tile(inputs["l1_gqa_k_gain"][SWAP64], 2)
    v[:, 2] = np.tile(inputs["l1_gqa_q_gain"][PERM64], 2)
    v[:, 3] = np.tile(inputs["l1_gqa_q_gain"][SWAP64], 2)
    v[:, 4] = np.tile(inputs["l1_na_k_gain"], 2)
    v[:, 5] = np.tile(inputs["l1_na_q_gain"], 2)
    m["l1_vec"] = v
    m["l1_nab"] = na_bias_tiles(inputs["l1_na_rpb"])
    return m


def declare_mixer_inputs(k, nc):
    def din(name, shape):
        k.inp[name] = nc.dram_tensor(name, list(shape), F32, kind="ExternalInput").ap()
        k.inshape[name] = tuple(shape)
    din("ropeA", [128, 2, S])
    din("ropeB", [128, 2, S])
    din("l0_wA", [15, 128, 1024])
    din("l0_wkn", [4, 128, 256])
    din("l0_wv", [1, 128, 1024])
    din("l0_wq", [16, 128, 288])
    din("l0_wo", [8, 128, 1024])
    din("l1_wo", [8, 128, 1024])
    din("l0_dw", [128, 4, 31])
    din("l0_vec", [128, 32])
    din("l1_wA", [20, 128, 1024])
    din("l1_wv", [1, 128, 8 * 640])
    din("l1_vec", [128, 8])
    din("l1_nab", [NSIG * 8, 128, 128])


def stats_rstd(k, st, rows, T, scale, extra_reads=()):
    P = k.P
    i = st["i"]
    r2, rstd, bank = st["r2"], st["rstd"], st["bank"]
    P.act(lambda e: e.activation(out=r2[rows, 0:T], in_=k.PS[bank][rows, 0:T], func=AF.Sqrt, bias=k.epsv[rows, 0:1], scale=scale),
          reads=[("ps", bank), ("epsv",)] + list(extra_reads), writes=[("r2", i)])
    P.dve(lambda e: e.reciprocal(out=rstd[rows, 0:T], in_=r2[rows, 0:T]), reads=[("r2", i)], writes=[("rstd", i)])


class PsRot:
    def __init__(self, banks=(0, 1, 2, 3, 4, 5)):
        self.banks = banks
        self.i = 0

    def next(self):
        b = self.banks[self.i % len(self.banks)]
        self.i += 1
        return b


def load_x_norm(k, l, who, t0, T, xb, xtoks, hT, w):
    P = k.P
    P.dma("sp", xb[:, :, 0:T], k.xTd[:, :, t0:t0 + T].rearrange("c p t -> p c t"), writes=xtoks)
    rms_norm_mod(k, xb, xtoks, T, k.der[:, l, who, 1, :], k.modv[:, l, who, 24:32], hT, "hTm",
                 w["tmp"], w["r1"], w["r2"], w["rstd"], w["sq"])


def mm_group(k, bank_ap, lhs_fn, rhs_fn, n, reads, btok):
    def f(e):
        ins = None
        for kc in range(n):
            ins = e.matmul(bank_ap, lhsT=lhs_fn(kc), rhs=rhs_fn(kc), start=(kc == 0), stop=(kc == n - 1))
        return ins
    k.P.pe(f, reads=reads, writes=[btok])


def common_work(k, A):
    w = {}
    w["tmp"] = [A.alloc(f"tmp{i}", [128, 512], F32) for i in range(2)]
    w["r1"] = A.alloc("r1", [128, 512], F32)
    w["r2"] = A.alloc("r2", [128, 512], F32)
    w["rstd"] = A.alloc("rstd", [128, 512], F32)
    w["sq"] = A.alloc("sq", [128, 8, 512], BF16)
    w["stats"] = [dict(i=i, r1=A.alloc(f"sr1_{i}", [128, 512], F32), r2=A.alloc(f"sr2_{i}", [128, 512], F32),
                       rstd=A.alloc(f"srstd_{i}", [128, 512], F32)) for i in range(3)]
    w["si"] = 0
    w["sqs"] = [A.alloc(f"sqs{i}", [128, 512], BF16) for i in range(4)]
    w["sqi"] = 0
    w["rab"] = [(A.alloc(f"ra{i}", [128, 512], F32), A.alloc(f"rb{i}", [128, 512], F32)) for i in range(2)]
    w["rai"] = 0
    return w


def l0_proj_phase(k):
    P, A, nc, PS = k.P, k.A, k.nc, k.PS
    l = 0
    m = A.mark()
    w = common_work(k, A)
    xt = [A.alloc(f"xt{i}", [128, 8, 512], F32) for i in range(2)]
    hT = A.alloc("hTm", [128, 8, 512], BF16)
    wA = A.alloc("wA", [128, 15, 8, 128], BF16)
    wkn = A.alloc("wkn", [128, 4, 2, 128], BF16)
    wv = A.alloc("wv", [128, 2, 512], BF16)
    wq = A.alloc("wq", [128, 16, 3, 96], BF16)
    vec = A.alloc("vec", [128, 32], F32)
    rope = [A.alloc(f"rope{i}", [128, 2, 512], F32) for i in range(2)]
    ckvT = A.alloc("ckvT", [128, 2, 512], BF16)
    cqT = A.alloc("cqT", [128, 3, 512], BF16)
    krT = A.alloc("krT", [128, 512], BF16)
    yT = [A.alloc(f"yT{i}", [128, 4, 512], BF16) for i in range(2)]
    KT = [A.alloc(f"KT{i}", [128, 8, 512], BF16) for i in range(2)]
    QT = [A.alloc(f"QT{i}", [128, 8, 512], BF16) for i in range(2)]
    Vt = [A.alloc(f"Vt{i}", [128, 8, 128], BF16) for i in range(2)]
    sig = [A.alloc(f"sig{i}", [128, 512], F32) for i in range(2)]
    P.dma("sp", vec[:], k.inp["l0_vec"], writes=[("vec",)])
    for i in range(15):
        P.dma("sp", wA[:, i], k.wbf["l0_wA"][i].rearrange("p (k n) -> p k n", k=8), reads=[("wbf", "l0_wA", i)], writes=[("wA",)])
    for i in range(4):
        P.dma("sp", wkn[:, i], k.wbf["l0_wkn"][i].rearrange("p (k n) -> p k n", k=2), reads=[("wbf", "l0_wkn", i)], writes=[("wkn",)])
    P.dma("sp", wv[:], k.wbf["l0_wv"][0].rearrange("p (k n) -> p k n", k=2), reads=[("wbf", "l0_wv", 0)], writes=[("wv",)])
    for i in range(16):
        P.dma("sp", wq[:, i], k.wbf["l0_wq"][i].rearrange("p (k n) -> p k n", k=3), reads=[("wbf", "l0_wq", i)], writes=[("wq",)])
    sc = 96.0 ** -0.5
    P.dve(lambda e: e.tensor_scalar(out=vec[:, 8:10], in0=vec[:, 8:10], scalar1=sc, scalar2=None, op0=ALU.mult),
          reads=[("vec",)], writes=[("vec",)])
    for i in range(2):
        P.dve(lambda e, i=i: e.memset(Vt[i][:], 1.0), writes=[("Vt", i), (("Vt", i), 1)])
    rot = PsRot()
    vcnt = 0
    ALLR = slice(0, 128)
    R = slice(64, 96)
    R96 = slice(0, 96)
    for ti, (who, t0, T) in enumerate(TILES):
        lat = (who == 0)
        xb = xt[ti % 2]
        xtoks = [("xtm", ti % 2, c) for c in range(8)]
        load_x_norm(k, l, who, t0, T, xb, xtoks, hT, w)
        htoks = [("hTm", c) for c in range(8)]
        rp = rope[ti % 2]
        if lat:
            P.dma("sp", rp[:, :, 0:T], k.inp["ropeB"][:, :, t0 - NCTX:t0 - NCTX + T], writes=[("rope", ti % 2)])

        def projA(bidx, M):
            b = rot.next()
            mm_group(k, PS[b][0:M, 0:T], lambda kc: wA[:, bidx, kc, 0:M], lambda kc: hT[:, kc, 0:T], 8,
                     [("wA",)] + htoks, ("ps", b))
            return b

        def square(b, M, dst_ap, dtok):
            P.act(lambda e: e.activation(out=dst_ap, in_=PS[b][0:M, 0:T], func=AF.Square), reads=[("ps", b)], writes=[dtok])

        bs = [projA(0, 128), projA(1, 128)]
        for c in range(2):
            square(bs[c], 128, w["sq"][:, c, 0:T], ("sq", c))
        st = next_stat(w)
        mm_group(k, PS[st["bank"]][:, 0:T], lambda kc: k.ones_bf, lambda kc: w["sq"][:, kc, 0:T], 2,
                 [("sq", 0), ("sq", 1), ("cbf",)], ("ps", st["bank"]))
        stats_rstd(k, st, ALLR, T, 1.0 / 256)
        for c in range(2):
            P.dve(lambda e: e.scalar_tensor_tensor(out=ckvT[:, c, 0:T], in0=PS[bs[c]][:, 0:T], scalar=vec[:, c:c + 1],
                                                   in1=st["rstd"][:, 0:T], op0=ALU.mult, op1=ALU.mult),
                  reads=[("ps", bs[c]), ("rstd", st["i"]), ("vec",)], writes=[("ckvT", c)])
        bm = projA(2, 96)
        bsw = projA(3, 96) if lat else None
        sb, stok = next_sqs(w)
        square(bm, 96, sb[0:96, 0:T], stok)
        st = next_stat(w)
        mm_group(k, PS[st["bank"]][0:96, 0:T], lambda kc: k.cbf[0:96, 3, 0:96], lambda kc: sb[0:96, 0:T], 1, [stok, ("cbf",)],
                 ("ps", st["bank"]))
        stats_rstd(k, st, R, T, 1.0 / 32)
        if not lat:
            P.dve(lambda e: e.scalar_tensor_tensor(out=krT[R, 0:T], in0=PS[bm][R, 0:T], scalar=vec[R, 6:7],
                                                   in1=st["rstd"][R, 0:T], op0=ALU.mult, op1=ALU.mult),
                  reads=[("ps", bm), ("rstd", st["i"]), ("vec",)], writes=[("krT",)])
        else:
            rope_apply(k, w, bm, bsw, R, T, vec[R, 6:7], vec[R, 7:8], st, rp, ("rope", ti % 2), krT[R, 0:T], [("krT",)])
        bs = [projA(4 + c, 128) for c in range(3)]
        for c in range(3):
            square(bs[c], 128, w["sq"][:, 2 + c, 0:T], ("sq", 2 + c))
        st = next_stat(w)
        mm_group(k, PS[st["bank"]][:, 0:T], lambda kc: k.ones_bf, lambda kc: w["sq"][:, 2 + kc, 0:T], 3,
                 [("sq", 2), ("sq", 3), ("sq", 4), ("cbf",)], ("ps", st["bank"]))
        stats_rstd(k, st, ALLR, T, 1.0 / 384)
        for c in range(3):
            P.dve(lambda e: e.scalar_tensor_tensor(out=cqT[:, c, 0:T], in0=PS[bs[c]][:, 0:T], scalar=vec[:, 2 + c:3 + c],
                                                   in1=st["rstd"][:, 0:T], op0=ALU.mult, op1=ALU.mult),
                  reads=[("ps", bs[c]), ("rstd", st["i"]), ("vec",)], writes=[("cqT", c)])
        yb = yT[ti % 2]
        for ch in range(4):
            ba_ = projA(7 + ch, 128)
            bg_ = projA(11 + ch, 128)
            sg_ = sig[ch % 2]
            P.act(lambda e: e.activation(out=sg_[:, 0:T], in_=PS[bg_][:, 0:T], func=AF.Sigmoid,
                                         bias=vec[:, 15 + ch:16 + ch], scale=1.0),
                  reads=[("ps", bg_), ("vec",)], writes=[("sig", ch % 2)])
            P.dve(lambda e: e.scalar_tensor_tensor(out=yb[:, ch, 0:T], in0=PS[ba_][:, 0:T],
                                                   scalar=vec[:, 11 + ch:12 + ch], in1=sg_[:, 0:T],
                                                   op0=ALU.add, op1=ALU.mult),
                  reads=[("ps", ba_), ("sig", ch % 2), ("vec",)], writes=[("yT", ti % 2, ch)])
        P.dma("sp", k.yTd[:, :, t0:t0 + T].rearrange("c p t -> p c t"), yb[:, :, 0:T],
              reads=[("yT", ti % 2, ch) for ch in range(4)], writes=[("yTd", ti)])
        Kb = KT[ti % 2]
        ktoks = [("KT", ti % 2, h) for h in range(8)]
        for pr in range(4):
            b = rot.next()
            mm_group(k, PS[b][:, 0:T], lambda kc: wkn[:, pr, kc, :], lambda kc: ckvT[:, kc, 0:T], 2,
                     [("wkn",), ("ckvT", 0), ("ckvT", 1)], ("ps", b))
            sb, stok = next_sqs(w)
            square(b, 128, sb[:, 0:T], stok)
            st = next_stat(w)
            mm_group(k, PS[st["bank"]][:, 0:T], lambda kc: k.bd64_bf, lambda kc: sb[:, 0:T], 1, [stok, ("cbf",)], ("ps", st["bank"]))
            stats_rstd(k, st, ALLR, T, 1.0 / 64)
            for e_ in range(2):
                h = 2 * pr + e_
                P.dve(lambda e: e.scalar_tensor_tensor(
                    out=Kb[0:64, h, 0:T], in0=PS[b][e_ * 64:(e_ + 1) * 64, 0:T], scalar=vec[e_ * 64:(e_ + 1) * 64, 5:6],
                    in1=st["rstd"][e_ * 64:(e_ + 1) * 64, 0:T], op0=ALU.mult, op1=ALU.mult),
                    reads=[("ps", b), ("rstd", st["i"]), ("vec",)], writes=[("KT", ti % 2, h)])
        for h in range(8):
            P.pool(lambda e: e.tensor_copy(out=Kb[R, h, 0:T], in_=krT[R, 0:T]), reads=[("krT",)], writes=[(("KT", ti % 2, h), "r")])
        P.dma("sp", k.KTd[:, :, t0:t0 + T].rearrange("h p t -> p h t"), Kb[0:96, :, 0:T],
              reads=ktoks + [(kt_, "r") for kt_ in ktoks], writes=[("KTd", ti)])
        for s_ in range(T // 128):
            b = rot.next()
            mm_group(k, PS[b][:, 0:512], lambda kc: ckvT[:, kc, s_ * 128:(s_ + 1) * 128], lambda kc: wv[:, kc, :], 2,
                     [("wv",), ("ckvT", 0), ("ckvT", 1)], ("ps", b))
            vb = Vt[vcnt % 2]
            vtok = ("Vt", vcnt % 2)
            vcnt += 1
            src = PS[b][:, 0:512].rearrange("p (h two d) -> p h two d", two=2, d=64)
            dst = vb[:].rearrange("p (h two) d -> p h two d", two=2)
            P.dve(lambda e: e.tensor_copy(out=dst[:, :, 0, 0:64], in_=src[:, :, 0, :]), reads=[("ps", b)], writes=[vtok])
            P.dve(lambda e: e.tensor_copy(out=dst[:, :, 1, 64:128], in_=src[:, :, 1, :]), reads=[("ps", b)], writes=[(vtok, 1)])
            chunk = (t0 + s_ * 128) // 128
            P.dma("sp", k.Vd[:, :, chunk, :].rearrange("h p d -> p h d"), vb[:], reads=[vtok, (vtok, 1)], writes=[("Vd", chunk)])
        Qb = QT[ti % 2]
        qtoks = [("QT", ti % 2, h) for h in range(8)]
        for h in range(8):
            b = rot.next()
            mm_group(k, PS[b][0:96, 0:T], lambda kc: wq[:, h, kc, :], lambda kc: cqT[:, kc, 0:T], 3,
                     [("wq",)] + [("cqT", c) for c in range(3)], ("ps", b))
            bsw = None
            if lat:
                bsw = rot.next()
                mm_group(k, PS[bsw][0:96, 0:T], lambda kc: wq[:, 8 + h, kc, :], lambda kc: cqT[:, kc, 0:T], 3,
                         [("wq",)] + [("cqT", c) for c in range(3)], ("ps", bsw))
            sb, stok = next_sqs(w)
            square(b, 96, sb[0:96, 0:T], stok)
            st = next_stat(w)
            mm_group(k, PS[st["bank"]][0:96, 0:T], lambda kc: k.cbf[0:96, 3, 0:96], lambda kc: sb[0:96, 0:T], 1, [stok, ("cbf",)],
                     ("ps", st["bank"]))
            stats_rstd(k, st, R96, T, vec[0:96, 10:11], extra_reads=[("vec",)])
            nr = 96 if not lat else 64
            P.dve(lambda e: e.scalar_tensor_tensor(out=Qb[0:nr, h, 0:T], in0=PS[b][0:nr, 0:T], scalar=vec[0:nr, 8:9],
                                                   in1=st["rstd"][0:nr, 0:T], op0=ALU.mult, op1=ALU.mult),
                  reads=[("ps", b), ("rstd", st["i"]), ("vec",)], writes=[("QT", ti % 2, h)])
            if lat:
                rope_apply(k, w, b, bsw, R, T, vec[R, 8:9], vec[R, 9:10], st, rp, ("rope", ti % 2),
                           Qb[R, h, 0:T], [(("QT", ti % 2, h), "r")])
        P.dma("sp", k.QTd[:, :, t0:t0 + T].rearrange("h p t -> p h t"), Qb[0:96, :, 0:T],
              reads=qtoks + [(q_, "r") for q_ in qtoks], writes=[("QTd", ti)])
    A.release(m)


def rope_apply(k, w, bm, bsw, R, T, g, gsw, st, rp, rptok, out_ap, otoks):
    P = k.P
    j = w["rai"] % 2
    w["rai"] += 1
    ra, rb = w["rab"][j]
    rstd, rtok = st["rstd"], ("rstd", st["i"])
    P.dve(lambda e: e.scalar_tensor_tensor(out=ra[R, 0:T], in0=k.PS[bm][R, 0:T], scalar=g, in1=rstd[R, 0:T],
                                           op0=ALU.mult, op1=ALU.mult),
          reads=[("ps", bm), rtok, ("vec",)], writes=[("ra", j)])
    P.dve(lambda e: e.scalar_tensor_tensor(out=rb[R, 0:T], in0=k.PS[bsw][R, 0:T], scalar=gsw, in1=rstd[R, 0:T],
                                           op0=ALU.mult, op1=ALU.mult),
          reads=[("ps", bsw), rtok, ("vec",)], writes=[("rb", j)])
    P.pool(lambda e: e.tensor_tensor(out=ra[R, 0:T], in0=ra[R, 0:T], in1=rp[R, 0, 0:T], op=ALU.mult),
           reads=[("ra", j), rptok], writes=[("ra", j)])
    P.pool(lambda e: e.tensor_tensor(out=rb[R, 0:T], in0=rb[R, 0:T], in1=rp[R, 1, 0:T], op=ALU.mult),
           reads=[("rb", j), rptok], writes=[("rb", j)])
    P.dve(lambda e: e.tensor_tensor(out=out_ap, in0=ra[R, 0:T], in1=rb[R, 0:T], op=ALU.add),
          reads=[("ra", j), ("rb", j)], writes=otoks)


def next_stat(w):
    st = w["stats"][w["si"] % len(w["stats"])]
    st["bank"] = 6 + (w["si"] % 2)
    w["si"] += 1
    return st


def next_sqs(w):
    i = w["sqi"] % len(w["sqs"])
    w["sqi"] += 1
    return w["sqs"][i], ("sqs", i)


def attn_phase(k, mode):
    P, A, PS = k.P, k.A, k.PS
    m = A.mark()
    mla = (mode == "mla")
    Kt = [A.alloc(f"Kt{i}", [128, TALL], BF16) for i in range(4)]
    Vh = [A.alloc(f"Vh{i}", [128, 34, 128], BF16) for i in range(4)]
    Qt = [A.alloc(f"Qt{i}", [128, 2, 512], BF16) for i in range(2)]
    PT = [A.alloc(f"PT{i}", [128, 512], BF16) for i in range(4)]
    rec = A.alloc("rec", [128, 512], F32)
    attT = [A.alloc(f"attT{i}", [128, 512], BF16) for i in range(2)]
    qtiles = TILES if mla else TILES[1:]
    qc = 0
    sc = 0
    pc = 0
    ac = 0
    for hp in range(4):
        kb = (hp % 2) * 2
        if mla:
            for e_ in range(2):
                P.dma("sp", Kt[kb + e_][0:96, :], k.KTd[2 * hp + e_], writes=[("Kt", kb + e_)])
                P.dma("sp", Vh[kb + e_][:], k.Vd[2 * hp + e_], writes=[("Vh", kb + e_)])
        else:
            for e_ in range(2):
                P.dma("sp", Kt[kb + e_][:], k.KdupTd[hp // 2], writes=[("Kt", kb + e_)])
                P.dve(lambda e: e.memset(Kt[kb + e_][(1 - e_) * 64:(2 - e_) * 64, :], 0.0), writes=[("Kt", kb + e_)])
                P.dma("sp", Vh[kb + e_][:], k.VgD[(hp // 2) * 2 + e_], writes=[("Vh", kb + e_)])
        for (who, t0, T) in qtiles:
            chunks = [0, 1] if who == 1 else list(range(34))
            Qb = Qt[qc % 2]
            qtok = ("Qt", qc % 2)
            ab = attT[qc % 2]
            atok = ("attT", qc % 2)
            qc += 1
            if mla:
                P.dma("sp", Qb[0:96, :, 0:T], k.QTd[2 * hp:2 * hp + 2, :, t0:t0 + T].rearrange("h p t -> p h t"), writes=[qtok])
            else:
                P.dma("sp", Qb[:, 0, 0:T], k.QgTd[hp, :, t0:t0 + T], writes=[qtok])
            for e_ in range(2):
                if mla:
                    Ksb, ktok, r0, r1 = Kt[kb + e_], ("Kt", kb + e_), 0, 96
                    qap = Qb[0:96, e_, 0:T]
                else:
                    Ksb, ktok, r0, r1 = Kt[kb + e_], ("Kt", kb + e_), 0, 128
                    qap = Qb[:, 0, 0:T]
                Vsb, vtok = Vh[kb + e_], ("Vh", kb + e_)
                acc = 4 + (ac % 2)
                ac += 1
                n = len(chunks)
                pend = []
                LOOK = 2

                def do_pv(item):
                    pidx, pc_, ppb = item
                    P.pe(lambda e: e.matmul(PS[acc][:, 0:T], lhsT=Vsb[:, pc_, :], rhs=PT[ppb][:, 0:T],
                                            start=(pidx == 0), stop=(pidx == n - 1)),
                         reads=[vtok, ("PT", ppb)], writes=[("ps", acc)])
                for idx, c in enumerate(chunks):
                    sbk = sc % 4
                    sc += 1
                    P.pe(lambda e: e.matmul(PS[sbk][:, 0:T], lhsT=Ksb[r0:r1, c * 128:(c + 1) * 128], rhs=qap,
                                            start=True, stop=True),
                         reads=[ktok, qtok], writes=[("ps", sbk)])
                    pb = pc % 4
                    pc += 1
                    P.act(lambda e: e.activation(out=PT[pb][:, 0:T], in_=PS[sbk][:, 0:T], func=AF.Exp),
                          reads=[("ps", sbk)], writes=[("PT", pb)])
                    pend.append((idx, c, pb))
                    if len(pend) > LOOK:
                        do_pv(pend.pop(0))
                while pend:
                    do_pv(pend.pop(0))
                nlo, dlo = (0, 64) if e_ == 0 else (64, 0)
                P.dve(lambda e: e.reciprocal(out=rec[nlo:nlo + 64, 0:T], in_=PS[acc][dlo:dlo + 64, 0:T]),
                      reads=[("ps", acc)], writes=[("rec", e_)])
                P.dve(lambda e: e.tensor_tensor(out=ab[nlo:nlo + 64, 0:T], in0=PS[acc][nlo:nlo + 64, 0:T], in1=rec[nlo:nlo + 64, 0:T],
                                                op=ALU.mult),
                      reads=[("ps", acc), ("rec", e_)], writes=[(atok, e_)])
            P.dma("sp", k.attTd[hp, :, t0:t0 + T], ab[:, 0:T], reads=[(atok, 0), (atok, 1)], writes=[("attTd", hp, t0)])
    A.release(m)


def l0_conv_phase(k):
    P, A, PS = k.P, k.A, k.PS
    m = A.mark()
    vec = A.alloc("vec", [128, 32], F32)
    dw = A.alloc("dw", [128, 4, 31], F32)
    diag = A.alloc("diag", [128, 4, 31, 128], BF16)
    ybuf = [A.alloc(f"ybuf{i}", [128, 4, 544], BF16) for i in range(2)]
    cv = A.alloc("cv", [128, 4, 512], F32)
    cb = A.alloc("cb", [128, 4, 512], BF16)
    sqv = A.alloc("sqv", [128, 4, 512], BF16)
    co = [A.alloc(f"co{i}", [128, 4, 512], BF16) for i in range(2)]
    mean = A.alloc("mean", [128, 512], F32)
    msq = A.alloc("msq", [128, 512], F32)
    r1 = A.alloc("r1", [128, 512], F32)
    r2 = A.alloc("r2", [128, 512], F32)
    rstd = A.alloc("rstd", [128, 512], F32)
    t1 = [A.alloc(f"t1{i}", [128, 512], F32) for i in range(2)]
    identb = k.cbf[:, 0, :]
    P.dma("sp", vec[:], k.inp["l0_vec"], writes=[("vec",)])
    P.dma("sp", dw[:], k.inp["l0_dw"], writes=[("dw",)])
    for ch in range(4):
        for j in range(31):
            P.dve(lambda e: e.tensor_scalar(out=diag[:, ch, j, :], in0=identb, scalar1=dw[:, ch, j:j + 1], scalar2=None, op0=ALU.mult),
                  reads=[("dw",), ("cbf",)], writes=[("diag", ch, j)])
    rot = PsRot(banks=(0, 1, 2, 3))
    for ti, (who, t0, T) in enumerate(TILES):
        s0, s1 = (0, NCTX) if who == 1 else (NCTX, TALL)
        lo = max(t0 - 15, s0)
        hi = min(t0 + T + 15, s1)
        yb = ybuf[ti % 2]
        toks = [("yb", ti % 2, x) for x in "LMR"]
        wr = [toks[1]]
        if lo > t0 - 15:
            P.pool(lambda e: e.memset(yb[:, :, 0:15], 0.0), writes=[toks[0]])
        else:
            wr.append(toks[0])
        if hi < t0 + T + 15:
            P.pool(lambda e: e.memset(yb[:, :, T + 15:T + 30], 0.0), writes=[toks[2]])
        else:
            wr.append(toks[2])
        P.dma("sp", yb[:, :, lo - (t0 - 15):hi - (t0 - 15)], k.yTd[:, :, lo:hi].rearrange("c p t -> p c t"), writes=wr)
        for ch in range(4):
            b = rot.next()

            def mm(e, ch=ch, b=b):
                for j in range(31):
                    e.matmul(PS[b][:, 0:T], lhsT=diag[:, ch, j, :], rhs=yb[:, ch, j:j + T], start=(j == 0), stop=(j == 30))
            P.pe(mm, reads=toks + [("diag", ch, j) for j in range(31)], writes=[("ps", b)])
            P.act(lambda e: e.activation(out=cv[:, ch, 0:T], in_=PS[b][:, 0:T], func=AF.Identity, bias=vec[:, 19 + ch:20 + ch], scale=1.0),
                  reads=[("ps", b), ("vec",)], writes=[("cv", ch)])
            P.pool(lambda e: e.tensor_copy(out=cb[:, ch, 0:T], in_=cv[:, ch, 0:T]), reads=[("cv", ch)], writes=[("cb", ch)])
            P.pool(lambda e: e.tensor_tensor(out=sqv[:, ch, 0:T], in0=cv[:, ch, 0:T], in1=cv[:, ch, 0:T], op=ALU.mult),
                   reads=[("cv", ch)], writes=[("sqv", ch)])
        mm_group(k, PS[6][:, 0:T], lambda kc: k.ones_bf, lambda kc: cb[:, kc, 0:T], 4, [("cb", c) for c in range(4)] + [("cbf",)], ("ps", 6))
        mm_group(k, PS[7][:, 0:T], lambda kc: k.ones_bf, lambda kc: sqv[:, kc, 0:T], 4, [("sqv", c) for c in range(4)] + [("cbf",)], ("ps", 7))
        P.dve(lambda e: e.tensor_scalar(out=mean[:, 0:T], in0=PS[6][:, 0:T], scalar1=1.0 / 512, scalar2=None, op0=ALU.mult),
              reads=[("ps", 6)], writes=[("mean",)])
        P.pool(lambda e: e.tensor_tensor(out=msq[:, 0:T], in0=mean[:, 0:T], in1=mean[:, 0:T], op=ALU.mult),
               reads=[("mean",)], writes=[("msq",)])
        P.dve(lambda e: e.scalar_tensor_tensor(out=r2[:, 0:T], in0=PS[7][:, 0:T], scalar=1.0 / 512, in1=msq[:, 0:T],
                                               op0=ALU.mult, op1=ALU.subtract),
              reads=[("ps", 7), ("msq",)], writes=[("r2",)])
        P.dve(lambda e: e.tensor_scalar(out=r1[:, 0:T], in0=r2[:, 0:T], scalar1=EPS, scalar2=None, op0=ALU.add),
              reads=[("r2",)], writes=[("r1",)])
        P.act(lambda e: e.activation(out=r2[:, 0:T], in_=r1[:, 0:T], func=AF.Sqrt), reads=[("r1",)], writes=[("r2",)])
        P.dve(lambda e: e.reciprocal(out=rstd[:, 0:T], in_=r2[:, 0:T]), reads=[("r2",)], writes=[("rstd",)])
        cob = co[ti % 2]
        for ch in range(4):
            tb = t1[ch % 2]
            P.dve(lambda e: e.tensor_tensor(out=tb[:, 0:T], in0=cv[:, ch, 0:T], in1=mean[:, 0:T], op=ALU.subtract),
                  reads=[("cv", ch), ("mean",)], writes=[("t1", ch % 2)])
            P.pool(lambda e: e.tensor_tensor(out=tb[:, 0:T], in0=tb[:, 0:T], in1=rstd[:, 0:T], op=ALU.mult),
                   reads=[("t1", ch % 2), ("rstd",)], writes=[("t1", ch % 2)])
            P.act(lambda e: e.activation(out=cob[:, ch, 0:T], in_=tb[:, 0:T], func=AF.Silu, bias=vec[:, 27 + ch:28 + ch],
                                         scale=vec[:, 23 + ch:24 + ch]),
                  reads=[("t1", ch % 2), ("vec",)], writes=[("co", ti % 2, ch)])
        P.dma("sp", k.attTd[4:8, :, t0:t0 + T].rearrange("c p t -> p c t"), cob[:, :, 0:T],
              reads=[("co", ti % 2, ch) for ch in range(4)], writes=[("attTd", "conv", ti)])
    A.release(m)


def mixout_phase(k, l, tiles):
    P, A, PS = k.P, k.A, k.PS
    m = A.mark()
    wo = A.alloc("wo", [128, 8, 8, 128], BF16)
    xt = [A.alloc(f"xt{i}", [128, 8, 512], F32) for i in range(2)]
    at = [A.alloc(f"at{i}", [128, 8, 512], BF16) for i in range(2)]
    name = f"l{l}_wo"
    for i in range(8):
        P.dma("sp", wo[:, i], k.wbf[name][i].rearrange("p (k n) -> p k n", k=8), reads=[("wbf", name, i)], writes=[("wo",)])
    rot = PsRot()
    for ti, (who, t0, T) in enumerate(tiles):
        xb = xt[ti % 2]
        ab = at[ti % 2]
        xtoks = [("xt", ti % 2, c) for c in range(8)]
        P.dma("sp", xb[:, :, 0:T], k.xTd[:, :, t0:t0 + T].rearrange("c p t -> p c t"), writes=xtoks)
        P.dma("sp", ab[:, :, 0:T], k.attTd[:, :, t0:t0 + T].rearrange("c p t -> p c t"), writes=[("at", ti % 2)])
        for d in range(8):
            b = rot.next()
            mm_group(k, PS[b][:, 0:T], lambda kc: wo[:, d, kc, :], lambda kc: ab[:, kc, 0:T], 8, [("wo",), ("at", ti % 2)], ("ps", b))
            P.dve(lambda e: e.scalar_tensor_tensor(out=xb[:, d, 0:T], in0=PS[b][:, 0:T], scalar=k.modv[:, l, who, 40 + d:41 + d],
                                                   in1=xb[:, d, 0:T], op0=ALU.mult, op1=ALU.add),
                  reads=[("ps", b), xtoks[d]], writes=[xtoks[d]])
        P.dma("sp", k.xTd[:, :, t0:t0 + T].rearrange("c p t -> p c t"), xb[:, :, 0:T], reads=xtoks, writes=[("xTd", t0)])
    A.release(m)


def l1_proj_phase(k):
    P, A, PS = k.P, k.A, k.PS
    l = 1
    m = A.mark()
    w = common_work(k, A)
    xt = [A.alloc(f"xt{i}", [128, 8, 512], F32) for i in range(2)]
    hT = A.alloc("hTm", [128, 8, 512], BF16)
    wA = A.alloc("wA", [128, 20, 8, 128], BF16)
    wv = A.alloc("wv", [128, 8, 640], BF16)
    vec = A.alloc("vec", [128, 8], F32)
    rope = [A.alloc(f"rope{i}", [128, 2, 512], F32) for i in range(2)]
    KdT = [A.alloc(f"KdT{i}", [128, 2, 512], BF16) for i in range(2)]
    nkT = [A.alloc(f"nkT{i}", [128, 4, 512], BF16) for i in range(2)]
    QgT = [A.alloc(f"QgT{i}", [128, 4, 512], BF16) for i in range(2)]
    nqT = [A.alloc(f"nqT{i}", [128, 4, 512], BF16) for i in range(2)]
    Vgt = [A.alloc(f"Vgt{i}", [128, 4, 128], BF16) for i in range(2)]
    NVt = [A.alloc(f"NVt{i}", [128, 8, 128], BF16) for i in range(2)]
    P.dma("sp", vec[:], k.inp["l1_vec"], writes=[("vec",)])
    for i in range(20):
        P.dma("sp", wA[:, i], k.wbf["l1_wA"][i].rearrange("p (k n) -> p k n", k=8), reads=[("wbf", "l1_wA", i)], writes=[("wA",)])
    P.dma("sp", wv[:], k.wbf["l1_wv"][0].rearrange("p (k n) -> p k n", k=8), reads=[("wbf", "l1_wv", 0)], writes=[("wv",)])
    for c0, c1 in ((2, 4), (5, 6)):
        P.dve(lambda e: e.tensor_scalar(out=vec[:, c0:c1], in0=vec[:, c0:c1], scalar1=0.125, scalar2=None, op0=ALU.mult),
              reads=[("vec",)], writes=[("vec",)])
    for i in range(2):
        P.dve(lambda e: e.memset(Vgt[i][:], 1.0), writes=[("Vgt", i), (("Vgt", i), 1)])
        P.dve(lambda e: e.memset(NVt[i][:], 1.0), writes=[("NVt", i), (("NVt", i), 1)])
    rot = PsRot()
    cnt = {"sq": 0, "v": 0}
    Rall = slice(0, 128)
    for ti, (who, t0, T) in enumerate(TILES):
        lat = (who == 0)
        xb = xt[ti % 2]
        xtoks = [("xtm", ti % 2, c) for c in range(8)]
        load_x_norm(k, l, who, t0, T, xb, xtoks, hT, w)
        htoks = [("hTm", c) for c in range(8)]
        rp = rope[ti % 2]
        if lat:
            P.dma("sp", rp[:, :, 0:T], k.inp["ropeA"][:, :, t0 - NCTX:t0 - NCTX + T], writes=[("rope", ti % 2)])

        def projA(bidx):
            b = rot.next()
            mm_group(k, PS[b][:, 0:T], lambda kc: wA[:, bidx, kc, :], lambda kc: hT[:, kc, 0:T], 8, [("wA",)] + htoks, ("ps", b))
            return b

        def normed_chunk(bmain, bswap, gcol, out_ap, otok, do_rope):
            b = projA(bmain)
            bsw = projA(bswap) if do_rope else None
            sb, stok = next_sqs(w)
            P.act(lambda e: e.activation(out=sb[:, 0:T], in_=PS[b][:, 0:T], func=AF.Square), reads=[("ps", b)], writes=[stok])
            st = next_stat(w)
            mm_group(k, PS[st["bank"]][:, 0:T], lambda kc: k.bd64_bf, lambda kc: sb[:, 0:T], 1, [stok, ("cbf",)], ("ps", st["bank"]))
            stats_rstd(k, st, Rall, T, 1.0 / 64)
            if do_rope:
                rope_apply(k, w, b, bsw, Rall, T, vec[:, gcol:gcol + 1], vec[:, gcol + 1:gcol + 2], st, rp,
                           ("rope", ti % 2), out_ap, [otok])
            else:
                P.dve(lambda e: e.scalar_tensor_tensor(out=out_ap, in0=PS[b][:, 0:T], scalar=vec[:, gcol:gcol + 1],
                                                       in1=st["rstd"][:, 0:T], op0=ALU.mult, op1=ALU.mult),
                      reads=[("ps", b), ("rstd", st["i"]), ("vec",)], writes=[otok])

        Kb = KdT[ti % 2]
        for hk in range(2):
            normed_chunk(hk, 2 + hk, 0, Kb[:, hk, 0:T], ("KdT", ti % 2, hk), lat)
        P.dma("sp", k.KdupTd[:, :, t0:t0 + T].rearrange("h p t -> p h t"), Kb[:, :, 0:T],
              reads=[("KdT", ti % 2, hk) for hk in range(2)], writes=[("KdupTd", ti)])
        nb = nkT[ti % 2]
        for c in range(4):
            normed_chunk(4 + c, None, 4, nb[:, c, 0:T], ("nkT", ti % 2, c), False)
        P.dma("sp", k.nkTd[:, :, t0:t0 + T].rearrange("c p t -> p c t"), nb[:, :, 0:T],
              reads=[("nkT", ti % 2, c) for c in range(4)], writes=[("nkTd", ti)])
        for st in range(T // 128):
            b1 = rot.next()
            mm_group(k, PS[b1][:, 0:128], lambda kc: hT[:, kc, st * 128:(st + 1) * 128], lambda kc: wv[:, kc, 0:128], 8,
                     [("wv",)] + htoks, ("ps", b1))
            b2 = rot.next()
            mm_group(k, PS[b2][:, 0:512], lambda kc: hT[:, kc, st * 128:(st + 1) * 128], lambda kc: wv[:, kc, 128:640], 8,
                     [("wv",)] + htoks, ("ps", b2))
            vi = cnt["v"] % 2
            cnt["v"] += 1
            vg, nv = Vgt[vi], NVt[vi]
            vgd = vg[:].rearrange("p (hk par) d -> p hk par d", par=2)
            s1 = PS[b1][:, 0:128].rearrange("p (hk d) -> p hk d", d=64)
            P.dve(lambda e: e.tensor_copy(out=vgd[:, :, 0, 0:64], in_=s1), reads=[("ps", b1)], writes=[("Vgt", vi)])
            P.dve(lambda e: e.tensor_copy(out=vgd[:, :, 1, 64:128], in_=s1), reads=[("ps", b1)], writes=[(("Vgt", vi), 1)])
            nvd = nv[:].rearrange("p (h two) d -> p h two d", two=2)
            s2 = PS[b2][:, 0:512].rearrange("p (h two d) -> p h two d", two=2, d=64)
            P.dve(lambda e: e.tensor_copy(out=nvd[:, :, 0, 0:64], in_=s2[:, :, 0, :]), reads=[("ps", b2)], writes=[("NVt", vi)])
            P.dve(lambda e: e.tensor_copy(out=nvd[:, :, 1, 64:128], in_=s2[:, :, 1, :]), reads=[("ps", b2)],
                  writes=[(("NVt", vi), 1)])
            chunk = (t0 + st * 128) // 128
            P.dma("sp", k.VgD[:, :, chunk, :].rearrange("v p d -> p v d"), vg[:], reads=[("Vgt", vi), (("Vgt", vi), 1)],
                  writes=[("VgD", chunk)])
            P.dma("sp", k.NVd[:, :, chunk, :].rearrange("h p d -> p h d"), nv[:], reads=[("NVt", vi), (("NVt", vi), 1)],
                  writes=[("NVd", chunk)])
        if lat:
            qb = QgT[ti % 2]
            for c in range(4):
                normed_chunk(8 + c, 12 + c, 2, qb[:, c, 0:T], ("QgT", ti % 2, c), True)
            P.dma("sp", k.QgTd[:, :, t0:t0 + T].rearrange("c p t -> p c t"), qb[:, :, 0:T],
                  reads=[("QgT", ti % 2, c) for c in range(4)], writes=[("QgTd", ti)])
            nq = nqT[ti % 2]
            for c in range(4):
                normed_chunk(16 + c, None, 5, nq[:, c, 0:T], ("nqT", ti % 2, c), False)
            P.dma("sp", k.nqTd[:, :, t0:t0 + T].rearrange("c p t -> p c t"), nq[:, :, 0:T],
                  reads=[("nqT", ti % 2, c) for c in range(4)], writes=[("nqTd", ti)])
    A.release(m)


def l1_na_phase(k):
    P, A, PS = k.P, k.A, k.PS
    m = A.mark()
    nk = A.alloc("nk", [128, 4, TALL], BF16)
    NV = A.alloc("NV", [128, 8, 34, 128], BF16)
    nab = A.alloc("nab", [128, NSIG * 8, 128], BF16)
    nq = [[A.alloc(f"nq{i}{e_}", [128, 4, 128], BF16) for e_ in range(2)] for i in range(2)]
    PT = [A.alloc(f"PT{i}", [128, 8, 128], BF16) for i in range(3)]
    for i in range(2):
        for e_ in range(2):
            P.dve(lambda e: e.memset(nq[i][e_][:], 0.0), writes=[("nq", i, e_)])
    rec = A.alloc("rec", [128, 128], F32)
    nat = [A.alloc(f"nat{i}", [128, 4, 128], BF16) for i in range(2)]
    identb = k.cbf[:, 0, :]
    P.dma("sp", nk[:], k.nkTd.rearrange("c p t -> p c t"), writes=[("nk",)])
    for h in range(8):
        P.dma("sp", NV[:, h], k.NVd[h], writes=[("NV", h)])
    for g in range(8):
        n0, n1 = g * NSIG, (g + 1) * NSIG
        P.dma("sp", nab[:, n0:n1, :], k.wbf["l1_nab"][n0:n1].rearrange("n p q -> p n q"),
              reads=[("wbf", "l1_nab", i) for i in range(n0, n1)], writes=[("nab", g)])
    nabtoks = [("nab", g) for g in range(8)]
    hc = 0
    for i in range(32):
        t0 = NCTX + i * 128
        nqb = nq[i % 2]
        natb = nat[i % 2]
        for e_ in range(2):
            P.dma("sp", nqb[e_][e_ * 64:(e_ + 1) * 64], k.nqTd[:, e_ * 64:(e_ + 1) * 64, t0:t0 + 128].rearrange("c p t -> p c t"),
                  writes=[("nq", i % 2, e_)])
        chunks = [(0, None), (1, None)] + [(2 + kc, NA_MAP[(i, kc)]) for kc in NA_CHUNKS[i]]
        n = len(chunks)
        for h in range(8):
            ch, off = h // 2, (h % 2) * 64
            sb0 = 2 * (hc % 3)
            acc = 6 + (hc % 2)
            pt = PT[hc % 3]
            pttok = ("PTn", hc % 3)
            hc += 1

            def smm(e):
                for idx, (c, sig) in enumerate(chunks):
                    o = PS[sb0 + idx // 4][:, (idx % 4) * 128:(idx % 4 + 1) * 128]
                    e.matmul(o, lhsT=nk[:, ch, c * 128:(c + 1) * 128], rhs=nqb[h % 2][:, ch, :],
                             start=True, stop=(sig is None))
                    if sig is not None:
                        e.matmul(o, lhsT=identb, rhs=nab[:, sig * 8 + h, :], start=False, stop=True)
            P.pe(smm, reads=[("nk",), ("nq", i % 2, h % 2), ("cbf",)] + nabtoks, writes=[("ps", sb0), ("ps", sb0 + 1)])
            P.act(lambda e: e.activation(out=pt[:, 0:4, :], in_=PS[sb0][:, 0:512].rearrange("p (c q) -> p c q", q=128), func=AF.Exp),
                  reads=[("ps", sb0)], writes=[(pttok, 0)])
            P.act(lambda e: e.activation(out=pt[:, 4:n, :], in_=PS[sb0 + 1][:, 0:(n - 4) * 128].rearrange("p (c q) -> p c q", q=128),
                                         func=AF.Exp),
                  reads=[("ps", sb0 + 1)], writes=[(pttok, 1)])

            def pv(e):
                for idx, (c, sig) in enumerate(chunks):
                    e.matmul(PS[acc][:, 0:128], lhsT=NV[:, h, c, :], rhs=pt[:, idx, :], start=(idx == 0), stop=(idx == n - 1))
            P.pe(pv, reads=[("NV", h), (pttok, 0), (pttok, 1)], writes=[("ps", acc)])
            nlo, dlo = (0, 64) if h % 2 == 0 else (64, 0)
            P.dve(lambda e: e.reciprocal(out=rec[nlo:nlo + 64, :], in_=PS[acc][dlo:dlo + 64, 0:128]),
                  reads=[("ps", acc)], writes=[("rec", h % 2)])
            P.dve(lambda e: e.tensor_tensor(out=natb[nlo:nlo + 64, ch, :], in0=PS[acc][nlo:nlo + 64, 0:128], in1=rec[nlo:nlo + 64, :],
                                            op=ALU.mult),
                  reads=[("ps", acc), ("rec", h % 2)], writes=[("nat", i % 2, h)])
        P.dma("sp", k.attTd[4:8, :, t0:t0 + 128].rearrange("c p t -> p c t"), natb[:],
              reads=[("nat", i % 2, h) for h in range(8)], writes=[("attTd", "na", i)])
    A.release(m)
```

```python
import numpy as np
import concourse.bass as bass
import concourse.mybir as mybir
from concourse.bass_utils import run_bass_kernel_spmd

F32 = mybir.dt.float32
BF16 = mybir.dt.bfloat16
AF = mybir.ActivationFunctionType
ALU = mybir.AluOpType

D = 1024
S = 4096
NCTX = 256
TALL = S + NCTX
DFF = 2816
NFC = DFF // 128
EPS = 1e-6
GRID_W = 64
NDMASEM = 40
RING = {"sp": (0, 28), "pool": (28, 12)}
TILES = [(1, 0, 256)] + [(0, 256 + 512 * i, 512) for i in range(8)]


class Op:
    __slots__ = ("stream", "kind", "fn", "idx", "event", "signal", "cclock", "waits", "slot")


class _Rec:
    def __init__(self):
        self.calls = []

    def __getattr__(self, name):
        def f(*a, **kw):
            self.calls.append((name, a, kw))
            return None
        return f


def _replay(calls):
    def fn(e):
        ins = None
        for name, a, kw in calls:
            ins = getattr(e, name)(*a, **kw)
        return ins
    return fn


class Prog:
    STREAMS = ("pe", "act", "dve", "pool", "sp")

    def __init__(self, nc):
        self.nc = nc
        self.ops = {s: [] for s in self.STREAMS}
        self.clock = {s: {} for s in self.STREAMS}
        self.lastw = {}
        self.readers = {}
        self.ncomp = {s: 0 for s in self.STREAMS}
        self.pending = {}
        self.dma_slot_last = [None] * NDMASEM
        self.dma_slot_cnt = [0] * NDMASEM
        self.dma_rr = {"sp": 0, "pool": 0}
        self.dmas_since_barrier = []
        self.bg_next = False
        self.bg = []
        self.last_comp = {}
        self.nops = 0
        self.maxops = None

    def _add(self, stream, kind, fn, reads, writes):
        if self.maxops is not None and self.nops >= self.maxops and fn is not None:
            return None
        op = Op()
        op.stream = stream
        op.kind = kind
        if fn is not None:
            rec = _Rec()
            fn(rec)
            fn = _replay(rec.calls)
        op.fn = fn
        op.signal = False
        op.slot = None
        self.nops += 1
        deps = []
        for r in reads:
            w = self.lastw.get(r)
            if w is not None:
                deps.append((w, True))
        for w_ in writes:
            lw = self.lastw.get(w_)
            if lw is not None:
                deps.append((lw, False))
            for rd in self.readers.get(w_, ()):
                deps.append((rd, False))
        for r in reads:
            self.readers.setdefault(r, []).append(op)
        for w_ in writes:
            self.readers[w_] = []
            self.lastw[w_] = op
        pend = self.pending.pop(stream, None)
        if pend:
            deps.extend((p, True) for p in pend)
        if kind == "d":
            base, cnt_ = RING[stream]
            slot = base + self.dma_rr[stream]
            self.dma_rr[stream] = (self.dma_rr[stream] + 1) % cnt_
            prev = self.dma_slot_last[slot]
            if prev is not None:
                deps.append((prev, True))
            self.dma_slot_cnt[slot] += 1
            self.dma_slot_last[slot] = op
            op.slot = slot
            op.event = (("d", slot), 16 * self.dma_slot_cnt[slot])
            if self.bg_next:
                self.bg.append(op)
            else:
                self.dmas_since_barrier.append(op)
        else:
            self.ncomp[stream] += 1
            op.event = (("e", stream), self.ncomp[stream])
            self.last_comp[stream] = op
        clk = self.clock[stream]
        waits = {}
        for d, raw in deps:
            if d is op:
                continue
            if d.stream == stream and d.kind == "c":
                if stream == "pe" or not raw:
                    continue
            key, val = d.event
            if clk.get(key, 0) >= val:
                continue
            if waits.get(key, (0,))[0] < val:
                waits[key] = (val, d)
            for k, v in d.cclock.items():
                if clk.get(k, 0) < v:
                    clk[k] = v
        for key, (val, d) in waits.items():
            d.signal = True
        op.waits = {k: v[0] for k, v in waits.items()}
        cc = dict(clk)
        cc[op.event[0]] = op.event[1]
        op.cclock = cc
        if kind == "c" and stream != "pe":
            pass
        self.ops[stream].append(op)
        return op

    def pe(self, fn, reads=(), writes=()):
        return self._add("pe", "c", fn, reads, writes)

    def act(self, fn, reads=(), writes=()):
        return self._add("act", "c", fn, reads, writes)

    def dve(self, fn, reads=(), writes=()):
        return self._add("dve", "c", fn, reads, writes)

    def pool(self, fn, reads=(), writes=()):
        return self._add("pool", "c", fn, reads, writes)

    def dma(self, q, out, in_, reads=(), writes=()):
        return self._add(q, "d", lambda e: e.dma_start(out=out, in_=in_), reads, writes)

    def barrier(self):
        lst = list(self.last_comp.values()) + self.dmas_since_barrier
        self.dmas_since_barrier = []
        for s in self.STREAMS:
            self.pending[s] = list(lst)

    def emit(self):
        nc = self.nc
        self.dmas_since_barrier = self.dmas_since_barrier + self.bg
        self.barrier()
        fin = self._add("sp", "c", None, (), ())
        esem = {s: nc.alloc_semaphore("es_" + s) for s in self.STREAMS}
        dsem = [nc.alloc_semaphore(f"ds{i}") for i in range(NDMASEM)]
        sigcount = {}
        for s in self.STREAMS:
            cnt = 0
            m = {}
            for o in self.ops[s]:
                if o.kind == "c":
                    if o.signal:
                        cnt += 1
                    m[o.event[1]] = (cnt, o.signal)
            sigcount[s] = m

        def resolve(key, val):
            if key[0] == "d":
                return dsem[key[1]], val
            cnt, sig = sigcount[key[1]][val]
            assert sig
            return esem[key[1]], cnt

        with nc.Block() as block:
            decos = {"pe": block.tensor, "act": block.scalar, "dve": block.vector,
                     "pool": block.gpsimd, "sp": block.sync}
            for s in self.STREAMS:
                ops = self.ops[s]

                def body(eng, ops=ops, s=s):
                    for o in ops:
                        for key, val in o.waits.items():
                            sem, v = resolve(key, val)
                            eng.wait_ge(sem, v)
                        if o.fn is None:
                            continue
                        ins = o.fn(eng)
                        if o.kind == "d":
                            ins.then_inc(dsem[o.slot], 16)
                        elif o.signal:
                            ins.then_inc(esem[s], 1)
                decos[s](body)


class Arena:
    def __init__(self, nc, limit=229376):
        self.nc = nc
        self.off = 16640
        self.limit = limit
        self.n = 0

    def alloc(self, name, shape, dtype):
        esz = 4 if dtype == F32 else 2
        sz = esz
        for d in shape[1:]:
            sz *= d
        sz = (sz + 63) // 64 * 64
        assert self.off + sz <= self.limit, (name, self.off, sz)
        self.n += 1
        t = self.nc.alloc_sbuf_tensor_at(f"{name}_{self.n}", list(shape), dtype, offset=self.off)
        self.off += sz
        return t

    def mark(self):
        return self.off

    def release(self, m):
        self.off = m


def kmajor(w, ncols_pad=None):
    K, N = w.shape
    return np.ascontiguousarray(w.reshape(K // 128, 128, N).transpose(1, 0, 2))


def pvec(v):
    return np.ascontiguousarray(v.reshape(-1, 128).T)


class K:
    pass


def declare_inputs(k, nc):
    k.inp = {}
    k.inshape = {}

    def din(name, shape):
        k.inp[name] = nc.dram_tensor(name, list(shape), F32, kind="ExternalInput").ap()
        k.inshape[name] = tuple(shape)

    din("x", [S, D])
    din("ctx", [NCTX, D])
    din("cT", [128, 8, 2])
    din("cmat", [128, 4, 128])
    for l in range(2):
        din(f"l{l}_mod_w", [18, 128, 8 * 512])
        din(f"l{l}_mod_b2", [2, 9216])
        din(f"l{l}_ng", [128, 3, 8])
        for w in (1, 2):
            din(f"l{l}_f{w}_win", [NFC, 128, 8 * 256])
            din(f"l{l}_f{w}_wout", [8, 128, NFC * 128])


def host_inputs(inputs, b):
    m = {}
    m["x"] = np.ascontiguousarray(inputs["x"][b])
    m["ctx"] = np.ascontiguousarray(inputs["ctx"][b])
    cT = np.stack([pvec(inputs["c"][b]), pvec(inputs["c_ctx"])], axis=-1)
    m["cT"] = np.ascontiguousarray(cT.astype(np.float32))
    return m


_SHARED = {}


def host_shared(inputs):
    m = {}
    cm = np.zeros((128, 4, 128), np.float32)
    cm[:, 0, :] = np.eye(128, dtype=np.float32)
    cm[:, 1, :] = 1.0
    for g in range(2):
        cm[g * 64:(g + 1) * 64, 2, g * 64:(g + 1) * 64] = 1.0
    cm[0:64, 3, 0:64] = 1.0
    cm[64:96, 3, 64:96] = 1.0
    m["cmat"] = cm
    for l in range(2):
        p = f"l{l}_"
        mw = inputs[p + "mod_w"]
        m[p + "mod_w"] = np.ascontiguousarray(
            mw.reshape(8, 128, 18, 512).transpose(2, 1, 0, 3).reshape(18, 128, 8 * 512))
        m[p + "mod_b2"] = np.ascontiguousarray(np.stack([inputs[p + "mod_b"], inputs[p + "mod_b"]], axis=0))
        m[p + "ng"] = np.ascontiguousarray(np.stack(
            [pvec(inputs[p + "ffn1_norm"]), pvec(inputs[p + "mix_norm"]), pvec(inputs[p + "ffn2_norm"])], axis=1))
        for w in (1, 2):
            wi = inputs[p + f"ffn{w}_w_in"]
            g = wi[:, :DFF].reshape(8, 128, NFC, 128)
            u = wi[:, DFF:].reshape(8, 128, NFC, 128)
            gu = np.stack([g, u], axis=3)
            m[p + f"f{w}_win"] = np.ascontiguousarray(gu.transpose(2, 1, 0, 3, 4).reshape(NFC, 128, 8 * 256))
            wo = inputs[p + f"ffn{w}_w_out"]
            m[p + f"f{w}_wout"] = np.ascontiguousarray(
                wo.reshape(NFC, 128, 8, 128).transpose(2, 1, 0, 3).reshape(8, 128, NFC * 128))
    m.update(host_mixer(inputs))
    return m


def build(stop_after=None, debug=False, skip=(), maxops=None):
    nc = bass.Bass("TRN2", target_bir_lowering=False)
    k = K()
    k.nc = nc
    k.debug = debug
    P = Prog(nc)
    k.maxops = maxops
    k.P = P
    A = Arena(nc)
    k.A = A
    declare_inputs(k, nc)
    k.out = nc.dram_tensor("out", [S, D], F32, kind="ExternalOutput").ap()
    k.dbg_outs = []
    skind = "ExternalOutput" if debug else "Internal"
    k.xTd = nc.dram_tensor("xTd", [8, 128, TALL], F32, kind=skind).ap()
    k.wbf = {}

    def precast(name):
        shape = k.inshape[name]
        t = nc.dram_tensor(name + "_bf", list(shape), BF16, kind="Internal").ap()
        k.wbf[name] = t
        P.bg_next = True
        for i in range(shape[0]):
            P.dma("pool", t[i], k.inp[name][i], reads=(), writes=[("wbf", name, i)])
        P.bg_next = False

    k.PS = [nc.alloc_psum_tensor(f"psb{i}", [128, 512], F32) for i in range(8)]

    k.ident = A.alloc("ident", [128, 128], F32)
    k.cbf = A.alloc("cbf", [128, 4, 128], BF16)
    k.modv = A.alloc("modv", [128, 2, 2, 72], F32)
    k.der = A.alloc("der", [128, 2, 2, 5, 8], F32)
    k.ng = A.alloc("ng", [128, 2, 3, 8], F32)
    k.epsv = A.alloc("epsv", [128, 1], F32)
    P.dve(lambda e: e.memset(k.epsv[:], EPS), writes=[("epsv",)])
    P.dma("sp", k.ident[:], k.inp["cmat"][:, 0, :], writes=[("ident",)])
    P.dma("pool", k.cbf[:], k.inp["cmat"], writes=[("cbf",)])
    for l in range(2):
        P.dma("sp", k.ng[:, l], k.inp[f"l{l}_ng"], writes=[("ng", l)])
    k.ones_bf = k.cbf[:, 1, :]
    k.bd64_bf = k.cbf[:, 2, :]

    for l in range(2):
        for w in (1, 2):
            precast(f"l{l}_f{w}_win")
            precast(f"l{l}_f{w}_wout")

    declare_mixer_inputs(k, nc)
    for name in ("l0_wA", "l0_wkn", "l0_wv", "l0_wq", "l0_wo", "l1_wo", "l1_wA", "l1_wv", "l1_nab"):
        precast(name)

    def dscr(name, shape, dt=BF16):
        return nc.dram_tensor(name, list(shape), dt, kind=skind if name in ("attTd",) else "Internal").ap()
    k.yTd = dscr("yTd", [4, 128, TALL])
    k.KTd = dscr("KTd", [8, 96, TALL])
    k.QTd = dscr("QTd", [8, 96, TALL])
    k.Vd = dscr("Vd", [8, 128, 34, 128])
    k.attTd = dscr("attTd", [8, 128, TALL])
    k.KdupTd = dscr("KdupTd", [2, 128, TALL])
    k.QgTd = dscr("QgTd", [4, 128, TALL])
    k.nkTd = dscr("nkTd", [4, 128, TALL])
    k.nqTd = dscr("nqTd", [4, 128, TALL])
    k.VgD = dscr("VgD", [4, 128, 34, 128])
    k.NVd = dscr("NVd", [8, 128, 34, 128])

    phases = [
        ("mod", lambda: setup_mod(k)),
        ("l0f1", lambda: ffn_phase(k, 0, 1, TILES, src="tok", dst="xT")),
        ("l0proj", lambda: l0_proj_phase(k)),
        ("l0att", lambda: attn_phase(k, "mla")),
        ("l0conv", lambda: l0_conv_phase(k)),
        ("l0mix", lambda: mixout_phase(k, 0, TILES)),
        ("l0", lambda: ffn_phase(k, 0, 2, TILES, src="xT", dst="xT")),
        ("l1f1", lambda: ffn_phase(k, 1, 1, TILES, src="xT", dst="xT")),
        ("l1proj", lambda: l1_proj_phase(k)),
        ("l1gqa", lambda: attn_phase(k, "gqa")),
        ("l1na", lambda: l1_na_phase(k)),
        ("l1mix", lambda: mixout_phase(k, 1, TILES[1:])),
        ("full", lambda: ffn_phase(k, 1, 2, TILES[1:], src="xT", dst="tok")),
    ]
    for name, fn in phases:
        if name in skip:
            continue
        n0 = P.nops
        if maxops is not None and name == stop_after:
            P.maxops = P.nops + maxops
        fn()
        P.barrier()
        if debug:
            print("phase", name, "ops", n0, P.nops, flush=True)
        if stop_after == name:
            break
    return finish(k)


def finish(k):
    k.P.emit()
    return k.nc


def setup_mod(k):
    P, A, nc = k.P, k.A, k.nc
    PS = k.PS
    m = A.mark()
    cT = A.alloc("cT", [128, 8, 2], F32)
    sc = A.alloc("sc", [128, 8, 2], F32)
    mb2 = A.alloc("mb2", [2, 9216], F32)
    modrow = A.alloc("modrow", [2, 9216], F32)
    wt = [A.alloc(f"mw{i}", [128, 8, 512], F32) for i in range(3)]
    P.dma("sp", cT[:], k.inp["cT"], writes=[("cT",)])
    P.act(lambda e: e.activation(out=sc[:], in_=cT[:], func=AF.Silu), reads=[("cT",)], writes=[("sc",)])
    n = 0
    for l in range(2):
        P.dma("sp", mb2[:], k.inp[f"l{l}_mod_b2"], writes=[("mb2",)])
        for blk in range(18):
            w = wt[n % 3]
            wtok = ("mw", n % 3)
            b = n % 4
            n += 1
            P.dma("sp", w[:], k.inp[f"l{l}_mod_w"][blk].rearrange("p (k n) -> p k n", k=8), writes=[wtok])

            def mm(e):
                for kc in range(8):
                    e.matmul(PS[b][0:2, 0:512], lhsT=sc[:, kc, :], rhs=w[:, kc, :], start=(kc == 0), stop=(kc == 7))
            P.pe(mm, reads=[wtok, ("sc",)], writes=[("ps", b)])
            P.dve(lambda e: e.tensor_tensor(out=modrow[0:2, blk * 512:(blk + 1) * 512], in0=PS[b][0:2, 0:512],
                                            in1=mb2[0:2, blk * 512:(blk + 1) * 512], op=ALU.add),
                  reads=[("ps", b), ("mb2",)], writes=[("modrow", blk)])
        pst = PS[4 + l]

        def tr(e):
            for j in range(72):
                e.transpose(pst[:, 2 * j:2 * j + 2], modrow[0:2, j * 128:(j + 1) * 128], k.ident[0:2, 0:2])
        P.pe(tr, reads=[("modrow", blk) for blk in range(18)] + [("ident",)], writes=[("ps", 4 + l)])
        for who in range(2):
            src = pst[:, 0:144].rearrange("p (j w) -> p j w", w=2)[:, :, who]
            P.dve(lambda e: e.tensor_copy(out=k.modv[:, l, who, :], in_=src),
                  reads=[("ps", 4 + l)], writes=[("modv", l, who)])
        for who in range(2):
            for gi, mi in enumerate((1, 4, 7)):
                P.dve(lambda e: e.scalar_tensor_tensor(
                    out=k.der[:, l, who, gi, :], in0=k.modv[:, l, who, mi * 8:mi * 8 + 8], scalar=1.0,
                    in1=k.ng[:, l, gi, :], op0=ALU.add, op1=ALU.mult),
                    reads=[("modv", l, who), ("ng", l)], writes=[("der", l, who, gi)])
            for gi, mi in ((3, 2), (4, 8)):
                P.dve(lambda e: e.tensor_scalar(
                    out=k.der[:, l, who, gi, :], in0=k.modv[:, l, who, mi * 8:mi * 8 + 8], scalar1=0.5,
                    scalar2=None, op0=ALU.mult),
                    reads=[("modv", l, who)], writes=[("der", l, who, gi)])
    if k.debug:
        dm = nc.dram_tensor("dbg_modv", [128, 2 * 2 * 72], F32, kind="ExternalOutput").ap()
        P.dma("sp", dm, k.modv[:].rearrange("p a b c -> p (a b c)"),
              reads=[("modv", l, w) for l in range(2) for w in range(2)], writes=[("dbg_modv",)])
    A.release(m)


def rms_norm_mod(k, xb, xtoks, T, gp, sh, hT, htok, tmp, r1, r2, rstd, sq):
    P = k.P
    P.act(lambda e: e.activation(out=sq[:, :, 0:T], in_=xb[:, :, 0:T], func=AF.Square),
          reads=xtoks, writes=[("sq", c) for c in range(8)])

    def mm(e):
        ins = None
        for c in range(8):
            ins = e.matmul(k.PS[6][:, 0:T], lhsT=k.ones_bf, rhs=sq[:, c, 0:T], start=(c == 0), stop=(c == 7))
        return ins
    P.pe(mm, reads=[("sq", c) for c in range(8)] + [("cbf",)], writes=[("ps", 6)])
    P.act(lambda e: e.activation(out=r2[:, 0:T], in_=k.PS[6][:, 0:T], func=AF.Sqrt, bias=k.epsv[:, 0:1], scale=1.0 / D),
          reads=[("ps", 6), ("epsv",)], writes=[("r2",)])
    P.dve(lambda e: e.reciprocal(out=rstd[:, 0:T], in_=r2[:, 0:T]), reads=[("r2",)], writes=[("rstd",)])
    for c in range(8):
        tb = tmp[c % 2]
        P.dve(lambda e, c=c, tb=tb: e.scalar_tensor_tensor(out=tb[:, 0:T], in0=xb[:, c, 0:T], scalar=gp[:, c:c + 1],
                                                           in1=rstd[:, 0:T], op0=ALU.mult, op1=ALU.mult),
              reads=[xtoks[c], ("rstd",)], writes=[("tmp", c % 2)])
        P.act(lambda e, c=c, tb=tb: e.activation(out=hT[:, c, 0:T], in_=tb[:, 0:T], func=AF.Identity,
                                                 bias=sh[:, c:c + 1], scale=1.0),
              reads=[("tmp", c % 2)], writes=[(htok, c)])


def ffn_phase(k, l, which, tiles, src, dst):
    P, A, nc = k.P, k.A, k.nc
    PS = k.PS
    m = A.mark()
    xt = [A.alloc(f"xt{i}", [128, 8, 512], F32) for i in range(3)]
    hT = [A.alloc(f"hT{i}", [128, 8, 512], BF16) for i in range(2)]
    actT = [A.alloc(f"actT{i}", [128, NFC, 512], BF16) for i in range(2)]
    sq = A.alloc("sq", [128, 8, 512], BF16)
    win = [A.alloc(f"win{i}", [128, 8, 256], BF16) for i in range(3)]
    wout = [A.alloc(f"wout{i}", [128, NFC, 128], BF16) for i in range(2)]
    r1 = A.alloc("r1", [128, 512], F32)
    r2 = A.alloc("r2", [128, 512], F32)
    rstd = A.alloc("rstd", [128, 512], F32)
    tmp = [A.alloc(f"tmp{i}", [128, 512], F32) for i in range(2)]
    sg = [A.alloc(f"sg{i}", [128, 512], F32) for i in range(3)]
    tokb = [A.alloc(f"tokb{i}", [128, 1024], F32) for i in range(2)]
    gi = 0 if which == 1 else 2
    shi = 0 if which == 1 else 6
    hgi = 3 if which == 1 else 4
    wname_in = f"l{l}_f{which}_win"
    wname_out = f"l{l}_f{which}_wout"
    groups = [tiles[i:i + 2] for i in range(0, len(tiles), 2)]
    tk = 0
    wi = 0
    wo = 0
    guc = 0
    yc = 0
    tbc = 0
    for grp in groups:
        bufs = []
        for s, (who, t0, T) in enumerate(grp):
            b = tk % 3
            tk += 1
            bufs.append(b)
            xb = xt[b]
            xtoks = [("xt", b, c) for c in range(8)]
            if src == "xT":
                P.dma("sp", xb[:, :, 0:T], k.xTd[:, :, t0:t0 + T].rearrange("c p t -> p c t"), writes=xtoks)
            else:
                srcap = k.inp["ctx"] if who == 1 else k.inp["x"]
                r0 = t0 if who == 1 else t0 - NCTX
                for st in range(T // 128):
                    tb = tokb[tbc % 2]
                    ttok = ("tokb", tbc % 2)
                    tbc += 1
                    P.dma("sp", tb[:], srcap[r0 + st * 128:r0 + (st + 1) * 128, :], writes=[ttok])
                    for half in range(2):
                        bank = PS[(guc % 4)]
                        btok = ("ps", guc % 4)
                        guc += 1

                        def tr(e, tb=tb, half=half, bank=bank):
                            ins = None
                            for cc in range(4):
                                c = half * 4 + cc
                                ins = e.transpose(bank[:, cc * 128:(cc + 1) * 128], tb[:, c * 128:(c + 1) * 128], k.ident[:])
                            return ins
                        P.pe(tr, reads=[ttok, ("ident",)], writes=[btok])
                        P.act(lambda e, xb=xb, half=half, bank=bank, st=st: e.activation(
                            out=xb[:, half * 4:half * 4 + 4, st * 128:(st + 1) * 128],
                            in_=bank[:].rearrange("p (c t) -> p c t", c=4), func=AF.Copy),
                            reads=[btok], writes=xtoks[half * 4:half * 4 + 4])
            rms_norm_mod(k, xb, xtoks, T, k.der[:, l, who, gi, :], k.modv[:, l, who, shi * 8:shi * 8 + 8],
                         hT[s], ("hT", s), tmp, r1, r2, rstd, sq)
        for j in range(NFC):
            wb = win[wi % 3]
            wtok = ("win", wi % 3)
            wi += 1
            P.dma("sp", wb[:], k.wbf[wname_in][j].rearrange("p (k n) -> p k n", k=8),
                  reads=[("wbf", wname_in, j)], writes=[wtok])
            for s, (who, t0, T) in enumerate(grp):
                bg = (guc % 3) * 2
                guc += 1
                htoks = [(("hT", s), c) for c in range(8)]

                def mmg(e, wb=wb, s=s, T=T, bank=PS[bg], off=0):
                    ins = None
                    for kc in range(8):
                        ins = e.matmul(bank[:, 0:T], lhsT=wb[:, kc, off:off + 128], rhs=hT[s][:, kc, 0:T],
                                       start=(kc == 0), stop=(kc == 7))
                    return ins
                P.pe(mmg, reads=[wtok] + htoks, writes=[("ps", bg)])
                P.pe(lambda e, wb=wb, s=s, T=T, bank=PS[bg + 1], f=mmg: f(e, wb, s, T, bank, 128),
                     reads=[wtok] + htoks, writes=[("ps", bg + 1)])
                q = guc % 3
                P.act(lambda e, q=q, bg=bg, T=T: e.activation(out=sg[q][:, 0:T], in_=PS[bg][:, 0:T], func=AF.Silu),
                      reads=[("ps", bg)], writes=[("sg", q)])
                P.dve(lambda e, q=q, bg=bg, T=T, s=s, j=j: e.tensor_tensor(
                    out=actT[s][:, j, 0:T], in0=PS[bg + 1][:, 0:T], in1=sg[q][:, 0:T], op=ALU.mult),
                    reads=[("ps", bg + 1), ("sg", q)], writes=[("actT", s, j)])
        for d in range(8):
            wb = wout[wo % 2]
            wtok = ("wout", wo % 2)
            wo += 1
            P.dma("sp", wb[:], k.wbf[wname_out][d].rearrange("p (k n) -> p k n", k=NFC),
                  reads=[("wbf", wname_out, d)], writes=[wtok])
            for s, (who, t0, T) in enumerate(grp):
                by = 6 + (yc % 2)
                yc += 1
                xb = xt[bufs[s]]

                def mmy(e, wb=wb, s=s, T=T, by=by):
                    ins = None
                    for fc in range(NFC):
                        ins = e.matmul(PS[by][:, 0:T], lhsT=wb[:, fc, :], rhs=actT[s][:, fc, 0:T],
                                       start=(fc == 0), stop=(fc == NFC - 1))
                    return ins
                P.pe(mmy, reads=[wtok] + [("actT", s, j) for j in range(NFC)], writes=[("ps", by)])
                hg = k.der[:, l, who, hgi, :]
                P.dve(lambda e, xb=xb, d=d, T=T, by=by, hg=hg: e.scalar_tensor_tensor(
                    out=xb[:, d, 0:T], in0=PS[by][:, 0:T], scalar=hg[:, d:d + 1], in1=xb[:, d, 0:T],
                    op0=ALU.mult, op1=ALU.add),
                    reads=[("ps", by), ("xt", bufs[s], d)], writes=[("xt", bufs[s], d)])
        for s, (who, t0, T) in enumerate(grp):
            b = bufs[s]
            xb = xt[b]
            xtoks = [("xt", b, c) for c in range(8)]
            if dst == "xT":
                P.dma("sp", k.xTd[:, :, t0:t0 + T].rearrange("c p t -> p c t"), xb[:, :, 0:T], reads=xtoks,
                      writes=[("xTd", t0)])
            else:
                r0 = t0 - NCTX
                for st in range(T // 128):
                    ob = tokb[tbc % 2]
                    otok = ("tokb", tbc % 2)
                    tbc += 1
                    for half in range(2):
                        bi = guc % 4
                        guc += 1

                        def tr(e, xb=xb, half=half, bi=bi, st=st):
                            ins = None
                            for cc in range(4):
                                c = half * 4 + cc
                                ins = e.transpose(PS[bi][:, cc * 128:(cc + 1) * 128], xb[:, c, st * 128:(st + 1) * 128],
                                                  k.ident[:])
                            return ins
                        P.pe(tr, reads=xtoks[half * 4:half * 4 + 4] + [("ident",)], writes=[("ps", bi)])
                        P.act(lambda e, ob=ob, half=half, bi=bi: e.activation(
                            out=ob[:, half * 512:(half + 1) * 512], in_=PS[bi][:], func=AF.Copy),
                            reads=[("ps", bi)], writes=[(otok, half)])
                    P.dma("sp", k.out[r0 + st * 128:r0 + (st + 1) * 128, :], ob[:], reads=[(otok, 0), (otok, 1)],
                          writes=[("out", r0, st)])
    A.release(m)


_CACHE = {}


def kernel(**inputs):
    inputs = {kk: np.asarray(v) for kk, v in inputs.items()}
    if "nc" not in _CACHE:
        _CACHE["nc"] = build()
    nc = _CACHE["nc"]
    shared = host_shared(inputs)
    in_maps = []
    for b in range(8):
        mm = dict(shared)
        mm.update(host_inputs(inputs, b))
        in_maps.append(mm)
    res = run_bass_kernel_spmd(nc, in_maps, core_ids=list(range(8)))
    return np.stack([r["out"] for r in res.results], axis=0)


NEG = -30000.0
PERM64 = list(range(0, 64, 2)) + list(range(1, 64, 2))
SWAP64 = list(range(1, 64, 2)) + list(range(0, 64, 2))
PERM32 = list(range(0, 32, 2)) + list(range(1, 32, 2))
SWAP32 = list(range(1, 32, 2)) + list(range(0, 32, 2))


def rope_tables():
    t = np.arange(S)
    row = (t // GRID_W).astype(np.float32)
    col = (t % GRID_W).astype(np.float32)

    def tab(dim):
        npairs = dim // 4
        inv = (10000.0 ** (-np.arange(npairs, dtype=np.float32) / npairs)).astype(np.float32)
        ang = np.concatenate([row[:, None] * inv, col[:, None] * inv], axis=-1).astype(np.float32)
        return np.cos(ang).astype(np.float32).T, np.sin(ang).astype(np.float32).T
    cA, sA = tab(64)
    cB, sB = tab(32)
    ta = np.zeros((128, 2, S), np.float32)
    for p in range(128):
        pp = p % 64
        pr = pp % 32
        ta[p, 0] = cA[pr]
        ta[p, 1] = -sA[pr] if pp < 32 else sA[pr]
    tb = np.zeros((128, 2, S), np.float32)
    for p in range(64, 96):
        pp = p - 64
        pr = pp % 16
        tb[p, 0] = cB[pr]
        tb[p, 1] = -sB[pr] if pp < 16 else sB[pr]
    return ta, tb


def na_geometry():
    sigs = []
    mp = {}
    chunks = {}
    for i in range(32):
        rs0 = min(max(2 * i - 4, 0), 56)
        rs1 = min(max(2 * i + 1 - 4, 0), 56)
        lo = rs0 // 2
        hi = (rs1 + 7) // 2
        chunks[i] = list(range(lo, hi + 1))
        for kc in chunks[i]:
            sig = (rs0 - 2 * i, rs1 - (2 * i + 1), kc - i)
            if sig not in sigs:
                sigs.append(sig)
            mp[(i, kc)] = sigs.index(sig)
    return sigs, mp, chunks


NA_SIGS, NA_MAP, NA_CHUNKS = na_geometry()
NSIG = len(NA_SIGS)


def na_bias_tiles(rpb):
    out = np.full((NSIG, 8, 128, 128), NEG, np.float32)
    kp = np.arange(128)
    qf = np.arange(128)
    kr_, kc_ = kp // 64, kp % 64
    qr_, qc_ = qf // 64, qf % 64
    cs = np.clip(qc_ - 8, 0, 48)
    for si, (a0, a1, dk) in enumerate(NA_SIGS):
        rsrel = np.where(qr_ == 0, a0, a1)
        krel = (2 * dk + kr_[:, None]) - qr_[None, :]
        vr = (krel >= rsrel[None, :]) & (krel <= rsrel[None, :] + 7)
        vc = (kc_[:, None] >= cs[None, :]) & (kc_[:, None] <= cs[None, :] + 15)
        valid = vr & vc
        dr = np.clip(krel + 7, 0, 14)
        dc = np.clip(kc_[:, None] - qc_[None, :] + 15, 0, 30)
        for h in range(8):
            g = rpb[h][dr, dc]
            out[si, h] = np.where(valid, g, np.float32(NEG))
    return out.reshape(NSIG * 8, 128, 128)


def blk(w, cols):
    K_ = w.shape[0]
    o = np.zeros((K_, len(cols)), np.float32)
    idx = [i for i, c in enumerate(cols) if c is not None]
    o[:, idx] = w[:, [cols[i] for i in idx]]
    return o.reshape(K_ // 128, 128, len(cols)).transpose(1, 0, 2)


def host_mixer(inputs):
    m = {}
    ta, tb = rope_tables()
    m["ropeA"] = ta
    m["ropeB"] = tb
    w = inputs["l0_w_in"]
    blocks = []
    blocks.append(blk(w, list(range(0, 128))))
    blocks.append(blk(w, list(range(128, 256))))
    blocks.append(blk(w, [None] * 64 + [256 + i for i in PERM32] + [None] * 32))
    blocks.append(blk(w, [None] * 64 + [256 + i for i in SWAP32] + [None] * 32))
    for c in range(3):
        blocks.append(blk(w, list(range(288 + c * 128, 288 + (c + 1) * 128))))
    for c in range(8):
        blocks.append(blk(w, list(range(672 + c * 128, 672 + (c + 1) * 128))))
    m["l0_wA"] = np.ascontiguousarray(np.stack(blocks, 0).reshape(15, 128, 8 * 128))
    wk = inputs["l0_mla_w_ukv"]
    kb = [blk(wk, [(2 * pr) * 128 + i for i in range(64)] + [(2 * pr + 1) * 128 + i for i in range(64)]) for pr in range(4)]
    m["l0_wkn"] = np.ascontiguousarray(np.stack(kb, 0).reshape(4, 128, 2 * 128))
    m["l0_wv"] = np.ascontiguousarray(blk(wk, [h * 128 + 64 + i for h in range(8) for i in range(64)]).reshape(1, 128, 2 * 512))
    wq = inputs["l0_mla_w_uq"]
    qb = []
    for h in range(8):
        qb.append(blk(wq, [h * 96 + i for i in range(64)] + [h * 96 + 64 + i for i in PERM32]))
    for h in range(8):
        qb.append(blk(wq, [None] * 64 + [h * 96 + 64 + i for i in SWAP32]))
    m["l0_wq"] = np.ascontiguousarray(np.stack(qb, 0).reshape(16, 128, 3 * 96))
    for l in range(2):
        wo = inputs[f"l{l}_w_out"]
        m[f"l{l}_wo"] = np.ascontiguousarray(wo.reshape(8, 128, 8, 128).transpose(2, 1, 0, 3).reshape(8, 128, 8 * 128))
    m["l0_dw"] = np.ascontiguousarray(inputs["l0_conv_dw_w"].reshape(31, 4, 128).transpose(2, 1, 0))
    v = np.zeros((128, 32), np.float32)
    v[:, 0:2] = pvec(inputs["l0_mla_kv_norm"])
    v[:, 2:5] = pvec(inputs["l0_mla_q_norm"])
    kg = inputs["l0_mla_k_gain"]
    qg = inputs["l0_mla_q_gain"]
    v[:, 5] = np.tile(kg[:64], 2)
    v[64:96, 6] = kg[64:][PERM32]
    v[64:96, 7] = kg[64:][SWAP32]
    v[0:64, 8] = qg[:64]
    v[64:96, 8] = qg[64:][PERM32]
    v[64:96, 9] = qg[64:][SWAP32]
    v[0:64, 10] = 1.0 / 64
    v[64:96, 10] = 1.0 / 32
    gb = inputs["l0_conv_glu_b"]
    v[:, 11:15] = pvec(gb[:512])
    v[:, 15:19] = pvec(gb[512:])
    v[:, 19:23] = pvec(inputs["l0_conv_dw_b"])
    v[:, 23:27] = pvec(inputs["l0_conv_ln_g"])
    v[:, 27:31] = pvec(inputs["l0_conv_ln_b"])
    m["l0_vec"] = v
    w = inputs["l1_w_in"]
    blocks = []
    for hk in range(2):
        blocks.append(blk(w, [hk * 64 + i for i in PERM64] * 2))
    for hk in range(2):
        blocks.append(blk(w, [hk * 64 + i for i in SWAP64] * 2))
    for c in range(4):
        blocks.append(blk(w, list(range(256 + c * 128, 256 + (c + 1) * 128))))
    for c in range(4):
        blocks.append(blk(w, [1280 + (2 * c + e) * 64 + i for e in range(2) for i in PERM64]))
    for c in range(4):
        blocks.append(blk(w, [1280 + (2 * c + e) * 64 + i for e in range(2) for i in SWAP64]))
    for c in range(4):
        blocks.append(blk(w, list(range(1792 + c * 128, 1792 + (c + 1) * 128))))
    m["l1_wA"] = np.ascontiguousarray(np.stack(blocks, 0).reshape(20, 128, 8 * 128))
    m["l1_wv"] = np.ascontiguousarray(blk(w, list(range(128, 256)) + list(range(768, 1280))).reshape(1, 128, 8 * 640))
    v = np.zeros((128, 8), np.float32)
    v[:, 0] = np.tile(inputs["l1_gqa_k_gain"][PERM64], 2)
    v[:, 1] = np.tile(inputs["l1_gqa_k_gain"][SWAP64], 2)
    v[:, 2] = np.tile(inputs["l1_gqa_q_gain"][PERM64], 2)
    v[:, 3] = np.tile(inputs["l1_gqa_q_gain"][SWAP64], 2)
    v[:, 4] = np.tile(inputs["l1_na_k_gain"], 2)
    v[:, 5] = np.tile(inputs["l1_na_q_gain"], 2)
    m["l1_vec"] = v
    m["l1_nab"] = na_bias_tiles(inputs["l1_na_rpb"])
    return m


def declare_mixer_inputs(k, nc):
    def din(name, shape):
        k.inp[name] = nc.dram_tensor(name, list(shape), F32, kind="ExternalInput").ap()
        k.inshape[name] = tuple(shape)
    din("ropeA", [128, 2, S])
    din("ropeB", [128, 2, S])
    din("l0_wA", [15, 128, 1024])
    din("l0_wkn", [4, 128, 256])
    din("l0_wv", [1, 128, 1024])
    din("l0_wq", [16, 128, 288])
    din("l0_wo", [8, 128, 1024])
    din("l1_wo", [8, 128, 1024])
    din("l0_dw", [128, 4, 31])
    din("l0_vec", [128, 32])
    din("l1_wA", [20, 128, 1024])
    din("l1_wv", [1, 128, 8 * 640])
    din("l1_vec", [128, 8])
    din("l1_nab", [NSIG * 8, 128, 128])


def stats_rstd(k, st, rows, T, scale, extra_reads=()):
    P = k.P
    i = st["i"]
    r2, rstd, bank = st["r2"], st["rstd"], st["bank"]
    P.act(lambda e: e.activation(out=r2[rows, 0:T], in_=k.PS[bank][rows, 0:T], func=AF.Sqrt, bias=k.epsv[rows, 0:1], scale=scale),
          reads=[("ps", bank), ("epsv",)] + list(extra_reads), writes=[("r2", i)])
    P.dve(lambda e: e.reciprocal(out=rstd[rows, 0:T], in_=r2[rows, 0:T]), reads=[("r2", i)], writes=[("rstd", i)])


class PsRot:
    def __init__(self, banks=(0, 1, 2, 3, 4, 5)):
        self.banks = banks
        self.i = 0

    def next(self):
        b = self.banks[self.i % len(self.banks)]
        self.i += 1
        return b


def load_x_norm(k, l, who, t0, T, xb, xtoks, hT, w):
    P = k.P
    P.dma("sp", xb[:, :, 0:T], k.xTd[:, :, t0:t0 + T].rearrange("c p t -> p c t"), writes=xtoks)
    rms_norm_mod(k, xb, xtoks, T, k.der[:, l, who, 1, :], k.modv[:, l, who, 24:32], hT, "hTm",
                 w["tmp"], w["r1"], w["r2"], w["rstd"], w["sq"])


def mm_group(k, bank_ap, lhs_fn, rhs_fn, n, reads, btok):
    def f(e):
        ins = None
        for kc in range(n):
            ins = e.matmul(bank_ap, lhsT=lhs_fn(kc), rhs=rhs_fn(kc), start=(kc == 0), stop=(kc == n - 1))
        return ins
    k.P.pe(f, reads=reads, writes=[btok])


def common_work(k, A):
    w = {}
    w["tmp"] = [A.alloc(f"tmp{i}", [128, 512], F32) for i in range(2)]
    w["r1"] = A.alloc("r1", [128, 512], F32)
    w["r2"] = A.alloc("r2", [128, 512], F32)
    w["rstd"] = A.alloc("rstd", [128, 512], F32)
    w["sq"] = A.alloc("sq", [128, 8, 512], BF16)
    w["stats"] = [dict(i=i, r1=A.alloc(f"sr1_{i}", [128, 512], F32), r2=A.alloc(f"sr2_{i}", [128, 512], F32),
                       rstd=A.alloc(f"srstd_{i}", [128, 512], F32)) for i in range(3)]
    w["si"] = 0
    w["sqs"] = [A.alloc(f"sqs{i}", [128, 512], BF16) for i in range(4)]
    w["sqi"] = 0
    w["rab"] = [(A.alloc(f"ra{i}", [128, 512], F32), A.alloc(f"rb{i}", [128, 512], F32)) for i in range(2)]
    w["rai"] = 0
    return w


def l0_proj_phase(k):
    P, A, nc, PS = k.P, k.A, k.nc, k.PS
    l = 0
    m = A.mark()
    w = common_work(k, A)
    xt = [A.alloc(f"xt{i}", [128, 8, 512], F32) for i in range(2)]
    hT = A.alloc("hTm", [128, 8, 512], BF16)
    wA = A.alloc("wA", [128, 15, 8, 128], BF16)
    wkn = A.alloc("wkn", [128, 4, 2, 128], BF16)
    wv = A.alloc("wv", [128, 2, 512], BF16)
    wq = A.alloc("wq", [128, 16, 3, 96], BF16)
    vec = A.alloc("vec", [128, 32], F32)
    rope = [A.alloc(f"rope{i}", [128, 2, 512], F32) for i in range(2)]
    ckvT = A.alloc("ckvT", [128, 2, 512], BF16)
    cqT = A.alloc("cqT", [128, 3, 512], BF16)
    krT = A.alloc("krT", [128, 512], BF16)
    yT = [A.alloc(f"yT{i}", [128, 4, 512], BF16) for i in range(2)]
    KT = [A.alloc(f"KT{i}", [128, 8, 512], BF16) for i in range(2)]
    QT = [A.alloc(f"QT{i}", [128, 8, 512], BF16) for i in range(2)]
    Vt = [A.alloc(f"Vt{i}", [128, 8, 128], BF16) for i in range(2)]
    sig = [A.alloc(f"sig{i}", [128, 512], F32) for i in range(2)]
    P.dma("sp", vec[:], k.inp["l0_vec"], writes=[("vec",)])
    for i in range(15):
        P.dma("sp", wA[:, i], k.wbf["l0_wA"][i].rearrange("p (k n) -> p k n", k=8), reads=[("wbf", "l0_wA", i)], writes=[("wA",)])
    for i in range(4):
        P.dma("sp", wkn[:, i], k.wbf["l0_wkn"][i].rearrange("p (k n) -> p k n", k=2), reads=[("wbf", "l0_wkn", i)], writes=[("wkn",)])
    P.dma("sp", wv[:], k.wbf["l0_wv"][0].rearrange("p (k n) -> p k n", k=2), reads=[("wbf", "l0_wv", 0)], writes=[("wv",)])
    for i in range(16):
        P.dma("sp", wq[:, i], k.wbf["l0_wq"][i].rearrange("p (k n) -> p k n", k=3), reads=[("wbf", "l0_wq", i)], writes=[("wq",)])
    sc = 96.0 ** -0.5
    P.dve(lambda e: e.tensor_scalar(out=vec[:, 8:10], in0=vec[:, 8:10], scalar1=sc, scalar2=None, op0=ALU.mult),
          reads=[("vec",)], writes=[("vec",)])
    for i in range(2):
        P.dve(lambda e, i=i: e.memset(Vt[i][:], 1.0), writes=[("Vt", i), (("Vt", i), 1)])
    rot = PsRot()
    vcnt = 0
    ALLR = slice(0, 128)
    R = slice(64, 96)
    R96 = slice(0, 96)
    for ti, (who, t0, T) in enumerate(TILES):
        lat = (who == 0)
        xb = xt[ti % 2]
        xtoks = [("xtm", ti % 2, c) for c in range(8)]
        load_x_norm(k, l, who, t0, T, xb, xtoks, hT, w)
        htoks = [("hTm", c) for c in range(8)]
        rp = rope[ti % 2]
        if lat:
            P.dma("sp", rp[:, :, 0:T], k.inp["ropeB"][:, :, t0 - NCTX:t0 - NCTX + T], writes=[("rope", ti % 2)])

        def projA(bidx, M):
            b = rot.next()
            mm_group(k, PS[b][0:M, 0:T], lambda kc: wA[:, bidx, kc, 0:M], lambda kc: hT[:, kc, 0:T], 8,
                     [("wA",)] + htoks, ("ps", b))
            return b

        def square(b, M, dst_ap, dtok):
            P.act(lambda e: e.activation(out=dst_ap, in_=PS[b][0:M, 0:T], func=AF.Square), reads=[("ps", b)], writes=[dtok])

        bs = [projA(0, 128), projA(1, 128)]
        for c in range(2):
            square(bs[c], 128, w["sq"][:, c, 0:T], ("sq", c))
        st = next_stat(w)
        mm_group(k, PS[st["bank"]][:, 0:T], lambda kc: k.ones_bf, lambda kc: w["sq"][:, kc, 0:T], 2,
                 [("sq", 0), ("sq", 1), ("cbf",)], ("ps", st["bank"]))
        stats_rstd(k, st, ALLR, T, 1.0 / 256)
        for c in range(2):
            P.dve(lambda e: e.scalar_tensor_tensor(out=ckvT[:, c, 0:T], in0=PS[bs[c]][:, 0:T], scalar=vec[:, c:c + 1],
                                                   in1=st["rstd"][:, 0:T], op0=ALU.mult, op1=ALU.mult),
                  reads=[("ps", bs[c]), ("rstd", st["i"]), ("vec",)], writes=[("ckvT", c)])
        bm = projA(2, 96)
        bsw = projA(3, 96) if lat else None
        sb, stok = next_sqs(w)
        square(bm, 96, sb[0:96, 0:T], stok)
        st = next_stat(w)
        mm_group(k, PS[st["bank"]][0:96, 0:T], lambda kc: k.cbf[0:96, 3, 0:96], lambda kc: sb[0:96, 0:T], 1, [stok, ("cbf",)],
                 ("ps", st["bank"]))
        stats_rstd(k, st, R, T, 1.0 / 32)
        if not lat:
            P.dve(lambda e: e.scalar_tensor_tensor(out=krT[R, 0:T], in0=PS[bm][R, 0:T], scalar=vec[R, 6:7],
                                                   in1=st["rstd"][R, 0:T], op0=ALU.mult, op1=ALU.mult),
                  reads=[("ps", bm), ("rstd", st["i"]), ("vec",)], writes=[("krT",)])
        else:
            rope_apply(k, w, bm, bsw, R, T, vec[R, 6:7], vec[R, 7:8], st, rp, ("rope", ti % 2), krT[R, 0:T], [("krT",)])
        bs = [projA(4 + c, 128) for c in range(3)]
        for c in range(3):
            square(bs[c], 128, w["sq"][:, 2 + c, 0:T], ("sq", 2 + c))
        st = next_stat(w)
        mm_group(k, PS[st["bank"]][:, 0:T], lambda kc: k.ones_bf, lambda kc: w["sq"][:, 2 + kc, 0:T], 3,
                 [("sq", 2), ("sq", 3), ("sq", 4), ("cbf",)], ("ps", st["bank"]))
        stats_rstd(k, st, ALLR, T, 1.0 / 384)
        for c in range(3):
            P.dve(lambda e: e.scalar_tensor_tensor(out=cqT[:, c, 0:T], in0=PS[bs[c]][:, 0:T], scalar=vec[:, 2 + c:3 + c],
                                                   in1=st["rstd"][:, 0:T], op0=ALU.mult, op1=ALU.mult),
                  reads=[("ps", bs[c]), ("rstd", st["i"]), ("vec",)], writes=[("cqT", c)])
        yb = yT[ti % 2]
        for ch in range(4):
            ba_ = projA(7 + ch, 128)
            bg_ = projA(11 + ch, 128)
            sg_ = sig[ch % 2]
            P.act(lambda e: e.activation(out=sg_[:, 0:T], in_=PS[bg_][:, 0:T], func=AF.Sigmoid,
                                         bias=vec[:, 15 + ch:16 + ch], scale=1.0),
                  reads=[("ps", bg_), ("vec",)], writes=[("sig", ch % 2)])
            P.dve(lambda e: e.scalar_tensor_tensor(out=yb[:, ch, 0:T], in0=PS[ba_][:, 0:T],
                                                   scalar=vec[:, 11 + ch:12 + ch], in1=sg_[:, 0:T],
                                                   op0=ALU.add, op1=ALU.mult),
                  reads=[("ps", ba_), ("sig", ch % 2), ("vec",)], writes=[("yT", ti % 2, ch)])
        P.dma("sp", k.yTd[:, :, t0:t0 + T].rearrange("c p t -> p c t"), yb[:, :, 0:T],
              reads=[("yT", ti % 2, ch) for ch in range(4)], writes=[("yTd", ti)])
        Kb = KT[ti % 2]
        ktoks = [("KT", ti % 2, h) for h in range(8)]
        for pr in range(4):
            b = rot.next()
            mm_group(k, PS[b][:, 0:T], lambda kc: wkn[:, pr, kc, :], lambda kc: ckvT[:, kc, 0:T], 2,
                     [("wkn",), ("ckvT", 0), ("ckvT", 1)], ("ps", b))
            sb, stok = next_sqs(w)
            square(b, 128, sb[:, 0:T], stok)
            st = next_stat(w)
            mm_group(k, PS[st["bank"]][:, 0:T], lambda kc: k.bd64_bf, lambda kc: sb[:, 0:T], 1, [stok, ("cbf",)], ("ps", st["bank"]))
            stats_rstd(k, st, ALLR, T, 1.0 / 64)
            for e_ in range(2):
                h = 2 * pr + e_
                P.dve(lambda e: e.scalar_tensor_tensor(
                    out=Kb[0:64, h, 0:T], in0=PS[b][e_ * 64:(e_ + 1) * 64, 0:T], scalar=vec[e_ * 64:(e_ + 1) * 64, 5:6],
                    in1=st["rstd"][e_ * 64:(e_ + 1) * 64, 0:T], op0=ALU.mult, op1=ALU.mult),
                    reads=[("ps", b), ("rstd", st["i"]), ("vec",)], writes=[("KT", ti % 2, h)])
        for h in range(8):
            P.pool(lambda e: e.tensor_copy(out=Kb[R, h, 0:T], in_=krT[R, 0:T]), reads=[("krT",)], writes=[(("KT", ti % 2, h), "r")])
        P.dma("sp", k.KTd[:, :, t0:t0 + T].rearrange("h p t -> p h t"), Kb[0:96, :, 0:T],
              reads=ktoks + [(kt_, "r") for kt_ in ktoks], writes=[("KTd", ti)])
        for s_ in range(T // 128):
            b = rot.next()
            mm_group(k, PS[b][:, 0:512], lambda kc: ckvT[:, kc, s_ * 128:(s_ + 1) * 128], lambda kc: wv[:, kc, :], 2,
                     [("wv",), ("ckvT", 0), ("ckvT", 1)], ("ps", b))
            vb = Vt[vcnt % 2]
            vtok = ("Vt", vcnt % 2)
            vcnt += 1
            src = PS[b][:, 0:512].rearrange("p (h two d) -> p h two d", two=2, d=64)
            dst = vb[:].rearrange("p (h two) d -> p h two d", two=2)
            P.dve(lambda e: e.tensor_copy(out=dst[:, :, 0, 0:64], in_=src[:, :, 0, :]), reads=[("ps", b)], writes=[vtok])
            P.dve(lambda e: e.tensor_copy(out=dst[:, :, 1, 64:128], in_=src[:, :, 1, :]), reads=[("ps", b)], writes=[(vtok, 1)])
            chunk = (t0 + s_ * 128) // 128
            P.dma("sp", k.Vd[:, :, chunk, :].rearrange("h p d -> p h d"), vb[:], reads=[vtok, (vtok, 1)], writes=[("Vd", chunk)])
        Qb = QT[ti % 2]
        qtoks = [("QT", ti % 2, h) for h in range(8)]
        for h in range(8):
            b = rot.next()
            mm_group(k, PS[b][0:96, 0:T], lambda kc: wq[:, h, kc, :], lambda kc: cqT[:, kc, 0:T], 3,
                     [("wq",)] + [("cqT", c) for c in range(3)], ("ps", b))
            bsw = None
            if lat:
                bsw = rot.next()
                mm_group(k, PS[bsw][0:96, 0:T], lambda kc: wq[:, 8 + h, kc, :], lambda kc: cqT[:, kc, 0:T], 3,
                         [("wq",)] + [("cqT", c) for c in range(3)], ("ps", bsw))
            sb, stok = next_sqs(w)
            square(b, 96, sb[0:96, 0:T], stok)
            st = next_stat(w)
            mm_group(k, PS[st["bank"]][0:96, 0:T], lambda kc: k.cbf[0:96, 3, 0:96], lambda kc: sb[0:96, 0:T], 1, [stok, ("cbf",)],
                     ("ps", st["bank"]))
            stats_rstd(k, st, R96, T, vec[0:96, 10:11], extra_reads=[("vec",)])
            nr = 96 if not lat else 64
            P.dve(lambda e: e.scalar_tensor_tensor(out=Qb[0:nr, h, 0:T], in0=PS[b][0:nr, 0:T], scalar=vec[0:nr, 8:9],
                                                   in1=st["rstd"][0:nr, 0:T], op0=ALU.mult, op1=ALU.mult),
                  reads=[("ps", b), ("rstd", st["i"]), ("vec",)], writes=[("QT", ti % 2, h)])
            if lat:
                rope_apply(k, w, b, bsw, R, T, vec[R, 8:9], vec[R, 9:10], st, rp, ("rope", ti % 2),
                           Qb[R, h, 0:T], [(("QT", ti % 2, h), "r")])
        P.dma("sp", k.QTd[:, :, t0:t0 + T].rearrange("h p t -> p h t"), Qb[0:96, :, 0:T],
              reads=qtoks + [(q_, "r") for q_ in qtoks], writes=[("QTd", ti)])
    A.release(m)


def rope_apply(k, w, bm, bsw, R, T, g, gsw, st, rp, rptok, out_ap, otoks):
    P = k.P
    j = w["rai"] % 2
    w["rai"] += 1
    ra, rb = w["rab"][j]
    rstd, rtok = st["rstd"], ("rstd", st["i"])
    P.dve(lambda e: e.scalar_tensor_tensor(out=ra[R, 0:T], in0=k.PS[bm][R, 0:T], scalar=g, in1=rstd[R, 0:T],
                                           op0=ALU.mult, op1=ALU.mult),
          reads=[("ps", bm), rtok, ("vec",)], writes=[("ra", j)])
    P.dve(lambda e: e.scalar_tensor_tensor(out=rb[R, 0:T], in0=k.PS[bsw][R, 0:T], scalar=gsw, in1=rstd[R, 0:T],
                                           op0=ALU.mult, op1=ALU.mult),
          reads=[("ps", bsw), rtok, ("vec",)], writes=[("rb", j)])
    P.pool(lambda e: e.tensor_tensor(out=ra[R, 0:T], in0=ra[R, 0:T], in1=rp[R, 0, 0:T], op=ALU.mult),
           reads=[("ra", j), rptok], writes=[("ra", j)])
    P.pool(lambda e: e.tensor_tensor(out=rb[R, 0:T], in0=rb[R, 0:T], in1=rp[R, 1, 0:T], op=ALU.mult),
           reads=[("rb", j), rptok], writes=[("rb", j)])
    P.dve(lambda e: e.tensor_tensor(out=out_ap, in0=ra[R, 0:T], in1=rb[R, 0:T], op=ALU.add),
          reads=[("ra", j), ("rb", j)], writes=otoks)


def next_stat(w):
    st = w["stats"][w["si"] % len(w["stats"])]
    st["bank"] = 6 + (w["si"] % 2)
    w["si"] += 1
    return st


def next_sqs(w):
    i = w["sqi"] % len(w["sqs"])
    w["sqi"] += 1
    return w["sqs"][i], ("sqs", i)


def attn_phase(k, mode):
    P, A, PS = k.P, k.A, k.PS
    m = A.mark()
    mla = (mode == "mla")
    Kt = [A.alloc(f"Kt{i}", [128, TALL], BF16) for i in range(4)]
    Vh = [A.alloc(f"Vh{i}", [128, 34, 128], BF16) for i in range(4)]
    Qt = [A.alloc(f"Qt{i}", [128, 2, 512], BF16) for i in range(2)]
    PT = [A.alloc(f"PT{i}", [128, 512], BF16) for i in range(4)]
    rec = A.alloc("rec", [128, 512], F32)
    attT = [A.alloc(f"attT{i}", [128, 512], BF16) for i in range(2)]
    qtiles = TILES if mla else TILES[1:]
    qc = 0
    sc = 0
    pc = 0
    ac = 0
    for hp in range(4):
        kb = (hp % 2) * 2
        if mla:
            for e_ in range(2):
                P.dma("sp", Kt[kb + e_][0:96, :], k.KTd[2 * hp + e_], writes=[("Kt", kb + e_)])
                P.dma("sp", Vh[kb + e_][:], k.Vd[2 * hp + e_], writes=[("Vh", kb + e_)])
        else:
            for e_ in range(2):
                P.dma("sp", Kt[kb + e_][:], k.KdupTd[hp // 2], writes=[("Kt", kb + e_)])
                P.dve(lambda e: e.memset(Kt[kb + e_][(1 - e_) * 64:(2 - e_) * 64, :], 0.0), writes=[("Kt", kb + e_)])
                P.dma("sp", Vh[kb + e_][:], k.VgD[(hp // 2) * 2 + e_], writes=[("Vh", kb + e_)])
        for (who, t0, T) in qtiles:
            chunks = [0, 1] if who == 1 else list(range(34))
            Qb = Qt[qc % 2]
            qtok = ("Qt", qc % 2)
            ab = attT[qc % 2]
            atok = ("attT", qc % 2)
            qc += 1
            if mla:
                P.dma("sp", Qb[0:96, :, 0:T], k.QTd[2 * hp:2 * hp + 2, :, t0:t0 + T].rearrange("h p t -> p h t"), writes=[qtok])
            else:
                P.dma("sp", Qb[:, 0, 0:T], k.QgTd[hp, :, t0:t0 + T], writes=[qtok])
            for e_ in range(2):
                if mla:
                    Ksb, ktok, r0, r1 = Kt[kb + e_], ("Kt", kb + e_), 0, 96
                    qap = Qb[0:96, e_, 0:T]
                else:
                    Ksb, ktok, r0, r1 = Kt[kb + e_], ("Kt", kb + e_), 0, 128
                    qap = Qb[:, 0, 0:T]
                Vsb, vtok = Vh[kb + e_], ("Vh", kb + e_)
                acc = 4 + (ac % 2)
                ac += 1
                n = len(chunks)
                pend = []
                LOOK = 2

                def do_pv(item):
                    pidx, pc_, ppb = item
                    P.pe(lambda e: e.matmul(PS[acc][:, 0:T], lhsT=Vsb[:, pc_, :], rhs=PT[ppb][:, 0:T],
                                            start=(pidx == 0), stop=(pidx == n - 1)),
                         reads=[vtok, ("PT", ppb)], writes=[("ps", acc)])
                for idx, c in enumerate(chunks):
                    sbk = sc % 4
                    sc += 1
                    P.pe(lambda e: e.matmul(PS[sbk][:, 0:T], lhsT=Ksb[r0:r1, c * 128:(c + 1) * 128], rhs=qap,
                                            start=True, stop=True),
                         reads=[ktok, qtok], writes=[("ps", sbk)])
                    pb = pc % 4
                    pc += 1
                    P.act(lambda e: e.activation(out=PT[pb][:, 0:T], in_=PS[sbk][:, 0:T], func=AF.Exp),
                          reads=[("ps", sbk)], writes=[("PT", pb)])
                    pend.append((idx, c, pb))
                    if len(pend) > LOOK:
                        do_pv(pend.pop(0))
                while pend:
                    do_pv(pend.pop(0))
                nlo, dlo = (0, 64) if e_ == 0 else (64, 0)
                P.dve(lambda e: e.reciprocal(out=rec[nlo:nlo + 64, 0:T], in_=PS[acc][dlo:dlo + 64, 0:T]),
                      reads=[("ps", acc)], writes=[("rec", e_)])
                P.dve(lambda e: e.tensor_tensor(out=ab[nlo:nlo + 64, 0:T], in0=PS[acc][nlo:nlo + 64, 0:T], in1=rec[nlo:nlo + 64, 0:T],
                                                op=ALU.mult),
                      reads=[("ps", acc), ("rec", e_)], writes=[(atok, e_)])
            P.dma("sp", k.attTd[hp, :, t0:t0 + T], ab[:, 0:T], reads=[(atok, 0), (atok, 1)], writes=[("attTd", hp, t0)])
    A.release(m)


def l0_conv_phase(k):
    P, A, PS = k.P, k.A, k.PS
    m = A.mark()
    vec = A.alloc("vec", [128, 32], F32)
    dw = A.alloc("dw", [128, 4, 31], F32)
    diag = A.alloc("diag", [128, 4, 31, 128], BF16)
    ybuf = [A.alloc(f"ybuf{i}", [128, 4, 544], BF16) for i in range(2)]
    cv = A.alloc("cv", [128, 4, 512], F32)
    cb = A.alloc("cb", [128, 4, 512], BF16)
    sqv = A.alloc("sqv", [128, 4, 512], BF16)
    co = [A.alloc(f"co{i}", [128, 4, 512], BF16) for i in range(2)]
    mean = A.alloc("mean", [128, 512], F32)
    msq = A.alloc("msq", [128, 512], F32)
    r1 = A.alloc("r1", [128, 512], F32)
    r2 = A.alloc("r2", [128, 512], F32)
    rstd = A.alloc("rstd", [128, 512], F32)
    t1 = [A.alloc(f"t1{i}", [128, 512], F32) for i in range(2)]
    identb = k.cbf[:, 0, :]
    P.dma("sp", vec[:], k.inp["l0_vec"], writes=[("vec",)])
    P.dma("sp", dw[:], k.inp["l0_dw"], writes=[("dw",)])
    for ch in range(4):
        for j in range(31):
            P.dve(lambda e: e.tensor_scalar(out=diag[:, ch, j, :], in0=identb, scalar1=dw[:, ch, j:j + 1], scalar2=None, op0=ALU.mult),
                  reads=[("dw",), ("cbf",)], writes=[("diag", ch, j)])
    rot = PsRot(banks=(0, 1, 2, 3))
    for ti, (who, t0, T) in enumerate(TILES):
        s0, s1 = (0, NCTX) if who == 1 else (NCTX, TALL)
        lo = max(t0 - 15, s0)
        hi = min(t0 + T + 15, s1)
        yb = ybuf[ti % 2]
        toks = [("yb", ti % 2, x) for x in "LMR"]
        wr = [toks[1]]
        if lo > t0 - 15:
            P.pool(lambda e: e.memset(yb[:, :, 0:15], 0.0), writes=[toks[0]])
        else:
            wr.append(toks[0])
        if hi < t0 + T + 15:
            P.pool(lambda e: e.memset(yb[:, :, T + 15:T + 30], 0.0), writes=[toks[2]])
        else:
            wr.append(toks[2])
        P.dma("sp", yb[:, :, lo - (t0 - 15):hi - (t0 - 15)], k.yTd[:, :, lo:hi].rearrange("c p t -> p c t"), writes=wr)
        for ch in range(4):
            b = rot.next()

            def mm(e, ch=ch, b=b):
                for j in range(31):
                    e.matmul(PS[b][:, 0:T], lhsT=diag[:, ch, j, :], rhs=yb[:, ch, j:j + T], start=(j == 0), stop=(j == 30))
            P.pe(mm, reads=toks + [("diag", ch, j) for j in range(31)], writes=[("ps", b)])
            P.act(lambda e: e.activation(out=cv[:, ch, 0:T], in_=PS[b][:, 0:T], func=AF.Identity, bias=vec[:, 19 + ch:20 + ch], scale=1.0),
                  reads=[("ps", b), ("vec",)], writes=[("cv", ch)])
            P.pool(lambda e: e.tensor_copy(out=cb[:, ch, 0:T], in_=cv[:, ch, 0:T]), reads=[("cv", ch)], writes=[("cb", ch)])
            P.pool(lambda e: e.tensor_tensor(out=sqv[:, ch, 0:T], in0=cv[:, ch, 0:T], in1=cv[:, ch, 0:T], op=ALU.mult),
                   reads=[("cv", ch)], writes=[("sqv", ch)])
        mm_group(k, PS[6][:, 0:T], lambda kc: k.ones_bf, lambda kc: cb[:, kc, 0:T], 4, [("cb", c) for c in range(4)] + [("cbf",)], ("ps", 6))
        mm_group(k, PS[7][:, 0:T], lambda kc: k.ones_bf, lambda kc: sqv[:, kc, 0:T], 4, [("sqv", c) for c in range(4)] + [("cbf",)], ("ps", 7))
        P.dve(lambda e: e.tensor_scalar(out=mean[:, 0:T], in0=PS[6][:, 0:T], scalar1=1.0 / 512, scalar2=None, op0=ALU.mult),
              reads=[("ps", 6)], writes=[("mean",)])
        P.pool(lambda e: e.tensor_tensor(out=msq[:, 0:T], in0=mean[:, 0:T], in1=mean[:, 0:T], op=ALU.mult),
               reads=[("mean",)], writes=[("msq",)])
        P.dve(lambda e: e.scalar_tensor_tensor(out=r2[:, 0:T], in0=PS[7][:, 0:T], scalar=1.0 / 512, in1=msq[:, 0:T],
                                               op0=ALU.mult, op1=ALU.subtract),
              reads=[("ps", 7), ("msq",)], writes=[("r2",)])
        P.dve(lambda e: e.tensor_scalar(out=r1[:, 0:T], in0=r2[:, 0:T], scalar1=EPS, scalar2=None, op0=ALU.add),
              reads=[("r2",)], writes=[("r1",)])
        P.act(lambda e: e.activation(out=r2[:, 0:T], in_=r1[:, 0:T], func=AF.Sqrt), reads=[("r1",)], writes=[("r2",)])
        P.dve(lambda e: e.reciprocal(out=rstd[:, 0:T], in_=r2[:, 0:T]), reads=[("r2",)], writes=[("rstd",)])
        cob = co[ti % 2]
        for ch in range(4):
            tb = t1[ch % 2]
            P.dve(lambda e: e.tensor_tensor(out=tb[:, 0:T], in0=cv[:, ch, 0:T], in1=mean[:, 0:T], op=ALU.subtract),
                  reads=[("cv", ch), ("mean",)], writes=[("t1", ch % 2)])
            P.pool(lambda e: e.tensor_tensor(out=tb[:, 0:T], in0=tb[:, 0:T], in1=rstd[:, 0:T], op=ALU.mult),
                   reads=[("t1", ch % 2), ("rstd",)], writes=[("t1", ch % 2)])
            P.act(lambda e: e.activation(out=cob[:, ch, 0:T], in_=tb[:, 0:T], func=AF.Silu, bias=vec[:, 27 + ch:28 + ch],
                                         scale=vec[:, 23 + ch:24 + ch]),
                  reads=[("t1", ch % 2), ("vec",)], writes=[("co", ti % 2, ch)])
        P.dma("sp", k.attTd[4:8, :, t0:t0 + T].rearrange("c p t -> p c t"), cob[:, :, 0:T],
              reads=[("co", ti % 2, ch) for ch in range(4)], writes=[("attTd", "conv", ti)])
    A.release(m)


def mixout_phase(k, l, tiles):
    P, A, PS = k.P, k.A, k.PS
    m = A.mark()
    wo = A.alloc("wo", [128, 8, 8, 128], BF16)
    xt = [A.alloc(f"xt{i}", [128, 8, 512], F32) for i in range(2)]
    at = [A.alloc(f"at{i}", [128, 8, 512], BF16) for i in range(2)]
    name = f"l{l}_wo"
    for i in range(8):
        P.dma("sp", wo[:, i], k.wbf[name][i].rearrange("p (k n) -> p k n", k=8), reads=[("wbf", name, i)], writes=[("wo",)])
    rot = PsRot()
    for ti, (who, t0, T) in enumerate(tiles):
        xb = xt[ti % 2]
        ab = at[ti % 2]
        xtoks = [("xt", ti % 2, c) for c in range(8)]
        P.dma("sp", xb[:, :, 0:T], k.xTd[:, :, t0:t0 + T].rearrange("c p t -> p c t"), writes=xtoks)
        P.dma("sp", ab[:, :, 0:T], k.attTd[:, :, t0:t0 + T].rearrange("c p t -> p c t"), writes=[("at", ti % 2)])
        for d in range(8):
            b = rot.next()
            mm_group(k, PS[b][:, 0:T], lambda kc: wo[:, d, kc, :], lambda kc: ab[:, kc, 0:T], 8, [("wo",), ("at", ti % 2)], ("ps", b))
            P.dve(lambda e: e.scalar_tensor_tensor(out=xb[:, d, 0:T], in0=PS[b][:, 0:T], scalar=k.modv[:, l, who, 40 + d:41 + d],
                                                   in1=xb[:, d, 0:T], op0=ALU.mult, op1=ALU.add),
                  reads=[("ps", b), xtoks[d]], writes=[xtoks[d]])
        P.dma("sp", k.xTd[:, :, t0:t0 + T].rearrange("c p t -> p c t"), xb[:, :, 0:T], reads=xtoks, writes=[("xTd", t0)])
    A.release(m)


def l1_proj_phase(k):
    P, A, PS = k.P, k.A, k.PS
    l = 1
    m = A.mark()
    w = common_work(k, A)
    xt = [A.alloc(f"xt{i}", [128, 8, 512], F32) for i in range(2)]
    hT = A.alloc("hTm", [128, 8, 512], BF16)
    wA = A.alloc("wA", [128, 20, 8, 128], BF16)
    wv = A.alloc("wv", [128, 8, 640], BF16)
    vec = A.alloc("vec", [128, 8], F32)
    rope = [A.alloc(f"rope{i}", [128, 2, 512], F32) for i in range(2)]
    KdT = [A.alloc(f"KdT{i}", [128, 2, 512], BF16) for i in range(2)]
    nkT = [A.alloc(f"nkT{i}", [128, 4, 512], BF16) for i in range(2)]
    QgT = [A.alloc(f"QgT{i}", [128, 4, 512], BF16) for i in range(2)]
    nqT = [A.alloc(f"nqT{i}", [128, 4, 512], BF16) for i in range(2)]
    Vgt = [A.alloc(f"Vgt{i}", [128, 4, 128], BF16) for i in range(2)]
    NVt = [A.alloc(f"NVt{i}", [128, 8, 128], BF16) for i in range(2)]
    P.dma("sp", vec[:], k.inp["l1_vec"], writes=[("vec",)])
    for i in range(20):
        P.dma("sp", wA[:, i], k.wbf["l1_wA"][i].rearrange("p (k n) -> p k n", k=8), reads=[("wbf", "l1_wA", i)], writes=[("wA",)])
    P.dma("sp", wv[:], k.wbf["l1_wv"][0].rearrange("p (k n) -> p k n", k=8), reads=[("wbf", "l1_wv", 0)], writes=[("wv",)])
    for c0, c1 in ((2, 4), (5, 6)):
        P.dve(lambda e: e.tensor_scalar(out=vec[:, c0:c1], in0=vec[:, c0:c1], scalar1=0.125, scalar2=None, op0=ALU.mult),
              reads=[("vec",)], writes=[("vec",)])
    for i in range(2):
        P.dve(lambda e: e.memset(Vgt[i][:], 1.0), writes=[("Vgt", i), (("Vgt", i), 1)])
        P.dve(lambda e: e.memset(NVt[i][:], 1.0), writes=[("NVt", i), (("NVt", i), 1)])
    rot = PsRot()
    cnt = {"sq": 0, "v": 0}
    Rall = slice(0, 128)
    for ti, (who, t0, T) in enumerate(TILES):
        lat = (who == 0)
        xb = xt[ti % 2]
        xtoks = [("xtm", ti % 2, c) for c in range(8)]
        load_x_norm(k, l, who, t0, T, xb, xtoks, hT, w)
        htoks = [("hTm", c) for c in range(8)]
        rp = rope[ti % 2]
        if lat:
            P.dma("sp", rp[:, :, 0:T], k.inp["ropeA"][:, :, t0 - NCTX:t0 - NCTX + T], writes=[("rope", ti % 2)])

        def projA(bidx):
            b = rot.next()
            mm_group(k, PS[b][:, 0:T], lambda kc: wA[:, bidx, kc, :], lambda kc: hT[:, kc, 0:T], 8, [("wA",)] + htoks, ("ps", b))
            return b

        def normed_chunk(bmain, bswap, gcol, out_ap, otok, do_rope):
            b = projA(bmain)
            bsw = projA(bswap) if do_rope else None
            sb, stok = next_sqs(w)
            P.act(lambda e: e.activation(out=sb[:, 0:T], in_=PS[b][:, 0:T], func=AF.Square), reads=[("ps", b)], writes=[stok])
            st = next_stat(w)
            mm_group(k, PS[st["bank"]][:, 0:T], lambda kc: k.bd64_bf, lambda kc: sb[:, 0:T], 1, [stok, ("cbf",)], ("ps", st["bank"]))
            stats_rstd(k, st, Rall, T, 1.0 / 64)
            if do_rope:
                rope_apply(k, w, b, bsw, Rall, T, vec[:, gcol:gcol + 1], vec[:, gcol + 1:gcol + 2], st, rp,
                           ("rope", ti % 2), out_ap, [otok])
            else:
                P.dve(lambda e: e.scalar_tensor_tensor(out=out_ap, in0=PS[b][:, 0:T], scalar=vec[:, gcol:gcol + 1],
                                                       in1=st["rstd"][:, 0:T], op0=ALU.mult, op1=ALU.mult),
                      reads=[("ps", b), ("rstd", st["i"]), ("vec",)], writes=[otok])

        Kb = KdT[ti % 2]
        for hk in range(2):
            normed_chunk(hk, 2 + hk, 0, Kb[:, hk, 0:T], ("KdT", ti % 2, hk), lat)
        P.dma("sp", k.KdupTd[:, :, t0:t0 + T].rearrange("h p t -> p h t"), Kb[:, :, 0:T],
              reads=[("KdT", ti % 2, hk) for hk in range(2)], writes=[("KdupTd", ti)])
        nb = nkT[ti % 2]
        for c in range(4):
            normed_chunk(4 + c, None, 4, nb[:, c, 0:T], ("nkT", ti % 2, c), False)
        P.dma("sp", k.nkTd[:, :, t0:t0 + T].rearrange("c p t -> p c t"), nb[:, :, 0:T],
              reads=[("nkT", ti % 2, c) for c in range(4)], writes=[("nkTd", ti)])
        for st in range(T // 128):
            b1 = rot.next()
            mm_group(k, PS[b1][:, 0:128], lambda kc: hT[:, kc, st * 128:(st + 1) * 128], lambda kc: wv[:, kc, 0:128], 8,
                     [("wv",)] + htoks, ("ps", b1))
            b2 = rot.next()
            mm_group(k, PS[b2][:, 0:512], lambda kc: hT[:, kc, st * 128:(st + 1) * 128], lambda kc: wv[:, kc, 128:640], 8,
                     [("wv",)] + htoks, ("ps", b2))
            vi = cnt["v"] % 2
            cnt["v"] += 1
            vg, nv = Vgt[vi], NVt[vi]
            vgd = vg[:].rearrange("p (hk par) d -> p hk par d", par=2)
            s1 = PS[b1][:, 0:128].rearrange("p (hk d) -> p hk d", d=64)
            P.dve(lambda e: e.tensor_copy(out=vgd[:, :, 0, 0:64], in_=s1), reads=[("ps", b1)], writes=[("Vgt", vi)])
            P.dve(lambda e: e.tensor_copy(out=vgd[:, :, 1, 64:128], in_=s1), reads=[("ps", b1)], writes=[(("Vgt", vi), 1)])
            nvd = nv[:].rearrange("p (h two) d -> p h two d", two=2)
            s2 = PS[b2][:, 0:512].rearrange("p (h two d) -> p h two d", two=2, d=64)
            P.dve(lambda e: e.tensor_copy(out=nvd[:, :, 0, 0:64], in_=s2[:, :, 0, :]), reads=[("ps", b2)], writes=[("NVt", vi)])
            P.dve(lambda e: e.tensor_copy(out=nvd[:, :, 1, 64:128], in_=s2[:, :, 1, :]), reads=[("ps", b2)],
                  writes=[(("NVt", vi), 1)])
            chunk = (t0 + st * 128) // 128
            P.dma("sp", k.VgD[:, :, chunk, :].rearrange("v p d -> p v d"), vg[:], reads=[("Vgt", vi), (("Vgt", vi), 1)],
                  writes=[("VgD", chunk)])
            P.dma("sp", k.NVd[:, :, chunk, :].rearrange("h p d -> p h d"), nv[:], reads=[("NVt", vi), (("NVt", vi), 1)],
                  writes=[("NVd", chunk)])
        if lat:
            qb = QgT[ti % 2]
            for c in range(4):
                normed_chunk(8 + c, 12 + c, 2, qb[:, c, 0:T], ("QgT", ti % 2, c), True)
            P.dma("sp", k.QgTd[:, :, t0:t0 + T].rearrange("c p t -> p c t"), qb[:, :, 0:T],
                  reads=[("QgT", ti % 2, c) for c in range(4)], writes=[("QgTd", ti)])
            nq = nqT[ti % 2]
            for c in range(4):
                normed_chunk(16 + c, None, 5, nq[:, c, 0:T], ("nqT", ti % 2, c), False)
            P.dma("sp", k.nqTd[:, :, t0:t0 + T].rearrange("c p t -> p c t"), nq[:, :, 0:T],
                  reads=[("nqT", ti % 2, c) for c in range(4)], writes=[("nqTd", ti)])
    A.release(m)


def l1_na_phase(k):
    P, A, PS = k.P, k.A, k.PS
    m = A.mark()
    nk = A.alloc("nk", [128, 4, TALL], BF16)
    NV = A.alloc("NV", [128, 8, 34, 128], BF16)
    nab = A.alloc("nab", [128, NSIG * 8, 128], BF16)
    nq = [[A.alloc(f"nq{i}{e_}", [128, 4, 128], BF16) for e_ in range(2)] for i in range(2)]
    PT = [A.alloc(f"PT{i}", [128, 8, 128], BF16) for i in range(3)]
    for i in range(2):
        for e_ in range(2):
            P.dve(lambda e: e.memset(nq[i][e_][:], 0.0), writes=[("nq", i, e_)])
    rec = A.alloc("rec", [128, 128], F32)
    nat = [A.alloc(f"nat{i}", [128, 4, 128], BF16) for i in range(2)]
    identb = k.cbf[:, 0, :]
    P.dma("sp", nk[:], k.nkTd.rearrange("c p t -> p c t"), writes=[("nk",)])
    for h in range(8):
        P.dma("sp", NV[:, h], k.NVd[h], writes=[("NV", h)])
    for g in range(8):
        n0, n1 = g * NSIG, (g + 1) * NSIG
        P.dma("sp", nab[:, n0:n1, :], k.wbf["l1_nab"][n0:n1].rearrange("n p q -> p n q"),
              reads=[("wbf", "l1_nab", i) for i in range(n0, n1)], writes=[("nab", g)])
    nabtoks = [("nab", g) for g in range(8)]
    hc = 0
    pending = []

    def pv_stage(i, h, chunks, n, acc, pt, pttok, natb):
        ch = h // 2
        t0 = NCTX + i * 128

        def pv(e):
            for idx, (c, sig) in enumerate(chunks):
                e.matmul(PS[acc][:, 0:128], lhsT=NV[:, h, c, :], rhs=pt[:, idx, :], start=(idx == 0), stop=(idx == n - 1))
        P.pe(pv, reads=[("NV", h), (pttok, 0), (pttok, 1)], writes=[("ps", acc)])
        nlo, dlo = (0, 64) if h % 2 == 0 else (64, 0)
        P.dve(lambda e: e.reciprocal(out=rec[nlo:nlo + 64, :], in_=PS[acc][dlo:dlo + 64, 0:128]),
              reads=[("ps", acc)], writes=[("rec", h % 2)])
        P.dve(lambda e: e.tensor_tensor(out=natb[nlo:nlo + 64, ch, :], in0=PS[acc][nlo:nlo + 64, 0:128], in1=rec[nlo:nlo + 64, :],
                                        op=ALU.mult),
              reads=[("ps", acc), ("rec", h % 2)], writes=[("nat", i % 2, h)])
        if h == 7:
            P.dma("sp", k.attTd[4:8, :, t0:t0 + 128].rearrange("c p t -> p c t"), natb[:],
                  reads=[("nat", i % 2, hh) for hh in range(8)], writes=[("attTd", "na", i)])

    for i in range(32):
        t0 = NCTX + i * 128
        nqb = nq[i % 2]
        natb = nat[i % 2]
        for e_ in range(2):
            P.dma("sp", nqb[e_][e_ * 64:(e_ + 1) * 64], k.nqTd[:, e_ * 64:(e_ + 1) * 64, t0:t0 + 128].rearrange("c p t -> p c t"),
                  writes=[("nq", i % 2, e_)])
        chunks = [(0, None), (1, None)] + [(2 + kc, NA_MAP[(i, kc)]) for kc in NA_CHUNKS[i]]
        n = len(chunks)
        for h in range(8):
            ch = h // 2
            sb0 = 2 * (hc % 3)
            acc = 6 + (hc % 2)
            pt = PT[hc % 3]
            pttok = ("PTn", hc % 3)
            hc += 1

            def smm(e):
                for idx, (c, sig) in enumerate(chunks):
                    o = PS[sb0 + idx // 4][:, (idx % 4) * 128:(idx % 4 + 1) * 128]
                    e.matmul(o, lhsT=nk[:, ch, c * 128:(c + 1) * 128], rhs=nqb[h % 2][:, ch, :],
                             start=True, stop=(sig is None))
                    if sig is not None:
                        e.matmul(o, lhsT=identb, rhs=nab[:, sig * 8 + h, :], start=False, stop=True)
            P.pe(smm, reads=[("nk",), ("nq", i % 2, h % 2), ("cbf",)] + nabtoks, writes=[("ps", sb0), ("ps", sb0 + 1)])
            P.act(lambda e: e.activation(out=pt[:, 0:4, :], in_=PS[sb0][:, 0:512].rearrange("p (c q) -> p c q", q=128), func=AF.Exp),
                  reads=[("ps", sb0)], writes=[(pttok, 0)])
            P.act(lambda e: e.activation(out=pt[:, 4:n, :], in_=PS[sb0 + 1][:, 0:(n - 4) * 128].rearrange("p (c q) -> p c q", q=128),
                                         func=AF.Exp),
                  reads=[("ps", sb0 + 1)], writes=[(pttok, 1)])
            pending.append((i, h, chunks, n, acc, pt, pttok, natb))
            if len(pending) > 1:
                pv_stage(*pending.pop(0))
    while pending:
        pv_stage(*pending.pop(0))
    A.release(m)
```

```python
import numpy as np
import concourse.bass as bass
import concourse.mybir as mybir
from concourse.bass_utils import run_bass_kernel_spmd

F32 = mybir.dt.float32
BF16 = mybir.dt.bfloat16
AF = mybir.ActivationFunctionType
ALU = mybir.AluOpType

D = 1024
S = 4096
NCTX = 256
TALL = S + NCTX
DFF = 2816
NFC = DFF // 128
EPS = 1e-6
GRID_W = 64
NDMASEM = 40
RING = {"sp": (0, 28), "pool": (28, 12)}
TILES = [(1, 0, 256)] + [(0, 256 + 512 * i, 512) for i in range(8)]


class Op:
    __slots__ = ("stream", "kind", "fn", "idx", "event", "signal", "cclock", "waits", "slot")


class _Rec:
    def __init__(self):
        self.calls = []

    def __getattr__(self, name):
        def f(*a, **kw):
            self.calls.append((name, a, kw))
            return None
        return f


def _replay(calls):
    def fn(e):
        ins = None
        for name, a, kw in calls:
            ins = getattr(e, name)(*a, **kw)
        return ins
    return fn


class Prog:
    STREAMS = ("pe", "act", "dve", "pool", "sp")

    def __init__(self, nc):
        self.nc = nc
        self.ops = {s: [] for s in self.STREAMS}
        self.clock = {s: {} for s in self.STREAMS}
        self.lastw = {}
        self.readers = {}
        self.ncomp = {s: 0 for s in self.STREAMS}
        self.pending = {}
        self.dma_slot_last = [None] * NDMASEM
        self.dma_slot_cnt = [0] * NDMASEM
        self.dma_rr = {"sp": 0, "pool": 0}
        self.dmas_since_barrier = []
        self.bg_next = False
        self.bg = []
        self.last_comp = {}
        self.nops = 0
        self.maxops = None

    def _add(self, stream, kind, fn, reads, writes):
        if self.maxops is not None and self.nops >= self.maxops and fn is not None:
            return None
        op = Op()
        op.stream = stream
        op.kind = kind
        if fn is not None:
            rec = _Rec()
            fn(rec)
            fn = _replay(rec.calls)
        op.fn = fn
        op.signal = False
        op.slot = None
        self.nops += 1
        deps = []
        for r in reads:
            w = self.lastw.get(r)
            if w is not None:
                deps.append((w, True))
        for w_ in writes:
            lw = self.lastw.get(w_)
            if lw is not None:
                deps.append((lw, False))
            for rd in self.readers.get(w_, ()):
                deps.append((rd, False))
        for r in reads:
            self.readers.setdefault(r, []).append(op)
        for w_ in writes:
            self.readers[w_] = []
            self.lastw[w_] = op
        pend = self.pending.pop(stream, None)
        if pend:
            deps.extend((p, True) for p in pend)
        if kind == "d":
            base, cnt_ = RING[stream]
            slot = base + self.dma_rr[stream]
            self.dma_rr[stream] = (self.dma_rr[stream] + 1) % cnt_
            prev = self.dma_slot_last[slot]
            if prev is not None:
                deps.append((prev, True))
            self.dma_slot_cnt[slot] += 1
            self.dma_slot_last[slot] = op
            op.slot = slot
            op.event = (("d", slot), 16 * self.dma_slot_cnt[slot])
            if self.bg_next:
                self.bg.append(op)
            else:
                self.dmas_since_barrier.append(op)
        else:
            self.ncomp[stream] += 1
            op.event = (("e", stream), self.ncomp[stream])
            self.last_comp[stream] = op
        clk = self.clock[stream]
        waits = {}
        for d, raw in deps:
            if d is op:
                continue
            if d.stream == stream and d.kind == "c":
                if stream == "pe" or not raw:
                    continue
            key, val = d.event
            if clk.get(key, 0) >= val:
                continue
            if waits.get(key, (0,))[0] < val:
                waits[key] = (val, d)
            for k, v in d.cclock.items():
                if clk.get(k, 0) < v:
                    clk[k] = v
        for key, (val, d) in waits.items():
            d.signal = True
        op.waits = {k: v[0] for k, v in waits.items()}
        cc = dict(clk)
        cc[op.event[0]] = op.event[1]
        op.cclock = cc
        if kind == "c" and stream != "pe":
            pass
        self.ops[stream].append(op)
        return op

    def pe(self, fn, reads=(), writes=()):
        return self._add("pe", "c", fn, reads, writes)

    def act(self, fn, reads=(), writes=()):
        return self._add("act", "c", fn, reads, writes)

    def dve(self, fn, reads=(), writes=()):
        return self._add("dve", "c", fn, reads, writes)

    def pool(self, fn, reads=(), writes=()):
        return self._add("pool", "c", fn, reads, writes)

    def dma(self, q, out, in_, reads=(), writes=()):
        return self._add(q, "d", lambda e: e.dma_start(out=out, in_=in_), reads, writes)

    def barrier(self):
        lst = list(self.last_comp.values()) + self.dmas_since_barrier
        self.dmas_since_barrier = []
        for s in self.STREAMS:
            self.pending[s] = list(lst)

    def emit(self):
        nc = self.nc
        self.dmas_since_barrier = self.dmas_since_barrier + self.bg
        self.barrier()
        fin = self._add("sp", "c", None, (), ())
        esem = {s: nc.alloc_semaphore("es_" + s) for s in self.STREAMS}
        dsem = [nc.alloc_semaphore(f"ds{i}") for i in range(NDMASEM)]
        sigcount = {}
        for s in self.STREAMS:
            cnt = 0
            m = {}
            for o in self.ops[s]:
                if o.kind == "c":
                    if o.signal:
                        cnt += 1
                    m[o.event[1]] = (cnt, o.signal)
            sigcount[s] = m

        def resolve(key, val):
            if key[0] == "d":
                return dsem[key[1]], val
            cnt, sig = sigcount[key[1]][val]
            assert sig
            return esem[key[1]], cnt

        with nc.Block() as block:
            decos = {"pe": block.tensor, "act": block.scalar, "dve": block.vector,
                     "pool": block.gpsimd, "sp": block.sync}
            for s in self.STREAMS:
                ops = self.ops[s]

                def body(eng, ops=ops, s=s):
                    for o in ops:
                        for key, val in o.waits.items():
                            sem, v = resolve(key, val)
                            eng.wait_ge(sem, v)
                        if o.fn is None:
                            continue
                        ins = o.fn(eng)
                        if o.kind == "d":
                            ins.then_inc(dsem[o.slot], 16)
                        elif o.signal:
                            ins.then_inc(esem[s], 1)
                decos[s](body)


class Arena:
    def __init__(self, nc, limit=229376):
        self.nc = nc
        self.off = 16640
        self.limit = limit
        self.n = 0

    def alloc(self, name, shape, dtype):
        esz = 4 if dtype == F32 else 2
        sz = esz
        for d in shape[1:]:
            sz *= d
        sz = (sz + 63) // 64 * 64
        assert self.off + sz <= self.limit, (name, self.off, sz)
        self.n += 1
        t = self.nc.alloc_sbuf_tensor_at(f"{name}_{self.n}", list(shape), dtype, offset=self.off)
        self.off += sz
        return t

    def mark(self):
        return self.off

    def release(self, m):
        self.off = m


def kmajor(w, ncols_pad=None):
    K, N = w.shape
    return np.ascontiguousarray(w.reshape(K // 128, 128, N).transpose(1, 0, 2))


def pvec(v):
    return np.ascontiguousarray(v.reshape(-1, 128).T)


class K:
    pass


def declare_inputs(k, nc):
    k.inp = {}
    k.inshape = {}

    def din(name, shape):
        k.inp[name] = nc.dram_tensor(name, list(shape), F32, kind="ExternalInput").ap()
        k.inshape[name] = tuple(shape)

    din("x", [S, D])
    din("ctx", [NCTX, D])
    din("cT", [128, 8, 2])
    din("cmat", [128, 4, 128])
    for l in range(2):
        din(f"l{l}_mod_w", [18, 128, 8 * 512])
        din(f"l{l}_mod_b2", [2, 9216])
        din(f"l{l}_ng", [128, 3, 8])
        for w in (1, 2):
            din(f"l{l}_f{w}_win", [NFC, 128, 8 * 256])
            din(f"l{l}_f{w}_wout", [8, 128, NFC * 128])


def host_inputs(inputs, b):
    m = {}
    m["x"] = np.ascontiguousarray(inputs["x"][b])
    m["ctx"] = np.ascontiguousarray(inputs["ctx"][b])
    cT = np.stack([pvec(inputs["c"][b]), pvec(inputs["c_ctx"])], axis=-1)
    m["cT"] = np.ascontiguousarray(cT.astype(np.float32))
    return m


_SHARED = {}


def host_shared(inputs):
    m = {}
    cm = np.zeros((128, 4, 128), np.float32)
    cm[:, 0, :] = np.eye(128, dtype=np.float32)
    cm[:, 1, :] = 1.0
    for g in range(2):
        cm[g * 64:(g + 1) * 64, 2, g * 64:(g + 1) * 64] = 1.0
    cm[0:64, 3, 0:64] = 1.0
    cm[64:96, 3, 64:96] = 1.0
    m["cmat"] = cm
    for l in range(2):
        p = f"l{l}_"
        mw = inputs[p + "mod_w"]
        m[p + "mod_w"] = np.ascontiguousarray(
            mw.reshape(8, 128, 18, 512).transpose(2, 1, 0, 3).reshape(18, 128, 8 * 512))
        m[p + "mod_b2"] = np.ascontiguousarray(np.stack([inputs[p + "mod_b"], inputs[p + "mod_b"]], axis=0))
        m[p + "ng"] = np.ascontiguousarray(np.stack(
            [pvec(inputs[p + "ffn1_norm"]), pvec(inputs[p + "mix_norm"]), pvec(inputs[p + "ffn2_norm"])], axis=1))
        for w in (1, 2):
            wi = inputs[p + f"ffn{w}_w_in"]
            g = wi[:, :DFF].reshape(8, 128, NFC, 128)
            u = wi[:, DFF:].reshape(8, 128, NFC, 128)
            gu = np.stack([g, u], axis=3)
            m[p + f"f{w}_win"] = np.ascontiguousarray(gu.transpose(2, 1, 0, 3, 4).reshape(NFC, 128, 8 * 256))
            wo = inputs[p + f"ffn{w}_w_out"]
            m[p + f"f{w}_wout"] = np.ascontiguousarray(
                wo.reshape(NFC, 128, 8, 128).transpose(2, 1, 0, 3).reshape(8, 128, NFC * 128))
    m.update(host_mixer(inputs))
    return m


def build(stop_after=None, debug=False, skip=(), maxops=None):
    nc = bass.Bass("TRN2", target_bir_lowering=False)
    k = K()
    k.nc = nc
    k.debug = debug
    P = Prog(nc)
    k.maxops = maxops
    k.P = P
    A = Arena(nc)
    k.A = A
    declare_inputs(k, nc)
    k.out = nc.dram_tensor("out", [S, D], F32, kind="ExternalOutput").ap()
    k.dbg_outs = []
    skind = "ExternalOutput" if debug else "Internal"
    k.xTd = nc.dram_tensor("xTd", [8, 128, TALL], F32, kind=skind).ap()
    k.wbf = {}

    def precast(name):
        shape = k.inshape[name]
        t = nc.dram_tensor(name + "_bf", list(shape), BF16, kind="Internal").ap()
        k.wbf[name] = t
        P.bg_next = True
        for i in range(shape[0]):
            P.dma("pool", t[i], k.inp[name][i], reads=(), writes=[("wbf", name, i)])
        P.bg_next = False

    k.PS = [nc.alloc_psum_tensor(f"psb{i}", [128, 512], F32) for i in range(8)]

    k.ident = A.alloc("ident", [128, 128], F32)
    k.cbf = A.alloc("cbf", [128, 4, 128], BF16)
    k.modv = A.alloc("modv", [128, 2, 2, 72], F32)
    k.der = A.alloc("der", [128, 2, 2, 5, 8], F32)
    k.ng = A.alloc("ng", [128, 2, 3, 8], F32)
    k.epsv = A.alloc("epsv", [128, 1], F32)
    P.dve(lambda e: e.memset(k.epsv[:], EPS), writes=[("epsv",)])
    P.dma("sp", k.ident[:], k.inp["cmat"][:, 0, :], writes=[("ident",)])
    P.dma("pool", k.cbf[:], k.inp["cmat"], writes=[("cbf",)])
    for l in range(2):
        P.dma("sp", k.ng[:, l], k.inp[f"l{l}_ng"], writes=[("ng", l)])
    k.ones_bf = k.cbf[:, 1, :]
    k.bd64_bf = k.cbf[:, 2, :]

    for l in range(2):
        for w in (1, 2):
            precast(f"l{l}_f{w}_win")
            precast(f"l{l}_f{w}_wout")

    declare_mixer_inputs(k, nc)
    for name in ("l0_wA", "l0_wkn", "l0_wv", "l0_wq", "l0_wo", "l1_wo", "l1_wA", "l1_wv", "l1_nab"):
        precast(name)

    def dscr(name, shape, dt=BF16):
        return nc.dram_tensor(name, list(shape), dt, kind=skind if name in ("attTd",) else "Internal").ap()
    k.yTd = dscr("yTd", [4, 128, TALL])
    k.KTd = dscr("KTd", [8, 96, TALL])
    k.QTd = dscr("QTd", [8, 96, TALL])
    k.Vd = dscr("Vd", [8, 128, 34, 128])
    k.attTd = dscr("attTd", [8, 128, TALL])
    k.KdupTd = dscr("KdupTd", [2, 128, TALL])
    k.QgTd = dscr("QgTd", [4, 128, TALL])
    k.nkTd = dscr("nkTd", [4, 128, TALL])
    k.nqTd = dscr("nqTd", [4, 128, TALL])
    k.VgD = dscr("VgD", [4, 128, 34, 128])
    k.NVd = dscr("NVd", [8, 128, 34, 128])

    phases = [
        ("mod", lambda: setup_mod(k)),
        ("l0f1", lambda: ffn_phase(k, 0, 1, TILES, src="tok", dst="xT")),
        ("l0proj", lambda: l0_proj_phase(k)),
        ("l0att", lambda: attn_phase(k, "mla")),
        ("l0conv", lambda: l0_conv_phase(k)),
        ("l0mix", lambda: mixout_phase(k, 0, TILES)),
        ("l0", lambda: ffn_phase(k, 0, 2, TILES, src="xT", dst="xT")),
        ("l1f1", lambda: ffn_phase(k, 1, 1, TILES, src="xT", dst="xT")),
        ("l1proj", lambda: l1_proj_phase(k)),
        ("l1gqa", lambda: attn_phase(k, "gqa")),
        ("l1na", lambda: l1_na_phase(k)),
        ("l1mix", lambda: mixout_phase(k, 1, TILES[1:])),
        ("full", lambda: ffn_phase(k, 1, 2, TILES[1:], src="xT", dst="tok")),
    ]
    for name, fn in phases:
        if name in skip:
            continue
        n0 = P.nops
        if maxops is not None and name == stop_after:
            P.maxops = P.nops + maxops
        fn()
        P.barrier()
        if debug:
            print("phase", name, "ops", n0, P.nops, flush=True)
        if stop_after == name:
            break
    return finish(k)


def finish(k):
    k.P.emit()
    return k.nc


def setup_mod(k):
    P, A, nc = k.P, k.A, k.nc
    PS = k.PS
    m = A.mark()
    cT = A.alloc("cT", [128, 8, 2], F32)
    sc = A.alloc("sc", [128, 8, 2], F32)
    mb2 = A.alloc("mb2", [2, 9216], F32)
    modrow = A.alloc("modrow", [2, 9216], F32)
    wt = [A.alloc(f"mw{i}", [128, 8, 512], F32) for i in range(3)]
    P.dma("sp", cT[:], k.inp["cT"], writes=[("cT",)])
    P.act(lambda e: e.activation(out=sc[:], in_=cT[:], func=AF.Silu), reads=[("cT",)], writes=[("sc",)])
    n = 0
    for l in range(2):
        P.dma("sp", mb2[:], k.inp[f"l{l}_mod_b2"], writes=[("mb2",)])
        for blk in range(18):
            w = wt[n % 3]
            wtok = ("mw", n % 3)
            b = n % 4
            n += 1
            P.dma("sp", w[:], k.inp[f"l{l}_mod_w"][blk].rearrange("p (k n) -> p k n", k=8), writes=[wtok])

            def mm(e):
                for kc in range(8):
                    e.matmul(PS[b][0:2, 0:512], lhsT=sc[:, kc, :], rhs=w[:, kc, :], start=(kc == 0), stop=(kc == 7))
            P.pe(mm, reads=[wtok, ("sc",)], writes=[("ps", b)])
            P.dve(lambda e: e.tensor_tensor(out=modrow[0:2, blk * 512:(blk + 1) * 512], in0=PS[b][0:2, 0:512],
                                            in1=mb2[0:2, blk * 512:(blk + 1) * 512], op=ALU.add),
                  reads=[("ps", b), ("mb2",)], writes=[("modrow", blk)])
        pst = PS[4 + l]

        def tr(e):
            for j in range(72):
                e.transpose(pst[:, 2 * j:2 * j + 2], modrow[0:2, j * 128:(j + 1) * 128], k.ident[0:2, 0:2])
        P.pe(tr, reads=[("modrow", blk) for blk in range(18)] + [("ident",)], writes=[("ps", 4 + l)])
        for who in range(2):
            src = pst[:, 0:144].rearrange("p (j w) -> p j w", w=2)[:, :, who]
            P.dve(lambda e: e.tensor_copy(out=k.modv[:, l, who, :], in_=src),
                  reads=[("ps", 4 + l)], writes=[("modv", l, who)])
        for who in range(2):
            for gi, mi in enumerate((1, 4, 7)):
                P.dve(lambda e: e.scalar_tensor_tensor(
                    out=k.der[:, l, who, gi, :], in0=k.modv[:, l, who, mi * 8:mi * 8 + 8], scalar=1.0,
                    in1=k.ng[:, l, gi, :], op0=ALU.add, op1=ALU.mult),
                    reads=[("modv", l, who), ("ng", l)], writes=[("der", l, who, gi)])
            for gi, mi in ((3, 2), (4, 8)):
                P.dve(lambda e: e.tensor_scalar(
                    out=k.der[:, l, who, gi, :], in0=k.modv[:, l, who, mi * 8:mi * 8 + 8], scalar1=0.5,
                    scalar2=None, op0=ALU.mult),
                    reads=[("modv", l, who)], writes=[("der", l, who, gi)])
    if k.debug:
        dm = nc.dram_tensor("dbg_modv", [128, 2 * 2 * 72], F32, kind="ExternalOutput").ap()
        P.dma("sp", dm, k.modv[:].rearrange("p a b c -> p (a b c)"),
              reads=[("modv", l, w) for l in range(2) for w in range(2)], writes=[("dbg_modv",)])
    A.release(m)


def rms_norm_mod(k, xb, xtoks, T, gp, sh, hT, htok, tmp, r1, r2, rstd, sq):
    P = k.P
    P.act(lambda e: e.activation(out=sq[:, :, 0:T], in_=xb[:, :, 0:T], func=AF.Square),
          reads=xtoks, writes=[("sq", c) for c in range(8)])

    def mm(e):
        ins = None
        for c in range(8):
            ins = e.matmul(k.PS[6][:, 0:T], lhsT=k.ones_bf, rhs=sq[:, c, 0:T], start=(c == 0), stop=(c == 7))
        return ins
    P.pe(mm, reads=[("sq", c) for c in range(8)] + [("cbf",)], writes=[("ps", 6)])
    P.act(lambda e: e.activation(out=r2[:, 0:T], in_=k.PS[6][:, 0:T], func=AF.Sqrt, bias=k.epsv[:, 0:1], scale=1.0 / D),
          reads=[("ps", 6), ("epsv",)], writes=[("r2",)])
    P.dve(lambda e: e.reciprocal(out=rstd[:, 0:T], in_=r2[:, 0:T]), reads=[("r2",)], writes=[("rstd",)])
    for c in range(8):
        tb = tmp[c % 2]
        P.dve(lambda e, c=c, tb=tb: e.scalar_tensor_tensor(out=tb[:, 0:T], in0=xb[:, c, 0:T], scalar=gp[:, c:c + 1],
                                                           in1=rstd[:, 0:T], op0=ALU.mult, op1=ALU.mult),
              reads=[xtoks[c], ("rstd",)], writes=[("tmp", c % 2)])
        P.act(lambda e, c=c, tb=tb: e.activation(out=hT[:, c, 0:T], in_=tb[:, 0:T], func=AF.Identity,
                                                 bias=sh[:, c:c + 1], scale=1.0),
              reads=[("tmp", c % 2)], writes=[(htok, c)])


def ffn_phase(k, l, which, tiles, src, dst):
    P, A, nc = k.P, k.A, k.nc
    PS = k.PS
    m = A.mark()
    xt = [A.alloc(f"xt{i}", [128, 8, 512], F32) for i in range(3)]
    hT = [A.alloc(f"hT{i}", [128, 8, 512], BF16) for i in range(2)]
    actT = [A.alloc(f"actT{i}", [128, NFC, 512], BF16) for i in range(2)]
    sq = A.alloc("sq", [128, 8, 512], BF16)
    win = [A.alloc(f"win{i}", [128, 8, 256], BF16) for i in range(3)]
    wout = [A.alloc(f"wout{i}", [128, NFC, 128], BF16) for i in range(2)]
    r1 = A.alloc("r1", [128, 512], F32)
    r2 = A.alloc("r2", [128, 512], F32)
    rstd = A.alloc("rstd", [128, 512], F32)
    tmp = [A.alloc(f"tmp{i}", [128, 512], F32) for i in range(2)]
    sg = [A.alloc(f"sg{i}", [128, 512], F32) for i in range(3)]
    tokb = [A.alloc(f"tokb{i}", [128, 1024], F32) for i in range(2)]
    gi = 0 if which == 1 else 2
    shi = 0 if which == 1 else 6
    hgi = 3 if which == 1 else 4
    wname_in = f"l{l}_f{which}_win"
    wname_out = f"l{l}_f{which}_wout"
    groups = [tiles[i:i + 2] for i in range(0, len(tiles), 2)]
    tk = 0
    wi = 0
    wo = 0
    guc = 0
    yc = 0
    tbc = 0
    for grp in groups:
        bufs = []
        for s, (who, t0, T) in enumerate(grp):
            b = tk % 3
            tk += 1
            bufs.append(b)
            xb = xt[b]
            xtoks = [("xt", b, c) for c in range(8)]
            if src == "xT":
                P.dma("sp", xb[:, :, 0:T], k.xTd[:, :, t0:t0 + T].rearrange("c p t -> p c t"), writes=xtoks)
            else:
                srcap = k.inp["ctx"] if who == 1 else k.inp["x"]
                r0 = t0 if who == 1 else t0 - NCTX
                for st in range(T // 128):
                    tb = tokb[tbc % 2]
                    ttok = ("tokb", tbc % 2)
                    tbc += 1
                    P.dma("sp", tb[:], srcap[r0 + st * 128:r0 + (st + 1) * 128, :], writes=[ttok])
                    for half in range(2):
                        bank = PS[(guc % 4)]
                        btok = ("ps", guc % 4)
                        guc += 1

                        def tr(e, tb=tb, half=half, bank=bank):
                            ins = None
                            for cc in range(4):
                                c = half * 4 + cc
                                ins = e.transpose(bank[:, cc * 128:(cc + 1) * 128], tb[:, c * 128:(c + 1) * 128], k.ident[:])
                            return ins
                        P.pe(tr, reads=[ttok, ("ident",)], writes=[btok])
                        P.act(lambda e, xb=xb, half=half, bank=bank, st=st: e.activation(
                            out=xb[:, half * 4:half * 4 + 4, st * 128:(st + 1) * 128],
                            in_=bank[:].rearrange("p (c t) -> p c t", c=4), func=AF.Copy),
                            reads=[btok], writes=xtoks[half * 4:half * 4 + 4])
            rms_norm_mod(k, xb, xtoks, T, k.der[:, l, who, gi, :], k.modv[:, l, who, shi * 8:shi * 8 + 8],
                         hT[s], ("hT", s), tmp, r1, r2, rstd, sq)
        for j in range(NFC):
            wb = win[wi % 3]
            wtok = ("win", wi % 3)
            wi += 1
            P.dma("sp", wb[:], k.wbf[wname_in][j].rearrange("p (k n) -> p k n", k=8),
                  reads=[("wbf", wname_in, j)], writes=[wtok])
            for s, (who, t0, T) in enumerate(grp):
                bg = (guc % 3) * 2
                guc += 1
                htoks = [(("hT", s), c) for c in range(8)]

                def mmg(e, wb=wb, s=s, T=T, bank=PS[bg], off=0):
                    ins = None
                    for kc in range(8):
                        ins = e.matmul(bank[:, 0:T], lhsT=wb[:, kc, off:off + 128], rhs=hT[s][:, kc, 0:T],
                                       start=(kc == 0), stop=(kc == 7))
                    return ins
                P.pe(mmg, reads=[wtok] + htoks, writes=[("ps", bg)])
                P.pe(lambda e, wb=wb, s=s, T=T, bank=PS[bg + 1], f=mmg: f(e, wb, s, T, bank, 128),
                     reads=[wtok] + htoks, writes=[("ps", bg + 1)])
                q = guc % 3
                P.act(lambda e, q=q, bg=bg, T=T: e.activation(out=sg[q][:, 0:T], in_=PS[bg][:, 0:T], func=AF.Silu),
                      reads=[("ps", bg)], writes=[("sg", q)])
                P.dve(lambda e, q=q, bg=bg, T=T, s=s, j=j: e.tensor_tensor(
                    out=actT[s][:, j, 0:T], in0=PS[bg + 1][:, 0:T], in1=sg[q][:, 0:T], op=ALU.mult),
                    reads=[("ps", bg + 1), ("sg", q)], writes=[("actT", s, j)])
        for d in range(8):
            wb = wout[wo % 2]
            wtok = ("wout", wo % 2)
            wo += 1
            P.dma("sp", wb[:], k.wbf[wname_out][d].rearrange("p (k n) -> p k n", k=NFC),
                  reads=[("wbf", wname_out, d)], writes=[wtok])
            for s, (who, t0, T) in enumerate(grp):
                by = 6 + (yc % 2)
                yc += 1
                xb = xt[bufs[s]]

                def mmy(e, wb=wb, s=s, T=T, by=by):
                    ins = None
                    for fc in range(NFC):
                        ins = e.matmul(PS[by][:, 0:T], lhsT=wb[:, fc, :], rhs=actT[s][:, fc, 0:T],
                                       start=(fc == 0), stop=(fc == NFC - 1))
                    return ins
                P.pe(mmy, reads=[wtok] + [("actT", s, j) for j in range(NFC)], writes=[("ps", by)])
                hg = k.der[:, l, who, hgi, :]
                P.dve(lambda e, xb=xb, d=d, T=T, by=by, hg=hg: e.scalar_tensor_tensor(
                    out=xb[:, d, 0:T], in0=PS[by][:, 0:T], scalar=hg[:, d:d + 1], in1=xb[:, d, 0:T],
                    op0=ALU.mult, op1=ALU.add),
                    reads=[("ps", by), ("xt", bufs[s], d)], writes=[("xt", bufs[s], d)])
        for s, (who, t0, T) in enumerate(grp):
            b = bufs[s]
            xb = xt[b]
            xtoks = [("xt", b, c) for c in range(8)]
            if dst == "xT":
                P.dma("sp", k.xTd[:, :, t0:t0 + T].rearrange("c p t -> p c t"), xb[:, :, 0:T], reads=xtoks,
                      writes=[("xTd", t0)])
            else:
                r0 = t0 - NCTX
                for st in range(T // 128):
                    ob = tokb[tbc % 2]
                    otok = ("tokb", tbc % 2)
                    tbc += 1
                    for half in range(2):
                        bi = guc % 4
                        guc += 1

                        def tr(e, xb=xb, half=half, bi=bi, st=st):
                            ins = None
                            for cc in range(4):
                                c = half * 4 + cc
                                ins = e.transpose(PS[bi][:, cc * 128:(cc + 1) * 128], xb[:, c, st * 128:(st + 1) * 128],
                                                  k.ident[:])
                            return ins
                        P.pe(tr, reads=xtoks[half * 4:half * 4 + 4] + [("ident",)], writes=[("ps", bi)])
                        P.act(lambda e, ob=ob, half=half, bi=bi: e.activation(
                            out=ob[:, half * 512:(half + 1) * 512], in_=PS[bi][:], func=AF.Copy),
                            reads=[("ps", bi)], writes=[(otok, half)])
                    P.dma("sp", k.out[r0 + st * 128:r0 + (st + 1) * 128, :], ob[:], reads=[(otok, 0), (otok, 1)],
                          writes=[("out", r0, st)])
    A.release(m)


_CACHE = {}


def kernel(**inputs):
    inputs = {kk: np.asarray(v) for kk, v in inputs.items()}
    if "nc" not in _CACHE:
        _CACHE["nc"] = build()
    nc = _CACHE["nc"]
    shared = host_shared(inputs)
    in_maps = []
    for b in range(8):
        mm = dict(shared)
        mm.update(host_inputs(inputs, b))
        in_maps.append(mm)
    res = run_bass_kernel_spmd(nc, in_maps, core_ids=list(range(8)))
    return np.stack([r["out"] for r in res.results], axis=0)


NEG = -30000.0
PERM64 = list(range(0, 64, 2)) + list(range(1, 64, 2))
SWAP64 = list(range(1, 64, 2)) + list(range(0, 64, 2))
PERM32 = list(range(0, 32, 2)) + list(range(1, 32, 2))
SWAP32 = list(range(1, 32, 2)) + list(range(0, 32, 2))


def rope_tables():
    t = np.arange(S)
    row = (t // GRID_W).astype(np.float32)
    col = (t % GRID_W).astype(np.float32)

    def tab(dim):
        npairs = dim // 4
        inv = (10000.0 ** (-np.arange(npairs, dtype=np.float32) / npairs)).astype(np.float32)
        ang = np.concatenate([row[:, None] * inv, col[:, None] * inv], axis=-1).astype(np.float32)
        return np.cos(ang).astype(np.float32).T, np.sin(ang).astype(np.float32).T
    cA, sA = tab(64)
    cB, sB = tab(32)
    ta = np.zeros((128, 2, S), np.float32)
    for p in range(128):
        pp = p % 64
        pr = pp % 32
        ta[p, 0] = cA[pr]
        ta[p, 1] = -sA[pr] if pp < 32 else sA[pr]
    tb = np.zeros((128, 2, S), np.float32)
    for p in range(64, 96):
        pp = p - 64
        pr = pp % 16
        tb[p, 0] = cB[pr]
        tb[p, 1] = -sB[pr] if pp < 16 else sB[pr]
    return ta, tb


def na_geometry():
    sigs = []
    mp = {}
    chunks = {}
    for i in range(32):
        rs0 = min(max(2 * i - 4, 0), 56)
        rs1 = min(max(2 * i + 1 - 4, 0), 56)
        lo = rs0 // 2
        hi = (rs1 + 7) // 2
        chunks[i] = list(range(lo, hi + 1))
        for kc in chunks[i]:
            sig = (rs0 - 2 * i, rs1 - (2 * i + 1), kc - i)
            if sig not in sigs:
                sigs.append(sig)
            mp[(i, kc)] = sigs.index(sig)
    return sigs, mp, chunks


NA_SIGS, NA_MAP, NA_CHUNKS = na_geometry()
NSIG = len(NA_SIGS)


def na_bias_tiles(rpb):
    out = np.full((NSIG, 8, 128, 128), NEG, np.float32)
    kp = np.arange(128)
    qf = np.arange(128)
    kr_, kc_ = kp // 64, kp % 64
    qr_, qc_ = qf // 64, qf % 64
    cs = np.clip(qc_ - 8, 0, 48)
    for si, (a0, a1, dk) in enumerate(NA_SIGS):
        rsrel = np.where(qr_ == 0, a0, a1)
        krel = (2 * dk + kr_[:, None]) - qr_[None, :]
        vr = (krel >= rsrel[None, :]) & (krel <= rsrel[None, :] + 7)
        vc = (kc_[:, None] >= cs[None, :]) & (kc_[:, None] <= cs[None, :] + 15)
        valid = vr & vc
        dr = np.clip(krel + 7, 0, 14)
        dc = np.clip(kc_[:, None] - qc_[None, :] + 15, 0, 30)
        for h in range(8):
            g = rpb[h][dr, dc]
            out[si, h] = np.where(valid, g, np.float32(NEG))
    return out.reshape(NSIG * 8, 128, 128)


def blk(w, cols):
    K_ = w.shape[0]
    o = np.zeros((K_, len(cols)), np.float32)
    idx = [i for i, c in enumerate(cols) if c is not None]
    o[:, idx] = w[:, [cols[i] for i in idx]]
    return o.reshape(K_ // 128, 128, len(cols)).transpose(1, 0, 2)


def host_mixer(inputs):
    m = {}
    ta, tb = rope_tables()
    m["ropeA"] = ta
    m["ropeB"] = tb
    w = inputs["l0_w_in"]
    blocks = []
    blocks.append(blk(w, list(range(0, 128))))
    blocks.append(blk(w, list(range(128, 256))))
    blocks.append(blk(w, [None] * 64 + [256 + i for i in PERM32] + [None] * 32))
    blocks.append(blk(w, [None] * 64 + [256 + i for i in SWAP32] + [None] * 32))
    for c in range(3):
        blocks.append(blk(w, list(range(288 + c * 128, 288 + (c + 1) * 128))))
    for c in range(8):
        blocks.append(blk(w, list(range(672 + c * 128, 672 + (c + 1) * 128))))
    m["l0_wA"] = np.ascontiguousarray(np.stack(blocks, 0).reshape(15, 128, 8 * 128))
    wk = inputs["l0_mla_w_ukv"]
    kb = [blk(wk, [(2 * pr) * 128 + i for i in range(64)] + [(2 * pr + 1) * 128 + i for i in range(64)]) for pr in range(4)]
    m["l0_wkn"] = np.ascontiguousarray(np.stack(kb, 0).reshape(4, 128, 2 * 128))
    m["l0_wv"] = np.ascontiguousarray(blk(wk, [h * 128 + 64 + i for h in range(8) for i in range(64)]).reshape(1, 128, 2 * 512))
    wq = inputs["l0_mla_w_uq"]
    qb = []
    for h in range(8):
        qb.append(blk(wq, [h * 96 + i for i in range(64)] + [h * 96 + 64 + i for i in PERM32]))
    for h in range(8):
        qb.append(blk(wq, [None] * 64 + [h * 96 + 64 + i for i in SWAP32]))
    m["l0_wq"] = np.ascontiguousarray(np.stack(qb, 0).reshape(16, 128, 3 * 96))
    for l in range(2):
        wo = inputs[f"l{l}_w_out"]
        m[f"l{l}_wo"] = np.ascontiguousarray(wo.reshape(8, 128, 8, 128).transpose(2, 1, 0, 3).reshape(8, 128, 8 * 128))
    m["l0_dw"] = np.ascontiguousarray(inputs["l0_conv_dw_w"].reshape(31, 4, 128).transpose(2, 1, 0))
    v = np.zeros((128, 32), np.float32)
    v[:, 0:2] = pvec(inputs["l0_mla_kv_norm"])
    v[:, 2:5] = pvec(inputs["l0_mla_q_norm"])
    kg = inputs["l0_mla_k_gain"]
    qg = inputs["l0_mla_q_gain"]
    v[:, 5] = np.tile(kg[:64], 2)
    v[64:96, 6] = kg[64:][PERM32]
    v[64:96, 7] = kg[64:][SWAP32]
    v[0:64, 8] = qg[:64]
    v[64:96, 8] = qg[64:][PERM32]
    v[64:96, 9] = qg[64:][SWAP32]
    v[0:64, 10] = 1.0 / 64
    v[64:96, 10] = 1.0 / 32
    gb = inputs["l0_conv_glu_b"]
    v[:, 11:15] = pvec(gb[:512])
    v[:, 15:19] = pvec(gb[512:])
    v[:, 19:23] = pvec(inputs["l0_conv_dw_b"])
    v[:, 23:27] = pvec(inputs["l0_conv_ln_g"])
    v[:, 27:31] = pvec(inputs["l0_conv_ln_b"])
    m["l0_vec"] = v
    w = inputs["l1_w_in"]
    blocks = []
    for hk in range(2):
        blocks.append(blk(w, [hk * 64 + i for i in PERM64] * 2))
    for hk in range(2):
        blocks.append(blk(w, [hk * 64 + i for i in SWAP64] * 2))
    for c in range(4):
        blocks.append(blk(w, list(range(256 + c * 128, 256 + (c + 1) * 128))))
    for c in range(4):
        blocks.append(blk(w, [1280 + (2 * c + e) * 64 + i for e in range(2) for i in PERM64]))
    for c in range(4):
        blocks.append(blk(w, [1280 + (2 * c + e) * 64 + i for e in range(2) for i in SWAP64]))
    for c in range(4):
        blocks.append(blk(w, list(range(1792 + c * 128, 1792 + (c + 1) * 128))))
    m["l1_wA"] = np.ascontiguousarray(np.stack(blocks, 0).reshape(20, 128, 8 * 128))
    m["l1_wv"] = np.ascontiguousarray(blk(w, list(range(128, 256)) + list(range(768, 1280))).reshape(1, 128, 8 * 640))
    v = np.zeros((128, 8), np.float32)
    v[:, 0] = np.tile(inputs["l1_gqa_k_gain"][PERM64], 2)
    v[:, 1] = np.tile(inputs["l1_gqa_k_gain"][SWAP64], 2)
    v[:, 2] = np.tile(inputs["l1_gqa_q_gain"][PERM64], 2)
    v[:, 3] = np.tile(inputs["l1_gqa_q_gain"][SWAP64], 2)
    v[:, 4] = np.tile(inputs["l1_na_k_gain"], 2)
    v[:, 5] = np.tile(inputs["l1_na_q_gain"], 2)
    m["l1_vec"] = v
    m["l1_nab"] = na_bias_tiles(inputs["l1_na_rpb"])
    return m


def declare_mixer_inputs(k, nc):
    def din(name, shape):
        k.inp[name] = nc.dram_tensor(name, list(shape), F32, kind="ExternalInput").ap()
        k.inshape[name] = tuple(shape)
    din("ropeA", [128, 2, S])
    din("ropeB", [128, 2, S])
    din("l0_wA", [15, 128, 1024])
    din("l0_wkn", [4, 128, 256])
    din("l0_wv", [1, 128, 1024])
    din("l0_wq", [16, 128, 288])
    din("l0_wo", [8, 128, 1024])
    din("l1_wo", [8, 128, 1024])
    din("l0_dw", [128, 4, 31])
    din("l0_vec", [128, 32])
    din("l1_wA", [20, 128, 1024])
    din("l1_wv", [1, 128, 8 * 640])
    din("l1_vec", [128, 8])
    din("l1_nab", [NSIG * 8, 128, 128])


def stats_rstd(k, st, rows, T, scale, extra_reads=()):
    P = k.P
    i = st["i"]
    r2, rstd, bank = st["r2"], st["rstd"], st["bank"]
    P.act(lambda e: e.activation(out=r2[rows, 0:T], in_=k.PS[bank][rows, 0:T], func=AF.Sqrt, bias=k.epsv[rows, 0:1], scale=scale),
          reads=[("ps", bank), ("epsv",)] + list(extra_reads), writes=[("r2", i)])
    P.dve(lambda e: e.reciprocal(out=rstd[rows, 0:T], in_=r2[rows, 0:T]), reads=[("r2", i)], writes=[("rstd", i)])


class PsRot:
    def __init__(self, banks=(0, 1, 2, 3, 4, 5)):
        self.banks = banks
        self.i = 0

    def next(self):
        b = self.banks[self.i % len(self.banks)]
        self.i += 1
        return b


def load_x_norm(k, l, who, t0, T, xb, xtoks, hT, w):
    P = k.P
    P.dma("sp", xb[:, :, 0:T], k.xTd[:, :, t0:t0 + T].rearrange("c p t -> p c t"), writes=xtoks)
    rms_norm_mod(k, xb, xtoks, T, k.der[:, l, who, 1, :], k.modv[:, l, who, 24:32], hT, "hTm",
                 w["tmp"], w["r1"], w["r2"], w["rstd"], w["sq"])


def mm_group(k, bank_ap, lhs_fn, rhs_fn, n, reads, btok):
    def f(e):
        ins = None
        for kc in range(n):
            ins = e.matmul(bank_ap, lhsT=lhs_fn(kc), rhs=rhs_fn(kc), start=(kc == 0), stop=(kc == n - 1))
        return ins
    k.P.pe(f, reads=reads, writes=[btok])


def common_work(k, A):
    w = {}
    w["tmp"] = [A.alloc(f"tmp{i}", [128, 512], F32) for i in range(2)]
    w["r1"] = A.alloc("r1", [128, 512], F32)
    w["r2"] = A.alloc("r2", [128, 512], F32)
    w["rstd"] = A.alloc("rstd", [128, 512], F32)
    w["sq"] = A.alloc("sq", [128, 8, 512], BF16)
    w["stats"] = [dict(i=i, r1=A.alloc(f"sr1_{i}", [128, 512], F32), r2=A.alloc(f"sr2_{i}", [128, 512], F32),
                       rstd=A.alloc(f"srstd_{i}", [128, 512], F32)) for i in range(3)]
    w["si"] = 0
    w["sqs"] = [A.alloc(f"sqs{i}", [128, 512], BF16) for i in range(4)]
    w["sqi"] = 0
    w["rab"] = [(A.alloc(f"ra{i}", [128, 512], F32), A.alloc(f"rb{i}", [128, 512], F32)) for i in range(2)]
    w["rai"] = 0
    return w


def l0_proj_phase(k):
    P, A, nc, PS = k.P, k.A, k.nc, k.PS
    l = 0
    m = A.mark()
    w = common_work(k, A)
    xt = [A.alloc(f"xt{i}", [128, 8, 512], F32) for i in range(2)]
    hT = A.alloc("hTm", [128, 8, 512], BF16)
    wA = A.alloc("wA", [128, 15, 8, 128], BF16)
    wkn = A.alloc("wkn", [128, 4, 2, 128], BF16)
    wv = A.alloc("wv", [128, 2, 512], BF16)
    wq = A.alloc("wq", [128, 16, 3, 96], BF16)
    vec = A.alloc("vec", [128, 32], F32)
    rope = [A.alloc(f"rope{i}", [128, 2, 512], F32) for i in range(2)]
    ckvT = A.alloc("ckvT", [128, 2, 512], BF16)
    cqT = A.alloc("cqT", [128, 3, 512], BF16)
    krT = A.alloc("krT", [128, 512], BF16)
    yT = [A.alloc(f"yT{i}", [128, 4, 512], BF16) for i in range(2)]
    KT = [A.alloc(f"KT{i}", [128, 8, 512], BF16) for i in range(2)]
    QT = [A.alloc(f"QT{i}", [128, 8, 512], BF16) for i in range(2)]
    Vt = [A.alloc(f"Vt{i}", [128, 8, 128], BF16) for i in range(2)]
    sig = [A.alloc(f"sig{i}", [128, 512], F32) for i in range(2)]
    P.dma("sp", vec[:], k.inp["l0_vec"], writes=[("vec",)])
    for i in range(15):
        P.dma("sp", wA[:, i], k.wbf["l0_wA"][i].rearrange("p (k n) -> p k n", k=8), reads=[("wbf", "l0_wA", i)], writes=[("wA",)])
    for i in range(4):
        P.dma("sp", wkn[:, i], k.wbf["l0_wkn"][i].rearrange("p (k n) -> p k n", k=2), reads=[("wbf", "l0_wkn", i)], writes=[("wkn",)])
    P.dma("sp", wv[:], k.wbf["l0_wv"][0].rearrange("p (k n) -> p k n", k=2), reads=[("wbf", "l0_wv", 0)], writes=[("wv",)])
    for i in range(16):
        P.dma("sp", wq[:, i], k.wbf["l0_wq"][i].rearrange("p (k n) -> p k n", k=3), reads=[("wbf", "l0_wq", i)], writes=[("wq",)])
    sc = 96.0 ** -0.5
    P.dve(lambda e: e.tensor_scalar(out=vec[:, 8:10], in0=vec[:, 8:10], scalar1=sc, scalar2=None, op0=ALU.mult),
          reads=[("vec",)], writes=[("vec",)])
    for i in range(2):
        P.dve(lambda e, i=i: e.memset(Vt[i][:], 1.0), writes=[("Vt", i), (("Vt", i), 1)])
    rot = PsRot()
    vcnt = 0
    ALLR = slice(0, 128)
    R = slice(64, 96)
    R96 = slice(0, 96)
    for ti, (who, t0, T) in enumerate(TILES):
        lat = (who == 0)
        xb = xt[ti % 2]
        xtoks = [("xtm", ti % 2, c) for c in range(8)]
        load_x_norm(k, l, who, t0, T, xb, xtoks, hT, w)
        htoks = [("hTm", c) for c in range(8)]
        rp = rope[ti % 2]
        if lat:
            P.dma("sp", rp[:, :, 0:T], k.inp["ropeB"][:, :, t0 - NCTX:t0 - NCTX + T], writes=[("rope", ti % 2)])

        def projA(bidx, M):
            b = rot.next()
            mm_group(k, PS[b][0:M, 0:T], lambda kc: wA[:, bidx, kc, 0:M], lambda kc: hT[:, kc, 0:T], 8,
                     [("wA",)] + htoks, ("ps", b))
            return b

        def square(b, M, dst_ap, dtok):
            P.act(lambda e: e.activation(out=dst_ap, in_=PS[b][0:M, 0:T], func=AF.Square), reads=[("ps", b)], writes=[dtok])

        bs = [projA(0, 128), projA(1, 128)]
        for c in range(2):
            square(bs[c], 128, w["sq"][:, c, 0:T], ("sq", c))
        st = next_stat(w)
        mm_group(k, PS[st["bank"]][:, 0:T], lambda kc: k.ones_bf, lambda kc: w["sq"][:, kc, 0:T], 2,
                 [("sq", 0), ("sq", 1), ("cbf",)], ("ps", st["bank"]))
        stats_rstd(k, st, ALLR, T, 1.0 / 256)
        for c in range(2):
            P.dve(lambda e: e.scalar_tensor_tensor(out=ckvT[:, c, 0:T], in0=PS[bs[c]][:, 0:T], scalar=vec[:, c:c + 1],
                                                   in1=st["rstd"][:, 0:T], op0=ALU.mult, op1=ALU.mult),
                  reads=[("ps", bs[c]), ("rstd", st["i"]), ("vec",)], writes=[("ckvT", c)])
        bm = projA(2, 96)
        bsw = projA(3, 96) if lat else None
        sb, stok = next_sqs(w)
        square(bm, 96, sb[0:96, 0:T], stok)
        st = next_stat(w)
        mm_group(k, PS[st["bank"]][0:96, 0:T], lambda kc: k.cbf[0:96, 3, 0:96], lambda kc: sb[0:96, 0:T], 1, [stok, ("cbf",)],
                 ("ps", st["bank"]))
        stats_rstd(k, st, R, T, 1.0 / 32)
        if not lat:
            P.dve(lambda e: e.scalar_tensor_tensor(out=krT[R, 0:T], in0=PS[bm][R, 0:T], scalar=vec[R, 6:7],
                                                   in1=st["rstd"][R, 0:T], op0=ALU.mult, op1=ALU.mult),
                  reads=[("ps", bm), ("rstd", st["i"]), ("vec",)], writes=[("krT",)])
        else:
            rope_apply(k, w, bm, bsw, R, T, vec[R, 6:7], vec[R, 7:8], st, rp, ("rope", ti % 2), krT[R, 0:T], [("krT",)])
        bs = [projA(4 + c, 128) for c in range(3)]
        for c in range(3):
            square(bs[c], 128, w["sq"][:, 2 + c, 0:T], ("sq", 2 + c))
        st = next_stat(w)
        mm_group(k, PS[st["bank"]][:, 0:T], lambda kc: k.ones_bf, lambda kc: w["sq"][:, 2 + kc, 0:T], 3,
                 [("sq", 2), ("sq", 3), ("sq", 4), ("cbf",)], ("ps", st["bank"]))
        stats_rstd(k, st, ALLR, T, 1.0 / 384)
        for c in range(3):
            P.dve(lambda e: e.scalar_tensor_tensor(out=cqT[:, c, 0:T], in0=PS[bs[c]][:, 0:T], scalar=vec[:, 2 + c:3 + c],
                                                   in1=st["rstd"][:, 0:T], op0=ALU.mult, op1=ALU.mult),
                  reads=[("ps", bs[c]), ("rstd", st["i"]), ("vec",)], writes=[("cqT", c)])
        yb = yT[ti % 2]
        for ch in range(4):
            ba_ = projA(7 + ch, 128)
            bg_ = projA(11 + ch, 128)
            sg_ = sig[ch % 2]
            P.act(lambda e: e.activation(out=sg_[:, 0:T], in_=PS[bg_][:, 0:T], func=AF.Sigmoid,
                                         bias=vec[:, 15 + ch:16 + ch], scale=1.0),
                  reads=[("ps", bg_), ("vec",)], writes=[("sig", ch % 2)])
            P.dve(lambda e: e.scalar_tensor_tensor(out=yb[:, ch, 0:T], in0=PS[ba_][:, 0:T],
                                                   scalar=vec[:, 11 + ch:12 + ch], in1=sg_[:, 0:T],
                                                   op0=ALU.add, op1=ALU.mult),
                  reads=[("ps", ba_), ("sig", ch % 2), ("vec",)], writes=[("yT", ti % 2, ch)])
        P.dma("sp", k.yTd[:, :, t0:t0 + T].rearrange("c p t -> p c t"), yb[:, :, 0:T],
              reads=[("yT", ti % 2, ch) for ch in range(4)], writes=[("yTd", ti)])
        Kb = KT[ti % 2]
        ktoks = [("KT", ti % 2, h) for h in range(8)]
        for pr in range(4):
            b = rot.next()
            mm_group(k, PS[b][:, 0:T], lambda kc: wkn[:, pr, kc, :], lambda kc: ckvT[:, kc, 0:T], 2,
                     [("wkn",), ("ckvT", 0), ("ckvT", 1)], ("ps", b))
            sb, stok = next_sqs(w)
            square(b, 128, sb[:, 0:T], stok)
            st = next_stat(w)
            mm_group(k, PS[st["bank"]][:, 0:T], lambda kc: k.bd64_bf, lambda kc: sb[:, 0:T], 1, [stok, ("cbf",)], ("ps", st["bank"]))
            stats_rstd(k, st, ALLR, T, 1.0 / 64)
            for e_ in range(2):
                h = 2 * pr + e_
                P.dve(lambda e: e.scalar_tensor_tensor(
                    out=Kb[0:64, h, 0:T], in0=PS[b][e_ * 64:(e_ + 1) * 64, 0:T], scalar=vec[e_ * 64:(e_ + 1) * 64, 5:6],
                    in1=st["rstd"][e_ * 64:(e_ + 1) * 64, 0:T], op0=ALU.mult, op1=ALU.mult),
                    reads=[("ps", b), ("rstd", st["i"]), ("vec",)], writes=[("KT", ti % 2, h)])
        for h in range(8):
            P.pool(lambda e: e.tensor_copy(out=Kb[R, h, 0:T], in_=krT[R, 0:T]), reads=[("krT",)], writes=[(("KT", ti % 2, h), "r")])
        P.dma("sp", k.KTd[:, :, t0:t0 + T].rearrange("h p t -> p h t"), Kb[0:96, :, 0:T],
              reads=ktoks + [(kt_, "r") for kt_ in ktoks], writes=[("KTd", ti)])
        for s_ in range(T // 128):
            b = rot.next()
            mm_group(k, PS[b][:, 0:512], lambda kc: ckvT[:, kc, s_ * 128:(s_ + 1) * 128], lambda kc: wv[:, kc, :], 2,
                     [("wv",), ("ckvT", 0), ("ckvT", 1)], ("ps", b))
            vb = Vt[vcnt % 2]
            vtok = ("Vt", vcnt % 2)
            vcnt += 1
            src = PS[b][:, 0:512].rearrange("p (h two d) -> p h two d", two=2, d=64)
            dst = vb[:].rearrange("p (h two) d -> p h two d", two=2)
            P.dve(lambda e: e.tensor_copy(out=dst[:, :, 0, 0:64], in_=src[:, :, 0, :]), reads=[("ps", b)], writes=[vtok])
            P.dve(lambda e: e.tensor_copy(out=dst[:, :, 1, 64:128], in_=src[:, :, 1, :]), reads=[("ps", b)], writes=[(vtok, 1)])
            chunk = (t0 + s_ * 128) // 128
            P.dma("sp", k.Vd[:, :, chunk, :].rearrange("h p d -> p h d"), vb[:], reads=[vtok, (vtok, 1)], writes=[("Vd", chunk)])
        Qb = QT[ti % 2]
        qtoks = [("QT", ti % 2, h) for h in range(8)]
        for h in range(8):
            b = rot.next()
            mm_group(k, PS[b][0:96, 0:T], lambda kc: wq[:, h, kc, :], lambda kc: cqT[:, kc, 0:T], 3,
                     [("wq",)] + [("cqT", c) for c in range(3)], ("ps", b))
            bsw = None
            if lat:
                bsw = rot.next()
                mm_group(k, PS[bsw][0:96, 0:T], lambda kc: wq[:, 8 + h, kc, :], lambda kc: cqT[:, kc, 0:T], 3,
                         [("wq",)] + [("cqT", c) for c in range(3)], ("ps", bsw))
            sb, stok = next_sqs(w)
            square(b, 96, sb[0:96, 0:T], stok)
            st = next_stat(w)
            mm_group(k, PS[st["bank"]][0:96, 0:T], lambda kc: k.cbf[0:96, 3, 0:96], lambda kc: sb[0:96, 0:T], 1, [stok, ("cbf",)],
                     ("ps", st["bank"]))
            stats_rstd(k, st, R96, T, vec[0:96, 10:11], extra_reads=[("vec",)])
            nr = 96 if not lat else 64
            P.dve(lambda e: e.scalar_tensor_tensor(out=Qb[0:nr, h, 0:T], in0=PS[b][0:nr, 0:T], scalar=vec[0:nr, 8:9],
                                                   in1=st["rstd"][0:nr, 0:T], op0=ALU.mult, op1=ALU.mult),
                  reads=[("ps", b), ("rstd", st["i"]), ("vec",)], writes=[("QT", ti % 2, h)])
            if lat:
                rope_apply(k, w, b, bsw, R, T, vec[R, 8:9], vec[R, 9:10], st, rp, ("rope", ti % 2),
                           Qb[R, h, 0:T], [(("QT", ti % 2, h), "r")])
        P.dma("sp", k.QTd[:, :, t0:t0 + T].rearrange("h p t -> p h t"), Qb[0:96, :, 0:T],
              reads=qtoks + [(q_, "r") for q_ in qtoks], writes=[("QTd", ti)])
    A.release(m)


def rope_apply(k, w, bm, bsw, R, T, g, gsw, st, rp, rptok, out_ap, otoks):
    P = k.P
    j = w["rai"] % 2
    w["rai"] += 1
    ra, rb = w["rab"][j]
    rstd, rtok = st["rstd"], ("rstd", st["i"])
    P.dve(lambda e: e.scalar_tensor_tensor(out=ra[R, 0:T], in0=k.PS[bm][R, 0:T], scalar=g, in1=rstd[R, 0:T],
                                           op0=ALU.mult, op1=ALU.mult),
          reads=[("ps", bm), rtok, ("vec",)], writes=[("ra", j)])
    P.dve(lambda e: e.scalar_tensor_tensor(out=rb[R, 0:T], in0=k.PS[bsw][R, 0:T], scalar=gsw, in1=rstd[R, 0:T],
                                           op0=ALU.mult, op1=ALU.mult),
          reads=[("ps", bsw), rtok, ("vec",)], writes=[("rb", j)])
    P.pool(lambda e: e.tensor_tensor(out=ra[R, 0:T], in0=ra[R, 0:T], in1=rp[R, 0, 0:T], op=ALU.mult),
           reads=[("ra", j), rptok], writes=[("ra", j)])
    P.pool(lambda e: e.tensor_tensor(out=rb[R, 0:T], in0=rb[R, 0:T], in1=rp[R, 1, 0:T], op=ALU.mult),
           reads=[("rb", j), rptok], writes=[("rb", j)])
    P.dve(lambda e: e.tensor_tensor(out=out_ap, in0=ra[R, 0:T], in1=rb[R, 0:T], op=ALU.add),
          reads=[("ra", j), ("rb", j)], writes=otoks)


def next_stat(w):
    st = w["stats"][w["si"] % len(w["stats"])]
    st["bank"] = 6 + (w["si"] % 2)
    w["si"] += 1
    return st


def next_sqs(w):
    i = w["sqi"] % len(w["sqs"])
    w["sqi"] += 1
    return w["sqs"][i], ("sqs", i)


def attn_phase(k, mode):
    P, A, PS = k.P, k.A, k.PS
    m = A.mark()
    mla = (mode == "mla")
    Kt = [A.alloc(f"Kt{i}", [128, TALL], BF16) for i in range(4)]
    Vh = [A.alloc(f"Vh{i}", [128, 34, 128], BF16) for i in range(4)]
    Qt = [A.alloc(f"Qt{i}", [128, 2, 512], BF16) for i in range(2)]
    PT = [A.alloc(f"PT{i}", [128, 512], BF16) for i in range(4)]
    rec = A.alloc("rec", [128, 512], F32)
    attT = [A.alloc(f"attT{i}", [128, 512], BF16) for i in range(2)]
    qtiles = TILES if mla else TILES[1:]
    qc = 0
    sc = 0
    pc = 0
    ac = 0
    for hp in range(4):
        kb = (hp % 2) * 2
        if mla:
            for e_ in range(2):
                P.dma("sp", Kt[kb + e_][0:96, :], k.KTd[2 * hp + e_], writes=[("Kt", kb + e_)])
                P.dma("sp", Vh[kb + e_][:], k.Vd[2 * hp + e_], writes=[("Vh", kb + e_)])
        else:
            for e_ in range(2):
                P.dma("sp", Kt[kb + e_][:], k.KdupTd[hp // 2], writes=[("Kt", kb + e_)])
                P.dve(lambda e: e.memset(Kt[kb + e_][(1 - e_) * 64:(2 - e_) * 64, :], 0.0), writes=[("Kt", kb + e_)])
                P.dma("sp", Vh[kb + e_][:], k.VgD[(hp // 2) * 2 + e_], writes=[("Vh", kb + e_)])
        for (who, t0, T) in qtiles:
            chunks = [0, 1] if who == 1 else list(range(34))
            Qb = Qt[qc % 2]
            qtok = ("Qt", qc % 2)
            ab = attT[qc % 2]
            atok = ("attT", qc % 2)
            qc += 1
            if mla:
                P.dma("sp", Qb[0:96, :, 0:T], k.QTd[2 * hp:2 * hp + 2, :, t0:t0 + T].rearrange("h p t -> p h t"), writes=[qtok])
            else:
                P.dma("sp", Qb[:, 0, 0:T], k.QgTd[hp, :, t0:t0 + T], writes=[qtok])
            for e_ in range(2):
                if mla:
                    Ksb, ktok, r0, r1 = Kt[kb + e_], ("Kt", kb + e_), 0, 96
                    qap = Qb[0:96, e_, 0:T]
                else:
                    Ksb, ktok, r0, r1 = Kt[kb + e_], ("Kt", kb + e_), 0, 128
                    qap = Qb[:, 0, 0:T]
                Vsb, vtok = Vh[kb + e_], ("Vh", kb + e_)
                acc = 4 + (ac % 2)
                ac += 1
                n = len(chunks)
                pend = []
                LOOK = 3

                def do_pv(item):
                    pidx, pc_, ppb = item
                    P.pe(lambda e: e.matmul(PS[acc][:, 0:T], lhsT=Vsb[:, pc_, :], rhs=PT[ppb][:, 0:T],
                                            start=(pidx == 0), stop=(pidx == n - 1)),
                         reads=[vtok, ("PT", ppb)], writes=[("ps", acc)])
                for idx, c in enumerate(chunks):
                    sbk = sc % 4
                    sc += 1
                    P.pe(lambda e: e.matmul(PS[sbk][:, 0:T], lhsT=Ksb[r0:r1, c * 128:(c + 1) * 128], rhs=qap,
                                            start=True, stop=True),
                         reads=[ktok, qtok], writes=[("ps", sbk)])
                    pb = pc % 4
                    pc += 1
                    P.act(lambda e: e.activation(out=PT[pb][:, 0:T], in_=PS[sbk][:, 0:T], func=AF.Exp),
                          reads=[("ps", sbk)], writes=[("PT", pb)])
                    pend.append((idx, c, pb))
                    if len(pend) > LOOK:
                        do_pv(pend.pop(0))
                while pend:
                    do_pv(pend.pop(0))
                nlo, dlo = (0, 64) if e_ == 0 else (64, 0)
                P.dve(lambda e: e.reciprocal(out=rec[nlo:nlo + 64, 0:T], in_=PS[acc][dlo:dlo + 64, 0:T]),
                      reads=[("ps", acc)], writes=[("rec", e_)])
                P.dve(lambda e: e.tensor_tensor(out=ab[nlo:nlo + 64, 0:T], in0=PS[acc][nlo:nlo + 64, 0:T], in1=rec[nlo:nlo + 64, 0:T],
                                                op=ALU.mult),
                      reads=[("ps", acc), ("rec", e_)], writes=[(atok, e_)])
            P.dma("sp", k.attTd[hp, :, t0:t0 + T], ab[:, 0:T], reads=[(atok, 0), (atok, 1)], writes=[("attTd", hp, t0)])
    A.release(m)


def l0_conv_phase(k):
    P, A, PS = k.P, k.A, k.PS
    m = A.mark()
    vec = A.alloc("vec", [128, 32], F32)
    dw = A.alloc("dw", [128, 4, 31], F32)
    diag = A.alloc("diag", [128, 4, 31, 128], BF16)
    ybuf = [A.alloc(f"ybuf{i}", [128, 4, 544], BF16) for i in range(2)]
    cv = A.alloc("cv", [128, 4, 512], F32)
    cb = A.alloc("cb", [128, 4, 512], BF16)
    sqv = A.alloc("sqv", [128, 4, 512], BF16)
    co = [A.alloc(f"co{i}", [128, 4, 512], BF16) for i in range(2)]
    mean = A.alloc("mean", [128, 512], F32)
    msq = A.alloc("msq", [128, 512], F32)
    r1 = A.alloc("r1", [128, 512], F32)
    r2 = A.alloc("r2", [128, 512], F32)
    rstd = A.alloc("rstd", [128, 512], F32)
    t1 = [A.alloc(f"t1{i}", [128, 512], F32) for i in range(2)]
    identb = k.cbf[:, 0, :]
    P.dma("sp", vec[:], k.inp["l0_vec"], writes=[("vec",)])
    P.dma("sp", dw[:], k.inp["l0_dw"], writes=[("dw",)])
    for ch in range(4):
        for j in range(31):
            P.dve(lambda e: e.tensor_scalar(out=diag[:, ch, j, :], in0=identb, scalar1=dw[:, ch, j:j + 1], scalar2=None, op0=ALU.mult),
                  reads=[("dw",), ("cbf",)], writes=[("diag", ch, j)])
    rot = PsRot(banks=(0, 1, 2, 3))
    for ti, (who, t0, T) in enumerate(TILES):
        s0, s1 = (0, NCTX) if who == 1 else (NCTX, TALL)
        lo = max(t0 - 15, s0)
        hi = min(t0 + T + 15, s1)
        yb = ybuf[ti % 2]
        toks = [("yb", ti % 2, x) for x in "LMR"]
        wr = [toks[1]]
        if lo > t0 - 15:
            P.pool(lambda e: e.memset(yb[:, :, 0:15], 0.0), writes=[toks[0]])
        else:
            wr.append(toks[0])
        if hi < t0 + T + 15:
            P.pool(lambda e: e.memset(yb[:, :, T + 15:T + 30], 0.0), writes=[toks[2]])
        else:
            wr.append(toks[2])
        P.dma("sp", yb[:, :, lo - (t0 - 15):hi - (t0 - 15)], k.yTd[:, :, lo:hi].rearrange("c p t -> p c t"), writes=wr)
        for ch in range(4):
            b = rot.next()

            def mm(e, ch=ch, b=b):
                for j in range(31):
                    e.matmul(PS[b][:, 0:T], lhsT=diag[:, ch, j, :], rhs=yb[:, ch, j:j + T], start=(j == 0), stop=(j == 30))
            P.pe(mm, reads=toks + [("diag", ch, j) for j in range(31)], writes=[("ps", b)])
            P.act(lambda e: e.activation(out=cv[:, ch, 0:T], in_=PS[b][:, 0:T], func=AF.Identity, bias=vec[:, 19 + ch:20 + ch], scale=1.0),
                  reads=[("ps", b), ("vec",)], writes=[("cv", ch)])
            P.pool(lambda e: e.tensor_copy(out=cb[:, ch, 0:T], in_=cv[:, ch, 0:T]), reads=[("cv", ch)], writes=[("cb", ch)])
            P.pool(lambda e: e.tensor_tensor(out=sqv[:, ch, 0:T], in0=cv[:, ch, 0:T], in1=cv[:, ch, 0:T], op=ALU.mult),
                   reads=[("cv", ch)], writes=[("sqv", ch)])
        mm_group(k, PS[6][:, 0:T], lambda kc: k.ones_bf, lambda kc: cb[:, kc, 0:T], 4, [("cb", c) for c in range(4)] + [("cbf",)], ("ps", 6))
        mm_group(k, PS[7][:, 0:T], lambda kc: k.ones_bf, lambda kc: sqv[:, kc, 0:T], 4, [("sqv", c) for c in range(4)] + [("cbf",)], ("ps", 7))
        P.dve(lambda e: e.tensor_scalar(out=mean[:, 0:T], in0=PS[6][:, 0:T], scalar1=1.0 / 512, scalar2=None, op0=ALU.mult),
              reads=[("ps", 6)], writes=[("mean",)])
        P.pool(lambda e: e.tensor_tensor(out=msq[:, 0:T], in0=mean[:, 0:T], in1=mean[:, 0:T], op=ALU.mult),
               reads=[("mean",)], writes=[("msq",)])
        P.dve(lambda e: e.scalar_tensor_tensor(out=r2[:, 0:T], in0=PS[7][:, 0:T], scalar=1.0 / 512, in1=msq[:, 0:T],
                                               op0=ALU.mult, op1=ALU.subtract),
              reads=[("ps", 7), ("msq",)], writes=[("r2",)])
        P.dve(lambda e: e.tensor_scalar(out=r1[:, 0:T], in0=r2[:, 0:T], scalar1=EPS, scalar2=None, op0=ALU.add),
              reads=[("r2",)], writes=[("r1",)])
        P.act(lambda e: e.activation(out=r2[:, 0:T], in_=r1[:, 0:T], func=AF.Sqrt), reads=[("r1",)], writes=[("r2",)])
        P.dve(lambda e: e.reciprocal(out=rstd[:, 0:T], in_=r2[:, 0:T]), reads=[("r2",)], writes=[("rstd",)])
        cob = co[ti % 2]
        for ch in range(4):
            tb = t1[ch % 2]
            P.dve(lambda e: e.tensor_tensor(out=tb[:, 0:T], in0=cv[:, ch, 0:T], in1=mean[:, 0:T], op=ALU.subtract),
                  reads=[("cv", ch), ("mean",)], writes=[("t1", ch % 2)])
            P.pool(lambda e: e.tensor_tensor(out=tb[:, 0:T], in0=tb[:, 0:T], in1=rstd[:, 0:T], op=ALU.mult),
                   reads=[("t1", ch % 2), ("rstd",)], writes=[("t1", ch % 2)])
            P.act(lambda e: e.activation(out=cob[:, ch, 0:T], in_=tb[:, 0:T], func=AF.Silu, bias=vec[:, 27 + ch:28 + ch],
                                         scale=vec[:, 23 + ch:24 + ch]),
                  reads=[("t1", ch % 2), ("vec",)], writes=[("co", ti % 2, ch)])
        P.dma("sp", k.attTd[4:8, :, t0:t0 + T].rearrange("c p t -> p c t"), cob[:, :, 0:T],
              reads=[("co", ti % 2, ch) for ch in range(4)], writes=[("attTd", "conv", ti)])
    A.release(m)


def mixout_phase(k, l, tiles):
    P, A, PS = k.P, k.A, k.PS
    m = A.mark()
    wo = A.alloc("wo", [128, 8, 8, 128], BF16)
    xt = [A.alloc(f"xt{i}", [128, 8, 512], F32) for i in range(2)]
    at = [A.alloc(f"at{i}", [128, 8, 512], BF16) for i in range(2)]
    name = f"l{l}_wo"
    for i in range(8):
        P.dma("sp", wo[:, i], k.wbf[name][i].rearrange("p (k n) -> p k n", k=8), reads=[("wbf", name, i)], writes=[("wo",)])
    rot = PsRot()
    for ti, (who, t0, T) in enumerate(tiles):
        xb = xt[ti % 2]
        ab = at[ti % 2]
        xtoks = [("xt", ti % 2, c) for c in range(8)]
        P.dma("sp", xb[:, :, 0:T], k.xTd[:, :, t0:t0 + T].rearrange("c p t -> p c t"), writes=xtoks)
        P.dma("sp", ab[:, :, 0:T], k.attTd[:, :, t0:t0 + T].rearrange("c p t -> p c t"), writes=[("at", ti % 2)])
        for d in range(8):
            b = rot.next()
            mm_group(k, PS[b][:, 0:T], lambda kc: wo[:, d, kc, :], lambda kc: ab[:, kc, 0:T], 8, [("wo",), ("at", ti % 2)], ("ps", b))
            P.dve(lambda e: e.scalar_tensor_tensor(out=xb[:, d, 0:T], in0=PS[b][:, 0:T], scalar=k.modv[:, l, who, 40 + d:41 + d],
                                                   in1=xb[:, d, 0:T], op0=ALU.mult, op1=ALU.add),
                  reads=[("ps", b), xtoks[d]], writes=[xtoks[d]])
        P.dma("sp", k.xTd[:, :, t0:t0 + T].rearrange("c p t -> p c t"), xb[:, :, 0:T], reads=xtoks, writes=[("xTd", t0)])
    A.release(m)


def l1_proj_phase(k):
    P, A, PS = k.P, k.A, k.PS
    l = 1
    m = A.mark()
    w = common_work(k, A)
    xt = [A.alloc(f"xt{i}", [128, 8, 512], F32) for i in range(2)]
    hT = A.alloc("hTm", [128, 8, 512], BF16)
    wA = A.alloc("wA", [128, 20, 8, 128], BF16)
    wv = A.alloc("wv", [128, 8, 640], BF16)
    vec = A.alloc("vec", [128, 8], F32)
    rope = [A.alloc(f"rope{i}", [128, 2, 512], F32) for i in range(2)]
    KdT = [A.alloc(f"KdT{i}", [128, 2, 512], BF16) for i in range(2)]
    nkT = [A.alloc(f"nkT{i}", [128, 4, 512], BF16) for i in range(2)]
    QgT = [A.alloc(f"QgT{i}", [128, 4, 512], BF16) for i in range(2)]
    nqT = [A.alloc(f"nqT{i}", [128, 4, 512], BF16) for i in range(2)]
    Vgt = [A.alloc(f"Vgt{i}", [128, 4, 128], BF16) for i in range(2)]
    NVt = [A.alloc(f"NVt{i}", [128, 8, 128], BF16) for i in range(2)]
    P.dma("sp", vec[:], k.inp["l1_vec"], writes=[("vec",)])
    for i in range(20):
        P.dma("sp", wA[:, i], k.wbf["l1_wA"][i].rearrange("p (k n) -> p k n", k=8), reads=[("wbf", "l1_wA", i)], writes=[("wA",)])
    P.dma("sp", wv[:], k.wbf["l1_wv"][0].rearrange("p (k n) -> p k n", k=8), reads=[("wbf", "l1_wv", 0)], writes=[("wv",)])
    for c0, c1 in ((2, 4), (5, 6)):
        P.dve(lambda e: e.tensor_scalar(out=vec[:, c0:c1], in0=vec[:, c0:c1], scalar1=0.125, scalar2=None, op0=ALU.mult),
              reads=[("vec",)], writes=[("vec",)])
    for i in range(2):
        P.dve(lambda e: e.memset(Vgt[i][:], 1.0), writes=[("Vgt", i), (("Vgt", i), 1)])
        P.dve(lambda e: e.memset(NVt[i][:], 1.0), writes=[("NVt", i), (("NVt", i), 1)])
    rot = PsRot()
    cnt = {"sq": 0, "v": 0}
    Rall = slice(0, 128)
    for ti, (who, t0, T) in enumerate(TILES):
        lat = (who == 0)
        xb = xt[ti % 2]
        xtoks = [("xtm", ti % 2, c) for c in range(8)]
        load_x_norm(k, l, who, t0, T, xb, xtoks, hT, w)
        htoks = [("hTm", c) for c in range(8)]
        rp = rope[ti % 2]
        if lat:
            P.dma("sp", rp[:, :, 0:T], k.inp["ropeA"][:, :, t0 - NCTX:t0 - NCTX + T], writes=[("rope", ti % 2)])

        def projA(bidx):
            b = rot.next()
            mm_group(k, PS[b][:, 0:T], lambda kc: wA[:, bidx, kc, :], lambda kc: hT[:, kc, 0:T], 8, [("wA",)] + htoks, ("ps", b))
            return b

        def normed_chunk(bmain, bswap, gcol, out_ap, otok, do_rope):
            b = projA(bmain)
            bsw = projA(bswap) if do_rope else None
            sb, stok = next_sqs(w)
            P.act(lambda e: e.activation(out=sb[:, 0:T], in_=PS[b][:, 0:T], func=AF.Square), reads=[("ps", b)], writes=[stok])
            st = next_stat(w)
            mm_group(k, PS[st["bank"]][:, 0:T], lambda kc: k.bd64_bf, lambda kc: sb[:, 0:T], 1, [stok, ("cbf",)], ("ps", st["bank"]))
            stats_rstd(k, st, Rall, T, 1.0 / 64)
            if do_rope:
                rope_apply(k, w, b, bsw, Rall, T, vec[:, gcol:gcol + 1], vec[:, gcol + 1:gcol + 2], st, rp,
                           ("rope", ti % 2), out_ap, [otok])
            else:
                P.dve(lambda e: e.scalar_tensor_tensor(out=out_ap, in0=PS[b][:, 0:T], scalar=vec[:, gcol:gcol + 1],
                                                       in1=st["rstd"][:, 0:T], op0=ALU.mult, op1=ALU.mult),
                      reads=[("ps", b), ("rstd", st["i"]), ("vec",)], writes=[otok])

        Kb = KdT[ti % 2]
        for hk in range(2):
            normed_chunk(hk, 2 + hk, 0, Kb[:, hk, 0:T], ("KdT", ti % 2, hk), lat)
        P.dma("sp", k.KdupTd[:, :, t0:t0 + T].rearrange("h p t -> p h t"), Kb[:, :, 0:T],
              reads=[("KdT", ti % 2, hk) for hk in range(2)], writes=[("KdupTd", ti)])
        nb = nkT[ti % 2]
        for c in range(4):
            normed_chunk(4 + c, None, 4, nb[:, c, 0:T], ("nkT", ti % 2, c), False)
        P.dma("sp", k.nkTd[:, :, t0:t0 + T].rearrange("c p t -> p c t"), nb[:, :, 0:T],
              reads=[("nkT", ti % 2, c) for c in range(4)], writes=[("nkTd", ti)])
        for st in range(T // 128):
            b1 = rot.next()
            mm_group(k, PS[b1][:, 0:128], lambda kc: hT[:, kc, st * 128:(st + 1) * 128], lambda kc: wv[:, kc, 0:128], 8,
                     [("wv",)] + htoks, ("ps", b1))
            b2 = rot.next()
            mm_group(k, PS[b2][:, 0:512], lambda kc: hT[:, kc, st * 128:(st + 1) * 128], lambda kc: wv[:, kc, 128:640], 8,
                     [("wv",)] + htoks, ("ps", b2))
            vi = cnt["v"] % 2
            cnt["v"] += 1
            vg, nv = Vgt[vi], NVt[vi]
            vgd = vg[:].rearrange("p (hk par) d -> p hk par d", par=2)
            s1 = PS[b1][:, 0:128].rearrange("p (hk d) -> p hk d", d=64)
            P.dve(lambda e: e.tensor_copy(out=vgd[:, :, 0, 0:64], in_=s1), reads=[("ps", b1)], writes=[("Vgt", vi)])
            P.dve(lambda e: e.tensor_copy(out=vgd[:, :, 1, 64:128], in_=s1), reads=[("ps", b1)], writes=[(("Vgt", vi), 1)])
            nvd = nv[:].rearrange("p (h two) d -> p h two d", two=2)
            s2 = PS[b2][:, 0:512].rearrange("p (h two d) -> p h two d", two=2, d=64)
            P.dve(lambda e: e.tensor_copy(out=nvd[:, :, 0, 0:64], in_=s2[:, :, 0, :]), reads=[("ps", b2)], writes=[("NVt", vi)])
            P.dve(lambda e: e.tensor_copy(out=nvd[:, :, 1, 64:128], in_=s2[:, :, 1, :]), reads=[("ps", b2)],
                  writes=[(("NVt", vi), 1)])
            chunk = (t0 + st * 128) // 128
            P.dma("sp", k.VgD[:, :, chunk, :].rearrange("v p d -> p v d"), vg[:], reads=[("Vgt", vi), (("Vgt", vi), 1)],
                  writes=[("VgD", chunk)])
            P.dma("sp", k.NVd[:, :, chunk, :].rearrange("h p d -> p h d"), nv[:], reads=[("NVt", vi), (("NVt", vi), 1)],
                  writes=[("NVd", chunk)])
        if lat:
            qb = QgT[ti % 2]
            for c in range(4):
                normed_chunk(8 + c, 12 + c, 2, qb[:, c, 0:T], ("QgT", ti % 2, c), True)
            P.dma("sp", k.QgTd[:, :, t0:t0 + T].rearrange("c p t -> p c t"), qb[:, :, 0:T],
                  reads=[("QgT", ti % 2, c) for c in range(4)], writes=[("QgTd", ti)])
            nq = nqT[ti % 2]
            for c in range(4):
                normed_chunk(16 + c, None, 5, nq[:, c, 0:T], ("nqT", ti % 2, c), False)
            P.dma("sp", k.nqTd[:, :, t0:t0 + T].rearrange("c p t -> p c t"), nq[:, :, 0:T],
                  reads=[("nqT", ti % 2, c) for c in range(4)], writes=[("nqTd", ti)])
    A.release(m)


def l1_na_phase(k):
    P, A, PS = k.P, k.A, k.PS
    m = A.mark()
    nk = A.alloc("nk", [128, 4, TALL], BF16)
    NV = A.alloc("NV", [128, 8, 34, 128], BF16)
    nab = A.alloc("nab", [128, NSIG * 8, 128], BF16)
    nq = [[A.alloc(f"nq{i}{e_}", [128, 4, 128], BF16) for e_ in range(2)] for i in range(2)]
    PT = [A.alloc(f"PT{i}", [128, 8, 128], BF16) for i in range(3)]
    for i in range(2):
        for e_ in range(2):
            P.dve(lambda e: e.memset(nq[i][e_][:], 0.0), writes=[("nq", i, e_)])
    rec = A.alloc("rec", [128, 128], F32)
    nat = [A.alloc(f"nat{i}", [128, 4, 128], BF16) for i in range(2)]
    identb = k.cbf[:, 0, :]
    P.dma("sp", nk[:], k.nkTd.rearrange("c p t -> p c t"), writes=[("nk",)])
    for h in range(8):
        P.dma("sp", NV[:, h], k.NVd[h], writes=[("NV", h)])
    for g in range(8):
        n0, n1 = g * NSIG, (g + 1) * NSIG
        P.dma("sp", nab[:, n0:n1, :], k.wbf["l1_nab"][n0:n1].rearrange("n p q -> p n q"),
              reads=[("wbf", "l1_nab", i) for i in range(n0, n1)], writes=[("nab", g)])
    nabtoks = [("nab", g) for g in range(8)]
    hc = 0
    pending = []

    def pv_stage(i, h, chunks, n, acc, pt, pttok, natb):
        ch = h // 2
        t0 = NCTX + i * 128

        def pv(e):
            for idx, (c, sig) in enumerate(chunks):
                e.matmul(PS[acc][:, 0:128], lhsT=NV[:, h, c, :], rhs=pt[:, idx, :], start=(idx == 0), stop=(idx == n - 1))
        P.pe(pv, reads=[("NV", h), (pttok, 0), (pttok, 1)], writes=[("ps", acc)])
        nlo, dlo = (0, 64) if h % 2 == 0 else (64, 0)
        P.dve(lambda e: e.reciprocal(out=rec[nlo:nlo + 64, :], in_=PS[acc][dlo:dlo + 64, 0:128]),
              reads=[("ps", acc)], writes=[("rec", h % 2)])
        P.dve(lambda e: e.tensor_tensor(out=natb[nlo:nlo + 64, ch, :], in0=PS[acc][nlo:nlo + 64, 0:128], in1=rec[nlo:nlo + 64, :],
                                        op=ALU.mult),
              reads=[("ps", acc), ("rec", h % 2)], writes=[("nat", i % 2, h)])
        if h == 7:
            P.dma("sp", k.attTd[4:8, :, t0:t0 + 128].rearrange("c p t -> p c t"), natb[:],
                  reads=[("nat", i % 2, hh) for hh in range(8)], writes=[("attTd", "na", i)])

    for i in range(32):
        t0 = NCTX + i * 128
        nqb = nq[i % 2]
        natb = nat[i % 2]
        for e_ in range(2):
            P.dma("sp", nqb[e_][e_ * 64:(e_ + 1) * 64], k.nqTd[:, e_ * 64:(e_ + 1) * 64, t0:t0 + 128].rearrange("c p t -> p c t"),
                  writes=[("nq", i % 2, e_)])
        chunks = [(0, None), (1, None)] + [(2 + kc, NA_MAP[(i, kc)]) for kc in NA_CHUNKS[i]]
        n = len(chunks)
        for h in range(8):
            ch = h // 2
            sb0 = 2 * (hc % 3)
            acc = 6 + (hc % 2)
            pt = PT[hc % 3]
            pttok = ("PTn", hc % 3)
            hc += 1

            def smm(e):
                for idx, (c, sig) in enumerate(chunks):
                    o = PS[sb0 + idx // 4][:, (idx % 4) * 128:(idx % 4 + 1) * 128]
                    e.matmul(o, lhsT=nk[:, ch, c * 128:(c + 1) * 128], rhs=nqb[h % 2][:, ch, :],
                             start=True, stop=(sig is None))
                    if sig is not None:
                        e.matmul(o, lhsT=identb, rhs=nab[:, sig * 8 + h, :], start=False, stop=True)
            P.pe(smm, reads=[("nk",), ("nq", i % 2, h % 2), ("cbf",)] + nabtoks, writes=[("ps", sb0), ("ps", sb0 + 1)])
            P.act(lambda e: e.activation(out=pt[:, 0:4, :], in_=PS[sb0][:, 0:512].rearrange("p (c q) -> p c q", q=128), func=AF.Exp),
                  reads=[("ps", sb0)], writes=[(pttok, 0)])
            P.act(lambda e: e.activation(out=pt[:, 4:n, :], in_=PS[sb0 + 1][:, 0:(n - 4) * 128].rearrange("p (c q) -> p c q", q=128),
                                         func=AF.Exp),
                  reads=[("ps", sb0 + 1)], writes=[(pttok, 1)])
            pending.append((i, h, chunks, n, acc, pt, pttok, natb))
            if len(pending) > 1:
                pv_stage(*pending.pop(0))
    while pending:
        pv_stage(*pending.pop(0))
    A.release(m)
```
